# Optimizing a Trainium2 kernel written in Bass

```python
import math
import jax
import jax.numpy as jnp
from jax import lax
import numpy as np

D_MODEL = 1024
BATCH = 32
SEQ = 256
DEPTH = 4
DEC_BATCH = 4
DEC_SEQ = 1024
PAST_LEN = 512

GRID_W = 64
N_EVEN = (DEPTH + 1) // 2
N_ODD = DEPTH // 2
ATT_HEAD_DIM = 64
ATT_HEADS = D_MODEL // (2 * ATT_HEAD_DIM)
ATT_KV_HEADS = ATT_HEADS // 4
ATT_GROUP = ATT_HEADS // ATT_KV_HEADS
ATT_Q_DIM = ATT_HEADS * ATT_HEAD_DIM
ATT_KV_DIM = ATT_KV_HEADS * ATT_HEAD_DIM
DN_KEY_DIM = 128
DN_VAL_DIM = 128
DN_HEADS = D_MODEL // (2 * DN_VAL_DIM)
DN_QK_DIM = DN_HEADS * DN_KEY_DIM
DN_V_DIM = DN_HEADS * DN_VAL_DIM
SSM_INNER = 2 * D_MODEL
SSM_HEAD_DIM = 64
SSM_HEADS = SSM_INNER // SSM_HEAD_DIM
SSM_GROUPS = 8
SSM_STATE = 128
MLP_HIDDEN = 4 * D_MODEL
CONV_K = 3
CHUNK = 64
Q_BLOCK = 128
ROPE_THETA = 10000.0
NORM_EPS = 1e-6
DT_MIN = 0.001
DT_MAX = 0.1
EVEN_SPLITS = (ATT_Q_DIM, ATT_KV_DIM, ATT_KV_DIM, 2 * DN_QK_DIM + DN_V_DIM, DN_V_DIM, 2 * DN_HEADS, 2 * DN_HEADS)
ODD_SPLITS = (SSM_INNER, SSM_INNER + 2 * SSM_GROUPS * SSM_STATE, 2 * SSM_HEADS)
EVEN_IN = sum(EVEN_SPLITS)
ODD_IN = sum(ODD_SPLITS)
F32 = jnp.float32

kernel_name = 'hybrid_prefix_flow_gqa_gdn_ssd_step'


def _split(x, sizes):
    return jnp.split(x, [int(s) for s in np.cumsum(sizes)[:-1]], axis=-1)


def _flip(x):
    return jnp.flip(x, axis=1)


def _rms_norm(x, gain):
    xf = x.astype(F32)
    y = xf * lax.rsqrt(jnp.mean(xf * xf, axis=-1, keepdims=True) + NORM_EPS)
    return (y * gain.astype(F32)).astype(x.dtype)


def _l2_normalize(x):
    xf = x.astype(F32)
    return xf * lax.rsqrt(jnp.sum(xf * xf, axis=-1, keepdims=True) + NORM_EPS)


def _modulate(x, gain, shift, scale):
    return _rms_norm(x, gain) * (1 + scale) + shift


def _axial_rope_tables(n_tok):
    rows = n_tok // GRID_W
    t = jnp.arange(rows * GRID_W)
    row = (t // GRID_W).astype(F32)
    col = (t % GRID_W).astype(F32)
    axis_dim = ATT_HEAD_DIM // 2
    inv_freq = ROPE_THETA ** (-jnp.arange(0, axis_dim, 2, dtype=F32) / axis_dim)
    ang_row = row[:, None] * inv_freq
    ang_col = col[:, None] * inv_freq
    return jnp.cos(ang_row), jnp.sin(ang_row), jnp.cos(ang_col), jnp.sin(ang_col)


def _rotate(x, cos, sin):
    x1, x2 = jnp.split(x, 2, axis=-1)
    cos = cos[:, None, :]
    sin = sin[:, None, :]
    return jnp.concatenate([x1 * cos - x2 * sin, x2 * cos + x1 * sin], axis=-1)


def _apply_axial_rope(x, tables):
    cos_r, sin_r, cos_c, sin_c = tables
    x_row, x_col = jnp.split(x.astype(F32), 2, axis=-1)
    return jnp.concatenate([_rotate(x_row, cos_r, sin_r), _rotate(x_col, cos_c, sin_c)], axis=-1).astype(x.dtype)


def _centred_dwconv(x, w):
    pad = CONV_K // 2
    return lax.conv_general_dilated(
        x, w[:, None, :].astype(x.dtype), window_strides=(1,), padding=[(pad, pad)],
        dimension_numbers=('NWC', 'WIO', 'NWC'), feature_group_count=x.shape[-1])


def _blocked_attention(q, k, v):
    b, sq, kvh, grp, hd = q.shape
    n_blk = sq // Q_BLOCK
    q_blocks = jnp.moveaxis(q.reshape(b, n_blk, Q_BLOCK, kvh, grp, hd), 1, 0)
    scale = hd ** -0.5

    def attend(q_blk):
        s = jnp.einsum('bqhgd,bkhd->bhgqk', q_blk, k, preferred_element_type=F32) * scale
        p = jax.nn.softmax(s, axis=-1).astype(v.dtype)
        return jnp.einsum('bhgqk,bkhd->bqhgd', p, v)

    out = lax.map(attend, q_blocks)
    return jnp.moveaxis(out, 0, 1).reshape(b, sq, kvh, grp, hd)


def _gated_delta_chunked(q, k, v, g, beta, s0):
    b, t, h, dk = k.shape
    dv = v.shape[-1]
    n = t // CHUNK

    def to_chunks(x):
        return jnp.moveaxis(x.reshape(b, n, CHUNK, h, *x.shape[3:]), 3, 2)

    q, k, v, g, beta = (to_chunks(a) for a in (q, k, v, g, beta))
    gc = jnp.cumsum(g, axis=-1)
    causal = jnp.tril(jnp.ones((CHUNK, CHUNK), bool))
    strict = jnp.tril(jnp.ones((CHUNK, CHUNK), bool), -1)
    decay = jnp.exp(jnp.where(causal, gc[..., :, None] - gc[..., None, :], -jnp.inf))
    kb = k * beta[..., None]
    m = jnp.where(strict, jnp.einsum('bnhid,bnhjd->bnhij', kb, k) * decay, 0.0)
    eye = jnp.eye(CHUNK, dtype=F32)
    t_mat = lax.linalg.triangular_solve(eye + m, jnp.broadcast_to(eye, m.shape),
                                        left_side=True, lower=True, unit_diagonal=True)
    u = t_mat @ (v * beta[..., None])
    w = t_mat @ (kb * jnp.exp(gc)[..., None])
    qk = jnp.where(causal, jnp.einsum('bnhid,bnhjd->bnhij', q, k) * decay, 0.0)
    q_dec = q * jnp.exp(gc)[..., None]
    k_dec = k * jnp.exp(gc[..., -1:] - gc)[..., None]
    chunk_decay = jnp.exp(gc[..., -1])

    def step(s, inp):
        u_c, w_c, qk_c, qd_c, kd_c, dec_c = inp
        v_new = u_c - w_c @ s
        o_c = qd_c @ s + qk_c @ v_new
        s = s * dec_c[..., None, None] + jnp.einsum('bhcd,bhce->bhde', kd_c, v_new)
        return s, o_c

    xs = tuple(jnp.moveaxis(a, 1, 0) for a in (u, w, qk, q_dec, k_dec, chunk_decay))
    s_fin, o = lax.scan(step, s0, xs)
    o = jnp.moveaxis(jnp.moveaxis(o, 0, 1), 3, 2).reshape(b, t, h, dv)
    return o, s_fin


def _ssd_chunked(x, a, bm, cm, s0):
    b, t, nh, p = x.shape
    ng, ns = bm.shape[2:]
    r = nh // ng
    n = t // CHUNK
    x = x.reshape(b, n, CHUNK, ng, r, p)
    bm = bm.reshape(b, n, CHUNK, ng, ns)
    cm = cm.reshape(b, n, CHUNK, ng, ns)
    a = jnp.moveaxis(a.reshape(b, n, CHUNK, ng, r), 2, 4)
    acum = jnp.cumsum(a, axis=-1)
    causal = jnp.tril(jnp.ones((CHUNK, CHUNK), bool))
    lmat = jnp.exp(jnp.where(causal, acum[..., :, None] - acum[..., None, :], -jnp.inf))
    cb = jnp.einsum('bcige,bcjge->bcgij', cm, bm)
    y_diag = jnp.einsum('bcgrij,bcjgrp->bcigrp', cb[:, :, :, None] * lmat, x)
    decay_in = jnp.exp(acum[..., -1:] - acum)
    xd = x * jnp.moveaxis(decay_in, 4, 2)[..., None]
    chunk_states = jnp.einsum('bcjge,bcjgrp->bcgrpe', bm, xd)
    chunk_decay = jnp.exp(acum[..., -1])

    def step(s, inp):
        st, dec = inp
        return s * dec[..., None, None] + st, s

    s_fin, s_in = lax.scan(step, s0.reshape(b, ng, r, p, ns),
                           (jnp.moveaxis(chunk_states, 1, 0), jnp.moveaxis(chunk_decay, 1, 0)))
    s_in = jnp.moveaxis(s_in, 0, 1)
    y_off = jnp.einsum('bcige,bcgrpe->bcigrp', cm, s_in) * jnp.moveaxis(jnp.exp(acum), 4, 2)[..., None]
    return (y_diag + y_off).reshape(b, t, nh, p), s_fin.reshape(b, nh, p, ns)


def _even_mixer(h, w_in, q_gain, k_gain, conv_w, a_log, dt_bias, out_gain, w_out, ctx):
    b, n, _ = h.shape
    aq, ak, av, dqkv, dz, dbeta, dalpha = _split(h @ w_in, EVEN_SPLITS)
    q = _rms_norm(aq.reshape(b, n, ATT_HEADS, ATT_HEAD_DIM), q_gain)
    k = _rms_norm(ak.reshape(b, n, ATT_KV_HEADS, ATT_HEAD_DIM), k_gain)
    v = av.reshape(b, n, ATT_KV_HEADS, ATT_HEAD_DIM)
    if ctx is None:
        keys, vals = k, v
    else:
        ctx_k, ctx_v, ctx_s = ctx
        tables = _axial_rope_tables(n)
        q = _apply_axial_rope(q, tables)
        keys = jnp.concatenate([_apply_axial_rope(k, tables), ctx_k.astype(k.dtype)], axis=1)
        vals = jnp.concatenate([v, ctx_v.astype(v.dtype)], axis=1)
    o_att = _blocked_attention(q.reshape(b, n, ATT_KV_HEADS, ATT_GROUP, ATT_HEAD_DIM), keys, vals)
    o_att = o_att.reshape(b, n, ATT_Q_DIM)
    dqkv = jax.nn.silu(_centred_dwconv(dqkv, conv_w))
    dq, dk, dv = _split(dqkv, (DN_QK_DIM, DN_QK_DIM, DN_V_DIM))
    dq = _l2_normalize(dq.reshape(b, n, DN_HEADS, DN_KEY_DIM)) * (DN_KEY_DIM ** -0.5)
    dk = _l2_normalize(dk.reshape(b, n, DN_HEADS, DN_KEY_DIM))
    dv = dv.reshape(b, n, DN_HEADS, DN_VAL_DIM).astype(F32)
    beta = jax.nn.sigmoid(dbeta.reshape(b, n, 2, DN_HEADS).astype(F32))
    log_decay = -jnp.exp(a_log.astype(F32)) * jax.nn.softplus(
        dalpha.reshape(b, n, 2, DN_HEADS).astype(F32) + dt_bias.astype(F32))
    s0 = jnp.zeros((b, 2, DN_HEADS, DN_KEY_DIM, DN_VAL_DIM), F32) if ctx is None else ctx_s.astype(F32)
    o_f, s_f = _gated_delta_chunked(dq, dk, dv, log_decay[:, :, 0], beta[:, :, 0], s0[:, 0])
    o_b, s_b = _gated_delta_chunked(_flip(dq), _flip(dk), _flip(dv), _flip(log_decay[:, :, 1]),
                                    _flip(beta[:, :, 1]), s0[:, 1])
    o_dn = _rms_norm(o_f + _flip(o_b), out_gain) * jax.nn.silu(
        dz.reshape(b, n, DN_HEADS, DN_VAL_DIM).astype(F32))
    y = jnp.concatenate([o_att, o_dn.reshape(b, n, DN_V_DIM).astype(h.dtype)], axis=-1) @ w_out
    new_ctx = (k, v, jnp.stack([s_f, s_b], axis=1)) if ctx is None else None
    return y, new_ctx


def _odd_mixer(h, w_in, conv_w, conv_b, a_log, dt_bias, d_skip, out_gain, w_out, ctx):
    b, n, _ = h.shape
    z, xbc, dt_raw = _split(h @ w_in, ODD_SPLITS)
    xbc = jax.nn.silu(_centred_dwconv(xbc, conv_w) + conv_b)
    xs, bm, cm = _split(xbc.astype(F32), (SSM_INNER, SSM_GROUPS * SSM_STATE, SSM_GROUPS * SSM_STATE))
    xs = xs.reshape(b, n, SSM_HEADS, SSM_HEAD_DIM)
    bm = bm.reshape(b, n, SSM_GROUPS, SSM_STATE)
    cm = cm.reshape(b, n, SSM_GROUPS, SSM_STATE)
    dt = jax.nn.softplus(dt_raw.reshape(b, n, 2, SSM_HEADS).astype(F32) + dt_bias.astype(F32))
    a = -jnp.exp(a_log.astype(F32))
    s0 = jnp.zeros((b, 2, SSM_HEADS, SSM_HEAD_DIM, SSM_STATE), F32) if ctx is None else ctx.astype(F32)
    y_f, s_f = _ssd_chunked(xs * dt[:, :, 0, :, None], dt[:, :, 0] * a[0], bm, cm, s0[:, 0])
    y_b, s_b = _ssd_chunked(_flip(xs * dt[:, :, 1, :, None]), _flip(dt[:, :, 1] * a[1]),
                            _flip(bm), _flip(cm), s0[:, 1])
    y = y_f + _flip(y_b) + d_skip.astype(F32)[:, None] * xs
    y = _rms_norm(y.reshape(b, n, SSM_INNER) * jax.nn.silu(z.astype(F32)), out_gain)
    out = y.astype(h.dtype) @ w_out
    new_ctx = jnp.stack([s_f, s_b], axis=1) if ctx is None else None
    return out, new_ctx


def _sq_relu_mlp(h, w_in, w_out):
    a = jnp.maximum(h @ w_in, 0)
    return (a * a) @ w_out


def setup_inputs(seed: int = 0) -> dict:
    key = jax.random.key(seed)
    ks = iter(jax.random.split(key, 40))
    d = D_MODEL

    def nrm(shape, scale):
        return jax.random.normal(next(ks), shape, F32) * scale

    def gain(shape):
        return 1.0 + nrm(shape, 0.02)

    def a_log(shape):
        return jnp.log(jax.random.uniform(next(ks), shape, F32, 1.0, 16.0))

    def dt_bias(shape):
        dt = jnp.exp(jax.random.uniform(next(ks), shape, F32, math.log(DT_MIN), math.log(DT_MAX)))
        return dt + jnp.log(-jnp.expm1(-dt))

    return {
        'x_prompt': nrm((BATCH, SEQ, d), 1.0),
        'x_sample': nrm((DEC_BATCH, DEC_SEQ, d), 1.0),
        'c': nrm((DEC_BATCH, d), 1.0),
        'cache_attn_k': nrm((DEC_BATCH, N_EVEN, PAST_LEN, ATT_KV_HEADS, ATT_HEAD_DIM), 1.0),
        'cache_attn_v': nrm((DEC_BATCH, N_EVEN, PAST_LEN, ATT_KV_HEADS, ATT_HEAD_DIM), 1.0),
        'state_delta': nrm((DEC_BATCH, N_EVEN, 2, DN_HEADS, DN_KEY_DIM, DN_VAL_DIM), 0.1),
        'state_ssm': nrm((DEC_BATCH, N_ODD, 2, SSM_HEADS, SSM_HEAD_DIM, SSM_STATE), 0.1),
        'c_ctx': nrm((d,), 1.0),
        'norm_mix_g': gain((DEPTH, d)),
        'norm_mlp_g': gain((DEPTH, d)),
        'w_mod': nrm((DEPTH, d, 6 * d), 0.5 * d ** -0.5),
        'b_mod': nrm((DEPTH, 6 * d), 0.01),
        'w_mlp_in': nrm((DEPTH, d, MLP_HIDDEN), d ** -0.5),
        'w_mlp_out': nrm((DEPTH, MLP_HIDDEN, d), MLP_HIDDEN ** -0.5),
        'w_in_even': nrm((N_EVEN, d, EVEN_IN), d ** -0.5),
        'attn_q_norm_g': gain((N_EVEN, ATT_HEAD_DIM)),
        'attn_k_norm_g': gain((N_EVEN, ATT_HEAD_DIM)),
        'delta_conv_w': nrm((N_EVEN, CONV_K, 2 * DN_QK_DIM + DN_V_DIM), CONV_K ** -0.5),
        'delta_a_log': a_log((N_EVEN, 2, DN_HEADS)),
        'delta_dt_bias': dt_bias((N_EVEN, 2, DN_HEADS)),
        'delta_norm_g': gain((N_EVEN, DN_VAL_DIM)),
        'w_out_even': nrm((N_EVEN, ATT_Q_DIM + DN_V_DIM, d), (ATT_Q_DIM + DN_V_DIM) ** -0.5),
        'w_in_odd': nrm((N_ODD, d, ODD_IN), d ** -0.5),
        'ssm_conv_w': nrm((N_ODD, CONV_K, SSM_INNER + 2 * SSM_GROUPS * SSM_STATE), CONV_K ** -0.5),
        'ssm_conv_b': nrm((N_ODD, SSM_INNER + 2 * SSM_GROUPS * SSM_STATE), 0.01),
        'ssm_a_log': a_log((N_ODD, 2, SSM_HEADS)),
        'ssm_dt_bias': dt_bias((N_ODD, 2, SSM_HEADS)),
        'ssm_d': 1.0 + nrm((N_ODD, SSM_HEADS), 0.1),
        'ssm_norm_g': gain((N_ODD, SSM_INNER)),
        'w_out_odd': nrm((N_ODD, SSM_INNER, d), SSM_INNER ** -0.5),
        'final_norm_g': gain((d,)),
    }


def reference(x_prompt, x_sample, c, cache_attn_k, cache_attn_v, state_delta, state_ssm, c_ctx,
              norm_mix_g, norm_mlp_g, w_mod, b_mod, w_mlp_in, w_mlp_out,
              w_in_even, attn_q_norm_g, attn_k_norm_g, delta_conv_w, delta_a_log, delta_dt_bias,
              delta_norm_g, w_out_even,
              w_in_odd, ssm_conv_w, ssm_conv_b, ssm_a_log, ssm_dt_bias, ssm_d, ssm_norm_g, w_out_odd,
              final_norm_g):

    def run_trunk(x, cond, caches):
        ctx_path = caches is None
        ks, vs, sds, sss = [], [], [], []
        for layer in range(DEPTH):
            j = layer // 2
            mod = (jax.nn.silu(cond) @ w_mod[layer] + b_mod[layer])[:, None, :]
            sh1, sc1, g1, sh2, sc2, g2 = jnp.split(mod, 6, axis=-1)
            h = _modulate(x, norm_mix_g[layer], sh1, sc1)
            if layer % 2 == 0:
                ctx = None if ctx_path else (caches[0][:, j], caches[1][:, j], caches[2][:, j])
                y, st = _even_mixer(h, w_in_even[j], attn_q_norm_g[j], attn_k_norm_g[j], delta_conv_w[j],
                                    delta_a_log[j], delta_dt_bias[j], delta_norm_g[j], w_out_even[j], ctx)
                if ctx_path:
                    ks.append(st[0])
                    vs.append(st[1])
                    sds.append(st[2])
            else:
                ctx = None if ctx_path else caches[3][:, j]
                y, st = _odd_mixer(h, w_in_odd[j], ssm_conv_w[j], ssm_conv_b[j], ssm_a_log[j],
                                   ssm_dt_bias[j], ssm_d[j], ssm_norm_g[j], w_out_odd[j], ctx)
                if ctx_path:
                    sss.append(st)
            x = x + g1 * y
            h = _modulate(x, norm_mlp_g[layer], sh2, sc2)
            x = x + g2 * _sq_relu_mlp(h, w_mlp_in[layer], w_mlp_out[layer])
        return _rms_norm(x, final_norm_g), ks, vs, sds, sss

    y_prompt, ks, vs, sds, sss = run_trunk(x_prompt, c_ctx[None, :], None)
    out_dtype = x_prompt.dtype
    new_cache_attn_k = jnp.stack(ks, axis=1).astype(out_dtype)
    new_cache_attn_v = jnp.stack(vs, axis=1).astype(out_dtype)
    new_state_delta = jnp.stack(sds, axis=1).astype(out_dtype)
    new_state_ssm = jnp.stack(sss, axis=1).astype(out_dtype)
    y_sample, _, _, _, _ = run_trunk(x_sample, c, (cache_attn_k, cache_attn_v, state_delta, state_ssm))
    return (y_prompt, y_sample, new_cache_attn_k, new_cache_attn_v, new_state_delta, new_state_ssm)
```

```python
import math
import numpy as np
from contextlib import ExitStack
import concourse.bass as bass
import concourse.mybir as mybir
from concourse.bass_utils import run_bass_kernel_spmd

F32 = mybir.dt.float32
BF16 = mybir.dt.bfloat16
AF = mybir.ActivationFunctionType
ALU = mybir.AluOpType

D = 1024
DEPTH = 4
EPS = 1e-6
NEG = -30000.0


class Dep:
    __slots__ = ("w", "r", "excl")

    def __init__(self):
        self.w = None
        self.r = {}
        self.excl = False


class V:
    __slots__ = ("ap", "deps")

    def __init__(self, ap, deps):
        self.ap = ap
        self.deps = deps

    def __getitem__(self, k):
        return V(self.ap[k], self.deps)

    def re(self, pat, **kw):
        return V(self.ap.rearrange(pat, **kw), self.deps)

    def bc(self, shape):
        return V(self.ap.to_broadcast(list(shape)), self.deps)

    def un(self, axis):
        return V(self.ap.unsqueeze(axis), self.deps)


class _P:
    __slots__ = ("t", "deps")

    def __init__(self, t, deps):
        self.t = t
        self.deps = deps

    def __getitem__(self, k):
        return V(self.t[k], self.deps)


class T:
    def __init__(self, t, nparts=1):
        self.t = t
        self.deps = [Dep() for _ in range(nparts)]

    def __getitem__(self, k):
        return V(self.t[k], self.deps)

    def p(self, *idx):
        return _P(self.t, [self.deps[i] for i in idx])


class Rot:
    def __init__(self, items):
        self.items = items
        self.i = 0

    def next(self):
        x = self.items[self.i]
        self.i = (self.i + 1) % len(self.items)
        return x


def _ap(x):
    return x.ap if isinstance(x, V) else x


def _deps(*xs):
    out = []
    for x in xs:
        if isinstance(x, V):
            out.extend(x.deps)
    return out


class FW:
    def __init__(self, nc, stack, n_io=16, n_w=4):
        self.nc = nc
        self.stack = stack
        self.engs = {"pe": nc.tensor, "act": nc.scalar, "dve": nc.vector, "pool": nc.gpsimd, "sp": nc.sync}
        self.sems = {}
        self.cnt = {}
        for k in ["pe", "act", "dve"]:
            self.sems[k] = stack.enter_context(nc.semaphore("s_" + k))
            self.cnt[k] = 0
        self.io_ch = []
        for i in range(n_io):
            k = "io%d" % i
            self.sems[k] = stack.enter_context(nc.semaphore("s_" + k))
            self.cnt[k] = 0
            self.io_ch.append(k)
        self.w_ch = []
        for i in range(n_w):
            k = "w%d" % i
            self.sems[k] = stack.enter_context(nc.semaphore("s_" + k))
            self.cnt[k] = 0
            self.w_ch.append(k)
        self.nio = 0
        self.nw = 0
        self.waited = {k: {} for k in self.engs}
        self.n_instr = 0
        self.prog = {k: [] for k in self.engs}

    def flush(self):
        prog = self.prog
        self.prog = {k: [] for k in self.engs}
        with self.nc.Block() as block:
            def mk(lst):
                def body(e):
                    for f in lst:
                        f(e)
                return body
            block.sync(mk(prog["sp"]))
            block.tensor(mk(prog["pe"]))
            block.scalar(mk(prog["act"]))
            block.vector(mk(prog["dve"]))
            block.gpsimd(mk(prog["pool"]))

    def sb(self, name, shape, dtype, nparts=1, stack=None):
        st = stack or self.stack
        self.uid = getattr(self, "uid", 0) + 1
        return T(st.enter_context(self.nc.sbuf_tensor("%s_%d" % (name, self.uid), list(shape), dtype)), nparts)

    def ps(self, name, shape, dtype, nparts=1):
        t = T(self.stack.enter_context(self.nc.psum_tensor(name, list(shape), dtype)), nparts)
        for d in t.deps:
            d.excl = True
        return t

    def _wait(self, issuer, key, val):
        if val <= 0:
            return
        w = self.waited[issuer]
        if w.get(key, 0) >= val:
            return
        w[key] = val
        sem = self.sems[key]
        self.prog[issuer].append(lambda e: e.wait_ge(sem, val))

    def _gather(self, reads, writes):
        need = {}
        for d in reads:
            if d.w is not None:
                k, c = d.w
                if need.get(k, 0) < c:
                    need[k] = c
            if d.excl:
                for k, c in d.r.items():
                    if need.get(k, 0) < c:
                        need[k] = c
        for d in writes:
            if d.w is not None:
                k, c = d.w
                if need.get(k, 0) < c:
                    need[k] = c
            for k, c in d.r.items():
                if need.get(k, 0) < c:
                    need[k] = c
        return need

    def _commit(self, key, val, reads, writes):
        for d in writes:
            d.w = (key, val)
            d.r = {}
        for d in reads:
            if d.r.get(key, 0) < val:
                d.r[key] = val

    def op(self, eng, fn, reads=(), writes=()):
        need = self._gather(reads, writes)
        for k, c in need.items():
            if k == eng and eng == "pe":
                continue
            self._wait(eng, k, c)
        sem = self.sems[eng]
        self.prog[eng].append(lambda e: fn(e).then_inc(sem, 1))
        self.cnt[eng] += 1
        self._commit(eng, self.cnt[eng], reads, writes)
        self.n_instr += 1

    def dma(self, out, in_, q="sp"):
        reads = _deps(in_)
        writes = _deps(out)
        if q == "pool":
            ch = self.w_ch[self.nw % len(self.w_ch)]
            self.nw += 1
        else:
            ch = self.io_ch[self.nio % len(self.io_ch)]
            self.nio += 1
        need = self._gather(reads, writes)
        need[ch] = max(need.get(ch, 0), self.cnt[ch])
        for k, c in need.items():
            self._wait(q, k, c)
        sem = self.sems[ch]
        o, i = _ap(out), _ap(in_)
        self.prog[q].append(lambda e: e.dma_start(out=o, in_=i).then_inc(sem, 16))
        self.cnt[ch] += 16
        self._commit(ch, self.cnt[ch], reads, writes)
        self.n_instr += 1

    def barrier(self, final=False):
        keys = ["pe", "act", "dve"] + self.io_ch + (self.w_ch if final else [])
        for issuer in ["sp", "pe", "act", "dve"] + (["pool"] if final else []):
            for k in keys:
                if k != issuer:
                    self._wait(issuer, k, self.cnt[k])

    def mm(self, out, lhsT, rhs, start=True, stop=True):
        o, l, r = _ap(out), _ap(lhsT), _ap(rhs)
        self.op("pe", lambda e: e.matmul(o, lhsT=l, rhs=r, start=start, stop=stop),
                _deps(lhsT, rhs), _deps(out))

    def tr(self, out, in_, ident):
        o, i, d = _ap(out), _ap(in_), _ap(ident)
        self.op("pe", lambda e: e.transpose(o, i, d), _deps(in_, ident), _deps(out))

    def act(self, out, in_, func, bias=None, scale=None, accum=None):
        o, i = _ap(out), _ap(in_)
        kw = {}
        if bias is not None:
            kw["bias"] = _ap(bias)
        if scale is not None:
            kw["scale"] = _ap(scale)
        if accum is not None:
            kw["accum_out"] = _ap(accum)
        self.op("act", lambda e: e.activation(out=o, in_=i, func=func, **kw),
                _deps(in_, bias, scale), _deps(out, accum))

    def tt(self, out, in0, in1, op, eng="dve"):
        o, a, b = _ap(out), _ap(in0), _ap(in1)
        self.op(eng, lambda e: e.tensor_tensor(out=o, in0=a, in1=b, op=op), _deps(in0, in1), _deps(out))

    def ts(self, out, in0, s1, op0, s2=None, op1=None, eng="dve"):
        o, a, x1, x2 = _ap(out), _ap(in0), _ap(s1), _ap(s2)
        if op1 is None:
            self.op(eng, lambda e: e.tensor_scalar(out=o, in0=a, scalar1=x1, scalar2=None, op0=op0),
                    _deps(in0, s1), _deps(out))
        else:
            self.op(eng, lambda e: e.tensor_scalar(out=o, in0=a, scalar1=x1, scalar2=x2, op0=op0, op1=op1),
                    _deps(in0, s1, s2), _deps(out))

    def stt(self, out, in0, scalar, in1, op0, op1):
        o, a, s, b = _ap(out), _ap(in0), _ap(scalar), _ap(in1)
        self.op("dve", lambda e: e.scalar_tensor_tensor(out=o, in0=a, scalar=s, in1=b, op0=op0, op1=op1),
                _deps(in0, scalar, in1), _deps(out))

    def cp(self, eng, out, in_):
        o, i = _ap(out), _ap(in_)
        if eng == "act":
            self.op("act", lambda e: e.activation(out=o, in_=i, func=AF.Copy), _deps(in_), _deps(out))
        else:
            self.op("dve", lambda e: e.tensor_copy(out=o, in_=i), _deps(in_), _deps(out))

    def recip(self, out, in_):
        o, i = _ap(out), _ap(in_)
        self.op("dve", lambda e: e.reciprocal(out=o, in_=i), _deps(in_), _deps(out))

    def memset(self, out, val):
        o = _ap(out)
        self.op("dve", lambda e: e.memset(o, val), (), _deps(out))


VEC_LAYOUT = [("gmix", 32), ("gmlp", 32), ("bmod", 192), ("fing", 8), ("qkg", 4), ("dconv", 72),
              ("dalog", 16), ("ddtb", 16), ("dng", 2), ("sconv", 256), ("salog", 128), ("sdtb", 128),
              ("sd", 64), ("sng", 32)]
VOFF = {}
_o = 0
for _n, _c in VEC_LAYOUT:
    VOFF[_n] = (_o, _c)
    _o += _c
NVEC = _o

CST_LAYOUT = [("ident", 128), ("ones", 128), ("blk", 128), ("triF", 128), ("triB", 128), ("ind0", 128),
              ("ind1", 128), ("nmF", 256), ("nmB", 256), ("nmFc", 128), ("nmBc", 128), ("ropeR", 64)]
COFF = {}
_o = 0
for _n, _c in CST_LAYOUT:
    COFF[_n] = (_o, _c)
    _o += _c
NCST = _o


def make_consts():
    c = np.zeros((128, NCST), np.float32)
    idx = np.arange(128)
    j = idx[:, None]
    i = idx[None, :]
    same = (j // 64) == (i // 64)

    def put(name, a):
        o, n = COFF[name]
        c[:a.shape[0], o:o + a.shape[1]] = a

    put("ident", np.eye(128, dtype=np.float32))
    put("ones", np.ones((128, 128), np.float32))
    put("blk", same.astype(np.float32))
    put("triF", (same & (j <= i)).astype(np.float32))
    put("triB", (same & (j >= i)).astype(np.float32))
    put("ind0", np.broadcast_to((idx < 64)[:, None], (128, 128)).astype(np.float32))
    put("ind1", np.broadcast_to((idx >= 64)[:, None], (128, 128)).astype(np.float32))
    fc = np.where(same & (i >= j), 0.0, NEG).astype(np.float32)
    fs = np.where(same & (i > j), 0.0, NEG).astype(np.float32)
    bc = np.where(same & (i <= j), 0.0, NEG).astype(np.float32)
    bs = np.where(same & (i < j), 0.0, NEG).astype(np.float32)
    put("nmF", np.concatenate([fc, fs], axis=1))
    put("nmB", np.concatenate([bc, bs], axis=1))
    put("nmFc", fc)
    put("nmBc", bc)
    R = np.zeros((64, 64), np.float32)
    for base in (0, 32):
        for t in range(16):
            R[base + 16 + t, base + t] = -1.0
            R[base + t, base + 16 + t] = 1.0
    put("ropeR", R)
    return c


def make_rope():
    t = np.arange(1024)
    row = (t // 64).astype(np.float32)
    col = (t % 64).astype(np.float32)
    inv = (10000.0 ** (-np.arange(0, 32, 2, dtype=np.float32) / 32)).astype(np.float32)
    ar = row[None, :] * inv[:, None]
    ac = col[None, :] * inv[:, None]
    ang = np.concatenate([ar, ar, ac, ac], axis=0)
    return np.stack([np.cos(ang), np.sin(ang)], axis=1).astype(np.float32)


def even_col_perm():
    cols = []
    for g2 in range(2):
        for hh in range(4):
            h = 4 * g2 + hh
            cols += list(range(h * 64, h * 64 + 64))
        cols += list(range(512 + g2 * 64, 512 + g2 * 64 + 64))
        cols += list(range(640 + g2 * 64, 640 + g2 * 64 + 64))
    for hd in range(4):
        cols += list(range(768 + hd * 128, 768 + hd * 128 + 128))
        cols += list(range(768 + 512 + hd * 128, 768 + 512 + hd * 128 + 128))
        cols += list(range(768 + 1024 + hd * 128, 768 + 1024 + hd * 128 + 128))
        cols += list(range(2304 + hd * 128, 2304 + hd * 128 + 128))
        cols += [2816 + hd, 2816 + 4 + hd, 2824 + hd, 2824 + 4 + hd]
        cols += [-1] * 124
    return np.array(cols)


def odd_col_perm():
    cols = []
    for sg in range(8):
        cols += list(range(2048 + sg * 256, 2048 + sg * 256 + 256))
        cols += list(range(4096 + sg * 128, 4096 + sg * 128 + 128))
        cols += list(range(5120 + sg * 128, 5120 + sg * 128 + 128))
        cols += list(range(sg * 256, sg * 256 + 256))
        cols += list(range(6144 + sg * 4, 6144 + sg * 4 + 4))
        cols += list(range(6144 + 32 + sg * 4, 6144 + 32 + sg * 4 + 4))
        cols += [-1] * 56
    return np.array(cols)


def permute_cols(w, perm):
    w = np.asarray(w, np.float32)
    out = np.zeros(w.shape[:-1] + (len(perm),), np.float32)
    m = perm >= 0
    out[..., m] = w[..., perm[m]]
    return out


def colmajor(v):
    v = np.asarray(v, np.float32)
    lead = v.shape[:-1]
    C = v.shape[-1] // 128
    a = v.reshape(lead + (C, 128))
    a = np.moveaxis(a, -1, 0)
    return np.ascontiguousarray(a).reshape(128, -1)


def bcast_rows(v):
    v = np.asarray(v, np.float32).reshape(1, -1)
    return np.ascontiguousarray(np.broadcast_to(v, (128, v.shape[1])))


def make_vecs(inp):
    vec = np.zeros((128, NVEC), np.float32)

    def put(name, a):
        o, n = VOFF[name]
        assert a.shape[1] == n, (name, a.shape, n)
        vec[:a.shape[0], o:o + n] = a

    put("gmix", colmajor(inp["norm_mix_g"]))
    put("gmlp", colmajor(inp["norm_mlp_g"]))
    put("bmod", colmajor(inp["b_mod"]))
    put("fing", colmajor(inp["final_norm_g"]))
    qk = np.stack([inp["attn_q_norm_g"], inp["attn_k_norm_g"]], axis=1)
    put("qkg", np.ascontiguousarray(np.moveaxis(qk, -1, 0)).reshape(64, 4))
    dc = np.asarray(inp["delta_conv_w"], np.float32).reshape(2, 3, 3, 4, 128)
    dc = np.transpose(dc, (4, 0, 3, 2, 1))
    put("dconv", np.ascontiguousarray(dc).reshape(128, 72))
    put("dalog", bcast_rows(inp["delta_a_log"]))
    put("ddtb", bcast_rows(inp["delta_dt_bias"]))
    put("dng", np.ascontiguousarray(np.asarray(inp["delta_norm_g"], np.float32).T))
    cw = np.asarray(inp["ssm_conv_w"], np.float32)
    cb = np.asarray(inp["ssm_conv_b"], np.float32)
    wb = np.concatenate([cw, cb[:, None, :]], axis=1)
    sc = np.zeros((128, 2, 8, 4, 4), np.float32)
    for sg in range(8):
        chans = [(sg * 256, 128), (sg * 256 + 128, 128), (2048 + sg * 128, 128), (3072 + sg * 128, 128)]
        for ci, (c0, n) in enumerate(chans):
            sc[:, :, sg, ci, :] = np.transpose(wb[:, :, c0:c0 + 128], (2, 0, 1))
    put("sconv", sc.reshape(128, 256))
    put("salog", bcast_rows(inp["ssm_a_log"]))
    put("sdtb", bcast_rows(inp["ssm_dt_bias"]))
    put("sd", bcast_rows(inp["ssm_d"]))
    put("sng", colmajor(inp["ssm_norm_g"]))
    return vec


import os as _os0
SILU = AF.Identity if "silu" in _os0.environ.get("DBGSKIP", "") else AF.Silu


def build(n_layers=DEPTH, stop=None):
    nc = bass.Bass("TRN2", target_bir_lowering=False)
    dr = {}

    def din(name, shape):
        dr[name] = nc.dram_tensor(name, list(shape), F32, kind="ExternalInput").ap()

    def dout(name, shape):
        dr[name] = nc.dram_tensor(name, list(shape), F32, kind="ExternalOutput").ap()

    din("xin", [2048, D])
    din("condT", [128, 16])
    din("consts", [128, NCST])
    din("rope", [64, 2048])
    din("vecs", [128, NVEC])
    din("ctxk", [2, 512, 128])
    din("ctxv", [2, 512, 128])
    din("sd0", [2, 2, 4, 128, 128])
    din("ss0", [2, 2, 16, 128, 128])
    din("w_mod", [DEPTH, D, 6 * D])
    din("w_mlp_in", [DEPTH, D, 4 * D])
    din("w_mlp_out", [DEPTH, 4 * D, D])
    din("wie", [2, D, 3328])
    din("w_out_even", [2, D, D])
    din("wio", [2, D, 6656])
    din("w_out_odd", [2, 2 * D, D])
    dout("y", [2048, D])
    dout("nk", [2, 1024, 128])
    dout("nv", [2, 1024, 128])
    dout("nsd", [4, 2, 2, 4, 128, 128])
    dout("nss", [4, 2, 2, 16, 128, 128])

    with ExitStack() as st:
        fw = FW(nc, st)
        dv = lambda name: V(dr[name], [])

        import os as _os
        _pad = int(_os.environ.get("DBGPAD", "0"))
        if _pad:
            fw.sb("dbgpad", [128, _pad * 256], F32)
        xT = fw.sb("xT", [128, 8, 2048], F32, 32)
        cst = fw.sb("cst", [128, NCST], F32)
        vec = fw.sb("vec", [128, NVEC], F32)
        identb = fw.sb("identb", [128, 128], BF16)
        onesb = fw.sb("onesb", [128, 128], BF16)
        scT = fw.sb("scT", [128, 8, 2], BF16)
        modv = fw.sb("modv", [128, 48, 2], F32)
        modA = fw.sb("modA", [128, 2, 8, 2], F32)
        ring = Rot([fw.sb("wr%d" % i, [128, 4096], BF16, 2) for i in range(3)])
        mmp = Rot([fw.ps("pm%d" % i, [128, 512], F32) for i in range(4)])
        accp = Rot([fw.ps("pa%d" % i, [128, 512], F32) for i in range(2)])
        trp = Rot([fw.ps("pt%d" % i, [128, 1024], BF16) for i in range(2)])

        def C(name, rows=128, c0=0, c1=None):
            o, n = COFF[name]
            c1 = n if c1 is None else c1
            return cst[0:rows, o + c0:o + c1]

        def VC(name, col, rows=128, n=1):
            o, _ = VOFF[name]
            return vec[0:rows, o + col:o + col + n]

        def xv(kc, tt):
            return xT.p(kc * 4 + tt)[:, kc, tt * 512:(tt + 1) * 512]

        def wload(src, KC, cols):
            t = ring.next()
            h = KC // 2
            s3 = src.rearrange("(kc p) c -> p kc c", p=128)
            for a in range(2):
                dst = t.p(a)[:, a * h * cols:(a + 1) * h * cols].re("p (kc c) -> p kc c", kc=h)
                fw.dma(dst, V(s3[:, a * h:(a + 1) * h, :], []), q="pool")
            return t[:, 0:KC * cols].re("p (kc c) -> p kc c", kc=KC)

        evac_i = [0]

        def evac(out, in_):
            evac_i[0] ^= 1
            fw.cp("act" if evac_i[0] else "dve", out, in_)

        fw.dma(cst[:, :], dv("consts"))
        fw.dma(vec[:, :], dv("vecs"))
        fw.cp("act", identb[:, :], C("ident"))
        fw.cp("dve", onesb[:, :], C("ones"))
        with ExitStack() as ph:
            ctmp = fw.sb("ctmp", [128, 16], F32, stack=ph)
            fw.dma(ctmp[:, :], dv("condT"))
            fw.act(scT[:, :, :].re("p a b -> p (a b)"), ctmp[:, :], SILU)
            xst = Rot([fw.sb("xst%d" % i, [128, D], F32, stack=ph) for i in range(2)])
            for ti in range(16):
                s = xst.next()
                fw.dma(s[:, :], dv("xin")[ti * 128:(ti + 1) * 128, :])
                for half in range(2):
                    ps = mmp.next()
                    for q in range(4):
                        kc = half * 4 + q
                        fw.tr(ps[:, q * 128:(q + 1) * 128], s[:, kc * 128:(kc + 1) * 128], C("ident"))
                    tt = ti // 4
                    dst = xT.p(*[(half * 4 + q) * 4 + tt for q in range(4)])[
                        :, half * 4:half * 4 + 4, ti * 128:(ti + 1) * 128]
                    evac(dst, ps[:, :].re("p (a b) -> p a b", a=4))
            fw.barrier()
            fw.flush()

        def mod_vectors(layer):
            for ot in range(12):
                wt = wload(dr["w_mod"][layer, :, ot * 512:(ot + 1) * 512], 8, 512)
                ps = mmp.next()
                for o4 in range(4):
                    for kc in range(8):
                        fw.mm(ps[:, o4 * 2:o4 * 2 + 2], wt[:, kc, o4 * 128:(o4 + 1) * 128], scT[:, kc, :],
                              start=(kc == 0), stop=(kc == 7))
                fw.tt(modv[:, ot * 4:(ot + 1) * 4, :], ps[:, 0:8].re("p (a b) -> p a b", a=4),
                      VC("bmod", layer * 48 + ot * 4, n=4).un(2).bc([128, 4, 2]), ALU.add)
            for which, (gname, sc0) in enumerate((("gmix", 8), ("gmlp", 32))):
                fw.ts(modA[:, which, :, :], modv[:, sc0:sc0 + 8, :], 1.0, ALU.add)
                fw.tt(modA[:, which, :, :], modA[:, which, :, :],
                      VC(gname, layer * 8, n=8).un(2).bc([128, 8, 2]), ALU.mult)

        def rstd_bc(ph, name, nfeat):
            sqp = Rot([fw.sb("%s_sq%d" % (name, i), [128, 512], BF16, stack=ph) for i in range(3)])
            rsp = Rot([fw.sb("%s_rs%d" % (name, i), [128, 512], F32, stack=ph) for i in range(2)])

            def f(srcs):
                ps = mmp.next()
                n = len(srcs)
                for i, s in enumerate(srcs):
                    sq = sqp.next()
                    fw.act(sq[:, :], s, AF.Square)
                    fw.mm(ps[:, :], onesb[:, :], sq[:, :], start=(i == 0), stop=(i == n - 1))
                rs = rsp.next()
                fw.act(rs[:, :], ps[:, :], AF.Sqrt, bias=EPS, scale=1.0 / nfeat)
                fw.recip(rs[:, :], rs[:, :])
                return rs
            return f

        def modulate(ph, name, hT, which, tts, t0, ntt):
            rfn = rstd_bc(ph, name, D)
            tmpp = Rot([fw.sb("%s_tmp%d" % (name, i), [128, 512], F32, stack=ph) for i in range(3)])
            sh0 = 0 if which == 0 else 24
            for tt in tts:
                r = 0 if tt < 2 else 1
                rs = rfn([xv(kc, tt) for kc in range(8)])
                for kc in range(8):
                    tmp = tmpp.next()
                    fw.tt(tmp[:, :], xv(kc, tt), rs[:, :], ALU.mult)
                    lt = tt - t0
                    fw.act(hT.p(kc * ntt + lt)[:, kc, lt * 512:(lt + 1) * 512], tmp[:, :], AF.Identity,
                           bias=modv[:, sh0 + kc, r:r + 1], scale=modA[:, which, kc, r:r + 1])

        def mlp(layer):
            with ExitStack() as ph:
                hT = fw.sb("hTm", [128, 8, 2048], BF16, 32, stack=ph)
                h1 = fw.sb("h1", [128, 8, 2048], BF16, 32, stack=ph)
                rl = Rot([fw.sb("rl%d" % i, [128, 512], F32, stack=ph) for i in range(3)])
                modulate(ph, "nm", hT, 1, range(4), 0, 4)
                for blk in range(4):
                    for ht in range(2):
                        c0 = blk * 1024 + ht * 512
                        wt = wload(dr["w_mlp_in"][layer, :, c0:c0 + 512], 8, 512)
                        for h4 in range(4):
                            hc = ht * 4 + h4
                            for tt in range(4):
                                ps = mmp.next()
                                for kc in range(8):
                                    fw.mm(ps[:, :], wt[:, kc, h4 * 128:(h4 + 1) * 128],
                                          hT.p(kc * 4 + tt)[:, kc, tt * 512:(tt + 1) * 512],
                                          start=(kc == 0), stop=(kc == 7))
                                r = rl.next()
                                fw.act(r[:, :], ps[:, :], AF.Relu)
                                fw.tt(h1.p(hc * 4 + tt)[:, hc, tt * 512:(tt + 1) * 512], r[:, :], r[:, :], ALU.mult)
                    for ot in range(2):
                        wt = wload(dr["w_mlp_out"][layer, blk * 1024:(blk + 1) * 1024, ot * 512:(ot + 1) * 512], 8, 512)
                        for o4 in range(4):
                            oc = ot * 4 + o4
                            for tt in range(4):
                                r = 0 if tt < 2 else 1
                                ps = mmp.next()
                                for kc in range(8):
                                    fw.mm(ps[:, :], wt[:, kc, o4 * 128:(o4 + 1) * 128],
                                          h1.p(kc * 4 + tt)[:, kc, tt * 512:(tt + 1) * 512],
                                          start=(kc == 0), stop=(kc == 7))
                                fw.stt(xv(oc, tt), ps[:, :], modv[:, 40 + oc, r:r + 1], xv(oc, tt), ALU.mult, ALU.add)
                fw.barrier()
                fw.flush()

        def final_out(lvl=9):
            with ExitStack() as ph:
                rfn = rstd_bc(ph, "fn", D)
                yT = fw.sb("yT", [128, 8, 512], F32, stack=ph)
                ost = Rot([fw.sb("ost%d" % i, [128, D], F32, stack=ph) for i in range(2)])
                for tt in range(4):
                    rs = rfn([xv(kc, tt) for kc in range(8)])
                    if lvl < 2:
                        continue
                    for kc in range(8):
                        fw.stt(yT[:, kc, :], xv(kc, tt), VC("fing", kc), rs[:, :], ALU.mult, ALU.mult)
                    if lvl < 3:
                        continue
                    for q in range(4):
                        ti = tt * 4 + q
                        o = ost.next()
                        for half in range(2):
                            ps = mmp.next()
                            for a in range(4):
                                kc = half * 4 + a
                                fw.tr(ps[:, a * 128:(a + 1) * 128], yT[:, kc, q * 128:(q + 1) * 128], C("ident"))
                            evac(o[:, half * 512:(half + 1) * 512], ps[:, :])
                        if lvl >= 4:
                            fw.dma(dv("y")[ti * 128:(ti + 1) * 128, :], o[:, :])
                fw.barrier()
                fw.flush()

        def out_proj_even(layer, g, catT):
            j = layer // 2
            for ot in range(2):
                wt = wload(dr["w_out_even"][j, :, ot * 512:(ot + 1) * 512], 8, 512)
                for o4 in range(4):
                    oc = ot * 4 + o4
                    for tg in range(2):
                        tt = 2 * g + tg
                        ps = mmp.next()
                        for kc in range(8):
                            fw.mm(ps[:, :], wt[:, kc, o4 * 128:(o4 + 1) * 128],
                                  catT.p(kc * 2 + tg)[:, kc, tg * 512:(tg + 1) * 512], start=(kc == 0), stop=(kc == 7))
                        fw.stt(xv(oc, tt), ps[:, :], modv[:, 16 + oc, g:g + 1], xv(oc, tt), ALU.mult, ALU.add)

        def out_proj_odd(ph, layer, g, catT):
            j = layer // 2
            rfn = rstd_bc(ph, "on", 2 * D)
            rsk = fw.sb("rsk", [128, 1024], F32, stack=ph)
            tmpp = Rot([fw.sb("opt%d" % i, [128, 512], F32, stack=ph) for i in range(2)])
            for tg in range(2):
                rs = rfn([catT.p(c * 2 + tg)[:, c, tg * 512:(tg + 1) * 512] for c in range(16)])
                fw.cp("dve", rsk[:, tg * 512:(tg + 1) * 512], rs[:, :])
            for ot in range(4):
                wt = wload(dr["w_out_odd"][j, :, ot * 256:(ot + 1) * 256], 16, 256)
                fw.tt(wt, wt, VC("sng", j * 16, n=16).un(2).bc([128, 16, 256]), ALU.mult)
                for o2 in range(2):
                    oc = ot * 2 + o2
                    for tg in range(2):
                        tt = 2 * g + tg
                        ps = mmp.next()
                        for kc in range(16):
                            fw.mm(ps[:, :], wt[:, kc, o2 * 128:(o2 + 1) * 128],
                                  catT.p(kc * 2 + tg)[:, kc, tg * 512:(tg + 1) * 512], start=(kc == 0), stop=(kc == 15))
                        tmp = tmpp.next()
                        fw.tt(tmp[:, :], ps[:, :], rsk[:, tg * 512:(tg + 1) * 512], ALU.mult)
                        fw.stt(xv(oc, tt), tmp[:, :], modv[:, 16 + oc, g:g + 1], xv(oc, tt), ALU.mult, ALU.add)
        def rsum(out, in_):
            o, i = _ap(out), _ap(in_)
            fw.op("dve", lambda e: e.reduce_sum(out=o, in_=i, axis=mybir.AxisListType.X), _deps(in_), _deps(out))

        def even_mixer(ph, layer, g, hT, catT, dbg=None):
            j = layer // 2
            ctx = (g == 0)
            nseq, L = (4, 256) if ctx else (1, 1024)
            tps = L // 128

            def hv(kc, lo, n):
                return hT.p(kc * 2 + lo // 512)[:, kc, lo:lo + n]

            with ExitStack() as ua:
                qT = fw.sb("qT", [64, 4, 1024], BF16, 4, stack=ua)
                kT = fw.sb("kT", [64, 1536], BF16, stack=ua)
                V1 = fw.sb("V1", [128, 12, 65], BF16, stack=ua)
                otok = fw.sb("otok", [128, 8, 256], BF16, 8, stack=ua)
                raw = Rot([fw.sb("araw%d" % i, [64, 1024], F32, stack=ua) for i in range(2)])
                qn = Rot([fw.sb("aqn%d" % i, [64, 512], F32, stack=ua) for i in range(2)])
                sqb = Rot([fw.sb("asq%d" % i, [64, 512], BF16, stack=ua) for i in range(2)])
                rsb = Rot([fw.sb("ars%d" % i, [64, 512], F32, stack=ua) for i in range(2)])
                t1p = Rot([fw.sb("at1%d" % i, [64, 512], F32, stack=ua) for i in range(2)])
                t2p = Rot([fw.sb("at2%d" % i, [64, 512], F32, stack=ua) for i in range(2)])
                ptp = Rot([fw.sb("apt%d" % i, [128, 512], BF16, stack=ua) for i in range(3)])
                rdp = Rot([fw.sb("ard%d" % i, [128, 1], F32, stack=ua) for i in range(4)])
                vst = Rot([fw.sb("avs%d" % i, [128, 64], F32, stack=ua) for i in range(2)])
                kst = Rot([fw.sb("aks%d" % i, [128, 64], F32, stack=ua) for i in range(2)])
                cks = Rot([fw.sb("ack%d" % i, [128, 128], F32, stack=ua) for i in range(2)])
                if not ctx:
                    rope = fw.sb("ropeT", [64, 2048], F32, stack=ua)
                    fw.dma(rope[:, :], dv("rope"))
                fw.memset(V1[:, :, 64:65], 1.0)
                for g2 in range(2):
                    wt = wload(dr["wie"][j, :, g2 * 384:(g2 + 1) * 384], 8, 384)

                    def proj_norm(col0, gain_col, dst_fn, is_k):
                        r = raw.next()
                        for tg in range(2):
                            ps = mmp.next()
                            for kc in range(8):
                                fw.mm(ps[0:64, :], wt[:, kc, col0:col0 + 64], hv(kc, tg * 512, 512),
                                      start=(kc == 0), stop=(kc == 7))
                            evac(r[:, tg * 512:(tg + 1) * 512], ps[0:64, :])
                        for tg in range(2):
                            sl = slice(tg * 512, (tg + 1) * 512)
                            sq = sqb.next()
                            fw.act(sq[:, :], r[:, sl], AF.Square)
                            ps = mmp.next()
                            fw.mm(ps[0:64, :], onesb[0:64, 0:64], sq[:, :])
                            rs = rsb.next()
                            fw.act(rs[:, :], ps[0:64, :], AF.Sqrt, bias=EPS, scale=1.0 / 64)
                            fw.recip(rs[:, :], rs[:, :])
                            if ctx and not is_k:
                                fw.stt(dst_fn(sl), r[:, sl], gain_col, rs[:, :], ALU.mult, ALU.mult)
                                continue
                            q = qn.next()
                            fw.stt(q[:, :], r[:, sl], gain_col, rs[:, :], ALU.mult, ALU.mult)
                            if ctx:
                                fw.cp("act", dst_fn(sl), q[:, :])
                                for t4 in range(4):
                                    ti = tg * 4 + t4
                                    ps2 = mmp.next()
                                    fw.tr(ps2[:, 0:64], q[:, t4 * 128:(t4 + 1) * 128], C("ident", 64, 0, 64))
                                    ks = kst.next()
                                    evac(ks[:, :], ps2[:, 0:64])
                                    fw.dma(dv("nk")[j, ti * 128:(ti + 1) * 128, g2 * 64:(g2 + 1) * 64], ks[:, :])
                            else:
                                ps2 = mmp.next()
                                fw.mm(ps2[0:64, :], C("ropeR", 64), q[:, :])
                                t1 = t1p.next()
                                fw.tt(t1[:, :], q[:, :], rope[:, sl], ALU.mult)
                                t2 = t2p.next()
                                fw.tt(t2[:, :], ps2[0:64, :], rope[:, 1024 + tg * 512:1024 + (tg + 1) * 512], ALU.mult)
                                fw.tt(dst_fn(sl), t1[:, :], t2[:, :], ALU.add)

                    for hh in range(4):
                        proj_norm(hh * 64, VC("qkg", j * 2 + 0, rows=64),
                                  (lambda sl, hh=hh: qT.p(hh)[:, hh, sl]), False)
                    proj_norm(256, VC("qkg", j * 2 + 1, rows=64), (lambda sl: kT[:, sl]), True)
                    for ti in range(8):
                        ps = mmp.next()
                        for kc in range(8):
                            fw.mm(ps[:, 0:64], hv(kc, ti * 128, 128), wt[:, kc, 320:384], start=(kc == 0), stop=(kc == 7))
                        fw.cp("act", V1[:, ti, 0:64], ps[:, 0:64])
                        if ctx:
                            vs = vst.next()
                            fw.cp("dve", vs[:, :], ps[:, 0:64])
                            fw.dma(dv("nv")[j, ti * 128:(ti + 1) * 128, g2 * 64:(g2 + 1) * 64], vs[:, :])
                    if not ctx:
                        for t in range(4):
                            ck = cks.next()
                            fw.dma(ck[:, 0:64], dv("ctxk")[j, t * 128:(t + 1) * 128, g2 * 64:(g2 + 1) * 64])
                            fw.dma(ck[:, 64:128], dv("ctxv")[j, t * 128:(t + 1) * 128, g2 * 64:(g2 + 1) * 64])
                            ps2 = mmp.next()
                            fw.tr(ps2[0:64, 0:128], ck[:, 0:64], C("ident"))
                            fw.cp("act", kT[:, 1024 + t * 128:1024 + (t + 1) * 128], ps2[0:64, 0:128])
                            fw.cp("dve", V1[:, 8 + t, 0:64], ck[:, 64:128])
                    QB = 256 if ctx else 512
                    for s in range(nseq):
                        if ctx:
                            kts = [(slice(ti * 128, (ti + 1) * 128), ti) for ti in (2 * s, 2 * s + 1)]
                        else:
                            kts = [(slice(t * 128, (t + 1) * 128), t) for t in range(12)]
                        for hh in range(4):
                            for qb in range(L // QB):
                                q0 = s * L + qb * QB
                                oacc = accp.next()
                                for idx, (ksl, vt) in enumerate(kts):
                                    ps = mmp.next()
                                    fw.mm(ps[:, 0:QB], kT[:, ksl], qT.p(hh)[:, hh, q0:q0 + QB])
                                    pt = ptp.next()
                                    fw.act(pt[:, 0:QB], ps[:, 0:QB], AF.Exp, scale=0.125)
                                    nqs = QB // 128
                                    for qs in range(nqs):
                                        fw.mm(oacc[:, qs * 128:qs * 128 + 65], pt[:, qs * 128:(qs + 1) * 128],
                                              V1[:, vt, :], start=(idx == 0 and qs == 0),
                                              stop=(idx == len(kts) - 1 and qs == nqs - 1))
                                for qs in range(QB // 128):
                                    ti = q0 // 128 + qs
                                    rd = rdp.next()
                                    fw.recip(rd[:, :], oacc[:, qs * 128 + 64:qs * 128 + 65])
                                    fw.act(otok.p(ti)[:, ti, hh * 64:(hh + 1) * 64], oacc[:, qs * 128:qs * 128 + 64],
                                           AF.Identity, scale=rd[:, 0:1])
                    for ti in range(8):
                        for c2 in range(2):
                            pT = trp.next()
                            fw.tr(pT[:, 0:128], otok.p(ti)[:, ti, c2 * 128:(c2 + 1) * 128], identb[:, :])
                            c = 2 * g2 + c2
                            evac(catT.p(c * 2 + ti // 4)[:, c, ti * 128:(ti + 1) * 128], pT[:, 0:128])
                fw.barrier()
                fw.flush()

            if dbg == "dbg_att":
                return
            with ExitStack() as ug:
                raw = Rot([fw.sb("graw%d" % i, [128, 1024], F32, stack=ug) for i in range(2)])
                cac = Rot([fw.sb("gcac%d" % i, [128, 1024], F32, stack=ug) for i in range(2)])
                qkf = Rot([fw.sb("gqkf%d" % i, [128, 1024], F32, stack=ug) for i in range(2)])
                fT = fw.sb("gfT", [128, 3, 1024], BF16, 3, stack=ug)
                ktok = fw.sb("gktok", [128, 8, 128], BF16, 8, stack=ug)
                vtok = fw.sb("gvtok", [128, 8, 128], BF16, 8, stack=ug)
                sz = fw.sb("gsz", [128, 8, 128], F32, 8, stack=ug)
                zraw = fw.sb("gzraw", [128, 8, 132], F32, 8, stack=ug)
                scr = zraw[:, :, 128:132]
                gs = fw.sb("ggs", [128, 10, 8, 2], F32, stack=ug)
                tmpa = fw.sb("gtmpa", [128, 8], F32, stack=ug)
                tmpb = fw.sb("gtmpb", [128, 8], F32, stack=ug)
                negA = fw.sb("gnegA", [128, 2], F32, stack=ug)
                decb = fw.sb("gdecb", [128, 2, 16], F32, stack=ug)
                oac = fw.sb("goac", [128, 8, 128], F32, 8, stack=ug)
                ssc = fw.sb("gssc", [128, 8], F32, stack=ug)
                junk = Rot([fw.sb("gjunk%d" % i, [128, 128], F32, stack=ug) for i in range(2)])
                onb = Rot([fw.sb("gonb%d" % i, [128, 128], BF16, stack=ug) for i in range(2)])
                sqb = Rot([fw.sb("gsq%d" % i, [128, 512], BF16, stack=ug) for i in range(2)])
                rsb = Rot([fw.sb("grs%d" % i, [128, 512], F32, stack=ug) for i in range(2)])
                S = [fw.sb("gS%d" % d, [128, 128], F32, stack=ug) for d in range(2)]
                Sb = [fw.sb("gSb%d" % d, [128, 128], BF16, stack=ug) for d in range(2)]
                dg = [fw.sb("gdg%d" % d, [128, 384], F32, stack=ug) for d in range(2)]
                E2 = [fw.sb("gE2%d" % d, [128, 256], F32, stack=ug) for d in range(2)]
                Mm = [fw.sb("gMm%d" % d, [128, 128], F32, stack=ug) for d in range(2)]
                qkT = [fw.sb("gqkT%d" % d, [128, 128], BF16, stack=ug) for d in range(2)]
                Bm = [[fw.sb("gB%d_%d" % (d, i), [128, 128], F32, stack=ug) for i in range(2)] for d in range(2)]
                AS = [[fw.sb("gAS%d_%d" % (d, i), [128, 256], F32, stack=ug) for i in range(2)] for d in range(2)]
                TT = [fw.sb("gTT%d" % d, [128, 128], BF16, stack=ug) for d in range(2)]
                vb = [fw.sb("gvb%d" % d, [128, 128], BF16, stack=ug) for d in range(2)]
                kbg = [fw.sb("gkbg%d" % d, [128, 128], BF16, stack=ug) for d in range(2)]
                kdec = [fw.sb("gkdec%d" % d, [128, 128], BF16, stack=ug) for d in range(2)]
                qdT = [fw.sb("gqdT%d" % d, [128, 128], BF16, stack=ug) for d in range(2)]
                usb = [fw.sb("gusb%d" % d, [128, 128], F32, stack=ug) for d in range(2)]
                wTs = [fw.sb("gwT%d" % d, [128, 128], BF16, stack=ug) for d in range(2)]
                vnew = [fw.sb("gvn%d" % d, [128, 128], BF16, stack=ug) for d in range(2)]
                for d in range(2):
                    fw.memset(vnew[d][:, :], 0.0)
                for hd in range(4):
                    base = 768 + hd * 640
                    wtA = wload(dr["wie"][j, :, base:base + 384], 8, 384)
                    import os
                    SK = os.environ.get("DBGSKIP", "")
                    if "wtb" not in SK:
                        wtB = wload(dr["wie"][j, :, base + 384:base + 640], 8, 256)
                    for wi in range(3):
                        r = raw.next()
                        for tg in range(2):
                            ps = mmp.next()
                            for kc in range(8):
                                fw.mm(ps[:, :], wtA[:, kc, wi * 128:(wi + 1) * 128], hv(kc, tg * 512, 512),
                                      start=(kc == 0), stop=(kc == 7))
                            evac(r[:, tg * 512:(tg + 1) * 512], ps[:, :])
                        a = cac.next()
                        cwc = lambda tap: VC("dconv", ((j * 4 + hd) * 3 + wi) * 3 + tap)
                        r3 = r[:, :].re("p (s l) -> p s l", s=nseq)
                        a3 = a[:, :].re("p (s l) -> p s l", s=nseq)
                        fw.ts(a[:, :], r[:, :], cwc(1), ALU.mult)
                        if "conv" not in SK:
                            fw.stt(a3[:, :, 1:L], r3[:, :, 0:L - 1], cwc(0), a3[:, :, 1:L], ALU.mult, ALU.add)
                            fw.stt(a3[:, :, 0:L - 1], r3[:, :, 1:L], cwc(2), a3[:, :, 0:L - 1], ALU.mult, ALU.add)
                        if wi == 2:
                            fw.act(fT.p(2)[:, 2, :], a[:, :], SILU)
                        else:
                            f = qkf.next()
                            fw.act(f[:, :], a[:, :], SILU)
                            for tg in range(2):
                                sl = slice(tg * 512, (tg + 1) * 512)
                                sq = sqb.next()
                                fw.act(sq[:, :], f[:, sl], AF.Square)
                                ps = mmp.next()
                                fw.mm(ps[:, :], onesb[:, :], sq[:, :])
                                rs = rsb.next()
                                fw.act(rs[:, :], ps[:, :], AF.Sqrt, bias=EPS, scale=1.0)
                                fw.recip(rs[:, :], rs[:, :])
                                fw.stt(fT.p(wi)[:, wi, sl], f[:, sl], (128 ** -0.5) if wi == 0 else 1.0, rs[:, :],
                                       ALU.mult, ALU.mult)
                    for ti in range(8):
                        tsl = slice(ti * 128, (ti + 1) * 128)
                        pT = trp.next()
                        fw.tr(pT[:, 0:128], fT.p(1)[:, 1, tsl], identb[:, :])
                        evac(ktok.p(ti)[:, ti, :], pT[:, 0:128])
                        pT = trp.next()
                        fw.tr(pT[:, 0:128], fT.p(2)[:, 2, tsl], identb[:, :])
                        evac(vtok.p(ti)[:, ti, :], pT[:, 0:128])
                    if dbg == "dbg_gdn1":
                        break
                    for ti in range(8):
                        ps = mmp.next()
                        for kc in range(8):
                            fw.mm(ps[:, 0:256], hv(kc, ti * 128, 128), wtB[:, kc, :], start=(kc == 0), stop=(kc == 7))
                        fw.cp("act", zraw.p(ti)[:, ti, :], ps[:, 0:132])
                        fw.act(sz.p(ti)[:, ti, :], zraw.p(ti)[:, ti, 0:128], SILU)
                    if "scal" in SK:
                        break
                    for d in range(2):
                        col = j * 8 + d * 4 + hd
                        fw.act(negA[:, d:d + 1], VC("dalog", col), AF.Exp)
                        fw.ts(negA[:, d:d + 1], negA[:, d:d + 1], -1.0, ALU.mult)
                        fw.act(tmpa[:, :], scr[:, :, 2 + d:3 + d].re('p t o -> p (t o)'), AF.Exp, bias=VC("ddtb", col), scale=1.0)
                        fw.act(tmpa[:, :], tmpa[:, :], AF.Ln, bias=1.0, scale=1.0)
                        fw.ts(gs[:, 2, :, d], tmpa[:, :], negA[:, d:d + 1], ALU.mult)
                        fw.act(tmpb[:, :], scr[:, :, d:d + 1].re('p t o -> p (t o)'), AF.Exp, scale=-1.0)
                        fw.act(gs[:, 1, :, d], tmpb[:, :], AF.Ln, bias=1.0, scale=1.0)
                        fw.act(gs[:, 0, :, d], gs[:, 1, :, d], AF.Exp, scale=-1.0)
                    if "cums" in SK:
                        break
                    ps = mmp.next()
                    fw.mm(ps[:, 0:8], C("triF"), gs[:, 2, :, 0])
                    fw.mm(ps[:, 8:16], C("triB"), gs[:, 2, :, 1])
                    fw.mm(ps[:, 16:24], C("blk"), gs[:, 2, :, 0])
                    fw.mm(ps[:, 24:32], C("blk"), gs[:, 2, :, 1])
                    for d in range(2):
                        fw.cp("dve", gs[:, 3, :, d], ps[:, d * 8:(d + 1) * 8])
                        fw.cp("dve", gs[:, 4, :, d], ps[:, 16 + d * 8:16 + (d + 1) * 8])
                    fw.act(gs[:, 5, :, :], gs[:, 3, :, :], AF.Exp)
                    fw.tt(gs[:, 6, :, :], gs[:, 4, :, :], gs[:, 3, :, :], ALU.subtract)
                    fw.act(gs[:, 6, :, :], gs[:, 6, :, :], AF.Exp)
                    fw.tt(gs[:, 7, :, :], gs[:, 3, :, :], gs[:, 1, :, :], ALU.subtract)
                    fw.ts(gs[:, 8, :, :], gs[:, 3, :, :], -1.0, ALU.mult)
                    fw.tt(gs[:, 9, :, :], gs[:, 0, :, :], gs[:, 5, :, :], ALU.mult)
                    ps = mmp.next()
                    g2d = gs[:, 2, :, :].re("p t d -> p (t d)")
                    fw.mm(ps[:, 0:16], C("ind0"), g2d)
                    fw.mm(ps[:, 16:32], C("ind1"), g2d)
                    fw.act(decb[:, :, :].re("p c n -> p (c n)"), ps[:, 0:32], AF.Exp)
                    for ti in range(8):
                        fw.memset(oac.p(ti)[:, ti, :], 0.0)
                    if dbg == "dbg_gdn2":
                        break
                    for n in range(8):
                        slots = [(n, 0), (7 - n, 1)]
                        def col(k, ti, d):
                            return gs[:, k, ti, d:d + 1]
                        for ti, d in slots:
                            fw.ts(dg[d][:, 0:128], C("ident"), col(3, ti, d), ALU.mult)
                            fw.ts(dg[d][:, 128:256], C("ident"), col(7, ti, d), ALU.mult)
                            fw.ts(dg[d][:, 256:384], C("ident"), col(5, ti, d), ALU.mult)
                        psD = {}
                        for ti, d in slots:
                            p = mmp.next()
                            psD[d] = p
                            fw.mm(p[:, 0:256], C("ones"), dg[d][:, 0:256], start=True, stop=False)
                            fw.mm(p[:, 0:256], C("ident"), C("nmF" if d == 0 else "nmB"), start=False, stop=True)
                            fw.mm(p[:, 256:384], C("ones"), dg[d][:, 256:384], start=True, stop=True)
                            fw.act(E2[d][:, :], p[:, 0:256], AF.Exp, bias=col(8, ti, d), scale=1.0)
                            fw.tt(qdT[d][:, :], fT.p(0)[:, 0, ti * 128:(ti + 1) * 128], p[:, 256:384], ALU.mult)
                        for ti, d in slots:
                            tsl = slice(ti * 128, (ti + 1) * 128)
                            p = mmp.next()
                            fw.mm(p[:, 0:128], fT.p(1)[:, 1, tsl], fT.p(0)[:, 0, tsl])
                            fw.mm(p[:, 128:256], fT.p(1)[:, 1, tsl], fT.p(1)[:, 1, tsl])
                            fw.tt(qkT[d][:, :], p[:, 0:128], E2[d][:, 0:128], ALU.mult)
                            fw.tt(Mm[d][:, :], p[:, 128:256], E2[d][:, 128:256], ALU.mult)
                        for ti, d in slots:
                            p = mmp.next()
                            fw.tr(p[:, 0:128], Mm[d][:, :], C("ident"))
                            fw.cp("act", Bm[d][0][:, :], p[:, 0:128])
                        for ti, d in slots:
                            p = mmp.next()
                            fw.mm(p[:, 0:128], Bm[d][0][:, :], Mm[d][:, :])
                            fw.mm(p[:, 128:256], Mm[d][:, :], Bm[d][0][:, :])
                            fw.cp("act", AS[d][0][:, 0:128], p[:, 0:128])
                            fw.cp("dve", Bm[d][1][:, :], p[:, 128:256])
                            fw.tt(AS[d][0][:, 128:256], C("ident"), Mm[d][:, :], ALU.subtract)
                        for k in range(1, 6):
                            for ti, d in slots:
                                cur, nxt, Bk, Bn = AS[d][(k - 1) % 2], AS[d][k % 2], Bm[d][k % 2], Bm[d][(k + 1) % 2]
                                p = mmp.next()
                                if k < 5:
                                    fw.mm(p[:, 0:256], Bk[:, :], cur[:, 0:256])
                                    fw.mm(p[:, 256:384], cur[:, 0:128], Bk[:, :])
                                    fw.cp("act", Bn[:, :], p[:, 256:384])
                                    if k < 4:
                                        fw.cp("act", nxt[:, 0:128], p[:, 0:128])
                                    fw.tt(nxt[:, 128:256], cur[:, 128:256], p[:, 128:256], ALU.add)
                                else:
                                    fw.mm(p[:, 0:128], Bk[:, :], cur[:, 128:256])
                                    fw.tt(TT[d][:, :], cur[:, 128:256], p[:, 0:128], ALU.add)
                        if dbg == "dbg_gdn3":
                            continue
                        for ti, d in slots:
                            fw.ts(vb[d][:, :], vtok.p(ti)[:, ti, :], col(0, ti, d), ALU.mult)
                            fw.ts(kbg[d][:, :], ktok.p(ti)[:, ti, :], col(9, ti, d), ALU.mult)
                            fw.act(kdec[d][:, :], ktok.p(ti)[:, ti, :], AF.Identity, scale=col(6, ti, d))
                            p = mmp.next()
                            fw.mm(p[:, 0:128], TT[d][:, :], vb[d][:, :])
                            fw.mm(p[:, 128:256], kbg[d][:, :], TT[d][:, :])
                            fw.cp("act", usb[d][:, :], p[:, 0:128])
                            fw.cp("dve", wTs[d][:, :], p[:, 128:256])
                        if dbg == "dbg_gdn4":
                            continue
                        for ci in range(2):
                            info = {}
                            pws, pos, psts = {}, {}, {}
                            for ti, d in slots:
                                c = ci if d == 0 else 1 - ci
                                rows = slice(c * 64, (c + 1) * 64)
                                first = (ti % tps == 0 and c == 0) if d == 0 else (ti % tps == tps - 1 and c == 1)
                                last = (ti % tps == tps - 1 and c == 1) if d == 0 else (ti % tps == 0 and c == 0)
                                info[d] = (c, rows, first, last, ti // tps)
                                if first:
                                    if ctx:
                                        fw.memset(S[d][:, :], 0.0)
                                    else:
                                        fw.dma(S[d][:, :], dv("sd0")[j, d, hd, :, :])
                                    fw.cp("act", Sb[d][:, :], S[d][:, :])
                                p = mmp.next()
                                pws[d] = p
                                fw.mm(p[:, 0:128], wTs[d][:, :], Sb[d][:, :])
                            for ti, d in slots:
                                c, rows, first, last, seq = info[d]
                                fw.tt(vnew[d][rows, :], usb[d][rows, :], pws[d][rows, 0:128], ALU.subtract)
                            for ti, d in slots:
                                c, rows, first, last, seq = info[d]
                                po = mmp.next()
                                pos[d] = po
                                fw.mm(po[:, 0:128], qdT[d][:, :], Sb[d][:, :], start=True, stop=False)
                                fw.mm(po[:, 0:128], qkT[d][rows, :], vnew[d][rows, :], start=False, stop=True)
                                pst = mmp.next()
                                psts[d] = pst
                                fw.mm(pst[:, 0:128], kdec[d][rows, :], vnew[d][rows, :])
                            for ti, d in slots:
                                c, rows, first, last, seq = info[d]
                                fw.stt(S[d][:, :], S[d][:, :], decb[:, c, ti * 2 + d:ti * 2 + d + 1], psts[d][:, 0:128],
                                       ALU.mult, ALU.add)
                                fw.cp("act", Sb[d][:, :], S[d][:, :])
                                fw.tt(oac.p(ti)[rows, ti, :], oac.p(ti)[rows, ti, :], pos[d][rows, 0:128], ALU.add)
                            for ti, d in slots:
                                c, rows, first, last, seq = info[d]
                                if last and ctx:
                                    fw.dma(dv("nsd")[seq, j, d, hd, :, :], S[d][:, :])
                    if dbg == "dbg_gdn5":
                        break
                    for ti in range(8):
                        jk = junk.next()
                        fw.tt(jk[:, :], oac.p(ti)[:, ti, :], oac.p(ti)[:, ti, :], ALU.mult)
                        rsum(ssc[:, ti:ti + 1], jk[:, :])
                    fw.act(ssc[:, :], ssc[:, :], AF.Sqrt, bias=EPS, scale=1.0 / 128)
                    fw.recip(ssc[:, :], ssc[:, :])
                    for ti in range(8):
                        ob = onb.next()
                        fw.stt(ob[:, :], oac.p(ti)[:, ti, :], ssc[:, ti:ti + 1], sz.p(ti)[:, ti, :], ALU.mult, ALU.mult)
                        pT = trp.next()
                        fw.tr(pT[:, 0:128], ob[:, :], identb[:, :])
                        c = 4 + hd
                        fw.act(catT.p(c * 2 + ti // 4)[:, c, ti * 128:(ti + 1) * 128], pT[:, 0:128], AF.Identity,
                               scale=VC("dng", j))
                fw.barrier()
                fw.flush()

        def odd_mixer(ph, layer, g, hT, catT):
            j = layer // 2
            ctx = (g == 0)
            nseq, L = (4, 256) if ctx else (1, 1024)
            tps = L // 128

            def hv(kc, lo, n):
                return hT.p(kc * 2 + lo // 512)[:, kc, lo:lo + n]

            with ExitStack() as us:
                raw = Rot([fw.sb("sraw%d" % i, [128, 1024], F32, stack=us) for i in range(1)])
                cac = Rot([fw.sb("scac%d" % i, [128, 1024], F32, stack=us) for i in range(1)])
                fT = fw.sb("sfT", [128, 4, 1024], BF16, 4, stack=us)
                xtok = fw.sb("sxtok", [128, 8, 256], BF16, 8, stack=us)
                Btok = fw.sb("sBtok", [128, 8, 128], BF16, 8, stack=us)
                zs = fw.sb("szs", [128, 8, 256], BF16, 8, stack=us)
                dtr = fw.sb("sdtr", [128, 8, 8], F32, stack=us)
                zr = Rot([fw.sb("szr%d" % i, [128, 264], F32, stack=us) for i in range(2)])
                ss = fw.sb("sss", [128, 7, 8, 8], F32, stack=us)
                negA = fw.sb("snegA", [128, 8], F32, stack=us)
                decb = fw.sb("sdecb", [128, 2, 64], F32, stack=us)
                yac = fw.sb("syac", [128, 8, 256], F32, 8, stack=us)
                xdt = [fw.sb("sxdt%d" % d, [128, 256], BF16, stack=us) for d in range(2)]
                xd = [fw.sb("sxd%d" % d, [128, 256], BF16, stack=us) for d in range(2)]
                cbs = [fw.sb("scbs%d" % d, [128, 128], F32, stack=us) for d in range(2)]
                dgs = [fw.sb("sdgs%d" % d, [128, 4, 128], F32, stack=us) for d in range(2)]
                Ls = dgs
                cbL = [fw.sb("scbL%d" % d, [128, 4, 128], BF16, stack=us) for d in range(2)]
                sT = [fw.sb("ssT%d" % d, [128, 256], F32, stack=us) for d in range(2)]
                sTb = [fw.sb("ssTb%d" % d, [128, 256], BF16, stack=us) for d in range(2)]
                ytmp = Rot([fw.sb("sytmp%d" % i, [128, 256], F32, stack=us) for i in range(2)])
                ygb = Rot([fw.sb("sygb%d" % i, [128, 256], BF16, stack=us) for i in range(2)])
                sst = Rot([fw.sb("ssst%d" % i, [128, 128], F32, stack=us) for i in range(2)])
                for sg in range(8):
                    base = sg * 832
                    wtA = wload(dr["wio"][j, :, base:base + 512], 8, 512)
                    wtB = wload(dr["wio"][j, :, base + 512:base + 832], 8, 320)
                    for wi in range(4):
                        r = raw.next()
                        for tg in range(2):
                            ps = mmp.next()
                            for kc in range(8):
                                fw.mm(ps[:, :], wtA[:, kc, wi * 128:(wi + 1) * 128], hv(kc, tg * 512, 512),
                                      start=(kc == 0), stop=(kc == 7))
                            evac(r[:, tg * 512:(tg + 1) * 512], ps[:, :])
                        a = cac.next()
                        cwc = lambda tap: VC("sconv", ((j * 8 + sg) * 4 + wi) * 4 + tap)
                        r3 = r[:, :].re("p (s l) -> p s l", s=nseq)
                        a3 = a[:, :].re("p (s l) -> p s l", s=nseq)
                        fw.ts(a[:, :], r[:, :], cwc(1), ALU.mult)
                        fw.stt(a3[:, :, 1:L], r3[:, :, 0:L - 1], cwc(0), a3[:, :, 1:L], ALU.mult, ALU.add)
                        fw.stt(a3[:, :, 0:L - 1], r3[:, :, 1:L], cwc(2), a3[:, :, 0:L - 1], ALU.mult, ALU.add)
                        fw.act(fT.p(wi)[:, wi, :], a[:, :], SILU, bias=cwc(3), scale=1.0)
                    for ti in range(8):
                        tsl = slice(ti * 128, (ti + 1) * 128)
                        for c2 in range(2):
                            pT = trp.next()
                            fw.tr(pT[:, 0:128], fT.p(c2)[:, c2, tsl], identb[:, :])
                            evac(xtok.p(ti)[:, ti, c2 * 128:(c2 + 1) * 128], pT[:, 0:128])
                        pT = trp.next()
                        fw.tr(pT[:, 0:128], fT.p(2)[:, 2, tsl], identb[:, :])
                        evac(Btok.p(ti)[:, ti, :], pT[:, 0:128])
                    for ti in range(8):
                        ps = mmp.next()
                        for kc in range(8):
                            fw.mm(ps[:, 0:320], hv(kc, ti * 128, 128), wtB[:, kc, :], start=(kc == 0), stop=(kc == 7))
                        fw.cp("act", zr.next()[:, :], ps[:, 0:264])
                        zr_ = zr.items[(zr.i - 1) % len(zr.items)]
                        fw.act(zs.p(ti)[:, ti, :], zr_[:, 0:256], SILU)
                        fw.cp("act", dtr[:, ti, :], zr_[:, 256:264])
                    for d in range(2):
                        c0 = j * 64 + d * 32 + sg * 4
                        fw.act(negA[:, d * 4:(d + 1) * 4], VC("salog", c0, n=4), AF.Exp)
                        fw.tt(ss[:, 0, :, d * 4:(d + 1) * 4], dtr[:, :, d * 4:(d + 1) * 4],
                              VC("sdtb", c0, n=4).un(1).bc([128, 8, 4]), ALU.add)
                    fw.ts(negA[:, :], negA[:, :], -1.0, ALU.mult)
                    fw.act(ss[:, 0, :, :], ss[:, 0, :, :], AF.Exp)
                    fw.act(ss[:, 0, :, :], ss[:, 0, :, :], AF.Ln, bias=1.0, scale=1.0)
                    fw.tt(ss[:, 1, :, :], ss[:, 0, :, :], negA[:, :].un(1).bc([128, 8, 8]), ALU.mult)
                    ps = mmp.next()
                    fw.mm(ps[:, 0:32], C("triF"), ss[:, 1, :, 0:4])
                    fw.mm(ps[:, 32:64], C("triB"), ss[:, 1, :, 4:8])
                    fw.mm(ps[:, 64:128], C("blk"), ss[:, 1, :, :])
                    for d in range(2):
                        fw.cp("dve", ss[:, 2, :, d * 4:(d + 1) * 4], ps[:, d * 32:(d + 1) * 32].re("p (t h) -> p t h", h=4))
                    fw.cp("dve", ss[:, 3, :, :], ps[:, 64:128].re("p (t h) -> p t h", h=8))
                    fw.act(ss[:, 4, :, :], ss[:, 2, :, :], AF.Exp)
                    fw.tt(ss[:, 5, :, :], ss[:, 3, :, :], ss[:, 2, :, :], ALU.subtract)
                    fw.act(ss[:, 5, :, :], ss[:, 5, :, :], AF.Exp)
                    fw.ts(ss[:, 6, :, :], ss[:, 2, :, :], -1.0, ALU.mult)
                    ps = mmp.next()
                    a2d = ss[:, 1, :, :].re("p t h -> p (t h)")
                    fw.mm(ps[:, 0:64], C("ind0"), a2d)
                    fw.mm(ps[:, 64:128], C("ind1"), a2d)
                    fw.act(decb[:, :, :].re("p c n -> p (c n)"), ps[:, 0:128], AF.Exp)
                    for ti in range(8):
                        fw.tt(yac.p(ti)[:, ti, :].re("p (h e) -> p h e", h=4),
                              xtok.p(ti)[:, ti, :].re("p (h e) -> p h e", h=4),
                              VC("sd", j * 32 + sg * 4, n=4).un(2).bc([128, 4, 64]), ALU.mult)
                    for n in range(8):
                        slots = [(n, 0), (7 - n, 1)]
                        pcb, pLs, pYs = {}, {}, {}
                        for ti, d in slots:
                            tsl = slice(ti * 128, (ti + 1) * 128)
                            dh = slice(d * 4, (d + 1) * 4)
                            x3 = xtok.p(ti)[:, ti, :].re("p (h e) -> p h e", h=4)
                            fw.tt(xdt[d][:, :].re("p (h e) -> p h e", h=4), x3,
                                  ss[:, 0, ti, dh].un(2).bc([128, 4, 64]), ALU.mult)
                            fw.tt(xd[d][:, :].re("p (h e) -> p h e", h=4), xdt[d][:, :].re("p (h e) -> p h e", h=4),
                                  ss[:, 5, ti, dh].un(2).bc([128, 4, 64]), ALU.mult)
                            p = mmp.next()
                            pcb[d] = p
                            fw.mm(p[:, 0:128], fT.p(2)[:, 2, tsl], fT.p(3)[:, 3, tsl])
                            fw.tt(dgs[d][:, :, :], C("ident").un(1).bc([128, 4, 128]),
                                  ss[:, 2, ti, dh].un(2).bc([128, 4, 128]), ALU.mult)
                        for ti, d in slots:
                            fw.cp("act", cbs[d][:, :], pcb[d][:, 0:128])
                            pL = mmp.next()
                            pLs[d] = pL
                            fw.mm(pL[:, :], C("ones"), dgs[d][:, :, :].re("p h i -> p (h i)"), start=True, stop=False)
                            for h in range(4):
                                fw.mm(pL[:, h * 128:(h + 1) * 128], C("ident"), C("nmFc" if d == 0 else "nmBc"),
                                      start=False, stop=(h == 3))
                        for h in range(4):
                            for ti, d in slots:
                                fw.act(Ls[d][:, h, :], pLs[d][:, h * 128:(h + 1) * 128], AF.Exp,
                                       bias=ss[:, 6, ti, d * 4 + h:d * 4 + h + 1], scale=1.0)
                        for ti, d in slots:
                            fw.tt(cbL[d][:, :, :], Ls[d][:, :, :], cbs[d][:, :].un(1).bc([128, 4, 128]), ALU.mult)
                        for ti, d in slots:
                            pY = mmp.next()
                            pYs[d] = pY
                            for h in range(4):
                                fw.mm(pY[:, h * 64:(h + 1) * 64], cbL[d][:, h, :], xdt[d][:, h * 64:(h + 1) * 64])
                        for ti, d in slots:
                            fw.tt(yac.p(ti)[:, ti, :], yac.p(ti)[:, ti, :], pYs[d][:, 0:256], ALU.add)
                        for ci in range(2):
                            info = {}
                            for ti, d in slots:
                                c = ci if d == 0 else 1 - ci
                                rows = slice(c * 64, (c + 1) * 64)
                                first = (ti % tps == 0 and c == 0) if d == 0 else (ti % tps == tps - 1 and c == 1)
                                last = (ti % tps == tps - 1 and c == 1) if d == 0 else (ti % tps == 0 and c == 0)
                                info[d] = (c, rows, first, last, ti // tps)
                                if first:
                                    if ctx:
                                        fw.memset(sT[d][:, :], 0.0)
                                    else:
                                        for half in range(2):
                                            st_ = sst.next()
                                            fw.dma(st_[:, :], dv("ss0")[j, d, 2 * sg + half, :, :])
                                            p = mmp.next()
                                            fw.tr(p[:, 0:128], st_[:, :], C("ident"))
                                            fw.cp("act", sT[d][:, half * 128:(half + 1) * 128], p[:, 0:128])
                                    fw.cp("act", sTb[d][:, :], sT[d][:, :])
                            pos, pSs, yts = {}, {}, {}
                            for ti, d in slots:
                                c, rows, first, last, seq = info[d]
                                tsl = slice(ti * 128, (ti + 1) * 128)
                                po = mmp.next()
                                pos[d] = po
                                fw.mm(po[:, 0:256], fT.p(3)[:, 3, tsl], sTb[d][:, :])
                                pS = mmp.next()
                                pSs[d] = pS
                                fw.mm(pS[:, 0:256], Btok.p(ti)[rows, ti, :], xd[d][rows, :])
                            for ti, d in slots:
                                c, rows, first, last, seq = info[d]
                                yt = ytmp.next()
                                yts[d] = yt
                                fw.tt(yt[rows, :].re("p (h e) -> p h e", h=4), pos[d][rows, 0:256].re("p (h e) -> p h e", h=4),
                                      ss[rows, 4, ti, d * 4:(d + 1) * 4].un(2).bc([64, 4, 64]), ALU.mult)
                                fw.tt(sT[d][:, :].re("p (h e) -> p h e", h=4), sT[d][:, :].re("p (h e) -> p h e", h=4),
                                      decb[:, c, ti * 8 + d * 4:ti * 8 + d * 4 + 4].un(2).bc([128, 4, 64]), ALU.mult)
                            for ti, d in slots:
                                c, rows, first, last, seq = info[d]
                                fw.tt(sT[d][:, :], sT[d][:, :], pSs[d][:, 0:256], ALU.add)
                                fw.cp("act", sTb[d][:, :], sT[d][:, :])
                                fw.tt(yac.p(ti)[rows, ti, :], yac.p(ti)[rows, ti, :], yts[d][rows, :], ALU.add)
                            for ti, d in slots:
                                c, rows, first, last, seq = info[d]
                                if last and ctx:
                                    for half in range(2):
                                        p = mmp.next()
                                        fw.tr(p[:, 0:128], sT[d][:, half * 128:(half + 1) * 128], C("ident"))
                                        st_ = sst.next()
                                        evac(st_[:, :], p[:, 0:128])
                                        fw.dma(dv("nss")[seq, j, d, 2 * sg + half, :, :], st_[:, :])
                    for ti in range(8):
                        yg = ygb.next()
                        fw.tt(yg[:, :], yac.p(ti)[:, ti, :], zs.p(ti)[:, ti, :], ALU.mult)
                        for c2 in range(2):
                            pT = trp.next()
                            fw.tr(pT[:, 0:128], yg[:, c2 * 128:(c2 + 1) * 128], identb[:, :])
                            c = 2 * sg + c2
                            evac(catT.p(c * 2 + ti // 4)[:, c, ti * 128:(ti + 1) * 128], pT[:, 0:128])
                fw.barrier()
                fw.flush()

        dbg = stop if (stop or "").startswith("dbg") else None
        for layer in range(n_layers if (stop is None or dbg) else 0):
            mod_vectors(layer)
            if dbg == "dbg_mod":
                break
            for g in range(2):
                with ExitStack() as ph:
                    hT = fw.sb("hTg", [128, 8, 1024], BF16, 16, stack=ph)
                    with ExitStack() as ph2:
                        modulate(ph2, "nx", hT, 0, [2 * g, 2 * g + 1], 2 * g, 2)
                        fw.barrier()
                        fw.flush()
                    if dbg == "dbg_norm":
                        continue
                    if layer % 2 == 0:
                        catT = fw.sb("catT", [128, 8, 1024], BF16, 16, stack=ph)
                        even_mixer(ph, layer, g, hT, catT, dbg)
                        if dbg is None or dbg == "dbg_mlp":
                            out_proj_even(layer, g, catT)
                    else:
                        catT = fw.sb("catT", [128, 16, 1024], BF16, 32, stack=ph)
                        odd_mixer(ph, layer, g, hT, catT)
                        out_proj_odd(ph, layer, g, catT)
                    fw.barrier()
                    fw.flush()
            if dbg in (None, "dbg_mlp"):
                mlp(layer)
        if stop is None or dbg:
            final_out()
        elif stop.startswith("final"):
            final_out(int(stop[5:]))
        fw.barrier(final=True)
        fw.flush()
    return nc, fw.n_instr


_PROG = {}


def _run(inp, n_layers=DEPTH, ncores=8):
    if n_layers not in _PROG:
        _PROG[n_layers] = build(n_layers)
    nc, _ = _PROG[n_layers]
    f = lambda k: np.ascontiguousarray(np.asarray(inp[k], dtype=np.float32))
    consts = make_consts()
    rope = np.ascontiguousarray(make_rope().reshape(64, 2048))
    vecs = make_vecs(inp)
    wie = permute_cols(f("w_in_even"), even_col_perm())
    wio = permute_cols(f("w_in_odd"), odd_col_perm())
    shared = {
        "consts": consts, "rope": rope, "vecs": vecs,
        "w_mod": f("w_mod"), "w_mlp_in": f("w_mlp_in"), "w_mlp_out": f("w_mlp_out"),
        "wie": wie, "w_out_even": f("w_out_even"), "wio": wio, "w_out_odd": f("w_out_odd"),
    }
    xp, xs, c, cctx = f("x_prompt"), f("x_sample"), f("c"), f("c_ctx")
    ck, cv, sd, ssm = f("cache_attn_k"), f("cache_attn_v"), f("state_delta"), f("state_ssm")
    in_maps = []
    for i in range(ncores):
        b = i % 4
        cond = np.stack([cctx, c[b]], axis=0)
        condT = np.ascontiguousarray(cond.reshape(2, 8, 128).transpose(2, 1, 0)).reshape(128, 16)
        m = dict(shared)
        m["xin"] = np.ascontiguousarray(np.concatenate([xp[4 * i:4 * i + 4].reshape(1024, D), xs[b]], axis=0))
        m["condT"] = condT
        m["ctxk"] = np.ascontiguousarray(ck[b].reshape(2, 512, 128))
        m["ctxv"] = np.ascontiguousarray(cv[b].reshape(2, 512, 128))
        m["sd0"] = np.ascontiguousarray(sd[b])
        m["ss0"] = np.ascontiguousarray(ssm[b].reshape(2, 2, 16, 128, 128))
        in_maps.append(m)
    res = run_bass_kernel_spmd(nc, in_maps, core_ids=list(range(ncores)))
    R = res.results
    if ncores < 8:
        return R
    y_p = np.concatenate([R[i]["y"][:1024].reshape(4, 256, D) for i in range(8)], axis=0)
    y_s = np.stack([R[b]["y"][1024:] for b in range(4)], axis=0)
    nk = np.concatenate([R[i]["nk"].reshape(2, 4, 256, 2, 64).transpose(1, 0, 2, 3, 4) for i in range(8)], axis=0)
    nv = np.concatenate([R[i]["nv"].reshape(2, 4, 256, 2, 64).transpose(1, 0, 2, 3, 4) for i in range(8)], axis=0)
    nsd = np.concatenate([R[i]["nsd"] for i in range(8)], axis=0)
    nss = np.concatenate([R[i]["nss"].reshape(4, 2, 2, 32, 64, 128) for i in range(8)], axis=0)
    out = (y_p, y_s, nk, nv, nsd, nss)
    return tuple(np.ascontiguousarray(o, dtype=np.float32) for o in out)


def kernel(**inputs):
    return _run(inputs, DEPTH)
```

```python
import math
import numpy as np
from contextlib import ExitStack
import concourse.bass as bass
import concourse.mybir as mybir
from concourse.bass_utils import run_bass_kernel_spmd

F32 = mybir.dt.float32
BF16 = mybir.dt.bfloat16
AF = mybir.ActivationFunctionType
ALU = mybir.AluOpType

D = 1024
DEPTH = 4
EPS = 1e-6
NEG = -30000.0


class Dep:
    __slots__ = ("w", "r", "excl")

    def __init__(self):
        self.w = None
        self.r = {}
        self.excl = False


class V:
    __slots__ = ("ap", "deps")

    def __init__(self, ap, deps):
        self.ap = ap
        self.deps = deps

    def __getitem__(self, k):
        return V(self.ap[k], self.deps)

    def re(self, pat, **kw):
        return V(self.ap.rearrange(pat, **kw), self.deps)

    def bc(self, shape):
        return V(self.ap.to_broadcast(list(shape)), self.deps)

    def un(self, axis):
        return V(self.ap.unsqueeze(axis), self.deps)


class _P:
    __slots__ = ("t", "deps")

    def __init__(self, t, deps):
        self.t = t
        self.deps = deps

    def __getitem__(self, k):
        return V(self.t[k], self.deps)


class T:
    def __init__(self, t, nparts=1):
        self.t = t
        self.deps = [Dep() for _ in range(nparts)]

    def __getitem__(self, k):
        return V(self.t[k], self.deps)

    def p(self, *idx):
        return _P(self.t, [self.deps[i] for i in idx])


class Rot:
    def __init__(self, items):
        self.items = items
        self.i = 0

    def next(self):
        x = self.items[self.i]
        self.i = (self.i + 1) % len(self.items)
        return x


def _ap(x):
    return x.ap if isinstance(x, V) else x


def _deps(*xs):
    out = []
    for x in xs:
        if isinstance(x, V):
            out.extend(x.deps)
    return out


class FW:
    def __init__(self, nc, stack, n_io=16, n_w=4):
        self.nc = nc
        self.stack = stack
        self.engs = {"pe": nc.tensor, "act": nc.scalar, "dve": nc.vector, "pool": nc.gpsimd, "sp": nc.sync}
        self.sems = {}
        self.cnt = {}
        for k in ["pe", "act", "dve"]:
            self.sems[k] = stack.enter_context(nc.semaphore("s_" + k))
            self.cnt[k] = 0
        self.io_ch = []
        for i in range(n_io):
            k = "io%d" % i
            self.sems[k] = stack.enter_context(nc.semaphore("s_" + k))
            self.cnt[k] = 0
            self.io_ch.append(k)
        self.w_ch = []
        for i in range(n_w):
            k = "w%d" % i
            self.sems[k] = stack.enter_context(nc.semaphore("s_" + k))
            self.cnt[k] = 0
            self.w_ch.append(k)
        self.nio = 0
        self.nw = 0
        self.waited = {k: {} for k in self.engs}
        self.n_instr = 0
        self.prog = {k: [] for k in self.engs}

    def flush(self):
        prog = self.prog
        self.prog = {k: [] for k in self.engs}
        with self.nc.Block() as block:
            def mk(lst):
                def body(e):
                    for f in lst:
                        f(e)
                return body
            block.sync(mk(prog["sp"]))
            block.tensor(mk(prog["pe"]))
            block.scalar(mk(prog["act"]))
            block.vector(mk(prog["dve"]))
            block.gpsimd(mk(prog["pool"]))

    def sb(self, name, shape, dtype, nparts=1, stack=None):
        st = stack or self.stack
        self.uid = getattr(self, "uid", 0) + 1
        return T(st.enter_context(self.nc.sbuf_tensor("%s_%d" % (name, self.uid), list(shape), dtype)), nparts)

    def ps(self, name, shape, dtype, nparts=1):
        t = T(self.stack.enter_context(self.nc.psum_tensor(name, list(shape), dtype)), nparts)
        for d in t.deps:
            d.excl = True
        return t

    def _wait(self, issuer, key, val):
        if val <= 0:
            return
        w = self.waited[issuer]
        if w.get(key, 0) >= val:
            return
        w[key] = val
        sem = self.sems[key]
        self.prog[issuer].append(lambda e: e.wait_ge(sem, val))

    def _gather(self, reads, writes):
        need = {}
        for d in reads:
            if d.w is not None:
                k, c = d.w
                if need.get(k, 0) < c:
                    need[k] = c
            if d.excl:
                for k, c in d.r.items():
                    if need.get(k, 0) < c:
                        need[k] = c
        for d in writes:
            if d.w is not None:
                k, c = d.w
                if need.get(k, 0) < c:
                    need[k] = c
            for k, c in d.r.items():
                if need.get(k, 0) < c:
                    need[k] = c
        return need

    def _commit(self, key, val, reads, writes):
        for d in writes:
            d.w = (key, val)
            d.r = {}
        for d in reads:
            if d.r.get(key, 0) < val:
                d.r[key] = val

    def op(self, eng, fn, reads=(), writes=()):
        need = self._gather(reads, writes)
        for k, c in need.items():
            if k == eng and eng == "pe":
                continue
            self._wait(eng, k, c)
        sem = self.sems[eng]
        self.prog[eng].append(lambda e: fn(e).then_inc(sem, 1))
        self.cnt[eng] += 1
        self._commit(eng, self.cnt[eng], reads, writes)
        self.n_instr += 1

    def dma(self, out, in_, q="sp"):
        reads = _deps(in_)
        writes = _deps(out)
        if q == "pool":
            ch = self.w_ch[self.nw % len(self.w_ch)]
            self.nw += 1
        else:
            ch = self.io_ch[self.nio % len(self.io_ch)]
            self.nio += 1
        need = self._gather(reads, writes)
        need[ch] = max(need.get(ch, 0), self.cnt[ch])
        for k, c in need.items():
            self._wait(q, k, c)
        sem = self.sems[ch]
        o, i = _ap(out), _ap(in_)
        self.prog[q].append(lambda e: e.dma_start(out=o, in_=i).then_inc(sem, 16))
        self.cnt[ch] += 16
        self._commit(ch, self.cnt[ch], reads, writes)
        self.n_instr += 1

    def barrier(self, final=False):
        keys = ["pe", "act", "dve"] + self.io_ch + (self.w_ch if final else [])
        for issuer in ["sp", "pe", "act", "dve"] + (["pool"] if final else []):
            for k in keys:
                if k != issuer:
                    self._wait(issuer, k, self.cnt[k])

    def mm(self, out, lhsT, rhs, start=True, stop=True):
        o, l, r = _ap(out), _ap(lhsT), _ap(rhs)
        self.op("pe", lambda e: e.matmul(o, lhsT=l, rhs=r, start=start, stop=stop),
                _deps(lhsT, rhs), _deps(out))

    def tr(self, out, in_, ident):
        o, i, d = _ap(out), _ap(in_), _ap(ident)
        self.op("pe", lambda e: e.transpose(o, i, d), _deps(in_, ident), _deps(out))

    def act(self, out, in_, func, bias=None, scale=None, accum=None):
        o, i = _ap(out), _ap(in_)
        kw = {}
        if bias is not None:
            kw["bias"] = _ap(bias)
        if scale is not None:
            kw["scale"] = _ap(scale)
        if accum is not None:
            kw["accum_out"] = _ap(accum)
        self.op("act", lambda e: e.activation(out=o, in_=i, func=func, **kw),
                _deps(in_, bias, scale), _deps(out, accum))

    def tt(self, out, in0, in1, op, eng="dve"):
        o, a, b = _ap(out), _ap(in0), _ap(in1)
        self.op(eng, lambda e: e.tensor_tensor(out=o, in0=a, in1=b, op=op), _deps(in0, in1), _deps(out))

    def ts(self, out, in0, s1, op0, s2=None, op1=None, eng="dve"):
        o, a, x1, x2 = _ap(out), _ap(in0), _ap(s1), _ap(s2)
        if op1 is None:
            self.op(eng, lambda e: e.tensor_scalar(out=o, in0=a, scalar1=x1, scalar2=None, op0=op0),
                    _deps(in0, s1), _deps(out))
        else:
            self.op(eng, lambda e: e.tensor_scalar(out=o, in0=a, scalar1=x1, scalar2=x2, op0=op0, op1=op1),
                    _deps(in0, s1, s2), _deps(out))

    def stt(self, out, in0, scalar, in1, op0, op1):
        o, a, s, b = _ap(out), _ap(in0), _ap(scalar), _ap(in1)
        self.op("dve", lambda e: e.scalar_tensor_tensor(out=o, in0=a, scalar=s, in1=b, op0=op0, op1=op1),
                _deps(in0, scalar, in1), _deps(out))

    def cp(self, eng, out, in_):
        o, i = _ap(out), _ap(in_)
        if eng == "act":
            self.op("act", lambda e: e.activation(out=o, in_=i, func=AF.Copy), _deps(in_), _deps(out))
        else:
            self.op("dve", lambda e: e.tensor_copy(out=o, in_=i), _deps(in_), _deps(out))

    def recip(self, out, in_):
        o, i = _ap(out), _ap(in_)
        self.op("dve", lambda e: e.reciprocal(out=o, in_=i), _deps(in_), _deps(out))

    def memset(self, out, val):
        o = _ap(out)
        self.op("dve", lambda e: e.memset(o, val), (), _deps(out))


VEC_LAYOUT = [("gmix", 32), ("gmlp", 32), ("bmod", 192), ("fing", 8), ("qkg", 4), ("dconv", 72),
              ("dalog", 16), ("ddtb", 16), ("dng", 2), ("sconv", 256), ("salog", 128), ("sdtb", 128),
              ("sd", 64), ("sng", 32)]
VOFF = {}
_o = 0
for _n, _c in VEC_LAYOUT:
    VOFF[_n] = (_o, _c)
    _o += _c
NVEC = _o

CST_LAYOUT = [("ident", 128), ("ones", 128), ("blk", 128), ("triF", 128), ("triB", 128), ("ind0", 128),
              ("ind1", 128), ("nmF", 256), ("nmB", 256), ("nmFc", 128), ("nmBc", 128), ("ropeR", 64)]
COFF = {}
_o = 0
for _n, _c in CST_LAYOUT:
    COFF[_n] = (_o, _c)
    _o += _c
NCST = _o


def make_consts():
    c = np.zeros((128, NCST), np.float32)
    idx = np.arange(128)
    j = idx[:, None]
    i = idx[None, :]
    same = (j // 64) == (i // 64)

    def put(name, a):
        o, n = COFF[name]
        c[:a.shape[0], o:o + a.shape[1]] = a

    put("ident", np.eye(128, dtype=np.float32))
    put("ones", np.ones((128, 128), np.float32))
    put("blk", same.astype(np.float32))
    put("triF", (same & (j <= i)).astype(np.float32))
    put("triB", (same & (j >= i)).astype(np.float32))
    put("ind0", np.broadcast_to((idx < 64)[:, None], (128, 128)).astype(np.float32))
    put("ind1", np.broadcast_to((idx >= 64)[:, None], (128, 128)).astype(np.float32))
    fc = np.where(same & (i >= j), 0.0, NEG).astype(np.float32)
    fs = np.where(same & (i > j), 0.0, NEG).astype(np.float32)
    bc = np.where(same & (i <= j), 0.0, NEG).astype(np.float32)
    bs = np.where(same & (i < j), 0.0, NEG).astype(np.float32)
    put("nmF", np.concatenate([fc, fs], axis=1))
    put("nmB", np.concatenate([bc, bs], axis=1))
    put("nmFc", fc)
    put("nmBc", bc)
    R = np.zeros((64, 64), np.float32)
    for base in (0, 32):
        for t in range(16):
            R[base + 16 + t, base + t] = -1.0
            R[base + t, base + 16 + t] = 1.0
    put("ropeR", R)
    return c


def make_rope():
    t = np.arange(1024)
    row = (t // 64).astype(np.float32)
    col = (t % 64).astype(np.float32)
    inv = (10000.0 ** (-np.arange(0, 32, 2, dtype=np.float32) / 32)).astype(np.float32)
    ar = row[None, :] * inv[:, None]
    ac = col[None, :] * inv[:, None]
    ang = np.concatenate([ar, ar, ac, ac], axis=0)
    return np.stack([np.cos(ang), np.sin(ang)], axis=1).astype(np.float32)


def even_col_perm():
    cols = []
    for g2 in range(2):
        for hh in range(4):
            h = 4 * g2 + hh
            cols += list(range(h * 64, h * 64 + 64))
        cols += list(range(512 + g2 * 64, 512 + g2 * 64 + 64))
        cols += list(range(640 + g2 * 64, 640 + g2 * 64 + 64))
    for hd in range(4):
        cols += list(range(768 + hd * 128, 768 + hd * 128 + 128))
        cols += list(range(768 + 512 + hd * 128, 768 + 512 + hd * 128 + 128))
        cols += list(range(768 + 1024 + hd * 128, 768 + 1024 + hd * 128 + 128))
        cols += list(range(2304 + hd * 128, 2304 + hd * 128 + 128))
        cols += [2816 + hd, 2816 + 4 + hd, 2824 + hd, 2824 + 4 + hd]
        cols += [-1] * 124
    return np.array(cols)


def odd_col_perm():
    cols = []
    for sg in range(8):
        cols += list(range(2048 + sg * 256, 2048 + sg * 256 + 256))
        cols += list(range(4096 + sg * 128, 4096 + sg * 128 + 128))
        cols += list(range(5120 + sg * 128, 5120 + sg * 128 + 128))
        cols += list(range(sg * 256, sg * 256 + 256))
        cols += list(range(6144 + sg * 4, 6144 + sg * 4 + 4))
        cols += list(range(6144 + 32 + sg * 4, 6144 + 32 + sg * 4 + 4))
        cols += [-1] * 56
    return np.array(cols)


def permute_cols(w, perm):
    w = np.asarray(w, np.float32)
    out = np.zeros(w.shape[:-1] + (len(perm),), np.float32)
    m = perm >= 0
    out[..., m] = w[..., perm[m]]
    return out


def colmajor(v):
    v = np.asarray(v, np.float32)
    lead = v.shape[:-1]
    C = v.shape[-1] // 128
    a = v.reshape(lead + (C, 128))
    a = np.moveaxis(a, -1, 0)
    return np.ascontiguousarray(a).reshape(128, -1)


def bcast_rows(v):
    v = np.asarray(v, np.float32).reshape(1, -1)
    return np.ascontiguousarray(np.broadcast_to(v, (128, v.shape[1])))


def make_vecs(inp):
    vec = np.zeros((128, NVEC), np.float32)

    def put(name, a):
        o, n = VOFF[name]
        assert a.shape[1] == n, (name, a.shape, n)
        vec[:a.shape[0], o:o + n] = a

    put("gmix", colmajor(inp["norm_mix_g"]))
    put("gmlp", colmajor(inp["norm_mlp_g"]))
    put("bmod", colmajor(inp["b_mod"]))
    put("fing", colmajor(inp["final_norm_g"]))
    qk = np.stack([inp["attn_q_norm_g"], inp["attn_k_norm_g"]], axis=1)
    put("qkg", np.ascontiguousarray(np.moveaxis(qk, -1, 0)).reshape(64, 4))
    dc = np.asarray(inp["delta_conv_w"], np.float32).reshape(2, 3, 3, 4, 128)
    dc = np.transpose(dc, (4, 0, 3, 2, 1))
    put("dconv", np.ascontiguousarray(dc).reshape(128, 72))
    put("dalog", bcast_rows(inp["delta_a_log"]))
    put("ddtb", bcast_rows(inp["delta_dt_bias"]))
    put("dng", np.ascontiguousarray(np.asarray(inp["delta_norm_g"], np.float32).T))
    cw = np.asarray(inp["ssm_conv_w"], np.float32)
    cb = np.asarray(inp["ssm_conv_b"], np.float32)
    wb = np.concatenate([cw, cb[:, None, :]], axis=1)
    sc = np.zeros((128, 2, 8, 4, 4), np.float32)
    for sg in range(8):
        chans = [(sg * 256, 128), (sg * 256 + 128, 128), (2048 + sg * 128, 128), (3072 + sg * 128, 128)]
        for ci, (c0, n) in enumerate(chans):
            sc[:, :, sg, ci, :] = np.transpose(wb[:, :, c0:c0 + 128], (2, 0, 1))
    put("sconv", sc.reshape(128, 256))
    put("salog", bcast_rows(inp["ssm_a_log"]))
    put("sdtb", bcast_rows(inp["ssm_dt_bias"]))
    put("sd", bcast_rows(inp["ssm_d"]))
    put("sng", colmajor(inp["ssm_norm_g"]))
    return vec


import os as _os0
SILU = AF.Identity if "silu" in _os0.environ.get("DBGSKIP", "") else AF.Silu


def build(n_layers=DEPTH, stop=None):
    nc = bass.Bass("TRN2", target_bir_lowering=False)
    dr = {}

    def din(name, shape):
        dr[name] = nc.dram_tensor(name, list(shape), F32, kind="ExternalInput").ap()

    def dout(name, shape):
        dr[name] = nc.dram_tensor(name, list(shape), F32, kind="ExternalOutput").ap()

    din("xin", [2048, D])
    din("condT", [128, 16])
    din("consts", [128, NCST])
    din("rope", [64, 2048])
    din("vecs", [128, NVEC])
    din("ctxk", [2, 512, 128])
    din("ctxv", [2, 512, 128])
    din("sd0", [2, 2, 4, 128, 128])
    din("ss0", [2, 2, 16, 128, 128])
    din("w_mod", [DEPTH, D, 6 * D])
    din("w_mlp_in", [DEPTH, D, 4 * D])
    din("w_mlp_out", [DEPTH, 4 * D, D])
    din("wie", [2, D, 3328])
    din("w_out_even", [2, D, D])
    din("wio", [2, D, 6656])
    din("w_out_odd", [2, 2 * D, D])
    dout("y", [2048, D])
    dout("nk", [2, 1024, 128])
    dout("nv", [2, 1024, 128])
    dout("nsd", [4, 2, 2, 4, 128, 128])
    dout("nss", [4, 2, 2, 16, 128, 128])

    with ExitStack() as st:
        fw = FW(nc, st)
        dv = lambda name: V(dr[name], [])

        import os as _os
        _pad = int(_os.environ.get("DBGPAD", "0"))
        if _pad:
            fw.sb("dbgpad", [128, _pad * 256], F32)
        xT = fw.sb("xT", [128, 8, 2048], F32, 32)
        cst = fw.sb("cst", [128, NCST], F32)
        vec = fw.sb("vec", [128, NVEC], F32)
        identb = fw.sb("identb", [128, 128], BF16)
        onesb = fw.sb("onesb", [128, 128], BF16)
        scT = fw.sb("scT", [128, 8, 2], BF16)
        modv = fw.sb("modv", [128, 48, 2], F32)
        modA = fw.sb("modA", [128, 2, 8, 2], F32)
        ring = Rot([fw.sb("wr%d" % i, [128, 4096], BF16, 2) for i in range(3)])
        mmp = Rot([fw.ps("pm%d" % i, [128, 512], F32) for i in range(4)])
        accp = Rot([fw.ps("pa%d" % i, [128, 512], F32) for i in range(2)])
        trp = Rot([fw.ps("pt%d" % i, [128, 1024], BF16) for i in range(2)])

        def C(name, rows=128, c0=0, c1=None):
            o, n = COFF[name]
            c1 = n if c1 is None else c1
            return cst[0:rows, o + c0:o + c1]

        def VC(name, col, rows=128, n=1):
            o, _ = VOFF[name]
            return vec[0:rows, o + col:o + col + n]

        def xv(kc, tt):
            return xT.p(kc * 4 + tt)[:, kc, tt * 512:(tt + 1) * 512]

        def wload(src, KC, cols):
            t = ring.next()
            h = KC // 2
            s3 = src.rearrange("(kc p) c -> p kc c", p=128)
            for a in range(2):
                dst = t.p(a)[:, a * h * cols:(a + 1) * h * cols].re("p (kc c) -> p kc c", kc=h)
                fw.dma(dst, V(s3[:, a * h:(a + 1) * h, :], []), q="pool")
            return t[:, 0:KC * cols].re("p (kc c) -> p kc c", kc=KC)

        evac_i = [0]

        def evac(out, in_):
            evac_i[0] ^= 1
            fw.cp("act" if evac_i[0] else "dve", out, in_)

        fw.dma(cst[:, :], dv("consts"))
        fw.dma(vec[:, :], dv("vecs"))
        fw.cp("act", identb[:, :], C("ident"))
        fw.cp("dve", onesb[:, :], C("ones"))
        with ExitStack() as ph:
            ctmp = fw.sb("ctmp", [128, 16], F32, stack=ph)
            fw.dma(ctmp[:, :], dv("condT"))
            fw.act(scT[:, :, :].re("p a b -> p (a b)"), ctmp[:, :], SILU)
            xst = Rot([fw.sb("xst%d" % i, [128, D], F32, stack=ph) for i in range(2)])
            for ti in range(16):
                s = xst.next()
                fw.dma(s[:, :], dv("xin")[ti * 128:(ti + 1) * 128, :])
                for half in range(2):
                    ps = mmp.next()
                    for q in range(4):
                        kc = half * 4 + q
                        fw.tr(ps[:, q * 128:(q + 1) * 128], s[:, kc * 128:(kc + 1) * 128], C("ident"))
                    tt = ti // 4
                    dst = xT.p(*[(half * 4 + q) * 4 + tt for q in range(4)])[
                        :, half * 4:half * 4 + 4, ti * 128:(ti + 1) * 128]
                    evac(dst, ps[:, :].re("p (a b) -> p a b", a=4))
            fw.barrier()
            fw.flush()

        def mod_vectors(layer):
            for ot in range(12):
                wt = wload(dr["w_mod"][layer, :, ot * 512:(ot + 1) * 512], 8, 512)
                ps = mmp.next()
                for o4 in range(4):
                    for kc in range(8):
                        fw.mm(ps[:, o4 * 2:o4 * 2 + 2], wt[:, kc, o4 * 128:(o4 + 1) * 128], scT[:, kc, :],
                              start=(kc == 0), stop=(kc == 7))
                fw.tt(modv[:, ot * 4:(ot + 1) * 4, :], ps[:, 0:8].re("p (a b) -> p a b", a=4),
                      VC("bmod", layer * 48 + ot * 4, n=4).un(2).bc([128, 4, 2]), ALU.add)
            for which, (gname, sc0) in enumerate((("gmix", 8), ("gmlp", 32))):
                fw.ts(modA[:, which, :, :], modv[:, sc0:sc0 + 8, :], 1.0, ALU.add)
                fw.tt(modA[:, which, :, :], modA[:, which, :, :],
                      VC(gname, layer * 8, n=8).un(2).bc([128, 8, 2]), ALU.mult)

        def rstd_bc(ph, name, nfeat):
            sqp = Rot([fw.sb("%s_sq%d" % (name, i), [128, 512], BF16, stack=ph) for i in range(3)])
            rsp = Rot([fw.sb("%s_rs%d" % (name, i), [128, 512], F32, stack=ph) for i in range(2)])

            def f(srcs):
                ps = mmp.next()
                n = len(srcs)
                for i, s in enumerate(srcs):
                    sq = sqp.next()
                    fw.act(sq[:, :], s, AF.Square)
                    fw.mm(ps[:, :], onesb[:, :], sq[:, :], start=(i == 0), stop=(i == n - 1))
                rs = rsp.next()
                fw.act(rs[:, :], ps[:, :], AF.Sqrt, bias=EPS, scale=1.0 / nfeat)
                fw.recip(rs[:, :], rs[:, :])
                return rs
            return f

        def modulate(ph, name, hT, which, tts, t0, ntt):
            rfn = rstd_bc(ph, name, D)
            tmpp = Rot([fw.sb("%s_tmp%d" % (name, i), [128, 512], F32, stack=ph) for i in range(3)])
            sh0 = 0 if which == 0 else 24
            for tt in tts:
                r = 0 if tt < 2 else 1
                rs = rfn([xv(kc, tt) for kc in range(8)])
                for kc in range(8):
                    tmp = tmpp.next()
                    fw.tt(tmp[:, :], xv(kc, tt), rs[:, :], ALU.mult)
                    lt = tt - t0
                    fw.act(hT.p(kc * ntt + lt)[:, kc, lt * 512:(lt + 1) * 512], tmp[:, :], AF.Identity,
                           bias=modv[:, sh0 + kc, r:r + 1], scale=modA[:, which, kc, r:r + 1])

        def mlp(layer):
            with ExitStack() as ph:
                hT = fw.sb("hTm", [128, 8, 2048], BF16, 32, stack=ph)
                h1 = fw.sb("h1", [128, 8, 2048], BF16, 32, stack=ph)
                rl = Rot([fw.sb("rl%d" % i, [128, 512], F32, stack=ph) for i in range(3)])
                modulate(ph, "nm", hT, 1, range(4), 0, 4)
                for blk in range(4):
                    for ht in range(2):
                        c0 = blk * 1024 + ht * 512
                        wt = wload(dr["w_mlp_in"][layer, :, c0:c0 + 512], 8, 512)
                        for h4 in range(4):
                            hc = ht * 4 + h4
                            for tt in range(4):
                                ps = mmp.next()
                                for kc in range(8):
                                    fw.mm(ps[:, :], wt[:, kc, h4 * 128:(h4 + 1) * 128],
                                          hT.p(kc * 4 + tt)[:, kc, tt * 512:(tt + 1) * 512],
                                          start=(kc == 0), stop=(kc == 7))
                                r = rl.next()
                                fw.act(r[:, :], ps[:, :], AF.Relu)
                                fw.tt(h1.p(hc * 4 + tt)[:, hc, tt * 512:(tt + 1) * 512], r[:, :], r[:, :], ALU.mult)
                    for ot in range(2):
                        wt = wload(dr["w_mlp_out"][layer, blk * 1024:(blk + 1) * 1024, ot * 512:(ot + 1) * 512], 8, 512)
                        for o4 in range(4):
                            oc = ot * 4 + o4
                            for tt in range(4):
                                r = 0 if tt < 2 else 1
                                ps = mmp.next()
                                for kc in range(8):
                                    fw.mm(ps[:, :], wt[:, kc, o4 * 128:(o4 + 1) * 128],
                                          h1.p(kc * 4 + tt)[:, kc, tt * 512:(tt + 1) * 512],
                                          start=(kc == 0), stop=(kc == 7))
                                fw.stt(xv(oc, tt), ps[:, :], modv[:, 40 + oc, r:r + 1], xv(oc, tt), ALU.mult, ALU.add)
                fw.barrier()
                fw.flush()

        def final_out(lvl=9):
            with ExitStack() as ph:
                rfn = rstd_bc(ph, "fn", D)
                yT = fw.sb("yT", [128, 8, 512], F32, stack=ph)
                ost = Rot([fw.sb("ost%d" % i, [128, D], F32, stack=ph) for i in range(2)])
                for tt in range(4):
                    rs = rfn([xv(kc, tt) for kc in range(8)])
                    if lvl < 2:
                        continue
                    for kc in range(8):
                        fw.stt(yT[:, kc, :], xv(kc, tt), VC("fing", kc), rs[:, :], ALU.mult, ALU.mult)
                    if lvl < 3:
                        continue
                    for q in range(4):
                        ti = tt * 4 + q
                        o = ost.next()
                        for half in range(2):
                            ps = mmp.next()
                            for a in range(4):
                                kc = half * 4 + a
                                fw.tr(ps[:, a * 128:(a + 1) * 128], yT[:, kc, q * 128:(q + 1) * 128], C("ident"))
                            evac(o[:, half * 512:(half + 1) * 512], ps[:, :])
                        if lvl >= 4:
                            fw.dma(dv("y")[ti * 128:(ti + 1) * 128, :], o[:, :])
                fw.barrier()
                fw.flush()

        def out_proj_even(layer, g, catT):
            j = layer // 2
            for ot in range(2):
                wt = wload(dr["w_out_even"][j, :, ot * 512:(ot + 1) * 512], 8, 512)
                for o4 in range(4):
                    oc = ot * 4 + o4
                    for tg in range(2):
                        tt = 2 * g + tg
                        ps = mmp.next()
                        for kc in range(8):
                            fw.mm(ps[:, :], wt[:, kc, o4 * 128:(o4 + 1) * 128],
                                  catT.p(kc * 2 + tg)[:, kc, tg * 512:(tg + 1) * 512], start=(kc == 0), stop=(kc == 7))
                        fw.stt(xv(oc, tt), ps[:, :], modv[:, 16 + oc, g:g + 1], xv(oc, tt), ALU.mult, ALU.add)

        def out_proj_odd(ph, layer, g, catT):
            j = layer // 2
            rfn = rstd_bc(ph, "on", 2 * D)
            rsk = fw.sb("rsk", [128, 1024], F32, stack=ph)
            tmpp = Rot([fw.sb("opt%d" % i, [128, 512], F32, stack=ph) for i in range(2)])
            for tg in range(2):
                rs = rfn([catT.p(c * 2 + tg)[:, c, tg * 512:(tg + 1) * 512] for c in range(16)])
                fw.cp("dve", rsk[:, tg * 512:(tg + 1) * 512], rs[:, :])
            for ot in range(4):
                wt = wload(dr["w_out_odd"][j, :, ot * 256:(ot + 1) * 256], 16, 256)
                fw.tt(wt, wt, VC("sng", j * 16, n=16).un(2).bc([128, 16, 256]), ALU.mult)
                for o2 in range(2):
                    oc = ot * 2 + o2
                    for tg in range(2):
                        tt = 2 * g + tg
                        ps = mmp.next()
                        for kc in range(16):
                            fw.mm(ps[:, :], wt[:, kc, o2 * 128:(o2 + 1) * 128],
                                  catT.p(kc * 2 + tg)[:, kc, tg * 512:(tg + 1) * 512], start=(kc == 0), stop=(kc == 15))
                        tmp = tmpp.next()
                        fw.tt(tmp[:, :], ps[:, :], rsk[:, tg * 512:(tg + 1) * 512], ALU.mult)
                        fw.stt(xv(oc, tt), tmp[:, :], modv[:, 16 + oc, g:g + 1], xv(oc, tt), ALU.mult, ALU.add)
        def rsum(out, in_):
            o, i = _ap(out), _ap(in_)
            fw.op("dve", lambda e: e.reduce_sum(out=o, in_=i, axis=mybir.AxisListType.X), _deps(in_), _deps(out))

        def even_mixer(ph, layer, g, hT, catT, dbg=None):
            j = layer // 2
            ctx = (g == 0)
            nseq, L = (4, 256) if ctx else (1, 1024)
            tps = L // 128

            def hv(kc, lo, n):
                return hT.p(kc * 2 + lo // 512)[:, kc, lo:lo + n]

            with ExitStack() as ua:
                qT = fw.sb("qT", [64, 4, 1024], BF16, 4, stack=ua)
                kT = fw.sb("kT", [64, 1536], BF16, stack=ua)
                V1 = fw.sb("V1", [128, 12, 65], BF16, stack=ua)
                otok = fw.sb("otok", [128, 8, 256], BF16, 8, stack=ua)
                raw = Rot([fw.sb("araw%d" % i, [64, 1024], F32, stack=ua) for i in range(2)])
                qn = Rot([fw.sb("aqn%d" % i, [64, 512], F32, stack=ua) for i in range(2)])
                sqb = Rot([fw.sb("asq%d" % i, [64, 512], BF16, stack=ua) for i in range(2)])
                rsb = Rot([fw.sb("ars%d" % i, [64, 512], F32, stack=ua) for i in range(2)])
                t1p = Rot([fw.sb("at1%d" % i, [64, 512], F32, stack=ua) for i in range(2)])
                t2p = Rot([fw.sb("at2%d" % i, [64, 512], F32, stack=ua) for i in range(2)])
                ptp = Rot([fw.sb("apt%d" % i, [128, 512], BF16, stack=ua) for i in range(3)])
                rdp = Rot([fw.sb("ard%d" % i, [128, 1], F32, stack=ua) for i in range(4)])
                vst = Rot([fw.sb("avs%d" % i, [128, 64], F32, stack=ua) for i in range(2)])
                kst = Rot([fw.sb("aks%d" % i, [128, 64], F32, stack=ua) for i in range(2)])
                cks = Rot([fw.sb("ack%d" % i, [128, 128], F32, stack=ua) for i in range(2)])
                if not ctx:
                    rope = fw.sb("ropeT", [64, 2048], F32, stack=ua)
                    fw.dma(rope[:, :], dv("rope"))
                fw.memset(V1[:, :, 64:65], 1.0)
                for g2 in range(2):
                    wt = wload(dr["wie"][j, :, g2 * 384:(g2 + 1) * 384], 8, 384)

                    def proj_norm(col0, gain_col, dst_fn, is_k):
                        r = raw.next()
                        for tg in range(2):
                            ps = mmp.next()
                            for kc in range(8):
                                fw.mm(ps[0:64, :], wt[:, kc, col0:col0 + 64], hv(kc, tg * 512, 512),
                                      start=(kc == 0), stop=(kc == 7))
                            evac(r[:, tg * 512:(tg + 1) * 512], ps[0:64, :])
                        for tg in range(2):
                            sl = slice(tg * 512, (tg + 1) * 512)
                            sq = sqb.next()
                            fw.act(sq[:, :], r[:, sl], AF.Square)
                            ps = mmp.next()
                            fw.mm(ps[0:64, :], onesb[0:64, 0:64], sq[:, :])
                            rs = rsb.next()
                            fw.act(rs[:, :], ps[0:64, :], AF.Sqrt, bias=EPS, scale=1.0 / 64)
                            fw.recip(rs[:, :], rs[:, :])
                            if ctx and not is_k:
                                fw.stt(dst_fn(sl), r[:, sl], gain_col, rs[:, :], ALU.mult, ALU.mult)
                                continue
                            q = qn.next()
                            fw.stt(q[:, :], r[:, sl], gain_col, rs[:, :], ALU.mult, ALU.mult)
                            if ctx:
                                fw.cp("act", dst_fn(sl), q[:, :])
                                for t4 in range(4):
                                    ti = tg * 4 + t4
                                    ps2 = mmp.next()
                                    fw.tr(ps2[:, 0:64], q[:, t4 * 128:(t4 + 1) * 128], C("ident", 64, 0, 64))
                                    ks = kst.next()
                                    evac(ks[:, :], ps2[:, 0:64])
                                    fw.dma(dv("nk")[j, ti * 128:(ti + 1) * 128, g2 * 64:(g2 + 1) * 64], ks[:, :])
                            else:
                                ps2 = mmp.next()
                                fw.mm(ps2[0:64, :], C("ropeR", 64), q[:, :])
                                t1 = t1p.next()
                                fw.tt(t1[:, :], q[:, :], rope[:, sl], ALU.mult)
                                t2 = t2p.next()
                                fw.tt(t2[:, :], ps2[0:64, :], rope[:, 1024 + tg * 512:1024 + (tg + 1) * 512], ALU.mult)
                                fw.tt(dst_fn(sl), t1[:, :], t2[:, :], ALU.add)

                    for hh in range(4):
                        proj_norm(hh * 64, VC("qkg", j * 2 + 0, rows=64),
                                  (lambda sl, hh=hh: qT.p(hh)[:, hh, sl]), False)
                    proj_norm(256, VC("qkg", j * 2 + 1, rows=64), (lambda sl: kT[:, sl]), True)
                    for ti in range(8):
                        ps = mmp.next()
                        for kc in range(8):
                            fw.mm(ps[:, 0:64], hv(kc, ti * 128, 128), wt[:, kc, 320:384], start=(kc == 0), stop=(kc == 7))
                        fw.cp("act", V1[:, ti, 0:64], ps[:, 0:64])
                        if ctx:
                            vs = vst.next()
                            fw.cp("dve", vs[:, :], ps[:, 0:64])
                            fw.dma(dv("nv")[j, ti * 128:(ti + 1) * 128, g2 * 64:(g2 + 1) * 64], vs[:, :])
                    if not ctx:
                        for t in range(4):
                            ck = cks.next()
                            fw.dma(ck[:, 0:64], dv("ctxk")[j, t * 128:(t + 1) * 128, g2 * 64:(g2 + 1) * 64])
                            fw.dma(ck[:, 64:128], dv("ctxv")[j, t * 128:(t + 1) * 128, g2 * 64:(g2 + 1) * 64])
                            ps2 = mmp.next()
                            fw.tr(ps2[0:64, 0:128], ck[:, 0:64], C("ident"))
                            fw.cp("act", kT[:, 1024 + t * 128:1024 + (t + 1) * 128], ps2[0:64, 0:128])
                            fw.cp("dve", V1[:, 8 + t, 0:64], ck[:, 64:128])
                    QB = 256 if ctx else 512
                    for s in range(nseq):
                        if ctx:
                            kts = [(slice(ti * 128, (ti + 1) * 128), ti) for ti in (2 * s, 2 * s + 1)]
                        else:
                            kts = [(slice(t * 128, (t + 1) * 128), t) for t in range(12)]
                        for hh in range(4):
                            for qb in range(L // QB):
                                q0 = s * L + qb * QB
                                oacc = accp.next()
                                for idx, (ksl, vt) in enumerate(kts):
                                    ps = mmp.next()
                                    fw.mm(ps[:, 0:QB], kT[:, ksl], qT.p(hh)[:, hh, q0:q0 + QB])
                                    pt = ptp.next()
                                    fw.act(pt[:, 0:QB], ps[:, 0:QB], AF.Exp, scale=0.125)
                                    nqs = QB // 128
                                    for qs in range(nqs):
                                        fw.mm(oacc[:, qs * 128:qs * 128 + 65], pt[:, qs * 128:(qs + 1) * 128],
                                              V1[:, vt, :], start=(idx == 0 and qs == 0),
                                              stop=(idx == len(kts) - 1 and qs == nqs - 1))
                                for qs in range(QB // 128):
                                    ti = q0 // 128 + qs
                                    rd = rdp.next()
                                    fw.recip(rd[:, :], oacc[:, qs * 128 + 64:qs * 128 + 65])
                                    fw.act(otok.p(ti)[:, ti, hh * 64:(hh + 1) * 64], oacc[:, qs * 128:qs * 128 + 64],
                                           AF.Identity, scale=rd[:, 0:1])
                    for ti in range(8):
                        for c2 in range(2):
                            pT = trp.next()
                            fw.tr(pT[:, 0:128], otok.p(ti)[:, ti, c2 * 128:(c2 + 1) * 128], identb[:, :])
                            c = 2 * g2 + c2
                            evac(catT.p(c * 2 + ti // 4)[:, c, ti * 128:(ti + 1) * 128], pT[:, 0:128])
                fw.barrier()
                fw.flush()

            if dbg == "dbg_att":
                return
            with ExitStack() as ug:
                raw = Rot([fw.sb("graw%d" % i, [128, 1024], F32, stack=ug) for i in range(2)])
                cac = Rot([fw.sb("gcac%d" % i, [128, 1024], F32, stack=ug) for i in range(2)])
                qkf = Rot([fw.sb("gqkf%d" % i, [128, 1024], F32, stack=ug) for i in range(1)])
                fT = fw.sb("gfT", [128, 3, 1024], BF16, 3, stack=ug)
                ktok = fw.sb("gktok", [128, 8, 128], BF16, 8, stack=ug)
                vtok = fw.sb("gvtok", [128, 8, 128], BF16, 8, stack=ug)
                sz = fw.sb("gsz", [128, 8, 128], F32, 8, stack=ug)
                zraw = fw.sb("gzraw", [128, 8, 132], F32, 8, stack=ug)
                scr = zraw[:, :, 128:132]
                gs = fw.sb("ggs", [128, 10, 8, 2], F32, stack=ug)
                tmpa = fw.sb("gtmpa", [128, 8], F32, stack=ug)
                tmpb = fw.sb("gtmpb", [128, 8], F32, stack=ug)
                negA = fw.sb("gnegA", [128, 2], F32, stack=ug)
                decb = fw.sb("gdecb", [128, 2, 16], F32, stack=ug)
                oac = fw.sb("goac", [128, 8, 128], F32, 8, stack=ug)
                ssc = fw.sb("gssc", [128, 8], F32, stack=ug)
                junk = Rot([fw.sb("gjunk%d" % i, [128, 128], F32, stack=ug) for i in range(2)])
                onb = Rot([fw.sb("gonb%d" % i, [128, 128], BF16, stack=ug) for i in range(2)])
                sqb = Rot([fw.sb("gsq%d" % i, [128, 512], BF16, stack=ug) for i in range(2)])
                rsb = Rot([fw.sb("grs%d" % i, [128, 512], F32, stack=ug) for i in range(2)])
                S = [fw.sb("gS%d" % d, [128, 128], F32, stack=ug) for d in range(2)]
                Sb = [fw.sb("gSb%d" % d, [128, 128], BF16, stack=ug) for d in range(2)]
                dg = [fw.sb("gdg%d" % d, [128, 384], F32, stack=ug) for d in range(2)]
                E2 = [fw.sb("gE2%d" % d, [128, 256], F32, stack=ug) for d in range(2)]
                Mm = [fw.sb("gMm%d" % d, [128, 128], F32, stack=ug) for d in range(2)]
                qkT = [[fw.sb("gqkT%d_%d" % (d, i), [128, 128], BF16, stack=ug) for i in range(2)] for d in range(2)]
                Bm = [[fw.sb("gB%d_%d" % (d, i), [128, 128], F32, stack=ug) for i in range(2)] for d in range(2)]
                AS = [[fw.sb("gAS%d_%d" % (d, i), [128, 256], F32, stack=ug) for i in range(2)] for d in range(2)]
                TT = [fw.sb("gTT%d" % d, [128, 128], BF16, stack=ug) for d in range(2)]
                vb = [fw.sb("gvb%d" % d, [128, 128], BF16, stack=ug) for d in range(2)]
                kbg = [fw.sb("gkbg%d" % d, [128, 128], BF16, stack=ug) for d in range(2)]
                kdec = [[fw.sb("gkdec%d_%d" % (d, i), [128, 128], BF16, stack=ug) for i in range(2)] for d in range(2)]
                qdT = [[fw.sb("gqdT%d_%d" % (d, i), [128, 128], BF16, stack=ug) for i in range(2)] for d in range(2)]
                usb = [[fw.sb("gusb%d_%d" % (d, i), [128, 128], F32, stack=ug) for i in range(2)] for d in range(2)]
                wTs = [[fw.sb("gwT%d_%d" % (d, i), [128, 128], BF16, stack=ug) for i in range(2)] for d in range(2)]
                vnew = [fw.sb("gvn%d" % d, [128, 128], BF16, stack=ug) for d in range(2)]
                for d in range(2):
                    fw.memset(vnew[d][:, :], 0.0)
                for hd in range(4):
                    base = 768 + hd * 640
                    wtA = wload(dr["wie"][j, :, base:base + 384], 8, 384)
                    import os
                    SK = os.environ.get("DBGSKIP", "")
                    if "wtb" not in SK:
                        wtB = wload(dr["wie"][j, :, base + 384:base + 640], 8, 256)
                    for wi in range(3):
                        r = raw.next()
                        for tg in range(2):
                            ps = mmp.next()
                            for kc in range(8):
                                fw.mm(ps[:, :], wtA[:, kc, wi * 128:(wi + 1) * 128], hv(kc, tg * 512, 512),
                                      start=(kc == 0), stop=(kc == 7))
                            evac(r[:, tg * 512:(tg + 1) * 512], ps[:, :])
                        a = cac.next()
                        cwc = lambda tap: VC("dconv", ((j * 4 + hd) * 3 + wi) * 3 + tap)
                        r3 = r[:, :].re("p (s l) -> p s l", s=nseq)
                        a3 = a[:, :].re("p (s l) -> p s l", s=nseq)
                        fw.ts(a[:, :], r[:, :], cwc(1), ALU.mult)
                        if "conv" not in SK:
                            fw.stt(a3[:, :, 1:L], r3[:, :, 0:L - 1], cwc(0), a3[:, :, 1:L], ALU.mult, ALU.add)
                            fw.stt(a3[:, :, 0:L - 1], r3[:, :, 1:L], cwc(2), a3[:, :, 0:L - 1], ALU.mult, ALU.add)
                        if wi == 2:
                            fw.act(fT.p(2)[:, 2, :], a[:, :], SILU)
                        else:
                            f = qkf.next()
                            fw.act(f[:, :], a[:, :], SILU)
                            for tg in range(2):
                                sl = slice(tg * 512, (tg + 1) * 512)
                                sq = sqb.next()
                                fw.act(sq[:, :], f[:, sl], AF.Square)
                                ps = mmp.next()
                                fw.mm(ps[:, :], onesb[:, :], sq[:, :])
                                rs = rsb.next()
                                fw.act(rs[:, :], ps[:, :], AF.Sqrt, bias=EPS, scale=1.0)
                                fw.recip(rs[:, :], rs[:, :])
                                fw.stt(fT.p(wi)[:, wi, sl], f[:, sl], (128 ** -0.5) if wi == 0 else 1.0, rs[:, :],
                                       ALU.mult, ALU.mult)
                    for ti in range(8):
                        tsl = slice(ti * 128, (ti + 1) * 128)
                        pT = trp.next()
                        fw.tr(pT[:, 0:128], fT.p(1)[:, 1, tsl], identb[:, :])
                        evac(ktok.p(ti)[:, ti, :], pT[:, 0:128])
                        pT = trp.next()
                        fw.tr(pT[:, 0:128], fT.p(2)[:, 2, tsl], identb[:, :])
                        evac(vtok.p(ti)[:, ti, :], pT[:, 0:128])
                    if dbg == "dbg_gdn1":
                        break
                    for ti in range(8):
                        ps = mmp.next()
                        for kc in range(8):
                            fw.mm(ps[:, 0:256], hv(kc, ti * 128, 128), wtB[:, kc, :], start=(kc == 0), stop=(kc == 7))
                        fw.cp("act", zraw.p(ti)[:, ti, :], ps[:, 0:132])
                        fw.act(sz.p(ti)[:, ti, :], zraw.p(ti)[:, ti, 0:128], SILU)
                    if "scal" in SK:
                        break
                    for d in range(2):
                        col = j * 8 + d * 4 + hd
                        fw.act(negA[:, d:d + 1], VC("dalog", col), AF.Exp)
                        fw.ts(negA[:, d:d + 1], negA[:, d:d + 1], -1.0, ALU.mult)
                        fw.act(tmpa[:, :], scr[:, :, 2 + d:3 + d].re('p t o -> p (t o)'), AF.Exp, bias=VC("ddtb", col), scale=1.0)
                        fw.act(tmpa[:, :], tmpa[:, :], AF.Ln, bias=1.0, scale=1.0)
                        fw.ts(gs[:, 2, :, d], tmpa[:, :], negA[:, d:d + 1], ALU.mult)
                        fw.act(tmpb[:, :], scr[:, :, d:d + 1].re('p t o -> p (t o)'), AF.Exp, scale=-1.0)
                        fw.act(gs[:, 1, :, d], tmpb[:, :], AF.Ln, bias=1.0, scale=1.0)
                        fw.act(gs[:, 0, :, d], gs[:, 1, :, d], AF.Exp, scale=-1.0)
                    if "cums" in SK:
                        break
                    ps = mmp.next()
                    fw.mm(ps[:, 0:8], C("triF"), gs[:, 2, :, 0])
                    fw.mm(ps[:, 8:16], C("triB"), gs[:, 2, :, 1])
                    fw.mm(ps[:, 16:24], C("blk"), gs[:, 2, :, 0])
                    fw.mm(ps[:, 24:32], C("blk"), gs[:, 2, :, 1])
                    for d in range(2):
                        fw.cp("dve", gs[:, 3, :, d], ps[:, d * 8:(d + 1) * 8])
                        fw.cp("dve", gs[:, 4, :, d], ps[:, 16 + d * 8:16 + (d + 1) * 8])
                    fw.act(gs[:, 5, :, :], gs[:, 3, :, :], AF.Exp)
                    fw.tt(gs[:, 6, :, :], gs[:, 4, :, :], gs[:, 3, :, :], ALU.subtract)
                    fw.act(gs[:, 6, :, :], gs[:, 6, :, :], AF.Exp)
                    fw.tt(gs[:, 7, :, :], gs[:, 3, :, :], gs[:, 1, :, :], ALU.subtract)
                    fw.ts(gs[:, 8, :, :], gs[:, 3, :, :], -1.0, ALU.mult)
                    fw.tt(gs[:, 9, :, :], gs[:, 0, :, :], gs[:, 5, :, :], ALU.mult)
                    ps = mmp.next()
                    g2d = gs[:, 2, :, :].re("p t d -> p (t d)")
                    fw.mm(ps[:, 0:16], C("ind0"), g2d)
                    fw.mm(ps[:, 16:32], C("ind1"), g2d)
                    fw.act(decb[:, :, :].re("p c n -> p (c n)"), ps[:, 0:32], AF.Exp)
                    for ti in range(8):
                        fw.memset(oac.p(ti)[:, ti, :], 0.0)
                    if dbg == "dbg_gdn2":
                        break
                    def gcol(k, ti, d):
                        return gs[:, k, ti, d:d + 1]

                    def solve(n):
                        par = n % 2
                        slots = [(n, 0), (7 - n, 1)]
                        for ti, d in slots:
                            fw.ts(dg[d][:, 0:128], C("ident"), gcol(3, ti, d), ALU.mult)
                            fw.ts(dg[d][:, 128:256], C("ident"), gcol(7, ti, d), ALU.mult)
                            fw.ts(dg[d][:, 256:384], C("ident"), gcol(5, ti, d), ALU.mult)
                        yield
                        for ti, d in slots:
                            p = mmp.next()
                            fw.mm(p[:, 0:256], C("ones"), dg[d][:, 0:256], start=True, stop=False)
                            fw.mm(p[:, 0:256], C("ident"), C("nmF" if d == 0 else "nmB"), start=False, stop=True)
                            fw.mm(p[:, 256:384], C("ones"), dg[d][:, 256:384], start=True, stop=True)
                            fw.act(E2[d][:, :], p[:, 0:256], AF.Exp, bias=gcol(8, ti, d), scale=1.0)
                            fw.tt(qdT[d][par][:, :], fT.p(0)[:, 0, ti * 128:(ti + 1) * 128], p[:, 256:384], ALU.mult)
                        yield
                        for ti, d in slots:
                            tsl = slice(ti * 128, (ti + 1) * 128)
                            p = mmp.next()
                            fw.mm(p[:, 0:128], fT.p(1)[:, 1, tsl], fT.p(0)[:, 0, tsl])
                            fw.mm(p[:, 128:256], fT.p(1)[:, 1, tsl], fT.p(1)[:, 1, tsl])
                            fw.tt(qkT[d][par][:, :], p[:, 0:128], E2[d][:, 0:128], ALU.mult)
                            fw.tt(Mm[d][:, :], p[:, 128:256], E2[d][:, 128:256], ALU.mult)
                        yield
                        for ti, d in slots:
                            p = mmp.next()
                            fw.tr(p[:, 0:128], Mm[d][:, :], C("ident"))
                            fw.cp("act", Bm[d][0][:, :], p[:, 0:128])
                        yield
                        for ti, d in slots:
                            p = mmp.next()
                            fw.mm(p[:, 0:128], Bm[d][0][:, :], Mm[d][:, :])
                            fw.mm(p[:, 128:256], Mm[d][:, :], Bm[d][0][:, :])
                            fw.cp("act", AS[d][0][:, 0:128], p[:, 0:128])
                            fw.cp("dve", Bm[d][1][:, :], p[:, 128:256])
                            fw.tt(AS[d][0][:, 128:256], C("ident"), Mm[d][:, :], ALU.subtract)
                        yield
                        for k in range(1, 6):
                            for ti, d in slots:
                                cur, nxt, Bk, Bn = AS[d][(k - 1) % 2], AS[d][k % 2], Bm[d][k % 2], Bm[d][(k + 1) % 2]
                                p = mmp.next()
                                if k < 5:
                                    fw.mm(p[:, 0:256], Bk[:, :], cur[:, 0:256])
                                    fw.mm(p[:, 256:384], cur[:, 0:128], Bk[:, :])
                                    fw.cp("act", Bn[:, :], p[:, 256:384])
                                    if k < 4:
                                        fw.cp("act", nxt[:, 0:128], p[:, 0:128])
                                    fw.tt(nxt[:, 128:256], cur[:, 128:256], p[:, 128:256], ALU.add)
                                else:
                                    fw.mm(p[:, 0:128], Bk[:, :], cur[:, 128:256])
                                    fw.tt(TT[d][:, :], cur[:, 128:256], p[:, 0:128], ALU.add)
                            yield
                        for ti, d in slots:
                            fw.ts(vb[d][:, :], vtok.p(ti)[:, ti, :], gcol(0, ti, d), ALU.mult)
                            fw.ts(kbg[d][:, :], ktok.p(ti)[:, ti, :], gcol(9, ti, d), ALU.mult)
                            fw.act(kdec[d][par][:, :], ktok.p(ti)[:, ti, :], AF.Identity, scale=gcol(6, ti, d))
                        yield
                        for ti, d in slots:
                            p = mmp.next()
                            fw.mm(p[:, 0:128], TT[d][:, :], vb[d][:, :])
                            fw.mm(p[:, 128:256], kbg[d][:, :], TT[d][:, :])
                            fw.cp("act", usb[d][par][:, :], p[:, 0:128])
                            fw.cp("dve", wTs[d][par][:, :], p[:, 128:256])
                        yield

                    def recur(n):
                        par = n % 2
                        slots = [(n, 0), (7 - n, 1)]
                        for ci in range(2):
                            for ti, d in slots:
                                c = ci if d == 0 else 1 - ci
                                rows = slice(c * 64, (c + 1) * 64)
                                first = (ti % tps == 0 and c == 0) if d == 0 else (ti % tps == tps - 1 and c == 1)
                                last = (ti % tps == tps - 1 and c == 1) if d == 0 else (ti % tps == 0 and c == 0)
                                seq = ti // tps
                                if first:
                                    if ctx:
                                        fw.memset(S[d][:, :], 0.0)
                                    else:
                                        fw.dma(S[d][:, :], dv("sd0")[j, d, hd, :, :])
                                    fw.cp("act", Sb[d][:, :], S[d][:, :])
                                p = mmp.next()
                                fw.mm(p[:, 0:128], wTs[d][par][:, :], Sb[d][:, :])
                                fw.tt(vnew[d][rows, :], usb[d][par][rows, :], p[rows, 0:128], ALU.subtract)
                                yield
                                po = mmp.next()
                                fw.mm(po[:, 0:128], qdT[d][par][:, :], Sb[d][:, :], start=True, stop=False)
                                fw.mm(po[:, 0:128], qkT[d][par][rows, :], vnew[d][rows, :], start=False, stop=True)
                                pst = mmp.next()
                                fw.mm(pst[:, 0:128], kdec[d][par][rows, :], vnew[d][rows, :])
                                fw.stt(S[d][:, :], S[d][:, :], decb[:, c, ti * 2 + d:ti * 2 + d + 1], pst[:, 0:128],
                                       ALU.mult, ALU.add)
                                fw.cp("act", Sb[d][:, :], S[d][:, :])
                                fw.tt(oac.p(ti)[rows, ti, :], oac.p(ti)[rows, ti, :], po[rows, 0:128], ALU.add)
                                if last and ctx:
                                    fw.dma(dv("nsd")[seq, j, d, hd, :, :], S[d][:, :])
                                yield

                    def drive(gens):
                        gens = list(gens)
                        while gens:
                            for g_ in list(gens):
                                try:
                                    next(g_)
                                except StopIteration:
                                    gens.remove(g_)

                    drive([solve(0)])
                    for n in range(8):
                        drive([recur(n)] + ([solve(n + 1)] if n < 7 else []))

                    if dbg == "dbg_gdn5":
                        break
                    for ti in range(8):
                        jk = junk.next()
                        fw.tt(jk[:, :], oac.p(ti)[:, ti, :], oac.p(ti)[:, ti, :], ALU.mult)
                        rsum(ssc[:, ti:ti + 1], jk[:, :])
                    fw.act(ssc[:, :], ssc[:, :], AF.Sqrt, bias=EPS, scale=1.0 / 128)
                    fw.recip(ssc[:, :], ssc[:, :])
                    for ti in range(8):
                        ob = onb.next()
                        fw.stt(ob[:, :], oac.p(ti)[:, ti, :], ssc[:, ti:ti + 1], sz.p(ti)[:, ti, :], ALU.mult, ALU.mult)
                        pT = trp.next()
                        fw.tr(pT[:, 0:128], ob[:, :], identb[:, :])
                        c = 4 + hd
                        fw.act(catT.p(c * 2 + ti // 4)[:, c, ti * 128:(ti + 1) * 128], pT[:, 0:128], AF.Identity,
                               scale=VC("dng", j))
                fw.barrier()
                fw.flush()

        def odd_mixer(ph, layer, g, hT, catT):
            j = layer // 2
            ctx = (g == 0)
            nseq, L = (4, 256) if ctx else (1, 1024)
            tps = L // 128

            def hv(kc, lo, n):
                return hT.p(kc * 2 + lo // 512)[:, kc, lo:lo + n]

            with ExitStack() as us:
                raw = Rot([fw.sb("sraw%d" % i, [128, 1024], F32, stack=us) for i in range(1)])
                cac = Rot([fw.sb("scac%d" % i, [128, 1024], F32, stack=us) for i in range(1)])
                fT = fw.sb("sfT", [128, 4, 1024], BF16, 4, stack=us)
                xtok = fw.sb("sxtok", [128, 8, 256], BF16, 8, stack=us)
                Btok = fw.sb("sBtok", [128, 8, 128], BF16, 8, stack=us)
                zs = fw.sb("szs", [128, 8, 256], BF16, 8, stack=us)
                dtr = fw.sb("sdtr", [128, 8, 8], F32, stack=us)
                zr = Rot([fw.sb("szr%d" % i, [128, 264], F32, stack=us) for i in range(2)])
                ss = fw.sb("sss", [128, 7, 8, 8], F32, stack=us)
                negA = fw.sb("snegA", [128, 8], F32, stack=us)
                decb = fw.sb("sdecb", [128, 2, 64], F32, stack=us)
                yac = fw.sb("syac", [128, 8, 256], F32, 8, stack=us)
                xdt = [fw.sb("sxdt%d" % d, [128, 256], BF16, stack=us) for d in range(2)]
                xd = [fw.sb("sxd%d" % d, [128, 256], BF16, stack=us) for d in range(2)]
                cbs = [fw.sb("scbs%d" % d, [128, 128], F32, stack=us) for d in range(2)]
                dgs = [fw.sb("sdgs%d" % d, [128, 4, 128], F32, stack=us) for d in range(2)]
                Ls = dgs
                cbL = [fw.sb("scbL%d" % d, [128, 4, 128], BF16, stack=us) for d in range(2)]
                sT = [fw.sb("ssT%d" % d, [128, 256], F32, stack=us) for d in range(2)]
                sTb = [fw.sb("ssTb%d" % d, [128, 256], BF16, stack=us) for d in range(2)]
                ytmp = Rot([fw.sb("sytmp%d" % i, [128, 256], F32, stack=us) for i in range(2)])
                ygb = Rot([fw.sb("sygb%d" % i, [128, 256], BF16, stack=us) for i in range(2)])
                sst = Rot([fw.sb("ssst%d" % i, [128, 128], F32, stack=us) for i in range(2)])
                for sg in range(8):
                    base = sg * 832
                    wtA = wload(dr["wio"][j, :, base:base + 512], 8, 512)
                    wtB = wload(dr["wio"][j, :, base + 512:base + 832], 8, 320)
                    for wi in range(4):
                        r = raw.next()
                        for tg in range(2):
                            ps = mmp.next()
                            for kc in range(8):
                                fw.mm(ps[:, :], wtA[:, kc, wi * 128:(wi + 1) * 128], hv(kc, tg * 512, 512),
                                      start=(kc == 0), stop=(kc == 7))
                            evac(r[:, tg * 512:(tg + 1) * 512], ps[:, :])
                        a = cac.next()
                        cwc = lambda tap: VC("sconv", ((j * 8 + sg) * 4 + wi) * 4 + tap)
                        r3 = r[:, :].re("p (s l) -> p s l", s=nseq)
                        a3 = a[:, :].re("p (s l) -> p s l", s=nseq)
                        fw.ts(a[:, :], r[:, :], cwc(1), ALU.mult)
                        fw.stt(a3[:, :, 1:L], r3[:, :, 0:L - 1], cwc(0), a3[:, :, 1:L], ALU.mult, ALU.add)
                        fw.stt(a3[:, :, 0:L - 1], r3[:, :, 1:L], cwc(2), a3[:, :, 0:L - 1], ALU.mult, ALU.add)
                        fw.act(fT.p(wi)[:, wi, :], a[:, :], SILU, bias=cwc(3), scale=1.0)
                    for ti in range(8):
                        tsl = slice(ti * 128, (ti + 1) * 128)
                        for c2 in range(2):
                            pT = trp.next()
                            fw.tr(pT[:, 0:128], fT.p(c2)[:, c2, tsl], identb[:, :])
                            evac(xtok.p(ti)[:, ti, c2 * 128:(c2 + 1) * 128], pT[:, 0:128])
                        pT = trp.next()
                        fw.tr(pT[:, 0:128], fT.p(2)[:, 2, tsl], identb[:, :])
                        evac(Btok.p(ti)[:, ti, :], pT[:, 0:128])
                    for ti in range(8):
                        ps = mmp.next()
                        for kc in range(8):
                            fw.mm(ps[:, 0:320], hv(kc, ti * 128, 128), wtB[:, kc, :], start=(kc == 0), stop=(kc == 7))
                        fw.cp("act", zr.next()[:, :], ps[:, 0:264])
                        zr_ = zr.items[(zr.i - 1) % len(zr.items)]
                        fw.act(zs.p(ti)[:, ti, :], zr_[:, 0:256], SILU)
                        fw.cp("act", dtr[:, ti, :], zr_[:, 256:264])
                    for d in range(2):
                        c0 = j * 64 + d * 32 + sg * 4
                        fw.act(negA[:, d * 4:(d + 1) * 4], VC("salog", c0, n=4), AF.Exp)
                        fw.tt(ss[:, 0, :, d * 4:(d + 1) * 4], dtr[:, :, d * 4:(d + 1) * 4],
                              VC("sdtb", c0, n=4).un(1).bc([128, 8, 4]), ALU.add)
                    fw.ts(negA[:, :], negA[:, :], -1.0, ALU.mult)
                    fw.act(ss[:, 0, :, :], ss[:, 0, :, :], AF.Exp)
                    fw.act(ss[:, 0, :, :], ss[:, 0, :, :], AF.Ln, bias=1.0, scale=1.0)
                    fw.tt(ss[:, 1, :, :], ss[:, 0, :, :], negA[:, :].un(1).bc([128, 8, 8]), ALU.mult)
                    ps = mmp.next()
                    fw.mm(ps[:, 0:32], C("triF"), ss[:, 1, :, 0:4])
                    fw.mm(ps[:, 32:64], C("triB"), ss[:, 1, :, 4:8])
                    fw.mm(ps[:, 64:128], C("blk"), ss[:, 1, :, :])
                    for d in range(2):
                        fw.cp("dve", ss[:, 2, :, d * 4:(d + 1) * 4], ps[:, d * 32:(d + 1) * 32].re("p (t h) -> p t h", h=4))
                    fw.cp("dve", ss[:, 3, :, :], ps[:, 64:128].re("p (t h) -> p t h", h=8))
                    fw.act(ss[:, 4, :, :], ss[:, 2, :, :], AF.Exp)
                    fw.tt(ss[:, 5, :, :], ss[:, 3, :, :], ss[:, 2, :, :], ALU.subtract)
                    fw.act(ss[:, 5, :, :], ss[:, 5, :, :], AF.Exp)
                    fw.ts(ss[:, 6, :, :], ss[:, 2, :, :], -1.0, ALU.mult)
                    ps = mmp.next()
                    a2d = ss[:, 1, :, :].re("p t h -> p (t h)")
                    fw.mm(ps[:, 0:64], C("ind0"), a2d)
                    fw.mm(ps[:, 64:128], C("ind1"), a2d)
                    fw.act(decb[:, :, :].re("p c n -> p (c n)"), ps[:, 0:128], AF.Exp)
                    for ti in range(8):
                        fw.tt(yac.p(ti)[:, ti, :].re("p (h e) -> p h e", h=4),
                              xtok.p(ti)[:, ti, :].re("p (h e) -> p h e", h=4),
                              VC("sd", j * 32 + sg * 4, n=4).un(2).bc([128, 4, 64]), ALU.mult)
                    for n in range(8):
                        slots = [(n, 0), (7 - n, 1)]
                        pcb, pLs, pYs = {}, {}, {}
                        for ti, d in slots:
                            tsl = slice(ti * 128, (ti + 1) * 128)
                            dh = slice(d * 4, (d + 1) * 4)
                            x3 = xtok.p(ti)[:, ti, :].re("p (h e) -> p h e", h=4)
                            fw.tt(xdt[d][:, :].re("p (h e) -> p h e", h=4), x3,
                                  ss[:, 0, ti, dh].un(2).bc([128, 4, 64]), ALU.mult)
                            fw.tt(xd[d][:, :].re("p (h e) -> p h e", h=4), xdt[d][:, :].re("p (h e) -> p h e", h=4),
                                  ss[:, 5, ti, dh].un(2).bc([128, 4, 64]), ALU.mult)
                            p = mmp.next()
                            pcb[d] = p
                            fw.mm(p[:, 0:128], fT.p(2)[:, 2, tsl], fT.p(3)[:, 3, tsl])
                            fw.tt(dgs[d][:, :, :], C("ident").un(1).bc([128, 4, 128]),
                                  ss[:, 2, ti, dh].un(2).bc([128, 4, 128]), ALU.mult)
                        for ti, d in slots:
                            fw.cp("act", cbs[d][:, :], pcb[d][:, 0:128])
                            pL = mmp.next()
                            pLs[d] = pL
                            fw.mm(pL[:, :], C("ones"), dgs[d][:, :, :].re("p h i -> p (h i)"), start=True, stop=False)
                            for h in range(4):
                                fw.mm(pL[:, h * 128:(h + 1) * 128], C("ident"), C("nmFc" if d == 0 else "nmBc"),
                                      start=False, stop=(h == 3))
                        for h in range(4):
                            for ti, d in slots:
                                fw.act(Ls[d][:, h, :], pLs[d][:, h * 128:(h + 1) * 128], AF.Exp,
                                       bias=ss[:, 6, ti, d * 4 + h:d * 4 + h + 1], scale=1.0)
                        for ti, d in slots:
                            fw.tt(cbL[d][:, :, :], Ls[d][:, :, :], cbs[d][:, :].un(1).bc([128, 4, 128]), ALU.mult)
                        for ti, d in slots:
                            pY = mmp.next()
                            pYs[d] = pY
                            for h in range(4):
                                fw.mm(pY[:, h * 64:(h + 1) * 64], cbL[d][:, h, :], xdt[d][:, h * 64:(h + 1) * 64])
                        for ti, d in slots:
                            fw.tt(yac.p(ti)[:, ti, :], yac.p(ti)[:, ti, :], pYs[d][:, 0:256], ALU.add)
                        for ci in range(2):
                            info = {}
                            for ti, d in slots:
                                c = ci if d == 0 else 1 - ci
                                rows = slice(c * 64, (c + 1) * 64)
                                first = (ti % tps == 0 and c == 0) if d == 0 else (ti % tps == tps - 1 and c == 1)
                                last = (ti % tps == tps - 1 and c == 1) if d == 0 else (ti % tps == 0 and c == 0)
                                info[d] = (c, rows, first, last, ti // tps)
                                if first:
                                    if ctx:
                                        fw.memset(sT[d][:, :], 0.0)
                                    else:
                                        for half in range(2):
                                            st_ = sst.next()
                                            fw.dma(st_[:, :], dv("ss0")[j, d, 2 * sg + half, :, :])
                                            p = mmp.next()
                                            fw.tr(p[:, 0:128], st_[:, :], C("ident"))
                                            fw.cp("act", sT[d][:, half * 128:(half + 1) * 128], p[:, 0:128])
                                    fw.cp("act", sTb[d][:, :], sT[d][:, :])
                            pos, pSs, yts = {}, {}, {}
                            for ti, d in slots:
                                c, rows, first, last, seq = info[d]
                                tsl = slice(ti * 128, (ti + 1) * 128)
                                po = mmp.next()
                                pos[d] = po
                                fw.mm(po[:, 0:256], fT.p(3)[:, 3, tsl], sTb[d][:, :])
                                pS = mmp.next()
                                pSs[d] = pS
                                fw.mm(pS[:, 0:256], Btok.p(ti)[rows, ti, :], xd[d][rows, :])
                            for ti, d in slots:
                                c, rows, first, last, seq = info[d]
                                yt = ytmp.next()
                                yts[d] = yt
                                fw.tt(yt[rows, :].re("p (h e) -> p h e", h=4), pos[d][rows, 0:256].re("p (h e) -> p h e", h=4),
                                      ss[rows, 4, ti, d * 4:(d + 1) * 4].un(2).bc([64, 4, 64]), ALU.mult)
                                fw.tt(sT[d][:, :].re("p (h e) -> p h e", h=4), sT[d][:, :].re("p (h e) -> p h e", h=4),
                                      decb[:, c, ti * 8 + d * 4:ti * 8 + d * 4 + 4].un(2).bc([128, 4, 64]), ALU.mult)
                            for ti, d in slots:
                                c, rows, first, last, seq = info[d]
                                fw.tt(sT[d][:, :], sT[d][:, :], pSs[d][:, 0:256], ALU.add)
                                fw.cp("act", sTb[d][:, :], sT[d][:, :])
                                fw.tt(yac.p(ti)[rows, ti, :], yac.p(ti)[rows, ti, :], yts[d][rows, :], ALU.add)
                            for ti, d in slots:
                                c, rows, first, last, seq = info[d]
                                if last and ctx:
                                    for half in range(2):
                                        p = mmp.next()
                                        fw.tr(p[:, 0:128], sT[d][:, half * 128:(half + 1) * 128], C("ident"))
                                        st_ = sst.next()
                                        evac(st_[:, :], p[:, 0:128])
                                        fw.dma(dv("nss")[seq, j, d, 2 * sg + half, :, :], st_[:, :])
                    for ti in range(8):
                        yg = ygb.next()
                        fw.tt(yg[:, :], yac.p(ti)[:, ti, :], zs.p(ti)[:, ti, :], ALU.mult)
                        for c2 in range(2):
                            pT = trp.next()
                            fw.tr(pT[:, 0:128], yg[:, c2 * 128:(c2 + 1) * 128], identb[:, :])
                            c = 2 * sg + c2
                            evac(catT.p(c * 2 + ti // 4)[:, c, ti * 128:(ti + 1) * 128], pT[:, 0:128])
                fw.barrier()
                fw.flush()

        dbg = stop if (stop or "").startswith("dbg") else None
        for layer in range(n_layers if (stop is None or dbg) else 0):
            mod_vectors(layer)
            if dbg == "dbg_mod":
                break
            for g in range(2):
                with ExitStack() as ph:
                    hT = fw.sb("hTg", [128, 8, 1024], BF16, 16, stack=ph)
                    with ExitStack() as ph2:
                        modulate(ph2, "nx", hT, 0, [2 * g, 2 * g + 1], 2 * g, 2)
                        fw.barrier()
                        fw.flush()
                    if dbg == "dbg_norm":
                        continue
                    if layer % 2 == 0:
                        catT = fw.sb("catT", [128, 8, 1024], BF16, 16, stack=ph)
                        even_mixer(ph, layer, g, hT, catT, dbg)
                        if dbg is None or dbg == "dbg_mlp":
                            out_proj_even(layer, g, catT)
                    else:
                        catT = fw.sb("catT", [128, 16, 1024], BF16, 32, stack=ph)
                        odd_mixer(ph, layer, g, hT, catT)
                        out_proj_odd(ph, layer, g, catT)
                    fw.barrier()
                    fw.flush()
            if dbg in (None, "dbg_mlp"):
                mlp(layer)
        if stop is None or dbg:
            final_out()
        elif stop.startswith("final"):
            final_out(int(stop[5:]))
        fw.barrier(final=True)
        fw.flush()
    return nc, fw.n_instr


_PROG = {}


def _run(inp, n_layers=DEPTH, ncores=8):
    if n_layers not in _PROG:
        _PROG[n_layers] = build(n_layers)
    nc, _ = _PROG[n_layers]
    f = lambda k: np.ascontiguousarray(np.asarray(inp[k], dtype=np.float32))
    consts = make_consts()
    rope = np.ascontiguousarray(make_rope().reshape(64, 2048))
    vecs = make_vecs(inp)
    wie = permute_cols(f("w_in_even"), even_col_perm())
    wio = permute_cols(f("w_in_odd"), odd_col_perm())
    shared = {
        "consts": consts, "rope": rope, "vecs": vecs,
        "w_mod": f("w_mod"), "w_mlp_in": f("w_mlp_in"), "w_mlp_out": f("w_mlp_out"),
        "wie": wie, "w_out_even": f("w_out_even"), "wio": wio, "w_out_odd": f("w_out_odd"),
    }
    xp, xs, c, cctx = f("x_prompt"), f("x_sample"), f("c"), f("c_ctx")
    ck, cv, sd, ssm = f("cache_attn_k"), f("cache_attn_v"), f("state_delta"), f("state_ssm")
    in_maps = []
    for i in range(ncores):
        b = i % 4
        cond = np.stack([cctx, c[b]], axis=0)
        condT = np.ascontiguousarray(cond.reshape(2, 8, 128).transpose(2, 1, 0)).reshape(128, 16)
        m = dict(shared)
        m["xin"] = np.ascontiguousarray(np.concatenate([xp[4 * i:4 * i + 4].reshape(1024, D), xs[b]], axis=0))
        m["condT"] = condT
        m["ctxk"] = np.ascontiguousarray(ck[b].reshape(2, 512, 128))
        m["ctxv"] = np.ascontiguousarray(cv[b].reshape(2, 512, 128))
        m["sd0"] = np.ascontiguousarray(sd[b])
        m["ss0"] = np.ascontiguousarray(ssm[b].reshape(2, 2, 16, 128, 128))
        in_maps.append(m)
    res = run_bass_kernel_spmd(nc, in_maps, core_ids=list(range(ncores)))
    R = res.results
    if ncores < 8:
        return R
    y_p = np.concatenate([R[i]["y"][:1024].reshape(4, 256, D) for i in range(8)], axis=0)
    y_s = np.stack([R[b]["y"][1024:] for b in range(4)], axis=0)
    nk = np.concatenate([R[i]["nk"].reshape(2, 4, 256, 2, 64).transpose(1, 0, 2, 3, 4) for i in range(8)], axis=0)
    nv = np.concatenate([R[i]["nv"].reshape(2, 4, 256, 2, 64).transpose(1, 0, 2, 3, 4) for i in range(8)], axis=0)
    nsd = np.concatenate([R[i]["nsd"] for i in range(8)], axis=0)
    nss = np.concatenate([R[i]["nss"].reshape(4, 2, 2, 32, 64, 128) for i in range(8)], axis=0)
    out = (y_p, y_s, nk, nv, nsd, nss)
    return tuple(np.ascontiguousarray(o, dtype=np.float32) for o in out)


def kernel(**inputs):
    return _run(inputs, DEPTH)
```

```python
import math
import numpy as np
from contextlib import ExitStack
import concourse.bass as bass
import concourse.mybir as mybir
from concourse.bass_utils import run_bass_kernel_spmd

F32 = mybir.dt.float32
BF16 = mybir.dt.bfloat16
AF = mybir.ActivationFunctionType
ALU = mybir.AluOpType

D = 1024
DEPTH = 4
EPS = 1e-6
NEG = -30000.0


class Dep:
    __slots__ = ("w", "r", "excl")

    def __init__(self):
        self.w = None
        self.r = {}
        self.excl = False


class V:
    __slots__ = ("ap", "deps")

    def __init__(self, ap, deps):
        self.ap = ap
        self.deps = deps

    def __getitem__(self, k):
        return V(self.ap[k], self.deps)

    def re(self, pat, **kw):
        return V(self.ap.rearrange(pat, **kw), self.deps)

    def bc(self, shape):
        return V(self.ap.to_broadcast(list(shape)), self.deps)

    def un(self, axis):
        return V(self.ap.unsqueeze(axis), self.deps)


class _P:
    __slots__ = ("t", "deps")

    def __init__(self, t, deps):
        self.t = t
        self.deps = deps

    def __getitem__(self, k):
        return V(self.t[k], self.deps)


class T:
    def __init__(self, t, nparts=1):
        self.t = t
        self.deps = [Dep() for _ in range(nparts)]

    def __getitem__(self, k):
        return V(self.t[k], self.deps)

    def p(self, *idx):
        return _P(self.t, [self.deps[i] for i in idx])


class Rot:
    def __init__(self, items):
        self.items = items
        self.i = 0

    def next(self):
        x = self.items[self.i]
        self.i = (self.i + 1) % len(self.items)
        return x


def _ap(x):
    return x.ap if isinstance(x, V) else x


def _deps(*xs):
    out = []
    for x in xs:
        if isinstance(x, V):
            out.extend(x.deps)
    return out


class FW:
    def __init__(self, nc, stack, n_io=16, n_w=4):
        self.nc = nc
        self.stack = stack
        self.engs = {"pe": nc.tensor, "act": nc.scalar, "dve": nc.vector, "pool": nc.gpsimd, "sp": nc.sync}
        self.sems = {}
        self.cnt = {}
        for k in ["pe", "act", "dve"]:
            self.sems[k] = stack.enter_context(nc.semaphore("s_" + k))
            self.cnt[k] = 0
        self.io_ch = []
        for i in range(n_io):
            k = "io%d" % i
            self.sems[k] = stack.enter_context(nc.semaphore("s_" + k))
            self.cnt[k] = 0
            self.io_ch.append(k)
        self.w_ch = []
        for i in range(n_w):
            k = "w%d" % i
            self.sems[k] = stack.enter_context(nc.semaphore("s_" + k))
            self.cnt[k] = 0
            self.w_ch.append(k)
        self.nio = 0
        self.nw = 0
        self.waited = {k: {} for k in self.engs}
        self.n_instr = 0
        self.prog = {k: [] for k in self.engs}

    def flush(self):
        prog = self.prog
        self.prog = {k: [] for k in self.engs}
        with self.nc.Block() as block:
            def mk(lst):
                def body(e):
                    for f in lst:
                        f(e)
                return body
            block.sync(mk(prog["sp"]))
            block.tensor(mk(prog["pe"]))
            block.scalar(mk(prog["act"]))
            block.vector(mk(prog["dve"]))
            block.gpsimd(mk(prog["pool"]))

    def sb(self, name, shape, dtype, nparts=1, stack=None):
        st = stack or self.stack
        self.uid = getattr(self, "uid", 0) + 1
        return T(st.enter_context(self.nc.sbuf_tensor("%s_%d" % (name, self.uid), list(shape), dtype)), nparts)

    def ps(self, name, shape, dtype, nparts=1):
        t = T(self.stack.enter_context(self.nc.psum_tensor(name, list(shape), dtype)), nparts)
        for d in t.deps:
            d.excl = True
        return t

    def _wait(self, issuer, key, val):
        if val <= 0:
            return
        w = self.waited[issuer]
        if w.get(key, 0) >= val:
            return
        w[key] = val
        sem = self.sems[key]
        self.prog[issuer].append(lambda e: e.wait_ge(sem, val))

    def _gather(self, reads, writes):
        need = {}
        for d in reads:
            if d.w is not None:
                k, c = d.w
                if need.get(k, 0) < c:
                    need[k] = c
            if d.excl:
                for k, c in d.r.items():
                    if need.get(k, 0) < c:
                        need[k] = c
        for d in writes:
            if d.w is not None:
                k, c = d.w
                if need.get(k, 0) < c:
                    need[k] = c
            for k, c in d.r.items():
                if need.get(k, 0) < c:
                    need[k] = c
        return need

    def _commit(self, key, val, reads, writes):
        for d in writes:
            d.w = (key, val)
            d.r = {}
        for d in reads:
            if d.r.get(key, 0) < val:
                d.r[key] = val

    def op(self, eng, fn, reads=(), writes=()):
        need = self._gather(reads, writes)
        for k, c in need.items():
            if k == eng and eng == "pe":
                continue
            self._wait(eng, k, c)
        sem = self.sems[eng]
        self.prog[eng].append(lambda e: fn(e).then_inc(sem, 1))
        self.cnt[eng] += 1
        self._commit(eng, self.cnt[eng], reads, writes)
        self.n_instr += 1

    def dma(self, out, in_, q="sp"):
        reads = _deps(in_)
        writes = _deps(out)
        if q == "pool":
            ch = self.w_ch[self.nw % len(self.w_ch)]
            self.nw += 1
        else:
            ch = self.io_ch[self.nio % len(self.io_ch)]
            self.nio += 1
        need = self._gather(reads, writes)
        need[ch] = max(need.get(ch, 0), self.cnt[ch])
        for k, c in need.items():
            self._wait(q, k, c)
        sem = self.sems[ch]
        o, i = _ap(out), _ap(in_)
        self.prog[q].append(lambda e: e.dma_start(out=o, in_=i).then_inc(sem, 16))
        self.cnt[ch] += 16
        self._commit(ch, self.cnt[ch], reads, writes)
        self.n_instr += 1

    def barrier(self, final=False):
        keys = ["pe", "act", "dve"] + self.io_ch + (self.w_ch if final else [])
        for issuer in ["sp", "pe", "act", "dve"] + (["pool"] if final else []):
            for k in keys:
                if k != issuer:
                    self._wait(issuer, k, self.cnt[k])

    def mm(self, out, lhsT, rhs, start=True, stop=True):
        o, l, r = _ap(out), _ap(lhsT), _ap(rhs)
        self.op("pe", lambda e: e.matmul(o, lhsT=l, rhs=r, start=start, stop=stop),
                _deps(lhsT, rhs), _deps(out))

    def tr(self, out, in_, ident):
        o, i, d = _ap(out), _ap(in_), _ap(ident)
        self.op("pe", lambda e: e.transpose(o, i, d), _deps(in_, ident), _deps(out))

    def act(self, out, in_, func, bias=None, scale=None, accum=None):
        o, i = _ap(out), _ap(in_)
        kw = {}
        if bias is not None:
            kw["bias"] = _ap(bias)
        if scale is not None:
            kw["scale"] = _ap(scale)
        if accum is not None:
            kw["accum_out"] = _ap(accum)
        self.op("act", lambda e: e.activation(out=o, in_=i, func=func, **kw),
                _deps(in_, bias, scale), _deps(out, accum))

    def tt(self, out, in0, in1, op, eng="dve"):
        o, a, b = _ap(out), _ap(in0), _ap(in1)
        self.op(eng, lambda e: e.tensor_tensor(out=o, in0=a, in1=b, op=op), _deps(in0, in1), _deps(out))

    def ts(self, out, in0, s1, op0, s2=None, op1=None, eng="dve"):
        o, a, x1, x2 = _ap(out), _ap(in0), _ap(s1), _ap(s2)
        if op1 is None:
            self.op(eng, lambda e: e.tensor_scalar(out=o, in0=a, scalar1=x1, scalar2=None, op0=op0),
                    _deps(in0, s1), _deps(out))
        else:
            self.op(eng, lambda e: e.tensor_scalar(out=o, in0=a, scalar1=x1, scalar2=x2, op0=op0, op1=op1),
                    _deps(in0, s1, s2), _deps(out))

    def stt(self, out, in0, scalar, in1, op0, op1):
        o, a, s, b = _ap(out), _ap(in0), _ap(scalar), _ap(in1)
        self.op("dve", lambda e: e.scalar_tensor_tensor(out=o, in0=a, scalar=s, in1=b, op0=op0, op1=op1),
                _deps(in0, scalar, in1), _deps(out))

    def cp(self, eng, out, in_):
        o, i = _ap(out), _ap(in_)
        if eng == "act":
            self.op("act", lambda e: e.activation(out=o, in_=i, func=AF.Copy), _deps(in_), _deps(out))
        else:
            self.op("dve", lambda e: e.tensor_copy(out=o, in_=i), _deps(in_), _deps(out))

    def recip(self, out, in_):
        o, i = _ap(out), _ap(in_)
        self.op("dve", lambda e: e.reciprocal(out=o, in_=i), _deps(in_), _deps(out))

    def memset(self, out, val):
        o = _ap(out)
        self.op("dve", lambda e: e.memset(o, val), (), _deps(out))


VEC_LAYOUT = [("gmix", 32), ("gmlp", 32), ("bmod", 192), ("fing", 8), ("qkg", 4), ("dconv", 72),
              ("dalog", 16), ("ddtb", 16), ("dng", 2), ("sconv", 256), ("salog", 128), ("sdtb", 128),
              ("sd", 64), ("sng", 32)]
VOFF = {}
_o = 0
for _n, _c in VEC_LAYOUT:
    VOFF[_n] = (_o, _c)
    _o += _c
NVEC = _o

CST_LAYOUT = [("ident", 128), ("ones", 128), ("blk", 128), ("triF", 128), ("triB", 128), ("ind0", 128),
              ("ind1", 128), ("nmF", 256), ("nmB", 256), ("nmFc", 128), ("nmBc", 128), ("ropeR", 64)]
COFF = {}
_o = 0
for _n, _c in CST_LAYOUT:
    COFF[_n] = (_o, _c)
    _o += _c
NCST = _o


def make_consts():
    c = np.zeros((128, NCST), np.float32)
    idx = np.arange(128)
    j = idx[:, None]
    i = idx[None, :]
    same = (j // 64) == (i // 64)

    def put(name, a):
        o, n = COFF[name]
        c[:a.shape[0], o:o + a.shape[1]] = a

    put("ident", np.eye(128, dtype=np.float32))
    put("ones", np.ones((128, 128), np.float32))
    put("blk", same.astype(np.float32))
    put("triF", (same & (j <= i)).astype(np.float32))
    put("triB", (same & (j >= i)).astype(np.float32))
    put("ind0", np.broadcast_to((idx < 64)[:, None], (128, 128)).astype(np.float32))
    put("ind1", np.broadcast_to((idx >= 64)[:, None], (128, 128)).astype(np.float32))
    fc = np.where(same & (i >= j), 0.0, NEG).astype(np.float32)
    fs = np.where(same & (i > j), 0.0, NEG).astype(np.float32)
    bc = np.where(same & (i <= j), 0.0, NEG).astype(np.float32)
    bs = np.where(same & (i < j), 0.0, NEG).astype(np.float32)
    put("nmF", np.concatenate([fc, fs], axis=1))
    put("nmB", np.concatenate([bc, bs], axis=1))
    put("nmFc", fc)
    put("nmBc", bc)
    R = np.zeros((64, 64), np.float32)
    for base in (0, 32):
        for t in range(16):
            R[base + 16 + t, base + t] = -1.0
            R[base + t, base + 16 + t] = 1.0
    put("ropeR", R)
    return c


def make_rope():
    t = np.arange(1024)
    row = (t // 64).astype(np.float32)
    col = (t % 64).astype(np.float32)
    inv = (10000.0 ** (-np.arange(0, 32, 2, dtype=np.float32) / 32)).astype(np.float32)
    ar = row[None, :] * inv[:, None]
    ac = col[None, :] * inv[:, None]
    ang = np.concatenate([ar, ar, ac, ac], axis=0)
    return np.stack([np.cos(ang), np.sin(ang)], axis=1).astype(np.float32)


def even_col_perm():
    cols = []
    for g2 in range(2):
        for hh in range(4):
            h = 4 * g2 + hh
            cols += list(range(h * 64, h * 64 + 64))
        cols += list(range(512 + g2 * 64, 512 + g2 * 64 + 64))
        cols += list(range(640 + g2 * 64, 640 + g2 * 64 + 64))
    for hd in range(4):
        cols += list(range(768 + hd * 128, 768 + hd * 128 + 128))
        cols += list(range(768 + 512 + hd * 128, 768 + 512 + hd * 128 + 128))
        cols += list(range(768 + 1024 + hd * 128, 768 + 1024 + hd * 128 + 128))
        cols += list(range(2304 + hd * 128, 2304 + hd * 128 + 128))
        cols += [2816 + hd, 2816 + 4 + hd, 2824 + hd, 2824 + 4 + hd]
        cols += [-1] * 124
    return np.array(cols)


def odd_col_perm():
    cols = []
    for sg in range(8):
        cols += list(range(2048 + sg * 256, 2048 + sg * 256 + 256))
        cols += list(range(4096 + sg * 128, 4096 + sg * 128 + 128))
        cols += list(range(5120 + sg * 128, 5120 + sg * 128 + 128))
        cols += list(range(sg * 256, sg * 256 + 256))
        cols += list(range(6144 + sg * 4, 6144 + sg * 4 + 4))
        cols += list(range(6144 + 32 + sg * 4, 6144 + 32 + sg * 4 + 4))
        cols += [-1] * 56
    return np.array(cols)


def permute_cols(w, perm):
    w = np.asarray(w, np.float32)
    out = np.zeros(w.shape[:-1] + (len(perm),), np.float32)
    m = perm >= 0
    out[..., m] = w[..., perm[m]]
    return out


def colmajor(v):
    v = np.asarray(v, np.float32)
    lead = v.shape[:-1]
    C = v.shape[-1] // 128
    a = v.reshape(lead + (C, 128))
    a = np.moveaxis(a, -1, 0)
    return np.ascontiguousarray(a).reshape(128, -1)


def bcast_rows(v):
    v = np.asarray(v, np.float32).reshape(1, -1)
    return np.ascontiguousarray(np.broadcast_to(v, (128, v.shape[1])))


def make_vecs(inp):
    vec = np.zeros((128, NVEC), np.float32)

    def put(name, a):
        o, n = VOFF[name]
        assert a.shape[1] == n, (name, a.shape, n)
        vec[:a.shape[0], o:o + n] = a

    put("gmix", colmajor(inp["norm_mix_g"]))
    put("gmlp", colmajor(inp["norm_mlp_g"]))
    put("bmod", colmajor(inp["b_mod"]))
    put("fing", colmajor(inp["final_norm_g"]))
    qk = np.stack([inp["attn_q_norm_g"], inp["attn_k_norm_g"]], axis=1)
    put("qkg", np.ascontiguousarray(np.moveaxis(qk, -1, 0)).reshape(64, 4))
    dc = np.asarray(inp["delta_conv_w"], np.float32).reshape(2, 3, 3, 4, 128)
    dc = np.transpose(dc, (4, 0, 3, 2, 1))
    put("dconv", np.ascontiguousarray(dc).reshape(128, 72))
    put("dalog", bcast_rows(inp["delta_a_log"]))
    put("ddtb", bcast_rows(inp["delta_dt_bias"]))
    put("dng", np.ascontiguousarray(np.asarray(inp["delta_norm_g"], np.float32).T))
    cw = np.asarray(inp["ssm_conv_w"], np.float32)
    cb = np.asarray(inp["ssm_conv_b"], np.float32)
    wb = np.concatenate([cw, cb[:, None, :]], axis=1)
    sc = np.zeros((128, 2, 8, 4, 4), np.float32)
    for sg in range(8):
        chans = [(sg * 256, 128), (sg * 256 + 128, 128), (2048 + sg * 128, 128), (3072 + sg * 128, 128)]
        for ci, (c0, n) in enumerate(chans):
            sc[:, :, sg, ci, :] = np.transpose(wb[:, :, c0:c0 + 128], (2, 0, 1))
    put("sconv", sc.reshape(128, 256))
    put("salog", bcast_rows(inp["ssm_a_log"]))
    put("sdtb", bcast_rows(inp["ssm_dt_bias"]))
    put("sd", bcast_rows(inp["ssm_d"]))
    put("sng", colmajor(inp["ssm_norm_g"]))
    return vec


import os as _os0
SILU = AF.Identity if "silu" in _os0.environ.get("DBGSKIP", "") else AF.Silu


def build(n_layers=DEPTH, stop=None):
    nc = bass.Bass("TRN2", target_bir_lowering=False)
    dr = {}

    def din(name, shape):
        dr[name] = nc.dram_tensor(name, list(shape), F32, kind="ExternalInput").ap()

    def dout(name, shape):
        dr[name] = nc.dram_tensor(name, list(shape), F32, kind="ExternalOutput").ap()

    din("xin", [2048, D])
    din("condT", [128, 16])
    din("consts", [128, NCST])
    din("rope", [64, 2048])
    din("vecs", [128, NVEC])
    din("ctxk", [2, 512, 128])
    din("ctxv", [2, 512, 128])
    din("sd0", [2, 2, 4, 128, 128])
    din("ss0", [2, 2, 16, 128, 128])
    din("w_mod", [DEPTH, D, 6 * D])
    din("w_mlp_in", [DEPTH, D, 4 * D])
    din("w_mlp_out", [DEPTH, 4 * D, D])
    din("wie", [2, D, 3328])
    din("w_out_even", [2, D, D])
    din("wio", [2, D, 6656])
    din("w_out_odd", [2, 2 * D, D])
    dout("y", [2048, D])
    dout("nk", [2, 1024, 128])
    dout("nv", [2, 1024, 128])
    dout("nsd", [4, 2, 2, 4, 128, 128])
    dout("nss", [4, 2, 2, 16, 128, 128])

    with ExitStack() as st:
        fw = FW(nc, st)
        dv = lambda name: V(dr[name], [])

        import os as _os
        _pad = int(_os.environ.get("DBGPAD", "0"))
        if _pad:
            fw.sb("dbgpad", [128, _pad * 256], F32)
        xT = fw.sb("xT", [128, 8, 2048], F32, 32)
        cst = fw.sb("cst", [128, NCST], F32)
        vec = fw.sb("vec", [128, NVEC], F32)
        identb = fw.sb("identb", [128, 128], BF16)
        onesb = fw.sb("onesb", [128, 128], BF16)
        scT = fw.sb("scT", [128, 8, 2], BF16)
        modv = fw.sb("modv", [128, 48, 2], F32)
        modA = fw.sb("modA", [128, 2, 8, 2], F32)
        ring = Rot([fw.sb("wr%d" % i, [128, 4096], BF16, 2) for i in range(3)])
        mmp = Rot([fw.ps("pm%d" % i, [128, 512], F32) for i in range(4)])
        accp = Rot([fw.ps("pa%d" % i, [128, 512], F32) for i in range(2)])
        trp = Rot([fw.ps("pt%d" % i, [128, 1024], BF16) for i in range(2)])

        def C(name, rows=128, c0=0, c1=None):
            o, n = COFF[name]
            c1 = n if c1 is None else c1
            return cst[0:rows, o + c0:o + c1]

        def VC(name, col, rows=128, n=1):
            o, _ = VOFF[name]
            return vec[0:rows, o + col:o + col + n]

        def xv(kc, tt):
            return xT.p(kc * 4 + tt)[:, kc, tt * 512:(tt + 1) * 512]

        def wload(src, KC, cols):
            t = ring.next()
            h = KC // 2
            s3 = src.rearrange("(kc p) c -> p kc c", p=128)
            for a in range(2):
                dst = t.p(a)[:, a * h * cols:(a + 1) * h * cols].re("p (kc c) -> p kc c", kc=h)
                fw.dma(dst, V(s3[:, a * h:(a + 1) * h, :], []), q="pool")
            return t[:, 0:KC * cols].re("p (kc c) -> p kc c", kc=KC)

        evac_i = [0]

        def evac(out, in_):
            evac_i[0] ^= 1
            fw.cp("act" if evac_i[0] else "dve", out, in_)

        fw.dma(cst[:, :], dv("consts"))
        fw.dma(vec[:, :], dv("vecs"))
        fw.cp("act", identb[:, :], C("ident"))
        fw.cp("dve", onesb[:, :], C("ones"))
        with ExitStack() as ph:
            ctmp = fw.sb("ctmp", [128, 16], F32, stack=ph)
            fw.dma(ctmp[:, :], dv("condT"))
            fw.act(scT[:, :, :].re("p a b -> p (a b)"), ctmp[:, :], SILU)
            xst = Rot([fw.sb("xst%d" % i, [128, D], F32, stack=ph) for i in range(2)])
            for ti in range(16):
                s = xst.next()
                fw.dma(s[:, :], dv("xin")[ti * 128:(ti + 1) * 128, :])
                for half in range(2):
                    ps = mmp.next()
                    for q in range(4):
                        kc = half * 4 + q
                        fw.tr(ps[:, q * 128:(q + 1) * 128], s[:, kc * 128:(kc + 1) * 128], C("ident"))
                    tt = ti // 4
                    dst = xT.p(*[(half * 4 + q) * 4 + tt for q in range(4)])[
                        :, half * 4:half * 4 + 4, ti * 128:(ti + 1) * 128]
                    evac(dst, ps[:, :].re("p (a b) -> p a b", a=4))
            fw.barrier()
            fw.flush()

        def mod_vectors(layer):
            for ot in range(12):
                wt = wload(dr["w_mod"][layer, :, ot * 512:(ot + 1) * 512], 8, 512)
                ps = mmp.next()
                for o4 in range(4):
                    for kc in range(8):
                        fw.mm(ps[:, o4 * 2:o4 * 2 + 2], wt[:, kc, o4 * 128:(o4 + 1) * 128], scT[:, kc, :],
                              start=(kc == 0), stop=(kc == 7))
                fw.tt(modv[:, ot * 4:(ot + 1) * 4, :], ps[:, 0:8].re("p (a b) -> p a b", a=4),
                      VC("bmod", layer * 48 + ot * 4, n=4).un(2).bc([128, 4, 2]), ALU.add)
            for which, (gname, sc0) in enumerate((("gmix", 8), ("gmlp", 32))):
                fw.ts(modA[:, which, :, :], modv[:, sc0:sc0 + 8, :], 1.0, ALU.add)
                fw.tt(modA[:, which, :, :], modA[:, which, :, :],
                      VC(gname, layer * 8, n=8).un(2).bc([128, 8, 2]), ALU.mult)

        def rstd_bc(ph, name, nfeat):
            sqp = Rot([fw.sb("%s_sq%d" % (name, i), [128, 512], BF16, stack=ph) for i in range(3)])
            rsp = Rot([fw.sb("%s_rs%d" % (name, i), [128, 512], F32, stack=ph) for i in range(2)])

            def f(srcs):
                ps = mmp.next()
                n = len(srcs)
                for i, s in enumerate(srcs):
                    sq = sqp.next()
                    fw.act(sq[:, :], s, AF.Square)
                    fw.mm(ps[:, :], onesb[:, :], sq[:, :], start=(i == 0), stop=(i == n - 1))
                rs = rsp.next()
                fw.act(rs[:, :], ps[:, :], AF.Sqrt, bias=EPS, scale=1.0 / nfeat)
                fw.recip(rs[:, :], rs[:, :])
                return rs
            return f

        def modulate(ph, name, hT, which, tts, t0, ntt):
            rfn = rstd_bc(ph, name, D)
            tmpp = Rot([fw.sb("%s_tmp%d" % (name, i), [128, 512], F32, stack=ph) for i in range(3)])
            sh0 = 0 if which == 0 else 24
            for tt in tts:
                r = 0 if tt < 2 else 1
                rs = rfn([xv(kc, tt) for kc in range(8)])
                for kc in range(8):
                    tmp = tmpp.next()
                    fw.tt(tmp[:, :], xv(kc, tt), rs[:, :], ALU.mult)
                    lt = tt - t0
                    fw.act(hT.p(kc * ntt + lt)[:, kc, lt * 512:(lt + 1) * 512], tmp[:, :], AF.Identity,
                           bias=modv[:, sh0 + kc, r:r + 1], scale=modA[:, which, kc, r:r + 1])

        def mlp(layer):
            with ExitStack() as ph:
                hT = fw.sb("hTm", [128, 8, 2048], BF16, 32, stack=ph)
                h1 = fw.sb("h1", [128, 8, 2048], BF16, 32, stack=ph)
                rl = Rot([fw.sb("rl%d" % i, [128, 512], F32, stack=ph) for i in range(3)])
                modulate(ph, "nm", hT, 1, range(4), 0, 4)
                for blk in range(4):
                    for ht in range(2):
                        c0 = blk * 1024 + ht * 512
                        wt = wload(dr["w_mlp_in"][layer, :, c0:c0 + 512], 8, 512)
                        for h4 in range(4):
                            hc = ht * 4 + h4
                            for tt in range(4):
                                ps = mmp.next()
                                for kc in range(8):
                                    fw.mm(ps[:, :], wt[:, kc, h4 * 128:(h4 + 1) * 128],
                                          hT.p(kc * 4 + tt)[:, kc, tt * 512:(tt + 1) * 512],
                                          start=(kc == 0), stop=(kc == 7))
                                r = rl.next()
                                fw.act(r[:, :], ps[:, :], AF.Relu)
                                fw.tt(h1.p(hc * 4 + tt)[:, hc, tt * 512:(tt + 1) * 512], r[:, :], r[:, :], ALU.mult)
                    for ot in range(2):
                        wt = wload(dr["w_mlp_out"][layer, blk * 1024:(blk + 1) * 1024, ot * 512:(ot + 1) * 512], 8, 512)
                        for o4 in range(4):
                            oc = ot * 4 + o4
                            for tt in range(4):
                                r = 0 if tt < 2 else 1
                                ps = mmp.next()
                                for kc in range(8):
                                    fw.mm(ps[:, :], wt[:, kc, o4 * 128:(o4 + 1) * 128],
                                          h1.p(kc * 4 + tt)[:, kc, tt * 512:(tt + 1) * 512],
                                          start=(kc == 0), stop=(kc == 7))
                                fw.stt(xv(oc, tt), ps[:, :], modv[:, 40 + oc, r:r + 1], xv(oc, tt), ALU.mult, ALU.add)
                fw.barrier()
                fw.flush()

        def final_out(lvl=9):
            with ExitStack() as ph:
                rfn = rstd_bc(ph, "fn", D)
                yT = fw.sb("yT", [128, 8, 512], F32, stack=ph)
                ost = Rot([fw.sb("ost%d" % i, [128, D], F32, stack=ph) for i in range(2)])
                for tt in range(4):
                    rs = rfn([xv(kc, tt) for kc in range(8)])
                    if lvl < 2:
                        continue
                    for kc in range(8):
                        fw.stt(yT[:, kc, :], xv(kc, tt), VC("fing", kc), rs[:, :], ALU.mult, ALU.mult)
                    if lvl < 3:
                        continue
                    for q in range(4):
                        ti = tt * 4 + q
                        o = ost.next()
                        for half in range(2):
                            ps = mmp.next()
                            for a in range(4):
                                kc = half * 4 + a
                                fw.tr(ps[:, a * 128:(a + 1) * 128], yT[:, kc, q * 128:(q + 1) * 128], C("ident"))
                            evac(o[:, half * 512:(half + 1) * 512], ps[:, :])
                        if lvl >= 4:
                            fw.dma(dv("y")[ti * 128:(ti + 1) * 128, :], o[:, :])
                fw.barrier()
                fw.flush()

        def out_proj_even(layer, g, catT):
            j = layer // 2
            for ot in range(2):
                wt = wload(dr["w_out_even"][j, :, ot * 512:(ot + 1) * 512], 8, 512)
                for o4 in range(4):
                    oc = ot * 4 + o4
                    for tg in range(2):
                        tt = 2 * g + tg
                        ps = mmp.next()
                        for kc in range(8):
                            fw.mm(ps[:, :], wt[:, kc, o4 * 128:(o4 + 1) * 128],
                                  catT.p(kc * 2 + tg)[:, kc, tg * 512:(tg + 1) * 512], start=(kc == 0), stop=(kc == 7))
                        fw.stt(xv(oc, tt), ps[:, :], modv[:, 16 + oc, g:g + 1], xv(oc, tt), ALU.mult, ALU.add)

        def out_proj_odd(ph, layer, g, catT):
            j = layer // 2
            rfn = rstd_bc(ph, "on", 2 * D)
            rsk = fw.sb("rsk", [128, 1024], F32, stack=ph)
            tmpp = Rot([fw.sb("opt%d" % i, [128, 512], F32, stack=ph) for i in range(2)])
            for tg in range(2):
                rs = rfn([catT.p(c * 2 + tg)[:, c, tg * 512:(tg + 1) * 512] for c in range(16)])
                fw.cp("dve", rsk[:, tg * 512:(tg + 1) * 512], rs[:, :])
            for ot in range(4):
                wt = wload(dr["w_out_odd"][j, :, ot * 256:(ot + 1) * 256], 16, 256)
                fw.tt(wt, wt, VC("sng", j * 16, n=16).un(2).bc([128, 16, 256]), ALU.mult)
                for o2 in range(2):
                    oc = ot * 2 + o2
                    for tg in range(2):
                        tt = 2 * g + tg
                        ps = mmp.next()
                        for kc in range(16):
                            fw.mm(ps[:, :], wt[:, kc, o2 * 128:(o2 + 1) * 128],
                                  catT.p(kc * 2 + tg)[:, kc, tg * 512:(tg + 1) * 512], start=(kc == 0), stop=(kc == 15))
                        tmp = tmpp.next()
                        fw.tt(tmp[:, :], ps[:, :], rsk[:, tg * 512:(tg + 1) * 512], ALU.mult)
                        fw.stt(xv(oc, tt), tmp[:, :], modv[:, 16 + oc, g:g + 1], xv(oc, tt), ALU.mult, ALU.add)
        def rsum(out, in_):
            o, i = _ap(out), _ap(in_)
            fw.op("dve", lambda e: e.reduce_sum(out=o, in_=i, axis=mybir.AxisListType.X), _deps(in_), _deps(out))

        def even_mixer(ph, layer, g, hT, catT, dbg=None):
            j = layer // 2
            ctx = (g == 0)
            nseq, L = (4, 256) if ctx else (1, 1024)
            tps = L // 128

            def hv(kc, lo, n):
                return hT.p(kc * 2 + lo // 512)[:, kc, lo:lo + n]

            with ExitStack() as ua:
                qT = fw.sb("qT", [64, 4, 1024], BF16, 4, stack=ua)
                kT = fw.sb("kT", [64, 1536], BF16, stack=ua)
                V1 = fw.sb("V1", [128, 12, 65], BF16, stack=ua)
                otok = fw.sb("otok", [128, 8, 256], BF16, 8, stack=ua)
                raw = Rot([fw.sb("araw%d" % i, [64, 1024], F32, stack=ua) for i in range(2)])
                qn = Rot([fw.sb("aqn%d" % i, [64, 512], F32, stack=ua) for i in range(2)])
                sqb = Rot([fw.sb("asq%d" % i, [64, 512], BF16, stack=ua) for i in range(2)])
                rsb = Rot([fw.sb("ars%d" % i, [64, 512], F32, stack=ua) for i in range(2)])
                t1p = Rot([fw.sb("at1%d" % i, [64, 512], F32, stack=ua) for i in range(2)])
                t2p = Rot([fw.sb("at2%d" % i, [64, 512], F32, stack=ua) for i in range(2)])
                ptp = Rot([fw.sb("apt%d" % i, [128, 512], BF16, stack=ua) for i in range(3)])
                rdp = Rot([fw.sb("ard%d" % i, [128, 1], F32, stack=ua) for i in range(4)])
                vst = Rot([fw.sb("avs%d" % i, [128, 64], F32, stack=ua) for i in range(2)])
                kst = Rot([fw.sb("aks%d" % i, [128, 64], F32, stack=ua) for i in range(2)])
                cks = Rot([fw.sb("ack%d" % i, [128, 128], F32, stack=ua) for i in range(2)])
                if not ctx:
                    rope = fw.sb("ropeT", [64, 2048], F32, stack=ua)
                    fw.dma(rope[:, :], dv("rope"))
                fw.memset(V1[:, :, 64:65], 1.0)
                for g2 in range(2):
                    wt = wload(dr["wie"][j, :, g2 * 384:(g2 + 1) * 384], 8, 384)

                    def proj_norm(col0, gain_col, dst_fn, is_k):
                        r = raw.next()
                        for tg in range(2):
                            ps = mmp.next()
                            for kc in range(8):
                                fw.mm(ps[0:64, :], wt[:, kc, col0:col0 + 64], hv(kc, tg * 512, 512),
                                      start=(kc == 0), stop=(kc == 7))
                            evac(r[:, tg * 512:(tg + 1) * 512], ps[0:64, :])
                        for tg in range(2):
                            sl = slice(tg * 512, (tg + 1) * 512)
                            sq = sqb.next()
                            fw.act(sq[:, :], r[:, sl], AF.Square)
                            ps = mmp.next()
                            fw.mm(ps[0:64, :], onesb[0:64, 0:64], sq[:, :])
                            rs = rsb.next()
                            fw.act(rs[:, :], ps[0:64, :], AF.Sqrt, bias=EPS, scale=1.0 / 64)
                            fw.recip(rs[:, :], rs[:, :])
                            if ctx and not is_k:
                                fw.stt(dst_fn(sl), r[:, sl], gain_col, rs[:, :], ALU.mult, ALU.mult)
                                continue
                            q = qn.next()
                            fw.stt(q[:, :], r[:, sl], gain_col, rs[:, :], ALU.mult, ALU.mult)
                            if ctx:
                                fw.cp("act", dst_fn(sl), q[:, :])
                                for t4 in range(4):
                                    ti = tg * 4 + t4
                                    ps2 = mmp.next()
                                    fw.tr(ps2[:, 0:64], q[:, t4 * 128:(t4 + 1) * 128], C("ident", 64, 0, 64))
                                    ks = kst.next()
                                    evac(ks[:, :], ps2[:, 0:64])
                                    fw.dma(dv("nk")[j, ti * 128:(ti + 1) * 128, g2 * 64:(g2 + 1) * 64], ks[:, :])
                            else:
                                ps2 = mmp.next()
                                fw.mm(ps2[0:64, :], C("ropeR", 64), q[:, :])
                                t1 = t1p.next()
                                fw.tt(t1[:, :], q[:, :], rope[:, sl], ALU.mult)
                                t2 = t2p.next()
                                fw.tt(t2[:, :], ps2[0:64, :], rope[:, 1024 + tg * 512:1024 + (tg + 1) * 512], ALU.mult)
                                fw.tt(dst_fn(sl), t1[:, :], t2[:, :], ALU.add)

                    for hh in range(4):
                        proj_norm(hh * 64, VC("qkg", j * 2 + 0, rows=64),
                                  (lambda sl, hh=hh: qT.p(hh)[:, hh, sl]), False)
                    proj_norm(256, VC("qkg", j * 2 + 1, rows=64), (lambda sl: kT[:, sl]), True)
                    for ti in range(8):
                        ps = mmp.next()
                        for kc in range(8):
                            fw.mm(ps[:, 0:64], hv(kc, ti * 128, 128), wt[:, kc, 320:384], start=(kc == 0), stop=(kc == 7))
                        fw.cp("act", V1[:, ti, 0:64], ps[:, 0:64])
                        if ctx:
                            vs = vst.next()
                            fw.cp("dve", vs[:, :], ps[:, 0:64])
                            fw.dma(dv("nv")[j, ti * 128:(ti + 1) * 128, g2 * 64:(g2 + 1) * 64], vs[:, :])
                    if not ctx:
                        for t in range(4):
                            ck = cks.next()
                            fw.dma(ck[:, 0:64], dv("ctxk")[j, t * 128:(t + 1) * 128, g2 * 64:(g2 + 1) * 64])
                            fw.dma(ck[:, 64:128], dv("ctxv")[j, t * 128:(t + 1) * 128, g2 * 64:(g2 + 1) * 64])
                            ps2 = mmp.next()
                            fw.tr(ps2[0:64, 0:128], ck[:, 0:64], C("ident"))
                            fw.cp("act", kT[:, 1024 + t * 128:1024 + (t + 1) * 128], ps2[0:64, 0:128])
                            fw.cp("dve", V1[:, 8 + t, 0:64], ck[:, 64:128])
                    QB = 256 if ctx else 512
                    for s in range(nseq):
                        if ctx:
                            kts = [(slice(ti * 128, (ti + 1) * 128), ti) for ti in (2 * s, 2 * s + 1)]
                        else:
                            kts = [(slice(t * 128, (t + 1) * 128), t) for t in range(12)]
                        for hh in range(4):
                            for qb in range(L // QB):
                                q0 = s * L + qb * QB
                                oacc = accp.next()
                                for idx, (ksl, vt) in enumerate(kts):
                                    ps = mmp.next()
                                    fw.mm(ps[:, 0:QB], kT[:, ksl], qT.p(hh)[:, hh, q0:q0 + QB])
                                    pt = ptp.next()
                                    fw.act(pt[:, 0:QB], ps[:, 0:QB], AF.Exp, scale=0.125)
                                    nqs = QB // 128
                                    for qs in range(nqs):
                                        fw.mm(oacc[:, qs * 128:qs * 128 + 65], pt[:, qs * 128:(qs + 1) * 128],
                                              V1[:, vt, :], start=(idx == 0 and qs == 0),
                                              stop=(idx == len(kts) - 1 and qs == nqs - 1))
                                for qs in range(QB // 128):
                                    ti = q0 // 128 + qs
                                    rd = rdp.next()
                                    fw.recip(rd[:, :], oacc[:, qs * 128 + 64:qs * 128 + 65])
                                    fw.act(otok.p(ti)[:, ti, hh * 64:(hh + 1) * 64], oacc[:, qs * 128:qs * 128 + 64],
                                           AF.Identity, scale=rd[:, 0:1])
                    for ti in range(8):
                        for c2 in range(2):
                            pT = trp.next()
                            fw.tr(pT[:, 0:128], otok.p(ti)[:, ti, c2 * 128:(c2 + 1) * 128], identb[:, :])
                            c = 2 * g2 + c2
                            evac(catT.p(c * 2 + ti // 4)[:, c, ti * 128:(ti + 1) * 128], pT[:, 0:128])
                fw.barrier()
                fw.flush()

            if dbg == "dbg_att":
                return
            with ExitStack() as ug:
                raw = Rot([fw.sb("graw%d" % i, [128, 1024], F32, stack=ug) for i in range(2)])
                cac = Rot([fw.sb("gcac%d" % i, [128, 1024], F32, stack=ug) for i in range(2)])
                qkf = Rot([fw.sb("gqkf%d" % i, [128, 1024], F32, stack=ug) for i in range(1)])
                fT = fw.sb("gfT", [128, 3, 1024], BF16, 3, stack=ug)
                ktok = fw.sb("gktok", [128, 8, 128], BF16, 8, stack=ug)
                vtok = fw.sb("gvtok", [128, 8, 128], BF16, 8, stack=ug)
                sz = fw.sb("gsz", [128, 8, 128], F32, 8, stack=ug)
                zraw = fw.sb("gzraw", [128, 8, 132], F32, 8, stack=ug)
                scr = zraw[:, :, 128:132]
                gs = fw.sb("ggs", [128, 10, 8, 2], F32, stack=ug)
                tmpa = fw.sb("gtmpa", [128, 8], F32, stack=ug)
                tmpb = fw.sb("gtmpb", [128, 8], F32, stack=ug)
                negA = fw.sb("gnegA", [128, 2], F32, stack=ug)
                decb = fw.sb("gdecb", [128, 2, 16], F32, stack=ug)
                oac = fw.sb("goac", [128, 8, 128], F32, 8, stack=ug)
                ssc = fw.sb("gssc", [128, 8], F32, stack=ug)
                junk = Rot([fw.sb("gjunk%d" % i, [128, 128], F32, stack=ug) for i in range(2)])
                onb = Rot([fw.sb("gonb%d" % i, [128, 128], BF16, stack=ug) for i in range(2)])
                sqb = Rot([fw.sb("gsq%d" % i, [128, 512], BF16, stack=ug) for i in range(2)])
                rsb = Rot([fw.sb("grs%d" % i, [128, 512], F32, stack=ug) for i in range(2)])
                S = [fw.sb("gS%d" % d, [128, 128], F32, stack=ug) for d in range(2)]
                Sb = [fw.sb("gSb%d" % d, [128, 128], BF16, stack=ug) for d in range(2)]
                dg = [fw.sb("gdg%d" % d, [128, 384], F32, stack=ug) for d in range(2)]
                E2 = [fw.sb("gE2%d" % d, [128, 256], F32, stack=ug) for d in range(2)]
                Mm = [fw.sb("gMm%d" % d, [128, 128], F32, stack=ug) for d in range(2)]
                qkT = [[fw.sb("gqkT%d_%d" % (d, i), [128, 128], BF16, stack=ug) for i in range(2)] for d in range(2)]
                Bm = [[fw.sb("gB%d_%d" % (d, i), [128, 128], F32, stack=ug) for i in range(2)] for d in range(2)]
                AS = [[fw.sb("gAS%d_%d" % (d, i), [128, 256], F32, stack=ug) for i in range(2)] for d in range(2)]
                TT = [fw.sb("gTT%d" % d, [128, 128], BF16, stack=ug) for d in range(2)]
                vb = [fw.sb("gvb%d" % d, [128, 128], BF16, stack=ug) for d in range(2)]
                kbg = [fw.sb("gkbg%d" % d, [128, 128], BF16, stack=ug) for d in range(2)]
                kdec = [[fw.sb("gkdec%d_%d" % (d, i), [128, 128], BF16, stack=ug) for i in range(2)] for d in range(2)]
                qdT = [[fw.sb("gqdT%d_%d" % (d, i), [128, 128], BF16, stack=ug) for i in range(2)] for d in range(2)]
                usb = [[fw.sb("gusb%d_%d" % (d, i), [128, 128], F32, stack=ug) for i in range(2)] for d in range(2)]
                wTs = [[fw.sb("gwT%d_%d" % (d, i), [128, 128], BF16, stack=ug) for i in range(2)] for d in range(2)]
                vnew = [fw.sb("gvn%d" % d, [128, 128], BF16, stack=ug) for d in range(2)]
                for d in range(2):
                    fw.memset(vnew[d][:, :], 0.0)
                for hd in range(4):
                    base = 768 + hd * 640
                    wtA = wload(dr["wie"][j, :, base:base + 384], 8, 384)
                    import os
                    SK = os.environ.get("DBGSKIP", "")
                    if "wtb" not in SK:
                        wtB = wload(dr["wie"][j, :, base + 384:base + 640], 8, 256)
                    for wi in range(3):
                        r = raw.next()
                        for tg in range(2):
                            ps = mmp.next()
                            for kc in range(8):
                                fw.mm(ps[:, :], wtA[:, kc, wi * 128:(wi + 1) * 128], hv(kc, tg * 512, 512),
                                      start=(kc == 0), stop=(kc == 7))
                            evac(r[:, tg * 512:(tg + 1) * 512], ps[:, :])
                        a = cac.next()
                        cwc = lambda tap: VC("dconv", ((j * 4 + hd) * 3 + wi) * 3 + tap)
                        r3 = r[:, :].re("p (s l) -> p s l", s=nseq)
                        a3 = a[:, :].re("p (s l) -> p s l", s=nseq)
                        fw.ts(a[:, :], r[:, :], cwc(1), ALU.mult)
                        if "conv" not in SK:
                            fw.stt(a3[:, :, 1:L], r3[:, :, 0:L - 1], cwc(0), a3[:, :, 1:L], ALU.mult, ALU.add)
                            fw.stt(a3[:, :, 0:L - 1], r3[:, :, 1:L], cwc(2), a3[:, :, 0:L - 1], ALU.mult, ALU.add)
                        if wi == 2:
                            fw.act(fT.p(2)[:, 2, :], a[:, :], SILU)
                        else:
                            f = qkf.next()
                            fw.act(f[:, :], a[:, :], SILU)
                            for tg in range(2):
                                sl = slice(tg * 512, (tg + 1) * 512)
                                sq = sqb.next()
                                fw.act(sq[:, :], f[:, sl], AF.Square)
                                ps = mmp.next()
                                fw.mm(ps[:, :], onesb[:, :], sq[:, :])
                                rs = rsb.next()
                                fw.act(rs[:, :], ps[:, :], AF.Sqrt, bias=EPS, scale=1.0)
                                fw.recip(rs[:, :], rs[:, :])
                                fw.stt(fT.p(wi)[:, wi, sl], f[:, sl], (128 ** -0.5) if wi == 0 else 1.0, rs[:, :],
                                       ALU.mult, ALU.mult)
                    for ti in range(8):
                        tsl = slice(ti * 128, (ti + 1) * 128)
                        pT = trp.next()
                        fw.tr(pT[:, 0:128], fT.p(1)[:, 1, tsl], identb[:, :])
                        evac(ktok.p(ti)[:, ti, :], pT[:, 0:128])
                        pT = trp.next()
                        fw.tr(pT[:, 0:128], fT.p(2)[:, 2, tsl], identb[:, :])
                        evac(vtok.p(ti)[:, ti, :], pT[:, 0:128])
                    if dbg == "dbg_gdn1":
                        break
                    for ti in range(8):
                        ps = mmp.next()
                        for kc in range(8):
                            fw.mm(ps[:, 0:256], hv(kc, ti * 128, 128), wtB[:, kc, :], start=(kc == 0), stop=(kc == 7))
                        fw.cp("act", zraw.p(ti)[:, ti, :], ps[:, 0:132])
                        fw.act(sz.p(ti)[:, ti, :], zraw.p(ti)[:, ti, 0:128], SILU)
                    if "scal" in SK:
                        break
                    for d in range(2):
                        col = j * 8 + d * 4 + hd
                        fw.act(negA[:, d:d + 1], VC("dalog", col), AF.Exp)
                        fw.ts(negA[:, d:d + 1], negA[:, d:d + 1], -1.0, ALU.mult)
                        fw.act(tmpa[:, :], scr[:, :, 2 + d:3 + d].re('p t o -> p (t o)'), AF.Exp, bias=VC("ddtb", col), scale=1.0)
                        fw.act(tmpa[:, :], tmpa[:, :], AF.Ln, bias=1.0, scale=1.0)
                        fw.ts(gs[:, 2, :, d], tmpa[:, :], negA[:, d:d + 1], ALU.mult)
                        fw.act(tmpb[:, :], scr[:, :, d:d + 1].re('p t o -> p (t o)'), AF.Exp, scale=-1.0)
                        fw.act(gs[:, 1, :, d], tmpb[:, :], AF.Ln, bias=1.0, scale=1.0)
                        fw.act(gs[:, 0, :, d], gs[:, 1, :, d], AF.Exp, scale=-1.0)
                    if "cums" in SK:
                        break
                    ps = mmp.next()
                    fw.mm(ps[:, 0:8], C("triF"), gs[:, 2, :, 0])
                    fw.mm(ps[:, 8:16], C("triB"), gs[:, 2, :, 1])
                    fw.mm(ps[:, 16:24], C("blk"), gs[:, 2, :, 0])
                    fw.mm(ps[:, 24:32], C("blk"), gs[:, 2, :, 1])
                    for d in range(2):
                        fw.cp("dve", gs[:, 3, :, d], ps[:, d * 8:(d + 1) * 8])
                        fw.cp("dve", gs[:, 4, :, d], ps[:, 16 + d * 8:16 + (d + 1) * 8])
                    fw.act(gs[:, 5, :, :], gs[:, 3, :, :], AF.Exp)
                    fw.tt(gs[:, 6, :, :], gs[:, 4, :, :], gs[:, 3, :, :], ALU.subtract)
                    fw.act(gs[:, 6, :, :], gs[:, 6, :, :], AF.Exp)
                    fw.tt(gs[:, 7, :, :], gs[:, 3, :, :], gs[:, 1, :, :], ALU.subtract)
                    fw.ts(gs[:, 8, :, :], gs[:, 3, :, :], -1.0, ALU.mult)
                    fw.tt(gs[:, 9, :, :], gs[:, 0, :, :], gs[:, 5, :, :], ALU.mult)
                    ps = mmp.next()
                    g2d = gs[:, 2, :, :].re("p t d -> p (t d)")
                    fw.mm(ps[:, 0:16], C("ind0"), g2d)
                    fw.mm(ps[:, 16:32], C("ind1"), g2d)
                    fw.act(decb[:, :, :].re("p c n -> p (c n)"), ps[:, 0:32], AF.Exp)
                    for ti in range(8):
                        fw.memset(oac.p(ti)[:, ti, :], 0.0)
                    if dbg == "dbg_gdn2":
                        break
                    def gcol(k, ti, d):
                        return gs[:, k, ti, d:d + 1]

                    def solve(n):
                        par = n % 2
                        slots = [(n, 0), (7 - n, 1)]
                        for ti, d in slots:
                            fw.ts(dg[d][:, 0:128], C("ident"), gcol(3, ti, d), ALU.mult)
                            fw.ts(dg[d][:, 128:256], C("ident"), gcol(7, ti, d), ALU.mult)
                            fw.ts(dg[d][:, 256:384], C("ident"), gcol(5, ti, d), ALU.mult)
                        yield
                        for ti, d in slots:
                            p = mmp.next()
                            fw.mm(p[:, 0:256], C("ones"), dg[d][:, 0:256], start=True, stop=False)
                            fw.mm(p[:, 0:256], C("ident"), C("nmF" if d == 0 else "nmB"), start=False, stop=True)
                            fw.mm(p[:, 256:384], C("ones"), dg[d][:, 256:384], start=True, stop=True)
                            fw.act(E2[d][:, :], p[:, 0:256], AF.Exp, bias=gcol(8, ti, d), scale=1.0)
                            fw.tt(qdT[d][par][:, :], fT.p(0)[:, 0, ti * 128:(ti + 1) * 128], p[:, 256:384], ALU.mult)
                        yield
                        for ti, d in slots:
                            tsl = slice(ti * 128, (ti + 1) * 128)
                            p = mmp.next()
                            fw.mm(p[:, 0:128], fT.p(1)[:, 1, tsl], fT.p(0)[:, 0, tsl])
                            fw.mm(p[:, 128:256], fT.p(1)[:, 1, tsl], fT.p(1)[:, 1, tsl])
                            fw.tt(qkT[d][par][:, :], p[:, 0:128], E2[d][:, 0:128], ALU.mult)
                            fw.tt(Mm[d][:, :], p[:, 128:256], E2[d][:, 128:256], ALU.mult)
                        yield
                        for ti, d in slots:
                            p = mmp.next()
                            fw.tr(p[:, 0:128], Mm[d][:, :], C("ident"))
                            fw.cp("act", Bm[d][0][:, :], p[:, 0:128])
                        yield
                        for ti, d in slots:
                            p = mmp.next()
                            fw.mm(p[:, 0:128], Bm[d][0][:, :], Mm[d][:, :])
                            fw.mm(p[:, 128:256], Mm[d][:, :], Bm[d][0][:, :])
                            fw.cp("act", AS[d][0][:, 0:128], p[:, 0:128])
                            fw.cp("dve", Bm[d][1][:, :], p[:, 128:256])
                            fw.tt(AS[d][0][:, 128:256], C("ident"), Mm[d][:, :], ALU.subtract)
                        yield
                        for k in range(1, 6):
                            for ti, d in slots:
                                cur, nxt, Bk, Bn = AS[d][(k - 1) % 2], AS[d][k % 2], Bm[d][k % 2], Bm[d][(k + 1) % 2]
                                p = mmp.next()
                                if k < 5:
                                    fw.mm(p[:, 0:256], Bk[:, :], cur[:, 0:256])
                                    fw.mm(p[:, 256:384], cur[:, 0:128], Bk[:, :])
                                    fw.cp("act", Bn[:, :], p[:, 256:384])
                                    if k < 4:
                                        fw.cp("act", nxt[:, 0:128], p[:, 0:128])
                                    fw.tt(nxt[:, 128:256], cur[:, 128:256], p[:, 128:256], ALU.add)
                                else:
                                    fw.mm(p[:, 0:128], Bk[:, :], cur[:, 128:256])
                                    fw.tt(TT[d][:, :], cur[:, 128:256], p[:, 0:128], ALU.add)
                            yield
                        for ti, d in slots:
                            fw.ts(vb[d][:, :], vtok.p(ti)[:, ti, :], gcol(0, ti, d), ALU.mult)
                            fw.ts(kbg[d][:, :], ktok.p(ti)[:, ti, :], gcol(9, ti, d), ALU.mult)
                            fw.act(kdec[d][par][:, :], ktok.p(ti)[:, ti, :], AF.Identity, scale=gcol(6, ti, d))
                        yield
                        for ti, d in slots:
                            p = mmp.next()
                            fw.mm(p[:, 0:128], TT[d][:, :], vb[d][:, :])
                            fw.mm(p[:, 128:256], kbg[d][:, :], TT[d][:, :])
                            fw.cp("act", usb[d][par][:, :], p[:, 0:128])
                            fw.cp("dve", wTs[d][par][:, :], p[:, 128:256])
                        yield

                    def recur(n):
                        par = n % 2
                        slots = [(n, 0), (7 - n, 1)]
                        for ci in range(2):
                            for ti, d in slots:
                                c = ci if d == 0 else 1 - ci
                                rows = slice(c * 64, (c + 1) * 64)
                                first = (ti % tps == 0 and c == 0) if d == 0 else (ti % tps == tps - 1 and c == 1)
                                last = (ti % tps == tps - 1 and c == 1) if d == 0 else (ti % tps == 0 and c == 0)
                                seq = ti // tps
                                if first:
                                    if ctx:
                                        fw.memset(S[d][:, :], 0.0)
                                    else:
                                        fw.dma(S[d][:, :], dv("sd0")[j, d, hd, :, :])
                                    fw.cp("act", Sb[d][:, :], S[d][:, :])
                                p = mmp.next()
                                fw.mm(p[:, 0:128], wTs[d][par][:, :], Sb[d][:, :])
                                fw.tt(vnew[d][rows, :], usb[d][par][rows, :], p[rows, 0:128], ALU.subtract)
                                yield
                                po = mmp.next()
                                fw.mm(po[:, 0:128], qdT[d][par][:, :], Sb[d][:, :], start=True, stop=False)
                                fw.mm(po[:, 0:128], qkT[d][par][rows, :], vnew[d][rows, :], start=False, stop=True)
                                pst = mmp.next()
                                fw.mm(pst[:, 0:128], kdec[d][par][rows, :], vnew[d][rows, :])
                                fw.stt(S[d][:, :], S[d][:, :], decb[:, c, ti * 2 + d:ti * 2 + d + 1], pst[:, 0:128],
                                       ALU.mult, ALU.add)
                                fw.cp("act", Sb[d][:, :], S[d][:, :])
                                fw.tt(oac.p(ti)[rows, ti, :], oac.p(ti)[rows, ti, :], po[rows, 0:128], ALU.add)
                                if last and ctx:
                                    fw.dma(dv("nsd")[seq, j, d, hd, :, :], S[d][:, :])
                                yield

                    def drive(gens):
                        gens = list(gens)
                        while gens:
                            for g_ in list(gens):
                                try:
                                    next(g_)
                                except StopIteration:
                                    gens.remove(g_)

                    drive([solve(0)])
                    for n in range(8):
                        drive([recur(n)] + ([solve(n + 1)] if n < 7 else []))

                    if dbg == "dbg_gdn5":
                        break
                    for ti in range(8):
                        jk = junk.next()
                        fw.tt(jk[:, :], oac.p(ti)[:, ti, :], oac.p(ti)[:, ti, :], ALU.mult)
                        rsum(ssc[:, ti:ti + 1], jk[:, :])
                    fw.act(ssc[:, :], ssc[:, :], AF.Sqrt, bias=EPS, scale=1.0 / 128)
                    fw.recip(ssc[:, :], ssc[:, :])
                    for ti in range(8):
                        ob = onb.next()
                        fw.stt(ob[:, :], oac.p(ti)[:, ti, :], ssc[:, ti:ti + 1], sz.p(ti)[:, ti, :], ALU.mult, ALU.mult)
                        pT = trp.next()
                        fw.tr(pT[:, 0:128], ob[:, :], identb[:, :])
                        c = 4 + hd
                        fw.act(catT.p(c * 2 + ti // 4)[:, c, ti * 128:(ti + 1) * 128], pT[:, 0:128], AF.Identity,
                               scale=VC("dng", j))
                fw.barrier()
                fw.flush()

        def odd_mixer(ph, layer, g, hT, catT):
            j = layer // 2
            ctx = (g == 0)
            nseq, L = (4, 256) if ctx else (1, 1024)
            tps = L // 128

            def hv(kc, lo, n):
                return hT.p(kc * 2 + lo // 512)[:, kc, lo:lo + n]

            with ExitStack() as us:
                raw = Rot([fw.sb("sraw%d" % i, [128, 1024], F32, stack=us) for i in range(1)])
                cac = Rot([fw.sb("scac%d" % i, [128, 1024], F32, stack=us) for i in range(1)])
                fT = fw.sb("sfT", [128, 4, 1024], BF16, 4, stack=us)
                xtok = fw.sb("sxtok", [128, 8, 256], BF16, 8, stack=us)
                Btok = fw.sb("sBtok", [128, 8, 128], BF16, 8, stack=us)
                zs = fw.sb("szs", [128, 8, 256], BF16, 8, stack=us)
                dtr = fw.sb("sdtr", [128, 8, 8], F32, stack=us)
                zr = Rot([fw.sb("szr%d" % i, [128, 264], F32, stack=us) for i in range(2)])
                ss = fw.sb("sss", [128, 7, 8, 8], F32, stack=us)
                negA = fw.sb("snegA", [128, 8], F32, stack=us)
                decb = fw.sb("sdecb", [128, 2, 64], F32, stack=us)
                yac = fw.sb("syac", [128, 8, 256], F32, 8, stack=us)
                xdt = [fw.sb("sxdt%d" % d, [128, 256], BF16, stack=us) for d in range(2)]
                xd = [[fw.sb("sxd%d_%d" % (d, i), [128, 256], BF16, stack=us) for i in range(2)] for d in range(2)]
                cbs = [fw.sb("scbs%d" % d, [128, 128], F32, stack=us) for d in range(2)]
                dgs = [fw.sb("sdgs%d" % d, [128, 4, 128], F32, stack=us) for d in range(2)]
                Ls = dgs
                cbL = [fw.sb("scbL%d" % d, [128, 4, 128], BF16, stack=us) for d in range(2)]
                sT = [fw.sb("ssT%d" % d, [128, 256], F32, stack=us) for d in range(2)]
                sTb = [fw.sb("ssTb%d" % d, [128, 256], BF16, stack=us) for d in range(2)]
                ytmp = Rot([fw.sb("sytmp%d" % i, [128, 256], F32, stack=us) for i in range(2)])
                ygb = Rot([fw.sb("sygb%d" % i, [128, 256], BF16, stack=us) for i in range(2)])
                sst = Rot([fw.sb("ssst%d" % i, [128, 128], F32, stack=us) for i in range(2)])
                for sg in range(8):
                    base = sg * 832
                    wtA = wload(dr["wio"][j, :, base:base + 512], 8, 512)
                    wtB = wload(dr["wio"][j, :, base + 512:base + 832], 8, 320)
                    for wi in range(4):
                        r = raw.next()
                        for tg in range(2):
                            ps = mmp.next()
                            for kc in range(8):
                                fw.mm(ps[:, :], wtA[:, kc, wi * 128:(wi + 1) * 128], hv(kc, tg * 512, 512),
                                      start=(kc == 0), stop=(kc == 7))
                            evac(r[:, tg * 512:(tg + 1) * 512], ps[:, :])
                        a = cac.next()
                        cwc = lambda tap: VC("sconv", ((j * 8 + sg) * 4 + wi) * 4 + tap)
                        r3 = r[:, :].re("p (s l) -> p s l", s=nseq)
                        a3 = a[:, :].re("p (s l) -> p s l", s=nseq)
                        fw.ts(a[:, :], r[:, :], cwc(1), ALU.mult)
                        fw.stt(a3[:, :, 1:L], r3[:, :, 0:L - 1], cwc(0), a3[:, :, 1:L], ALU.mult, ALU.add)
                        fw.stt(a3[:, :, 0:L - 1], r3[:, :, 1:L], cwc(2), a3[:, :, 0:L - 1], ALU.mult, ALU.add)
                        fw.act(fT.p(wi)[:, wi, :], a[:, :], SILU, bias=cwc(3), scale=1.0)
                    for ti in range(8):
                        tsl = slice(ti * 128, (ti + 1) * 128)
                        for c2 in range(2):
                            pT = trp.next()
                            fw.tr(pT[:, 0:128], fT.p(c2)[:, c2, tsl], identb[:, :])
                            evac(xtok.p(ti)[:, ti, c2 * 128:(c2 + 1) * 128], pT[:, 0:128])
                        pT = trp.next()
                        fw.tr(pT[:, 0:128], fT.p(2)[:, 2, tsl], identb[:, :])
                        evac(Btok.p(ti)[:, ti, :], pT[:, 0:128])
                    for ti in range(8):
                        ps = mmp.next()
                        for kc in range(8):
                            fw.mm(ps[:, 0:320], hv(kc, ti * 128, 128), wtB[:, kc, :], start=(kc == 0), stop=(kc == 7))
                        fw.cp("act", zr.next()[:, :], ps[:, 0:264])
                        zr_ = zr.items[(zr.i - 1) % len(zr.items)]
                        fw.act(zs.p(ti)[:, ti, :], zr_[:, 0:256], SILU)
                        fw.cp("act", dtr[:, ti, :], zr_[:, 256:264])
                    for d in range(2):
                        c0 = j * 64 + d * 32 + sg * 4
                        fw.act(negA[:, d * 4:(d + 1) * 4], VC("salog", c0, n=4), AF.Exp)
                        fw.tt(ss[:, 0, :, d * 4:(d + 1) * 4], dtr[:, :, d * 4:(d + 1) * 4],
                              VC("sdtb", c0, n=4).un(1).bc([128, 8, 4]), ALU.add)
                    fw.ts(negA[:, :], negA[:, :], -1.0, ALU.mult)
                    fw.act(ss[:, 0, :, :], ss[:, 0, :, :], AF.Exp)
                    fw.act(ss[:, 0, :, :], ss[:, 0, :, :], AF.Ln, bias=1.0, scale=1.0)
                    fw.tt(ss[:, 1, :, :], ss[:, 0, :, :], negA[:, :].un(1).bc([128, 8, 8]), ALU.mult)
                    ps = mmp.next()
                    fw.mm(ps[:, 0:32], C("triF"), ss[:, 1, :, 0:4])
                    fw.mm(ps[:, 32:64], C("triB"), ss[:, 1, :, 4:8])
                    fw.mm(ps[:, 64:128], C("blk"), ss[:, 1, :, :])
                    for d in range(2):
                        fw.cp("dve", ss[:, 2, :, d * 4:(d + 1) * 4], ps[:, d * 32:(d + 1) * 32].re("p (t h) -> p t h", h=4))
                    fw.cp("dve", ss[:, 3, :, :], ps[:, 64:128].re("p (t h) -> p t h", h=8))
                    fw.act(ss[:, 4, :, :], ss[:, 2, :, :], AF.Exp)
                    fw.tt(ss[:, 5, :, :], ss[:, 3, :, :], ss[:, 2, :, :], ALU.subtract)
                    fw.act(ss[:, 5, :, :], ss[:, 5, :, :], AF.Exp)
                    fw.ts(ss[:, 6, :, :], ss[:, 2, :, :], -1.0, ALU.mult)
                    ps = mmp.next()
                    a2d = ss[:, 1, :, :].re("p t h -> p (t h)")
                    fw.mm(ps[:, 0:64], C("ind0"), a2d)
                    fw.mm(ps[:, 64:128], C("ind1"), a2d)
                    fw.act(decb[:, :, :].re("p c n -> p (c n)"), ps[:, 0:128], AF.Exp)
                    for ti in range(8):
                        fw.tt(yac.p(ti)[:, ti, :].re("p (h e) -> p h e", h=4),
                              xtok.p(ti)[:, ti, :].re("p (h e) -> p h e", h=4),
                              VC("sd", j * 32 + sg * 4, n=4).un(2).bc([128, 4, 64]), ALU.mult)
                    def lpart(n):
                        par = n % 2
                        slots = [(n, 0), (7 - n, 1)]
                        for ti, d in slots:
                            tsl = slice(ti * 128, (ti + 1) * 128)
                            dh = slice(d * 4, (d + 1) * 4)
                            x3 = xtok.p(ti)[:, ti, :].re("p (h e) -> p h e", h=4)
                            fw.tt(xdt[d][:, :].re("p (h e) -> p h e", h=4), x3,
                                  ss[:, 0, ti, dh].un(2).bc([128, 4, 64]), ALU.mult)
                            fw.tt(xd[d][par][:, :].re("p (h e) -> p h e", h=4), xdt[d][:, :].re("p (h e) -> p h e", h=4),
                                  ss[:, 5, ti, dh].un(2).bc([128, 4, 64]), ALU.mult)
                            p = mmp.next()
                            fw.mm(p[:, 0:128], fT.p(2)[:, 2, tsl], fT.p(3)[:, 3, tsl])
                            fw.cp("act", cbs[d][:, :], p[:, 0:128])
                            fw.tt(dgs[d][:, :, :], C("ident").un(1).bc([128, 4, 128]),
                                  ss[:, 2, ti, dh].un(2).bc([128, 4, 128]), ALU.mult)
                        yield
                        for ti, d in slots:
                            pL = mmp.next()
                            fw.mm(pL[:, :], C("ones"), dgs[d][:, :, :].re("p h i -> p (h i)"), start=True, stop=False)
                            for h in range(4):
                                fw.mm(pL[:, h * 128:(h + 1) * 128], C("ident"), C("nmFc" if d == 0 else "nmBc"),
                                      start=False, stop=(h == 3))
                            for h in range(4):
                                fw.act(Ls[d][:, h, :], pL[:, h * 128:(h + 1) * 128], AF.Exp,
                                       bias=ss[:, 6, ti, d * 4 + h:d * 4 + h + 1], scale=1.0)
                            yield
                        for ti, d in slots:
                            fw.tt(cbL[d][:, :, :], Ls[d][:, :, :], cbs[d][:, :].un(1).bc([128, 4, 128]), ALU.mult)
                        yield
                        for ti, d in slots:
                            pY = mmp.next()
                            for h in range(4):
                                fw.mm(pY[:, h * 64:(h + 1) * 64], cbL[d][:, h, :], xdt[d][:, h * 64:(h + 1) * 64])
                            fw.tt(yac.p(ti)[:, ti, :], yac.p(ti)[:, ti, :], pY[:, 0:256], ALU.add)
                            yield

                    def srec(n):
                        par = n % 2
                        slots = [(n, 0), (7 - n, 1)]
                        for ci in range(2):
                            for ti, d in slots:
                                tsl = slice(ti * 128, (ti + 1) * 128)
                                c = ci if d == 0 else 1 - ci
                                rows = slice(c * 64, (c + 1) * 64)
                                first = (ti % tps == 0 and c == 0) if d == 0 else (ti % tps == tps - 1 and c == 1)
                                last = (ti % tps == tps - 1 and c == 1) if d == 0 else (ti % tps == 0 and c == 0)
                                seq = ti // tps
                                if first:
                                    if ctx:
                                        fw.memset(sT[d][:, :], 0.0)
                                    else:
                                        for half in range(2):
                                            st_ = sst.next()
                                            fw.dma(st_[:, :], dv("ss0")[j, d, 2 * sg + half, :, :])
                                            p = mmp.next()
                                            fw.tr(p[:, 0:128], st_[:, :], C("ident"))
                                            fw.cp("act", sT[d][:, half * 128:(half + 1) * 128], p[:, 0:128])
                                    fw.cp("act", sTb[d][:, :], sT[d][:, :])
                                po = mmp.next()
                                fw.mm(po[:, 0:256], fT.p(3)[:, 3, tsl], sTb[d][:, :])
                                pS = mmp.next()
                                fw.mm(pS[:, 0:256], Btok.p(ti)[rows, ti, :], xd[d][par][rows, :])
                                yt = ytmp.next()
                                fw.tt(yt[rows, :].re("p (h e) -> p h e", h=4), po[rows, 0:256].re("p (h e) -> p h e", h=4),
                                      ss[rows, 4, ti, d * 4:(d + 1) * 4].un(2).bc([64, 4, 64]), ALU.mult)
                                fw.tt(sT[d][:, :].re("p (h e) -> p h e", h=4), sT[d][:, :].re("p (h e) -> p h e", h=4),
                                      decb[:, c, ti * 8 + d * 4:ti * 8 + d * 4 + 4].un(2).bc([128, 4, 64]), ALU.mult)
                                fw.tt(sT[d][:, :], sT[d][:, :], pS[:, 0:256], ALU.add)
                                fw.cp("act", sTb[d][:, :], sT[d][:, :])
                                fw.tt(yac.p(ti)[rows, ti, :], yac.p(ti)[rows, ti, :], yt[rows, :], ALU.add)
                                if last and ctx:
                                    for half in range(2):
                                        p = mmp.next()
                                        fw.tr(p[:, 0:128], sT[d][:, half * 128:(half + 1) * 128], C("ident"))
                                        st_ = sst.next()
                                        evac(st_[:, :], p[:, 0:128])
                                        fw.dma(dv("nss")[seq, j, d, 2 * sg + half, :, :], st_[:, :])
                                yield

                    def sdrive(gens):
                        gens = list(gens)
                        while gens:
                            for g_ in list(gens):
                                try:
                                    next(g_)
                                except StopIteration:
                                    gens.remove(g_)

                    sdrive([lpart(0)])
                    for n in range(8):
                        sdrive([srec(n)] + ([lpart(n + 1)] if n < 7 else []))
                    for ti in range(8):
                        yg = ygb.next()
                        fw.tt(yg[:, :], yac.p(ti)[:, ti, :], zs.p(ti)[:, ti, :], ALU.mult)
                        for c2 in range(2):
                            pT = trp.next()
                            fw.tr(pT[:, 0:128], yg[:, c2 * 128:(c2 + 1) * 128], identb[:, :])
                            c = 2 * sg + c2
                            evac(catT.p(c * 2 + ti // 4)[:, c, ti * 128:(ti + 1) * 128], pT[:, 0:128])
                fw.barrier()
                fw.flush()

        dbg = stop if (stop or "").startswith("dbg") else None
        for layer in range(n_layers if (stop is None or dbg) else 0):
            mod_vectors(layer)
            if dbg == "dbg_mod":
                break
            for g in range(2):
                with ExitStack() as ph:
                    hT = fw.sb("hTg", [128, 8, 1024], BF16, 16, stack=ph)
                    with ExitStack() as ph2:
                        modulate(ph2, "nx", hT, 0, [2 * g, 2 * g + 1], 2 * g, 2)
                        fw.barrier()
                        fw.flush()
                    if dbg == "dbg_norm":
                        continue
                    if layer % 2 == 0:
                        catT = fw.sb("catT", [128, 8, 1024], BF16, 16, stack=ph)
                        even_mixer(ph, layer, g, hT, catT, dbg)
                        if dbg is None or dbg == "dbg_mlp":
                            out_proj_even(layer, g, catT)
                    else:
                        catT = fw.sb("catT", [128, 16, 1024], BF16, 32, stack=ph)
                        odd_mixer(ph, layer, g, hT, catT)
                        out_proj_odd(ph, layer, g, catT)
                    fw.barrier()
                    fw.flush()
            if dbg in (None, "dbg_mlp"):
                mlp(layer)
        if stop is None or dbg:
            final_out()
        elif stop.startswith("final"):
            final_out(int(stop[5:]))
        fw.barrier(final=True)
        fw.flush()
    return nc, fw.n_instr


_PROG = {}


def _run(inp, n_layers=DEPTH, ncores=8):
    if n_layers not in _PROG:
        _PROG[n_layers] = build(n_layers)
    nc, _ = _PROG[n_layers]
    f = lambda k: np.ascontiguousarray(np.asarray(inp[k], dtype=np.float32))
    consts = make_consts()
    rope = np.ascontiguousarray(make_rope().reshape(64, 2048))
    vecs = make_vecs(inp)
    wie = permute_cols(f("w_in_even"), even_col_perm())
    wio = permute_cols(f("w_in_odd"), odd_col_perm())
    shared = {
        "consts": consts, "rope": rope, "vecs": vecs,
        "w_mod": f("w_mod"), "w_mlp_in": f("w_mlp_in"), "w_mlp_out": f("w_mlp_out"),
        "wie": wie, "w_out_even": f("w_out_even"), "wio": wio, "w_out_odd": f("w_out_odd"),
    }
    xp, xs, c, cctx = f("x_prompt"), f("x_sample"), f("c"), f("c_ctx")
    ck, cv, sd, ssm = f("cache_attn_k"), f("cache_attn_v"), f("state_delta"), f("state_ssm")
    in_maps = []
    for i in range(ncores):
        b = i % 4
        cond = np.stack([cctx, c[b]], axis=0)
        condT = np.ascontiguousarray(cond.reshape(2, 8, 128).transpose(2, 1, 0)).reshape(128, 16)
        m = dict(shared)
        m["xin"] = np.ascontiguousarray(np.concatenate([xp[4 * i:4 * i + 4].reshape(1024, D), xs[b]], axis=0))
        m["condT"] = condT
        m["ctxk"] = np.ascontiguousarray(ck[b].reshape(2, 512, 128))
        m["ctxv"] = np.ascontiguousarray(cv[b].reshape(2, 512, 128))
        m["sd0"] = np.ascontiguousarray(sd[b])
        m["ss0"] = np.ascontiguousarray(ssm[b].reshape(2, 2, 16, 128, 128))
        in_maps.append(m)
    res = run_bass_kernel_spmd(nc, in_maps, core_ids=list(range(ncores)))
    R = res.results
    if ncores < 8:
        return R
    y_p = np.concatenate([R[i]["y"][:1024].reshape(4, 256, D) for i in range(8)], axis=0)
    y_s = np.stack([R[b]["y"][1024:] for b in range(4)], axis=0)
    nk = np.concatenate([R[i]["nk"].reshape(2, 4, 256, 2, 64).transpose(1, 0, 2, 3, 4) for i in range(8)], axis=0)
    nv = np.concatenate([R[i]["nv"].reshape(2, 4, 256, 2, 64).transpose(1, 0, 2, 3, 4) for i in range(8)], axis=0)
    nsd = np.concatenate([R[i]["nsd"] for i in range(8)], axis=0)
    nss = np.concatenate([R[i]["nss"].reshape(4, 2, 2, 32, 64, 128) for i in range(8)], axis=0)
    out = (y_p, y_s, nk, nv, nsd, nss)
    return tuple(np.ascontiguousarray(o, dtype=np.float32) for o in out)


def kernel(**inputs):
    return _run(inputs, DEPTH)
```

```python
import math
import numpy as np
from contextlib import ExitStack
import concourse.bass as bass
import concourse.mybir as mybir
from concourse.bass_utils import run_bass_kernel_spmd

F32 = mybir.dt.float32
BF16 = mybir.dt.bfloat16
AF = mybir.ActivationFunctionType
ALU = mybir.AluOpType

D = 1024
DEPTH = 4
EPS = 1e-6
NEG = -30000.0


class Dep:
    __slots__ = ("w", "r", "excl")

    def __init__(self):
        self.w = None
        self.r = {}
        self.excl = False


class V:
    __slots__ = ("ap", "deps")

    def __init__(self, ap, deps):
        self.ap = ap
        self.deps = deps

    def __getitem__(self, k):
        return V(self.ap[k], self.deps)

    def re(self, pat, **kw):
        return V(self.ap.rearrange(pat, **kw), self.deps)

    def bc(self, shape):
        return V(self.ap.to_broadcast(list(shape)), self.deps)

    def un(self, axis):
        return V(self.ap.unsqueeze(axis), self.deps)


class _P:
    __slots__ = ("t", "deps")

    def __init__(self, t, deps):
        self.t = t
        self.deps = deps

    def __getitem__(self, k):
        return V(self.t[k], self.deps)


class T:
    def __init__(self, t, nparts=1):
        self.t = t
        self.deps = [Dep() for _ in range(nparts)]

    def __getitem__(self, k):
        return V(self.t[k], self.deps)

    def p(self, *idx):
        return _P(self.t, [self.deps[i] for i in idx])


class Rot:
    def __init__(self, items):
        self.items = items
        self.i = 0

    def next(self):
        x = self.items[self.i]
        self.i = (self.i + 1) % len(self.items)
        return x


def _ap(x):
    return x.ap if isinstance(x, V) else x


def _deps(*xs):
    out = []
    for x in xs:
        if isinstance(x, V):
            out.extend(x.deps)
    return out


class FW:
    def __init__(self, nc, stack, n_io=16, n_w=4):
        self.nc = nc
        self.stack = stack
        self.engs = {"pe": nc.tensor, "act": nc.scalar, "dve": nc.vector, "pool": nc.gpsimd, "sp": nc.sync}
        self.sems = {}
        self.cnt = {}
        for k in ["pe", "act", "dve"]:
            self.sems[k] = stack.enter_context(nc.semaphore("s_" + k))
            self.cnt[k] = 0
        self.io_ch = []
        for i in range(n_io):
            k = "io%d" % i
            self.sems[k] = stack.enter_context(nc.semaphore("s_" + k))
            self.cnt[k] = 0
            self.io_ch.append(k)
        self.w_ch = []
        for i in range(n_w):
            k = "w%d" % i
            self.sems[k] = stack.enter_context(nc.semaphore("s_" + k))
            self.cnt[k] = 0
            self.w_ch.append(k)
        self.nio = 0
        self.nw = 0
        self.waited = {k: {} for k in self.engs}
        self.n_instr = 0
        self.prog = {k: [] for k in self.engs}

    def flush(self):
        prog = self.prog
        self.prog = {k: [] for k in self.engs}
        with self.nc.Block() as block:
            def mk(lst):
                def body(e):
                    for f in lst:
                        f(e)
                return body
            block.sync(mk(prog["sp"]))
            block.tensor(mk(prog["pe"]))
            block.scalar(mk(prog["act"]))
            block.vector(mk(prog["dve"]))
            block.gpsimd(mk(prog["pool"]))

    def sb(self, name, shape, dtype, nparts=1, stack=None):
        st = stack or self.stack
        self.uid = getattr(self, "uid", 0) + 1
        return T(st.enter_context(self.nc.sbuf_tensor("%s_%d" % (name, self.uid), list(shape), dtype)), nparts)

    def ps(self, name, shape, dtype, nparts=1):
        t = T(self.stack.enter_context(self.nc.psum_tensor(name, list(shape), dtype)), nparts)
        for d in t.deps:
            d.excl = True
        return t

    def _wait(self, issuer, key, val):
        if val <= 0:
            return
        w = self.waited[issuer]
        if w.get(key, 0) >= val:
            return
        w[key] = val
        sem = self.sems[key]
        self.prog[issuer].append(lambda e: e.wait_ge(sem, val))

    def _gather(self, reads, writes):
        need = {}
        for d in reads:
            if d.w is not None:
                k, c = d.w
                if need.get(k, 0) < c:
                    need[k] = c
            if d.excl:
                for k, c in d.r.items():
                    if need.get(k, 0) < c:
                        need[k] = c
        for d in writes:
            if d.w is not None:
                k, c = d.w
                if need.get(k, 0) < c:
                    need[k] = c
            for k, c in d.r.items():
                if need.get(k, 0) < c:
                    need[k] = c
        return need

    def _commit(self, key, val, reads, writes):
        for d in writes:
            d.w = (key, val)
            d.r = {}
        for d in reads:
            if d.r.get(key, 0) < val:
                d.r[key] = val

    def op(self, eng, fn, reads=(), writes=()):
        need = self._gather(reads, writes)
        for k, c in need.items():
            if k == eng and eng == "pe":
                continue
            self._wait(eng, k, c)
        sem = self.sems[eng]
        self.prog[eng].append(lambda e: fn(e).then_inc(sem, 1))
        self.cnt[eng] += 1
        self._commit(eng, self.cnt[eng], reads, writes)
        self.n_instr += 1

    def dma(self, out, in_, q="sp"):
        reads = _deps(in_)
        writes = _deps(out)
        if q == "pool":
            ch = self.w_ch[self.nw % len(self.w_ch)]
            self.nw += 1
        else:
            ch = self.io_ch[self.nio % len(self.io_ch)]
            self.nio += 1
        need = self._gather(reads, writes)
        need[ch] = max(need.get(ch, 0), self.cnt[ch])
        for k, c in need.items():
            self._wait(q, k, c)
        sem = self.sems[ch]
        o, i = _ap(out), _ap(in_)
        self.prog[q].append(lambda e: e.dma_start(out=o, in_=i).then_inc(sem, 16))
        self.cnt[ch] += 16
        self._commit(ch, self.cnt[ch], reads, writes)
        self.n_instr += 1

    def barrier(self, final=False):
        keys = ["pe", "act", "dve"] + self.io_ch + (self.w_ch if final else [])
        for issuer in ["sp", "pe", "act", "dve"] + (["pool"] if final else []):
            for k in keys:
                if k != issuer:
                    self._wait(issuer, k, self.cnt[k])

    def mm(self, out, lhsT, rhs, start=True, stop=True):
        o, l, r = _ap(out), _ap(lhsT), _ap(rhs)
        self.op("pe", lambda e: e.matmul(o, lhsT=l, rhs=r, start=start, stop=stop),
                _deps(lhsT, rhs), _deps(out))

    def tr(self, out, in_, ident):
        o, i, d = _ap(out), _ap(in_), _ap(ident)
        self.op("pe", lambda e: e.transpose(o, i, d), _deps(in_, ident), _deps(out))

    def act(self, out, in_, func, bias=None, scale=None, accum=None):
        o, i = _ap(out), _ap(in_)
        kw = {}
        if bias is not None:
            kw["bias"] = _ap(bias)
        if scale is not None:
            kw["scale"] = _ap(scale)
        if accum is not None:
            kw["accum_out"] = _ap(accum)
        self.op("act", lambda e: e.activation(out=o, in_=i, func=func, **kw),
                _deps(in_, bias, scale), _deps(out, accum))

    def tt(self, out, in0, in1, op, eng="dve"):
        o, a, b = _ap(out), _ap(in0), _ap(in1)
        self.op(eng, lambda e: e.tensor_tensor(out=o, in0=a, in1=b, op=op), _deps(in0, in1), _deps(out))

    def ts(self, out, in0, s1, op0, s2=None, op1=None, eng="dve"):
        o, a, x1, x2 = _ap(out), _ap(in0), _ap(s1), _ap(s2)
        if op1 is None:
            self.op(eng, lambda e: e.tensor_scalar(out=o, in0=a, scalar1=x1, scalar2=None, op0=op0),
                    _deps(in0, s1), _deps(out))
        else:
            self.op(eng, lambda e: e.tensor_scalar(out=o, in0=a, scalar1=x1, scalar2=x2, op0=op0, op1=op1),
                    _deps(in0, s1, s2), _deps(out))

    def stt(self, out, in0, scalar, in1, op0, op1):
        o, a, s, b = _ap(out), _ap(in0), _ap(scalar), _ap(in1)
        self.op("dve", lambda e: e.scalar_tensor_tensor(out=o, in0=a, scalar=s, in1=b, op0=op0, op1=op1),
                _deps(in0, scalar, in1), _deps(out))

    def cp(self, eng, out, in_):
        o, i = _ap(out), _ap(in_)
        if eng == "act":
            self.op("act", lambda e: e.activation(out=o, in_=i, func=AF.Copy), _deps(in_), _deps(out))
        else:
            self.op("dve", lambda e: e.tensor_copy(out=o, in_=i), _deps(in_), _deps(out))

    def recip(self, out, in_):
        o, i = _ap(out), _ap(in_)
        self.op("dve", lambda e: e.reciprocal(out=o, in_=i), _deps(in_), _deps(out))

    def memset(self, out, val):
        o = _ap(out)
        self.op("dve", lambda e: e.memset(o, val), (), _deps(out))


VEC_LAYOUT = [("gmix", 32), ("gmlp", 32), ("bmod", 192), ("fing", 8), ("qkg", 4), ("dconv", 72),
              ("dalog", 16), ("ddtb", 16), ("dng", 2), ("sconv", 256), ("salog", 128), ("sdtb", 128),
              ("sd", 64), ("sng", 32)]
VOFF = {}
_o = 0
for _n, _c in VEC_LAYOUT:
    VOFF[_n] = (_o, _c)
    _o += _c
NVEC = _o

CST_LAYOUT = [("ident", 128), ("ones", 128), ("blk", 128), ("triF", 128), ("triB", 128), ("ind0", 128),
              ("ind1", 128), ("nmF", 256), ("nmB", 256), ("nmFc", 128), ("nmBc", 128), ("ropeR", 64)]
COFF = {}
_o = 0
for _n, _c in CST_LAYOUT:
    COFF[_n] = (_o, _c)
    _o += _c
NCST = _o


def make_consts():
    c = np.zeros((128, NCST), np.float32)
    idx = np.arange(128)
    j = idx[:, None]
    i = idx[None, :]
    same = (j // 64) == (i // 64)

    def put(name, a):
        o, n = COFF[name]
        c[:a.shape[0], o:o + a.shape[1]] = a

    put("ident", np.eye(128, dtype=np.float32))
    put("ones", np.ones((128, 128), np.float32))
    put("blk", same.astype(np.float32))
    put("triF", (same & (j <= i)).astype(np.float32))
    put("triB", (same & (j >= i)).astype(np.float32))
    put("ind0", np.broadcast_to((idx < 64)[:, None], (128, 128)).astype(np.float32))
    put("ind1", np.broadcast_to((idx >= 64)[:, None], (128, 128)).astype(np.float32))
    fc = np.where(same & (i >= j), 0.0, NEG).astype(np.float32)
    fs = np.where(same & (i > j), 0.0, NEG).astype(np.float32)
    bc = np.where(same & (i <= j), 0.0, NEG).astype(np.float32)
    bs = np.where(same & (i < j), 0.0, NEG).astype(np.float32)
    put("nmF", np.concatenate([fc, fs], axis=1))
    put("nmB", np.concatenate([bc, bs], axis=1))
    put("nmFc", fc)
    put("nmBc", bc)
    R = np.zeros((64, 64), np.float32)
    for base in (0, 32):
        for t in range(16):
            R[base + 16 + t, base + t] = -1.0
            R[base + t, base + 16 + t] = 1.0
    put("ropeR", R)
    return c


def make_rope():
    t = np.arange(1024)
    row = (t // 64).astype(np.float32)
    col = (t % 64).astype(np.float32)
    inv = (10000.0 ** (-np.arange(0, 32, 2, dtype=np.float32) / 32)).astype(np.float32)
    ar = row[None, :] * inv[:, None]
    ac = col[None, :] * inv[:, None]
    ang = np.concatenate([ar, ar, ac, ac], axis=0)
    return np.stack([np.cos(ang), np.sin(ang)], axis=1).astype(np.float32)


def even_col_perm():
    cols = []
    for g2 in range(2):
        for hh in range(4):
            h = 4 * g2 + hh
            cols += list(range(h * 64, h * 64 + 64))
        cols += list(range(512 + g2 * 64, 512 + g2 * 64 + 64))
        cols += list(range(640 + g2 * 64, 640 + g2 * 64 + 64))
    for hd in range(4):
        cols += list(range(768 + hd * 128, 768 + hd * 128 + 128))
        cols += list(range(768 + 512 + hd * 128, 768 + 512 + hd * 128 + 128))
        cols += list(range(768 + 1024 + hd * 128, 768 + 1024 + hd * 128 + 128))
        cols += list(range(2304 + hd * 128, 2304 + hd * 128 + 128))
        cols += [2816 + hd, 2816 + 4 + hd, 2824 + hd, 2824 + 4 + hd]
        cols += [-1] * 124
    return np.array(cols)


def odd_col_perm():
    cols = []
    for sg in range(8):
        cols += list(range(2048 + sg * 256, 2048 + sg * 256 + 256))
        cols += list(range(4096 + sg * 128, 4096 + sg * 128 + 128))
        cols += list(range(5120 + sg * 128, 5120 + sg * 128 + 128))
        cols += list(range(sg * 256, sg * 256 + 256))
        cols += list(range(6144 + sg * 4, 6144 + sg * 4 + 4))
        cols += list(range(6144 + 32 + sg * 4, 6144 + 32 + sg * 4 + 4))
        cols += [-1] * 56
    return np.array(cols)


def permute_cols(w, perm):
    w = np.asarray(w, np.float32)
    out = np.zeros(w.shape[:-1] + (len(perm),), np.float32)
    m = perm >= 0
    out[..., m] = w[..., perm[m]]
    return out


def colmajor(v):
    v = np.asarray(v, np.float32)
    lead = v.shape[:-1]
    C = v.shape[-1] // 128
    a = v.reshape(lead + (C, 128))
    a = np.moveaxis(a, -1, 0)
    return np.ascontiguousarray(a).reshape(128, -1)


def bcast_rows(v):
    v = np.asarray(v, np.float32).reshape(1, -1)
    return np.ascontiguousarray(np.broadcast_to(v, (128, v.shape[1])))


def make_vecs(inp):
    vec = np.zeros((128, NVEC), np.float32)

    def put(name, a):
        o, n = VOFF[name]
        assert a.shape[1] == n, (name, a.shape, n)
        vec[:a.shape[0], o:o + n] = a

    put("gmix", colmajor(inp["norm_mix_g"]))
    put("gmlp", colmajor(inp["norm_mlp_g"]))
    put("bmod", colmajor(inp["b_mod"]))
    put("fing", colmajor(inp["final_norm_g"]))
    qk = np.stack([inp["attn_q_norm_g"], inp["attn_k_norm_g"]], axis=1)
    put("qkg", np.ascontiguousarray(np.moveaxis(qk, -1, 0)).reshape(64, 4))
    dc = np.asarray(inp["delta_conv_w"], np.float32).reshape(2, 3, 3, 4, 128)
    dc = np.transpose(dc, (4, 0, 3, 2, 1))
    put("dconv", np.ascontiguousarray(dc).reshape(128, 72))
    put("dalog", bcast_rows(inp["delta_a_log"]))
    put("ddtb", bcast_rows(inp["delta_dt_bias"]))
    put("dng", np.ascontiguousarray(np.asarray(inp["delta_norm_g"], np.float32).T))
    cw = np.asarray(inp["ssm_conv_w"], np.float32)
    cb = np.asarray(inp["ssm_conv_b"], np.float32)
    wb = np.concatenate([cw, cb[:, None, :]], axis=1)
    sc = np.zeros((128, 2, 8, 4, 4), np.float32)
    for sg in range(8):
        chans = [(sg * 256, 128), (sg * 256 + 128, 128), (2048 + sg * 128, 128), (3072 + sg * 128, 128)]
        for ci, (c0, n) in enumerate(chans):
            sc[:, :, sg, ci, :] = np.transpose(wb[:, :, c0:c0 + 128], (2, 0, 1))
    put("sconv", sc.reshape(128, 256))
    put("salog", bcast_rows(inp["ssm_a_log"]))
    put("sdtb", bcast_rows(inp["ssm_dt_bias"]))
    put("sd", bcast_rows(inp["ssm_d"]))
    put("sng", colmajor(inp["ssm_norm_g"]))
    return vec


import os as _os0
SILU = AF.Identity if "silu" in _os0.environ.get("DBGSKIP", "") else AF.Silu


def build(n_layers=DEPTH, stop=None):
    nc = bass.Bass("TRN2", target_bir_lowering=False)
    dr = {}

    def din(name, shape):
        dr[name] = nc.dram_tensor(name, list(shape), F32, kind="ExternalInput").ap()

    def dout(name, shape):
        dr[name] = nc.dram_tensor(name, list(shape), F32, kind="ExternalOutput").ap()

    din("xin", [2048, D])
    din("condT", [128, 16])
    din("consts", [128, NCST])
    din("rope", [64, 2048])
    din("vecs", [128, NVEC])
    din("ctxk", [2, 512, 128])
    din("ctxv", [2, 512, 128])
    din("sd0", [2, 2, 4, 128, 128])
    din("ss0", [2, 2, 16, 128, 128])
    din("w_mod", [DEPTH, D, 6 * D])
    din("w_mlp_in", [DEPTH, D, 4 * D])
    din("w_mlp_out", [DEPTH, 4 * D, D])
    din("wie", [2, D, 3328])
    din("w_out_even", [2, D, D])
    din("wio", [2, D, 6656])
    din("w_out_odd", [2, 2 * D, D])
    dout("y", [2048, D])
    dout("nk", [2, 1024, 128])
    dout("nv", [2, 1024, 128])
    dout("nsd", [4, 2, 2, 4, 128, 128])
    dout("nss", [4, 2, 2, 16, 128, 128])

    with ExitStack() as st:
        fw = FW(nc, st)
        dv = lambda name: V(dr[name], [])

        import os as _os
        _pad = int(_os.environ.get("DBGPAD", "0"))
        if _pad:
            fw.sb("dbgpad", [128, _pad * 256], F32)
        xT = fw.sb("xT", [128, 8, 2048], F32, 32)
        cst = fw.sb("cst", [128, NCST], F32)
        vec = fw.sb("vec", [128, NVEC], F32)
        identb = fw.sb("identb", [128, 128], BF16)
        onesb = fw.sb("onesb", [128, 128], BF16)
        scT = fw.sb("scT", [128, 8, 2], BF16)
        modv2 = [fw.sb("modv%d" % i, [128, 48, 2], F32) for i in range(2)]
        modA2 = [fw.sb("modA%d" % i, [128, 2, 8, 2], F32) for i in range(2)]
        modv, modA = modv2[0], modA2[0]
        pump_holder = [None]

        def pump(k=1):
            g_ = pump_holder[0]
            if g_ is None:
                return
            for _ in range(k):
                try:
                    next(g_)
                except StopIteration:
                    pump_holder[0] = None
                    return
        ring = Rot([fw.sb("wr%d" % i, [128, 4096], BF16, 2) for i in range(3)])
        mmp = Rot([fw.ps("pm%d" % i, [128, 512], F32) for i in range(4)])
        accp = Rot([fw.ps("pa%d" % i, [128, 512], F32) for i in range(2)])
        trp = Rot([fw.ps("pt%d" % i, [128, 1024], BF16) for i in range(2)])

        def C(name, rows=128, c0=0, c1=None):
            o, n = COFF[name]
            c1 = n if c1 is None else c1
            return cst[0:rows, o + c0:o + c1]

        def VC(name, col, rows=128, n=1):
            o, _ = VOFF[name]
            return vec[0:rows, o + col:o + col + n]

        def xv(kc, tt):
            return xT.p(kc * 4 + tt)[:, kc, tt * 512:(tt + 1) * 512]

        def wload(src, KC, cols):
            t = ring.next()
            h = KC // 2
            s3 = src.rearrange("(kc p) c -> p kc c", p=128)
            for a in range(2):
                dst = t.p(a)[:, a * h * cols:(a + 1) * h * cols].re("p (kc c) -> p kc c", kc=h)
                fw.dma(dst, V(s3[:, a * h:(a + 1) * h, :], []), q="pool")
            return t[:, 0:KC * cols].re("p (kc c) -> p kc c", kc=KC)

        evac_i = [0]

        def evac(out, in_):
            evac_i[0] ^= 1
            fw.cp("act" if evac_i[0] else "dve", out, in_)

        fw.dma(cst[:, :], dv("consts"))
        fw.dma(vec[:, :], dv("vecs"))
        fw.cp("act", identb[:, :], C("ident"))
        fw.cp("dve", onesb[:, :], C("ones"))
        with ExitStack() as ph:
            ctmp = fw.sb("ctmp", [128, 16], F32, stack=ph)
            fw.dma(ctmp[:, :], dv("condT"))
            fw.act(scT[:, :, :].re("p a b -> p (a b)"), ctmp[:, :], SILU)
            xst = Rot([fw.sb("xst%d" % i, [128, D], F32, stack=ph) for i in range(2)])
            for ti in range(16):
                s = xst.next()
                fw.dma(s[:, :], dv("xin")[ti * 128:(ti + 1) * 128, :])
                for half in range(2):
                    ps = mmp.next()
                    for q in range(4):
                        kc = half * 4 + q
                        fw.tr(ps[:, q * 128:(q + 1) * 128], s[:, kc * 128:(kc + 1) * 128], C("ident"))
                    tt = ti // 4
                    dst = xT.p(*[(half * 4 + q) * 4 + tt for q in range(4)])[
                        :, half * 4:half * 4 + 4, ti * 128:(ti + 1) * 128]
                    evac(dst, ps[:, :].re("p (a b) -> p a b", a=4))
            fw.barrier()
            fw.flush()

        def mod_vectors(layer):
            modv, modA = modv2[layer % 2], modA2[layer % 2]
            for ot in range(12):
                wt = wload(dr["w_mod"][layer, :, ot * 512:(ot + 1) * 512], 8, 512)
                ps = mmp.next()
                for o4 in range(4):
                    for kc in range(8):
                        fw.mm(ps[:, o4 * 2:o4 * 2 + 2], wt[:, kc, o4 * 128:(o4 + 1) * 128], scT[:, kc, :],
                              start=(kc == 0), stop=(kc == 7))
                fw.tt(modv[:, ot * 4:(ot + 1) * 4, :], ps[:, 0:8].re("p (a b) -> p a b", a=4),
                      VC("bmod", layer * 48 + ot * 4, n=4).un(2).bc([128, 4, 2]), ALU.add)
                yield
            for which, (gname, sc0) in enumerate((("gmix", 8), ("gmlp", 32))):
                fw.ts(modA[:, which, :, :], modv[:, sc0:sc0 + 8, :], 1.0, ALU.add)
                fw.tt(modA[:, which, :, :], modA[:, which, :, :],
                      VC(gname, layer * 8, n=8).un(2).bc([128, 8, 2]), ALU.mult)

        def rstd_bc(ph, name, nfeat):
            sqp = Rot([fw.sb("%s_sq%d" % (name, i), [128, 512], BF16, stack=ph) for i in range(3)])
            rsp = Rot([fw.sb("%s_rs%d" % (name, i), [128, 512], F32, stack=ph) for i in range(2)])

            def f(srcs):
                ps = mmp.next()
                n = len(srcs)
                for i, s in enumerate(srcs):
                    sq = sqp.next()
                    fw.act(sq[:, :], s, AF.Square)
                    fw.mm(ps[:, :], onesb[:, :], sq[:, :], start=(i == 0), stop=(i == n - 1))
                rs = rsp.next()
                fw.act(rs[:, :], ps[:, :], AF.Sqrt, bias=EPS, scale=1.0 / nfeat)
                fw.recip(rs[:, :], rs[:, :])
                return rs
            return f

        def modulate(ph, name, hT, which, tts, t0, ntt):
            rfn = rstd_bc(ph, name, D)
            tmpp = Rot([fw.sb("%s_tmp%d" % (name, i), [128, 512], F32, stack=ph) for i in range(3)])
            sh0 = 0 if which == 0 else 24
            for tt in tts:
                r = 0 if tt < 2 else 1
                rs = rfn([xv(kc, tt) for kc in range(8)])
                for kc in range(8):
                    tmp = tmpp.next()
                    fw.tt(tmp[:, :], xv(kc, tt), rs[:, :], ALU.mult)
                    lt = tt - t0
                    fw.act(hT.p(kc * ntt + lt)[:, kc, lt * 512:(lt + 1) * 512], tmp[:, :], AF.Identity,
                           bias=modv[:, sh0 + kc, r:r + 1], scale=modA[:, which, kc, r:r + 1])

        def mlp(layer):
            with ExitStack() as ph:
                hT = fw.sb("hTm", [128, 8, 2048], BF16, 32, stack=ph)
                h1 = fw.sb("h1", [128, 8, 2048], BF16, 32, stack=ph)
                rl = Rot([fw.sb("rl%d" % i, [128, 512], F32, stack=ph) for i in range(3)])
                modulate(ph, "nm", hT, 1, range(4), 0, 4)
                for blk in range(4):
                    for ht in range(2):
                        c0 = blk * 1024 + ht * 512
                        wt = wload(dr["w_mlp_in"][layer, :, c0:c0 + 512], 8, 512)
                        for h4 in range(4):
                            hc = ht * 4 + h4
                            for tt in range(4):
                                ps = mmp.next()
                                for kc in range(8):
                                    fw.mm(ps[:, :], wt[:, kc, h4 * 128:(h4 + 1) * 128],
                                          hT.p(kc * 4 + tt)[:, kc, tt * 512:(tt + 1) * 512],
                                          start=(kc == 0), stop=(kc == 7))
                                r = rl.next()
                                fw.act(r[:, :], ps[:, :], AF.Relu)
                                fw.tt(h1.p(hc * 4 + tt)[:, hc, tt * 512:(tt + 1) * 512], r[:, :], r[:, :], ALU.mult)
                    for ot in range(2):
                        wt = wload(dr["w_mlp_out"][layer, blk * 1024:(blk + 1) * 1024, ot * 512:(ot + 1) * 512], 8, 512)
                        for o4 in range(4):
                            oc = ot * 4 + o4
                            for tt in range(4):
                                r = 0 if tt < 2 else 1
                                ps = mmp.next()
                                for kc in range(8):
                                    fw.mm(ps[:, :], wt[:, kc, o4 * 128:(o4 + 1) * 128],
                                          h1.p(kc * 4 + tt)[:, kc, tt * 512:(tt + 1) * 512],
                                          start=(kc == 0), stop=(kc == 7))
                                fw.stt(xv(oc, tt), ps[:, :], modv[:, 40 + oc, r:r + 1], xv(oc, tt), ALU.mult, ALU.add)
                fw.barrier()
                fw.flush()

        def final_out(lvl=9):
            with ExitStack() as ph:
                rfn = rstd_bc(ph, "fn", D)
                yT = fw.sb("yT", [128, 8, 512], F32, stack=ph)
                ost = Rot([fw.sb("ost%d" % i, [128, D], F32, stack=ph) for i in range(2)])
                for tt in range(4):
                    rs = rfn([xv(kc, tt) for kc in range(8)])
                    if lvl < 2:
                        continue
                    for kc in range(8):
                        fw.stt(yT[:, kc, :], xv(kc, tt), VC("fing", kc), rs[:, :], ALU.mult, ALU.mult)
                    if lvl < 3:
                        continue
                    for q in range(4):
                        ti = tt * 4 + q
                        o = ost.next()
                        for half in range(2):
                            ps = mmp.next()
                            for a in range(4):
                                kc = half * 4 + a
                                fw.tr(ps[:, a * 128:(a + 1) * 128], yT[:, kc, q * 128:(q + 1) * 128], C("ident"))
                            evac(o[:, half * 512:(half + 1) * 512], ps[:, :])
                        if lvl >= 4:
                            fw.dma(dv("y")[ti * 128:(ti + 1) * 128, :], o[:, :])
                fw.barrier()
                fw.flush()

        def out_proj_even(layer, g, catT):
            j = layer // 2
            for ot in range(2):
                wt = wload(dr["w_out_even"][j, :, ot * 512:(ot + 1) * 512], 8, 512)
                for o4 in range(4):
                    oc = ot * 4 + o4
                    for tg in range(2):
                        tt = 2 * g + tg
                        ps = mmp.next()
                        for kc in range(8):
                            fw.mm(ps[:, :], wt[:, kc, o4 * 128:(o4 + 1) * 128],
                                  catT.p(kc * 2 + tg)[:, kc, tg * 512:(tg + 1) * 512], start=(kc == 0), stop=(kc == 7))
                        fw.stt(xv(oc, tt), ps[:, :], modv[:, 16 + oc, g:g + 1], xv(oc, tt), ALU.mult, ALU.add)

        def out_proj_odd(ph, layer, g, catT):
            j = layer // 2
            rfn = rstd_bc(ph, "on", 2 * D)
            rsk = fw.sb("rsk", [128, 1024], F32, stack=ph)
            tmpp = Rot([fw.sb("opt%d" % i, [128, 512], F32, stack=ph) for i in range(2)])
            for tg in range(2):
                rs = rfn([catT.p(c * 2 + tg)[:, c, tg * 512:(tg + 1) * 512] for c in range(16)])
                fw.cp("dve", rsk[:, tg * 512:(tg + 1) * 512], rs[:, :])
            for ot in range(4):
                wt = wload(dr["w_out_odd"][j, :, ot * 256:(ot + 1) * 256], 16, 256)
                fw.tt(wt, wt, VC("sng", j * 16, n=16).un(2).bc([128, 16, 256]), ALU.mult)
                for o2 in range(2):
                    oc = ot * 2 + o2
                    for tg in range(2):
                        tt = 2 * g + tg
                        ps = mmp.next()
                        for kc in range(16):
                            fw.mm(ps[:, :], wt[:, kc, o2 * 128:(o2 + 1) * 128],
                                  catT.p(kc * 2 + tg)[:, kc, tg * 512:(tg + 1) * 512], start=(kc == 0), stop=(kc == 15))
                        tmp = tmpp.next()
                        fw.tt(tmp[:, :], ps[:, :], rsk[:, tg * 512:(tg + 1) * 512], ALU.mult)
                        fw.stt(xv(oc, tt), tmp[:, :], modv[:, 16 + oc, g:g + 1], xv(oc, tt), ALU.mult, ALU.add)
        def rsum(out, in_):
            o, i = _ap(out), _ap(in_)
            fw.op("dve", lambda e: e.reduce_sum(out=o, in_=i, axis=mybir.AxisListType.X), _deps(in_), _deps(out))

        def even_mixer(ph, layer, g, hT, catT, dbg=None):
            j = layer // 2
            ctx = (g == 0)
            nseq, L = (4, 256) if ctx else (1, 1024)
            tps = L // 128

            def hv(kc, lo, n):
                return hT.p(kc * 2 + lo // 512)[:, kc, lo:lo + n]

            with ExitStack() as ua:
                qT = fw.sb("qT", [64, 4, 1024], BF16, 4, stack=ua)
                kT = fw.sb("kT", [64, 1536], BF16, stack=ua)
                V1 = fw.sb("V1", [128, 12, 65], BF16, stack=ua)
                otok = fw.sb("otok", [128, 8, 256], BF16, 8, stack=ua)
                raw = Rot([fw.sb("araw%d" % i, [64, 1024], F32, stack=ua) for i in range(2)])
                qn = Rot([fw.sb("aqn%d" % i, [64, 512], F32, stack=ua) for i in range(2)])
                sqb = Rot([fw.sb("asq%d" % i, [64, 512], BF16, stack=ua) for i in range(2)])
                rsb = Rot([fw.sb("ars%d" % i, [64, 512], F32, stack=ua) for i in range(2)])
                t1p = Rot([fw.sb("at1%d" % i, [64, 512], F32, stack=ua) for i in range(2)])
                t2p = Rot([fw.sb("at2%d" % i, [64, 512], F32, stack=ua) for i in range(2)])
                ptp = Rot([fw.sb("apt%d" % i, [128, 512], BF16, stack=ua) for i in range(3)])
                rdp = Rot([fw.sb("ard%d" % i, [128, 1], F32, stack=ua) for i in range(4)])
                vst = Rot([fw.sb("avs%d" % i, [128, 64], F32, stack=ua) for i in range(2)])
                kst = Rot([fw.sb("aks%d" % i, [128, 64], F32, stack=ua) for i in range(2)])
                cks = Rot([fw.sb("ack%d" % i, [128, 128], F32, stack=ua) for i in range(2)])
                if not ctx:
                    rope = fw.sb("ropeT", [64, 2048], F32, stack=ua)
                    fw.dma(rope[:, :], dv("rope"))
                fw.memset(V1[:, :, 64:65], 1.0)
                for g2 in range(2):
                    wt = wload(dr["wie"][j, :, g2 * 384:(g2 + 1) * 384], 8, 384)

                    def proj_norm(col0, gain_col, dst_fn, is_k):
                        r = raw.next()
                        for tg in range(2):
                            ps = mmp.next()
                            for kc in range(8):
                                fw.mm(ps[0:64, :], wt[:, kc, col0:col0 + 64], hv(kc, tg * 512, 512),
                                      start=(kc == 0), stop=(kc == 7))
                            evac(r[:, tg * 512:(tg + 1) * 512], ps[0:64, :])
                        for tg in range(2):
                            sl = slice(tg * 512, (tg + 1) * 512)
                            sq = sqb.next()
                            fw.act(sq[:, :], r[:, sl], AF.Square)
                            ps = mmp.next()
                            fw.mm(ps[0:64, :], onesb[0:64, 0:64], sq[:, :])
                            rs = rsb.next()
                            fw.act(rs[:, :], ps[0:64, :], AF.Sqrt, bias=EPS, scale=1.0 / 64)
                            fw.recip(rs[:, :], rs[:, :])
                            if ctx and not is_k:
                                fw.stt(dst_fn(sl), r[:, sl], gain_col, rs[:, :], ALU.mult, ALU.mult)
                                continue
                            q = qn.next()
                            fw.stt(q[:, :], r[:, sl], gain_col, rs[:, :], ALU.mult, ALU.mult)
                            if ctx:
                                fw.cp("act", dst_fn(sl), q[:, :])
                                for t4 in range(4):
                                    ti = tg * 4 + t4
                                    ps2 = mmp.next()
                                    fw.tr(ps2[:, 0:64], q[:, t4 * 128:(t4 + 1) * 128], C("ident", 64, 0, 64))
                                    ks = kst.next()
                                    evac(ks[:, :], ps2[:, 0:64])
                                    fw.dma(dv("nk")[j, ti * 128:(ti + 1) * 128, g2 * 64:(g2 + 1) * 64], ks[:, :])
                            else:
                                ps2 = mmp.next()
                                fw.mm(ps2[0:64, :], C("ropeR", 64), q[:, :])
                                t1 = t1p.next()
                                fw.tt(t1[:, :], q[:, :], rope[:, sl], ALU.mult)
                                t2 = t2p.next()
                                fw.tt(t2[:, :], ps2[0:64, :], rope[:, 1024 + tg * 512:1024 + (tg + 1) * 512], ALU.mult)
                                fw.tt(dst_fn(sl), t1[:, :], t2[:, :], ALU.add)

                    for hh in range(4):
                        proj_norm(hh * 64, VC("qkg", j * 2 + 0, rows=64),
                                  (lambda sl, hh=hh: qT.p(hh)[:, hh, sl]), False)
                    proj_norm(256, VC("qkg", j * 2 + 1, rows=64), (lambda sl: kT[:, sl]), True)
                    for ti in range(8):
                        ps = mmp.next()
                        for kc in range(8):
                            fw.mm(ps[:, 0:64], hv(kc, ti * 128, 128), wt[:, kc, 320:384], start=(kc == 0), stop=(kc == 7))
                        fw.cp("act", V1[:, ti, 0:64], ps[:, 0:64])
                        if ctx:
                            vs = vst.next()
                            fw.cp("dve", vs[:, :], ps[:, 0:64])
                            fw.dma(dv("nv")[j, ti * 128:(ti + 1) * 128, g2 * 64:(g2 + 1) * 64], vs[:, :])
                    if not ctx:
                        for t in range(4):
                            ck = cks.next()
                            fw.dma(ck[:, 0:64], dv("ctxk")[j, t * 128:(t + 1) * 128, g2 * 64:(g2 + 1) * 64])
                            fw.dma(ck[:, 64:128], dv("ctxv")[j, t * 128:(t + 1) * 128, g2 * 64:(g2 + 1) * 64])
                            ps2 = mmp.next()
                            fw.tr(ps2[0:64, 0:128], ck[:, 0:64], C("ident"))
                            fw.cp("act", kT[:, 1024 + t * 128:1024 + (t + 1) * 128], ps2[0:64, 0:128])
                            fw.cp("dve", V1[:, 8 + t, 0:64], ck[:, 64:128])
                    QB = 256 if ctx else 512
                    for s in range(nseq):
                        if ctx:
                            kts = [(slice(ti * 128, (ti + 1) * 128), ti) for ti in (2 * s, 2 * s + 1)]
                        else:
                            kts = [(slice(t * 128, (t + 1) * 128), t) for t in range(12)]
                        for hh in range(4):
                            for qb in range(L // QB):
                                q0 = s * L + qb * QB
                                oacc = accp.next()
                                for idx, (ksl, vt) in enumerate(kts):
                                    ps = mmp.next()
                                    fw.mm(ps[:, 0:QB], kT[:, ksl], qT.p(hh)[:, hh, q0:q0 + QB])
                                    pt = ptp.next()
                                    fw.act(pt[:, 0:QB], ps[:, 0:QB], AF.Exp, scale=0.125)
                                    nqs = QB // 128
                                    for qs in range(nqs):
                                        fw.mm(oacc[:, qs * 128:qs * 128 + 65], pt[:, qs * 128:(qs + 1) * 128],
                                              V1[:, vt, :], start=(idx == 0 and qs == 0),
                                              stop=(idx == len(kts) - 1 and qs == nqs - 1))
                                for qs in range(QB // 128):
                                    ti = q0 // 128 + qs
                                    rd = rdp.next()
                                    fw.recip(rd[:, :], oacc[:, qs * 128 + 64:qs * 128 + 65])
                                    fw.act(otok.p(ti)[:, ti, hh * 64:(hh + 1) * 64], oacc[:, qs * 128:qs * 128 + 64],
                                           AF.Identity, scale=rd[:, 0:1])
                    for ti in range(8):
                        for c2 in range(2):
                            pT = trp.next()
                            fw.tr(pT[:, 0:128], otok.p(ti)[:, ti, c2 * 128:(c2 + 1) * 128], identb[:, :])
                            c = 2 * g2 + c2
                            evac(catT.p(c * 2 + ti // 4)[:, c, ti * 128:(ti + 1) * 128], pT[:, 0:128])
                    pump()
                fw.barrier()
                fw.flush()

            if dbg == "dbg_att":
                return
            with ExitStack() as ug:
                raw = Rot([fw.sb("graw%d" % i, [128, 1024], F32, stack=ug) for i in range(2)])
                cac = Rot([fw.sb("gcac%d" % i, [128, 1024], F32, stack=ug) for i in range(2)])
                qkf = Rot([fw.sb("gqkf%d" % i, [128, 1024], F32, stack=ug) for i in range(1)])
                fT = fw.sb("gfT", [128, 3, 1024], BF16, 3, stack=ug)
                ktok = fw.sb("gktok", [128, 8, 128], BF16, 8, stack=ug)
                vtok = fw.sb("gvtok", [128, 8, 128], BF16, 8, stack=ug)
                sz = fw.sb("gsz", [128, 8, 128], F32, 8, stack=ug)
                zraw = fw.sb("gzraw", [128, 8, 132], F32, 8, stack=ug)
                scr = zraw[:, :, 128:132]
                gs = fw.sb("ggs", [128, 10, 8, 2], F32, stack=ug)
                tmpa = fw.sb("gtmpa", [128, 8], F32, stack=ug)
                tmpb = fw.sb("gtmpb", [128, 8], F32, stack=ug)
                negA = fw.sb("gnegA", [128, 2], F32, stack=ug)
                decb = fw.sb("gdecb", [128, 2, 16], F32, stack=ug)
                oac = fw.sb("goac", [128, 8, 128], F32, 8, stack=ug)
                ssc = fw.sb("gssc", [128, 8], F32, stack=ug)
                junk = Rot([fw.sb("gjunk%d" % i, [128, 128], F32, stack=ug) for i in range(2)])
                onb = Rot([fw.sb("gonb%d" % i, [128, 128], BF16, stack=ug) for i in range(2)])
                sqb = Rot([fw.sb("gsq%d" % i, [128, 512], BF16, stack=ug) for i in range(2)])
                rsb = Rot([fw.sb("grs%d" % i, [128, 512], F32, stack=ug) for i in range(2)])
                S = [fw.sb("gS%d" % d, [128, 128], F32, stack=ug) for d in range(2)]
                Sb = [fw.sb("gSb%d" % d, [128, 128], BF16, stack=ug) for d in range(2)]
                dg = [fw.sb("gdg%d" % d, [128, 384], F32, stack=ug) for d in range(2)]
                E2 = [fw.sb("gE2%d" % d, [128, 256], F32, stack=ug) for d in range(2)]
                Mm = [fw.sb("gMm%d" % d, [128, 128], F32, stack=ug) for d in range(2)]
                qkT = [[fw.sb("gqkT%d_%d" % (d, i), [128, 128], BF16, stack=ug) for i in range(2)] for d in range(2)]
                Bm = [[fw.sb("gB%d_%d" % (d, i), [128, 128], F32, stack=ug) for i in range(2)] for d in range(2)]
                AS = [[fw.sb("gAS%d_%d" % (d, i), [128, 256], F32, stack=ug) for i in range(2)] for d in range(2)]
                TT = [fw.sb("gTT%d" % d, [128, 128], BF16, stack=ug) for d in range(2)]
                vb = [fw.sb("gvb%d" % d, [128, 128], BF16, stack=ug) for d in range(2)]
                kbg = [fw.sb("gkbg%d" % d, [128, 128], BF16, stack=ug) for d in range(2)]
                kdec = [[fw.sb("gkdec%d_%d" % (d, i), [128, 128], BF16, stack=ug) for i in range(2)] for d in range(2)]
                qdT = [[fw.sb("gqdT%d_%d" % (d, i), [128, 128], BF16, stack=ug) for i in range(2)] for d in range(2)]
                usb = [[fw.sb("gusb%d_%d" % (d, i), [128, 128], F32, stack=ug) for i in range(2)] for d in range(2)]
                wTs = [[fw.sb("gwT%d_%d" % (d, i), [128, 128], BF16, stack=ug) for i in range(2)] for d in range(2)]
                vnew = [fw.sb("gvn%d" % d, [128, 128], BF16, stack=ug) for d in range(2)]
                for d in range(2):
                    fw.memset(vnew[d][:, :], 0.0)
                for hd in range(4):
                    base = 768 + hd * 640
                    wtA = wload(dr["wie"][j, :, base:base + 384], 8, 384)
                    import os
                    SK = os.environ.get("DBGSKIP", "")
                    if "wtb" not in SK:
                        wtB = wload(dr["wie"][j, :, base + 384:base + 640], 8, 256)
                    for wi in range(3):
                        r = raw.next()
                        for tg in range(2):
                            ps = mmp.next()
                            for kc in range(8):
                                fw.mm(ps[:, :], wtA[:, kc, wi * 128:(wi + 1) * 128], hv(kc, tg * 512, 512),
                                      start=(kc == 0), stop=(kc == 7))
                            evac(r[:, tg * 512:(tg + 1) * 512], ps[:, :])
                        a = cac.next()
                        cwc = lambda tap: VC("dconv", ((j * 4 + hd) * 3 + wi) * 3 + tap)
                        r3 = r[:, :].re("p (s l) -> p s l", s=nseq)
                        a3 = a[:, :].re("p (s l) -> p s l", s=nseq)
                        fw.ts(a[:, :], r[:, :], cwc(1), ALU.mult)
                        if "conv" not in SK:
                            fw.stt(a3[:, :, 1:L], r3[:, :, 0:L - 1], cwc(0), a3[:, :, 1:L], ALU.mult, ALU.add)
                            fw.stt(a3[:, :, 0:L - 1], r3[:, :, 1:L], cwc(2), a3[:, :, 0:L - 1], ALU.mult, ALU.add)
                        if wi == 2:
                            fw.act(fT.p(2)[:, 2, :], a[:, :], SILU)
                        else:
                            f = qkf.next()
                            fw.act(f[:, :], a[:, :], SILU)
                            for tg in range(2):
                                sl = slice(tg * 512, (tg + 1) * 512)
                                sq = sqb.next()
                                fw.act(sq[:, :], f[:, sl], AF.Square)
                                ps = mmp.next()
                                fw.mm(ps[:, :], onesb[:, :], sq[:, :])
                                rs = rsb.next()
                                fw.act(rs[:, :], ps[:, :], AF.Sqrt, bias=EPS, scale=1.0)
                                fw.recip(rs[:, :], rs[:, :])
                                fw.stt(fT.p(wi)[:, wi, sl], f[:, sl], (128 ** -0.5) if wi == 0 else 1.0, rs[:, :],
                                       ALU.mult, ALU.mult)
                    for ti in range(8):
                        tsl = slice(ti * 128, (ti + 1) * 128)
                        pT = trp.next()
                        fw.tr(pT[:, 0:128], fT.p(1)[:, 1, tsl], identb[:, :])
                        evac(ktok.p(ti)[:, ti, :], pT[:, 0:128])
                        pT = trp.next()
                        fw.tr(pT[:, 0:128], fT.p(2)[:, 2, tsl], identb[:, :])
                        evac(vtok.p(ti)[:, ti, :], pT[:, 0:128])
                    if dbg == "dbg_gdn1":
                        break
                    for ti in range(8):
                        ps = mmp.next()
                        for kc in range(8):
                            fw.mm(ps[:, 0:256], hv(kc, ti * 128, 128), wtB[:, kc, :], start=(kc == 0), stop=(kc == 7))
                        fw.cp("act", zraw.p(ti)[:, ti, :], ps[:, 0:132])
                        fw.act(sz.p(ti)[:, ti, :], zraw.p(ti)[:, ti, 0:128], SILU)
                    if "scal" in SK:
                        break
                    for d in range(2):
                        col = j * 8 + d * 4 + hd
                        fw.act(negA[:, d:d + 1], VC("dalog", col), AF.Exp)
                        fw.ts(negA[:, d:d + 1], negA[:, d:d + 1], -1.0, ALU.mult)
                        fw.act(tmpa[:, :], scr[:, :, 2 + d:3 + d].re('p t o -> p (t o)'), AF.Exp, bias=VC("ddtb", col), scale=1.0)
                        fw.act(tmpa[:, :], tmpa[:, :], AF.Ln, bias=1.0, scale=1.0)
                        fw.ts(gs[:, 2, :, d], tmpa[:, :], negA[:, d:d + 1], ALU.mult)
                        fw.act(tmpb[:, :], scr[:, :, d:d + 1].re('p t o -> p (t o)'), AF.Exp, scale=-1.0)
                        fw.act(gs[:, 1, :, d], tmpb[:, :], AF.Ln, bias=1.0, scale=1.0)
                        fw.act(gs[:, 0, :, d], gs[:, 1, :, d], AF.Exp, scale=-1.0)
                    if "cums" in SK:
                        break
                    ps = mmp.next()
                    fw.mm(ps[:, 0:8], C("triF"), gs[:, 2, :, 0])
                    fw.mm(ps[:, 8:16], C("triB"), gs[:, 2, :, 1])
                    fw.mm(ps[:, 16:24], C("blk"), gs[:, 2, :, 0])
                    fw.mm(ps[:, 24:32], C("blk"), gs[:, 2, :, 1])
                    for d in range(2):
                        fw.cp("dve", gs[:, 3, :, d], ps[:, d * 8:(d + 1) * 8])
                        fw.cp("dve", gs[:, 4, :, d], ps[:, 16 + d * 8:16 + (d + 1) * 8])
                    fw.act(gs[:, 5, :, :], gs[:, 3, :, :], AF.Exp)
                    fw.tt(gs[:, 6, :, :], gs[:, 4, :, :], gs[:, 3, :, :], ALU.subtract)
                    fw.act(gs[:, 6, :, :], gs[:, 6, :, :], AF.Exp)
                    fw.tt(gs[:, 7, :, :], gs[:, 3, :, :], gs[:, 1, :, :], ALU.subtract)
                    fw.ts(gs[:, 8, :, :], gs[:, 3, :, :], -1.0, ALU.mult)
                    fw.tt(gs[:, 9, :, :], gs[:, 0, :, :], gs[:, 5, :, :], ALU.mult)
                    ps = mmp.next()
                    g2d = gs[:, 2, :, :].re("p t d -> p (t d)")
                    fw.mm(ps[:, 0:16], C("ind0"), g2d)
                    fw.mm(ps[:, 16:32], C("ind1"), g2d)
                    fw.act(decb[:, :, :].re("p c n -> p (c n)"), ps[:, 0:32], AF.Exp)
                    for ti in range(8):
                        fw.memset(oac.p(ti)[:, ti, :], 0.0)
                    if dbg == "dbg_gdn2":
                        break
                    def gcol(k, ti, d):
                        return gs[:, k, ti, d:d + 1]

                    def solve(n):
                        par = n % 2
                        slots = [(n, 0), (7 - n, 1)]
                        for ti, d in slots:
                            fw.ts(dg[d][:, 0:128], C("ident"), gcol(3, ti, d), ALU.mult)
                            fw.ts(dg[d][:, 128:256], C("ident"), gcol(7, ti, d), ALU.mult)
                            fw.ts(dg[d][:, 256:384], C("ident"), gcol(5, ti, d), ALU.mult)
                        yield
                        for ti, d in slots:
                            p = mmp.next()
                            fw.mm(p[:, 0:256], C("ones"), dg[d][:, 0:256], start=True, stop=False)
                            fw.mm(p[:, 0:256], C("ident"), C("nmF" if d == 0 else "nmB"), start=False, stop=True)
                            fw.mm(p[:, 256:384], C("ones"), dg[d][:, 256:384], start=True, stop=True)
                            fw.act(E2[d][:, :], p[:, 0:256], AF.Exp, bias=gcol(8, ti, d), scale=1.0)
                            fw.tt(qdT[d][par][:, :], fT.p(0)[:, 0, ti * 128:(ti + 1) * 128], p[:, 256:384], ALU.mult)
                        yield
                        for ti, d in slots:
                            tsl = slice(ti * 128, (ti + 1) * 128)
                            p = mmp.next()
                            fw.mm(p[:, 0:128], fT.p(1)[:, 1, tsl], fT.p(0)[:, 0, tsl])
                            fw.mm(p[:, 128:256], fT.p(1)[:, 1, tsl], fT.p(1)[:, 1, tsl])
                            fw.tt(qkT[d][par][:, :], p[:, 0:128], E2[d][:, 0:128], ALU.mult)
                            fw.tt(Mm[d][:, :], p[:, 128:256], E2[d][:, 128:256], ALU.mult)
                        yield
                        for ti, d in slots:
                            p = mmp.next()
                            fw.tr(p[:, 0:128], Mm[d][:, :], C("ident"))
                            fw.cp("act", Bm[d][0][:, :], p[:, 0:128])
                        yield
                        for ti, d in slots:
                            p = mmp.next()
                            fw.mm(p[:, 0:128], Bm[d][0][:, :], Mm[d][:, :])
                            fw.mm(p[:, 128:256], Mm[d][:, :], Bm[d][0][:, :])
                            fw.cp("act", AS[d][0][:, 0:128], p[:, 0:128])
                            fw.cp("dve", Bm[d][1][:, :], p[:, 128:256])
                            fw.tt(AS[d][0][:, 128:256], C("ident"), Mm[d][:, :], ALU.subtract)
                        yield
                        for k in range(1, 6):
                            for ti, d in slots:
                                cur, nxt, Bk, Bn = AS[d][(k - 1) % 2], AS[d][k % 2], Bm[d][k % 2], Bm[d][(k + 1) % 2]
                                p = mmp.next()
                                if k < 5:
                                    fw.mm(p[:, 0:256], Bk[:, :], cur[:, 0:256])
                                    fw.mm(p[:, 256:384], cur[:, 0:128], Bk[:, :])
                                    fw.cp("act", Bn[:, :], p[:, 256:384])
                                    if k < 4:
                                        fw.cp("act", nxt[:, 0:128], p[:, 0:128])
                                    fw.tt(nxt[:, 128:256], cur[:, 128:256], p[:, 128:256], ALU.add)
                                else:
                                    fw.mm(p[:, 0:128], Bk[:, :], cur[:, 128:256])
                                    fw.tt(TT[d][:, :], cur[:, 128:256], p[:, 0:128], ALU.add)
                            yield
                        for ti, d in slots:
                            fw.ts(vb[d][:, :], vtok.p(ti)[:, ti, :], gcol(0, ti, d), ALU.mult)
                            fw.ts(kbg[d][:, :], ktok.p(ti)[:, ti, :], gcol(9, ti, d), ALU.mult)
                            fw.act(kdec[d][par][:, :], ktok.p(ti)[:, ti, :], AF.Identity, scale=gcol(6, ti, d))
                        yield
                        for ti, d in slots:
                            p = mmp.next()
                            fw.mm(p[:, 0:128], TT[d][:, :], vb[d][:, :])
                            fw.mm(p[:, 128:256], kbg[d][:, :], TT[d][:, :])
                            fw.cp("act", usb[d][par][:, :], p[:, 0:128])
                            fw.cp("dve", wTs[d][par][:, :], p[:, 128:256])
                        yield

                    def recur(n):
                        par = n % 2
                        slots = [(n, 0), (7 - n, 1)]
                        for ci in range(2):
                            for ti, d in slots:
                                c = ci if d == 0 else 1 - ci
                                rows = slice(c * 64, (c + 1) * 64)
                                first = (ti % tps == 0 and c == 0) if d == 0 else (ti % tps == tps - 1 and c == 1)
                                last = (ti % tps == tps - 1 and c == 1) if d == 0 else (ti % tps == 0 and c == 0)
                                seq = ti // tps
                                if first:
                                    if ctx:
                                        fw.memset(S[d][:, :], 0.0)
                                    else:
                                        fw.dma(S[d][:, :], dv("sd0")[j, d, hd, :, :])
                                    fw.cp("act", Sb[d][:, :], S[d][:, :])
                                p = mmp.next()
                                fw.mm(p[:, 0:128], wTs[d][par][:, :], Sb[d][:, :])
                                fw.tt(vnew[d][rows, :], usb[d][par][rows, :], p[rows, 0:128], ALU.subtract)
                                yield
                                po = mmp.next()
                                fw.mm(po[:, 0:128], qdT[d][par][:, :], Sb[d][:, :], start=True, stop=False)
                                fw.mm(po[:, 0:128], qkT[d][par][rows, :], vnew[d][rows, :], start=False, stop=True)
                                pst = mmp.next()
                                fw.mm(pst[:, 0:128], kdec[d][par][rows, :], vnew[d][rows, :])
                                fw.stt(S[d][:, :], S[d][:, :], decb[:, c, ti * 2 + d:ti * 2 + d + 1], pst[:, 0:128],
                                       ALU.mult, ALU.add)
                                fw.cp("act", Sb[d][:, :], S[d][:, :])
                                fw.tt(oac.p(ti)[rows, ti, :], oac.p(ti)[rows, ti, :], po[rows, 0:128], ALU.add)
                                if last and ctx:
                                    fw.dma(dv("nsd")[seq, j, d, hd, :, :], S[d][:, :])
                                yield

                    def drive(gens):
                        gens = list(gens)
                        while gens:
                            for g_ in list(gens):
                                try:
                                    next(g_)
                                except StopIteration:
                                    gens.remove(g_)

                    drive([solve(0)])
                    for n in range(8):
                        drive([recur(n)] + ([solve(n + 1)] if n < 7 else []))

                    if dbg == "dbg_gdn5":
                        break
                    for ti in range(8):
                        jk = junk.next()
                        fw.tt(jk[:, :], oac.p(ti)[:, ti, :], oac.p(ti)[:, ti, :], ALU.mult)
                        rsum(ssc[:, ti:ti + 1], jk[:, :])
                    fw.act(ssc[:, :], ssc[:, :], AF.Sqrt, bias=EPS, scale=1.0 / 128)
                    fw.recip(ssc[:, :], ssc[:, :])
                    for ti in range(8):
                        ob = onb.next()
                        fw.stt(ob[:, :], oac.p(ti)[:, ti, :], ssc[:, ti:ti + 1], sz.p(ti)[:, ti, :], ALU.mult, ALU.mult)
                        pT = trp.next()
                        fw.tr(pT[:, 0:128], ob[:, :], identb[:, :])
                        c = 4 + hd
                        fw.act(catT.p(c * 2 + ti // 4)[:, c, ti * 128:(ti + 1) * 128], pT[:, 0:128], AF.Identity,
                               scale=VC("dng", j))
                    pump()
                fw.barrier()
                fw.flush()

        def odd_mixer(ph, layer, g, hT, catT):
            j = layer // 2
            ctx = (g == 0)
            nseq, L = (4, 256) if ctx else (1, 1024)
            tps = L // 128

            def hv(kc, lo, n):
                return hT.p(kc * 2 + lo // 512)[:, kc, lo:lo + n]

            with ExitStack() as us:
                raw = Rot([fw.sb("sraw%d" % i, [128, 1024], F32, stack=us) for i in range(1)])
                cac = Rot([fw.sb("scac%d" % i, [128, 1024], F32, stack=us) for i in range(1)])
                fT = fw.sb("sfT", [128, 4, 1024], BF16, 4, stack=us)
                xtok = fw.sb("sxtok", [128, 8, 256], BF16, 8, stack=us)
                Btok = fw.sb("sBtok", [128, 8, 128], BF16, 8, stack=us)
                zs = fw.sb("szs", [128, 8, 256], BF16, 8, stack=us)
                dtr = fw.sb("sdtr", [128, 8, 8], F32, stack=us)
                zr = Rot([fw.sb("szr%d" % i, [128, 264], F32, stack=us) for i in range(2)])
                ss = fw.sb("sss", [128, 7, 8, 8], F32, stack=us)
                negA = fw.sb("snegA", [128, 8], F32, stack=us)
                decb = fw.sb("sdecb", [128, 2, 64], F32, stack=us)
                yac = fw.sb("syac", [128, 8, 256], F32, 8, stack=us)
                xdt = [fw.sb("sxdt%d" % d, [128, 256], BF16, stack=us) for d in range(2)]
                xd = [[fw.sb("sxd%d_%d" % (d, i), [128, 256], BF16, stack=us) for i in range(2)] for d in range(2)]
                cbs = [fw.sb("scbs%d" % d, [128, 128], F32, stack=us) for d in range(2)]
                dgs = [fw.sb("sdgs%d" % d, [128, 4, 128], F32, stack=us) for d in range(2)]
                Ls = dgs
                cbL = [fw.sb("scbL%d" % d, [128, 4, 128], BF16, stack=us) for d in range(2)]
                sT = [fw.sb("ssT%d" % d, [128, 256], F32, stack=us) for d in range(2)]
                sTb = [fw.sb("ssTb%d" % d, [128, 256], BF16, stack=us) for d in range(2)]
                ytmp = Rot([fw.sb("sytmp%d" % i, [128, 256], F32, stack=us) for i in range(2)])
                ygb = Rot([fw.sb("sygb%d" % i, [128, 256], BF16, stack=us) for i in range(2)])
                sst = Rot([fw.sb("ssst%d" % i, [128, 128], F32, stack=us) for i in range(2)])
                for sg in range(8):
                    base = sg * 832
                    wtA = wload(dr["wio"][j, :, base:base + 512], 8, 512)
                    wtB = wload(dr["wio"][j, :, base + 512:base + 832], 8, 320)
                    for wi in range(4):
                        r = raw.next()
                        for tg in range(2):
                            ps = mmp.next()
                            for kc in range(8):
                                fw.mm(ps[:, :], wtA[:, kc, wi * 128:(wi + 1) * 128], hv(kc, tg * 512, 512),
                                      start=(kc == 0), stop=(kc == 7))
                            evac(r[:, tg * 512:(tg + 1) * 512], ps[:, :])
                        a = cac.next()
                        cwc = lambda tap: VC("sconv", ((j * 8 + sg) * 4 + wi) * 4 + tap)
                        r3 = r[:, :].re("p (s l) -> p s l", s=nseq)
                        a3 = a[:, :].re("p (s l) -> p s l", s=nseq)
                        fw.ts(a[:, :], r[:, :], cwc(1), ALU.mult)
                        fw.stt(a3[:, :, 1:L], r3[:, :, 0:L - 1], cwc(0), a3[:, :, 1:L], ALU.mult, ALU.add)
                        fw.stt(a3[:, :, 0:L - 1], r3[:, :, 1:L], cwc(2), a3[:, :, 0:L - 1], ALU.mult, ALU.add)
                        fw.act(fT.p(wi)[:, wi, :], a[:, :], SILU, bias=cwc(3), scale=1.0)
                    for ti in range(8):
                        tsl = slice(ti * 128, (ti + 1) * 128)
                        for c2 in range(2):
                            pT = trp.next()
                            fw.tr(pT[:, 0:128], fT.p(c2)[:, c2, tsl], identb[:, :])
                            evac(xtok.p(ti)[:, ti, c2 * 128:(c2 + 1) * 128], pT[:, 0:128])
                        pT = trp.next()
                        fw.tr(pT[:, 0:128], fT.p(2)[:, 2, tsl], identb[:, :])
                        evac(Btok.p(ti)[:, ti, :], pT[:, 0:128])
                    for ti in range(8):
                        ps = mmp.next()
                        for kc in range(8):
                            fw.mm(ps[:, 0:320], hv(kc, ti * 128, 128), wtB[:, kc, :], start=(kc == 0), stop=(kc == 7))
                        fw.cp("act", zr.next()[:, :], ps[:, 0:264])
                        zr_ = zr.items[(zr.i - 1) % len(zr.items)]
                        fw.act(zs.p(ti)[:, ti, :], zr_[:, 0:256], SILU)
                        fw.cp("act", dtr[:, ti, :], zr_[:, 256:264])
                    for d in range(2):
                        c0 = j * 64 + d * 32 + sg * 4
                        fw.act(negA[:, d * 4:(d + 1) * 4], VC("salog", c0, n=4), AF.Exp)
                        fw.tt(ss[:, 0, :, d * 4:(d + 1) * 4], dtr[:, :, d * 4:(d + 1) * 4],
                              VC("sdtb", c0, n=4).un(1).bc([128, 8, 4]), ALU.add)
                    fw.ts(negA[:, :], negA[:, :], -1.0, ALU.mult)
                    fw.act(ss[:, 0, :, :], ss[:, 0, :, :], AF.Exp)
                    fw.act(ss[:, 0, :, :], ss[:, 0, :, :], AF.Ln, bias=1.0, scale=1.0)
                    fw.tt(ss[:, 1, :, :], ss[:, 0, :, :], negA[:, :].un(1).bc([128, 8, 8]), ALU.mult)
                    ps = mmp.next()
                    fw.mm(ps[:, 0:32], C("triF"), ss[:, 1, :, 0:4])
                    fw.mm(ps[:, 32:64], C("triB"), ss[:, 1, :, 4:8])
                    fw.mm(ps[:, 64:128], C("blk"), ss[:, 1, :, :])
                    for d in range(2):
                        fw.cp("dve", ss[:, 2, :, d * 4:(d + 1) * 4], ps[:, d * 32:(d + 1) * 32].re("p (t h) -> p t h", h=4))
                    fw.cp("dve", ss[:, 3, :, :], ps[:, 64:128].re("p (t h) -> p t h", h=8))
                    fw.act(ss[:, 4, :, :], ss[:, 2, :, :], AF.Exp)
                    fw.tt(ss[:, 5, :, :], ss[:, 3, :, :], ss[:, 2, :, :], ALU.subtract)
                    fw.act(ss[:, 5, :, :], ss[:, 5, :, :], AF.Exp)
                    fw.ts(ss[:, 6, :, :], ss[:, 2, :, :], -1.0, ALU.mult)
                    ps = mmp.next()
                    a2d = ss[:, 1, :, :].re("p t h -> p (t h)")
                    fw.mm(ps[:, 0:64], C("ind0"), a2d)
                    fw.mm(ps[:, 64:128], C("ind1"), a2d)
                    fw.act(decb[:, :, :].re("p c n -> p (c n)"), ps[:, 0:128], AF.Exp)
                    for ti in range(8):
                        fw.tt(yac.p(ti)[:, ti, :].re("p (h e) -> p h e", h=4),
                              xtok.p(ti)[:, ti, :].re("p (h e) -> p h e", h=4),
                              VC("sd", j * 32 + sg * 4, n=4).un(2).bc([128, 4, 64]), ALU.mult)
                    def lpart(n):
                        par = n % 2
                        slots = [(n, 0), (7 - n, 1)]
                        for ti, d in slots:
                            tsl = slice(ti * 128, (ti + 1) * 128)
                            dh = slice(d * 4, (d + 1) * 4)
                            x3 = xtok.p(ti)[:, ti, :].re("p (h e) -> p h e", h=4)
                            fw.tt(xdt[d][:, :].re("p (h e) -> p h e", h=4), x3,
                                  ss[:, 0, ti, dh].un(2).bc([128, 4, 64]), ALU.mult)
                            fw.tt(xd[d][par][:, :].re("p (h e) -> p h e", h=4), xdt[d][:, :].re("p (h e) -> p h e", h=4),
                                  ss[:, 5, ti, dh].un(2).bc([128, 4, 64]), ALU.mult)
                            p = mmp.next()
                            fw.mm(p[:, 0:128], fT.p(2)[:, 2, tsl], fT.p(3)[:, 3, tsl])
                            fw.cp("act", cbs[d][:, :], p[:, 0:128])
                            fw.tt(dgs[d][:, :, :], C("ident").un(1).bc([128, 4, 128]),
                                  ss[:, 2, ti, dh].un(2).bc([128, 4, 128]), ALU.mult)
                        yield
                        for ti, d in slots:
                            pL = mmp.next()
                            fw.mm(pL[:, :], C("ones"), dgs[d][:, :, :].re("p h i -> p (h i)"), start=True, stop=False)
                            for h in range(4):
                                fw.mm(pL[:, h * 128:(h + 1) * 128], C("ident"), C("nmFc" if d == 0 else "nmBc"),
                                      start=False, stop=(h == 3))
                            for h in range(4):
                                fw.act(Ls[d][:, h, :], pL[:, h * 128:(h + 1) * 128], AF.Exp,
                                       bias=ss[:, 6, ti, d * 4 + h:d * 4 + h + 1], scale=1.0)
                            yield
                        for ti, d in slots:
                            fw.tt(cbL[d][:, :, :], Ls[d][:, :, :], cbs[d][:, :].un(1).bc([128, 4, 128]), ALU.mult)
                        yield
                        for ti, d in slots:
                            pY = mmp.next()
                            for h in range(4):
                                fw.mm(pY[:, h * 64:(h + 1) * 64], cbL[d][:, h, :], xdt[d][:, h * 64:(h + 1) * 64])
                            fw.tt(yac.p(ti)[:, ti, :], yac.p(ti)[:, ti, :], pY[:, 0:256], ALU.add)
                            yield

                    def srec(n):
                        par = n % 2
                        slots = [(n, 0), (7 - n, 1)]
                        for ci in range(2):
                            for ti, d in slots:
                                tsl = slice(ti * 128, (ti + 1) * 128)
                                c = ci if d == 0 else 1 - ci
                                rows = slice(c * 64, (c + 1) * 64)
                                first = (ti % tps == 0 and c == 0) if d == 0 else (ti % tps == tps - 1 and c == 1)
                                last = (ti % tps == tps - 1 and c == 1) if d == 0 else (ti % tps == 0 and c == 0)
                                seq = ti // tps
                                if first:
                                    if ctx:
                                        fw.memset(sT[d][:, :], 0.0)
                                    else:
                                        for half in range(2):
                                            st_ = sst.next()
                                            fw.dma(st_[:, :], dv("ss0")[j, d, 2 * sg + half, :, :])
                                            p = mmp.next()
                                            fw.tr(p[:, 0:128], st_[:, :], C("ident"))
                                            fw.cp("act", sT[d][:, half * 128:(half + 1) * 128], p[:, 0:128])
                                    fw.cp("act", sTb[d][:, :], sT[d][:, :])
                                po = mmp.next()
                                fw.mm(po[:, 0:256], fT.p(3)[:, 3, tsl], sTb[d][:, :])
                                pS = mmp.next()
                                fw.mm(pS[:, 0:256], Btok.p(ti)[rows, ti, :], xd[d][par][rows, :])
                                yt = ytmp.next()
                                fw.tt(yt[rows, :].re("p (h e) -> p h e", h=4), po[rows, 0:256].re("p (h e) -> p h e", h=4),
                                      ss[rows, 4, ti, d * 4:(d + 1) * 4].un(2).bc([64, 4, 64]), ALU.mult)
                                fw.tt(sT[d][:, :].re("p (h e) -> p h e", h=4), sT[d][:, :].re("p (h e) -> p h e", h=4),
                                      decb[:, c, ti * 8 + d * 4:ti * 8 + d * 4 + 4].un(2).bc([128, 4, 64]), ALU.mult)
                                fw.tt(sT[d][:, :], sT[d][:, :], pS[:, 0:256], ALU.add)
                                fw.cp("act", sTb[d][:, :], sT[d][:, :])
                                fw.tt(yac.p(ti)[rows, ti, :], yac.p(ti)[rows, ti, :], yt[rows, :], ALU.add)
                                if last and ctx:
                                    for half in range(2):
                                        p = mmp.next()
                                        fw.tr(p[:, 0:128], sT[d][:, half * 128:(half + 1) * 128], C("ident"))
                                        st_ = sst.next()
                                        evac(st_[:, :], p[:, 0:128])
                                        fw.dma(dv("nss")[seq, j, d, 2 * sg + half, :, :], st_[:, :])
                                yield

                    def sdrive(gens):
                        gens = list(gens)
                        while gens:
                            for g_ in list(gens):
                                try:
                                    next(g_)
                                except StopIteration:
                                    gens.remove(g_)

                    sdrive([lpart(0)])
                    for n in range(8):
                        sdrive([srec(n)] + ([lpart(n + 1)] if n < 7 else []))
                    for ti in range(8):
                        yg = ygb.next()
                        fw.tt(yg[:, :], yac.p(ti)[:, ti, :], zs.p(ti)[:, ti, :], ALU.mult)
                        for c2 in range(2):
                            pT = trp.next()
                            fw.tr(pT[:, 0:128], yg[:, c2 * 128:(c2 + 1) * 128], identb[:, :])
                            c = 2 * sg + c2
                            evac(catT.p(c * 2 + ti // 4)[:, c, ti * 128:(ti + 1) * 128], pT[:, 0:128])
                    pump()
                fw.barrier()
                fw.flush()

        dbg = stop if (stop or "").startswith("dbg") else None
        nl_run = n_layers if (stop is None or dbg) else 0
        if nl_run > 0:
            for _ in mod_vectors(0):
                pass
        for layer in range(nl_run):
            modv, modA = modv2[layer % 2], modA2[layer % 2]
            pump_holder[0] = mod_vectors(layer + 1) if layer + 1 < nl_run else None
            if dbg == "dbg_mod":
                break
            for g in range(2):
                with ExitStack() as ph:
                    hT = fw.sb("hTg", [128, 8, 1024], BF16, 16, stack=ph)
                    with ExitStack() as ph2:
                        modulate(ph2, "nx", hT, 0, [2 * g, 2 * g + 1], 2 * g, 2)
                        fw.barrier()
                        fw.flush()
                    if dbg == "dbg_norm":
                        continue
                    if layer % 2 == 0:
                        catT = fw.sb("catT", [128, 8, 1024], BF16, 16, stack=ph)
                        even_mixer(ph, layer, g, hT, catT, dbg)
                        if dbg is None or dbg == "dbg_mlp":
                            out_proj_even(layer, g, catT)
                    else:
                        catT = fw.sb("catT", [128, 16, 1024], BF16, 32, stack=ph)
                        odd_mixer(ph, layer, g, hT, catT)
                        out_proj_odd(ph, layer, g, catT)
                    fw.barrier()
                    fw.flush()
            pump(100)
            if dbg in (None, "dbg_mlp"):
                mlp(layer)
        if stop is None or dbg:
            final_out()
        elif stop.startswith("final"):
            final_out(int(stop[5:]))
        fw.barrier(final=True)
        fw.flush()
    return nc, fw.n_instr


_PROG = {}


def _run(inp, n_layers=DEPTH, ncores=8):
    if n_layers not in _PROG:
        _PROG[n_layers] = build(n_layers)
    nc, _ = _PROG[n_layers]
    f = lambda k: np.ascontiguousarray(np.asarray(inp[k], dtype=np.float32))
    consts = make_consts()
    rope = np.ascontiguousarray(make_rope().reshape(64, 2048))
    vecs = make_vecs(inp)
    wie = permute_cols(f("w_in_even"), even_col_perm())
    wio = permute_cols(f("w_in_odd"), odd_col_perm())
    shared = {
        "consts": consts, "rope": rope, "vecs": vecs,
        "w_mod": f("w_mod"), "w_mlp_in": f("w_mlp_in"), "w_mlp_out": f("w_mlp_out"),
        "wie": wie, "w_out_even": f("w_out_even"), "wio": wio, "w_out_odd": f("w_out_odd"),
    }
    xp, xs, c, cctx = f("x_prompt"), f("x_sample"), f("c"), f("c_ctx")
    ck, cv, sd, ssm = f("cache_attn_k"), f("cache_attn_v"), f("state_delta"), f("state_ssm")
    in_maps = []
    for i in range(ncores):
        b = i % 4
        cond = np.stack([cctx, c[b]], axis=0)
        condT = np.ascontiguousarray(cond.reshape(2, 8, 128).transpose(2, 1, 0)).reshape(128, 16)
        m = dict(shared)
        m["xin"] = np.ascontiguousarray(np.concatenate([xp[4 * i:4 * i + 4].reshape(1024, D), xs[b]], axis=0))
        m["condT"] = condT
        m["ctxk"] = np.ascontiguousarray(ck[b].reshape(2, 512, 128))
        m["ctxv"] = np.ascontiguousarray(cv[b].reshape(2, 512, 128))
        m["sd0"] = np.ascontiguousarray(sd[b])
        m["ss0"] = np.ascontiguousarray(ssm[b].reshape(2, 2, 16, 128, 128))
        in_maps.append(m)
    res = run_bass_kernel_spmd(nc, in_maps, core_ids=list(range(ncores)))
    R = res.results
    if ncores < 8:
        return R
    y_p = np.concatenate([R[i]["y"][:1024].reshape(4, 256, D) for i in range(8)], axis=0)
    y_s = np.stack([R[b]["y"][1024:] for b in range(4)], axis=0)
    nk = np.concatenate([R[i]["nk"].reshape(2, 4, 256, 2, 64).transpose(1, 0, 2, 3, 4) for i in range(8)], axis=0)
    nv = np.concatenate([R[i]["nv"].reshape(2, 4, 256, 2, 64).transpose(1, 0, 2, 3, 4) for i in range(8)], axis=0)
    nsd = np.concatenate([R[i]["nsd"] for i in range(8)], axis=0)
    nss = np.concatenate([R[i]["nss"].reshape(4, 2, 2, 32, 64, 128) for i in range(8)], axis=0)
    out = (y_p, y_s, nk, nv, nsd, nss)
    return tuple(np.ascontiguousarray(o, dtype=np.float32) for o in out)


def kernel(**inputs):
    return _run(inputs, DEPTH)
```

```python
import math
import numpy as np
from contextlib import ExitStack
import concourse.bass as bass
import concourse.mybir as mybir
from concourse.bass_utils import run_bass_kernel_spmd

F32 = mybir.dt.float32
BF16 = mybir.dt.bfloat16
AF = mybir.ActivationFunctionType
ALU = mybir.AluOpType

D = 1024
DEPTH = 4
EPS = 1e-6
NEG = -30000.0


class Dep:
    __slots__ = ("w", "r", "excl")

    def __init__(self):
        self.w = None
        self.r = {}
        self.excl = False


class V:
    __slots__ = ("ap", "deps")

    def __init__(self, ap, deps):
        self.ap = ap
        self.deps = deps

    def __getitem__(self, k):
        return V(self.ap[k], self.deps)

    def re(self, pat, **kw):
        return V(self.ap.rearrange(pat, **kw), self.deps)

    def bc(self, shape):
        return V(self.ap.to_broadcast(list(shape)), self.deps)

    def un(self, axis):
        return V(self.ap.unsqueeze(axis), self.deps)


class _P:
    __slots__ = ("t", "deps")

    def __init__(self, t, deps):
        self.t = t
        self.deps = deps

    def __getitem__(self, k):
        return V(self.t[k], self.deps)


class T:
    def __init__(self, t, nparts=1):
        self.t = t
        self.deps = [Dep() for _ in range(nparts)]

    def __getitem__(self, k):
        return V(self.t[k], self.deps)

    def p(self, *idx):
        return _P(self.t, [self.deps[i] for i in idx])


class Rot:
    def __init__(self, items):
        self.items = items
        self.i = 0

    def next(self):
        x = self.items[self.i]
        self.i = (self.i + 1) % len(self.items)
        return x


def _ap(x):
    return x.ap if isinstance(x, V) else x


def _deps(*xs):
    out = []
    for x in xs:
        if isinstance(x, V):
            out.extend(x.deps)
    return out


class FW:
    def __init__(self, nc, stack, n_io=16, n_w=4):
        self.nc = nc
        self.stack = stack
        self.engs = {"pe": nc.tensor, "act": nc.scalar, "dve": nc.vector, "pool": nc.gpsimd, "sp": nc.sync}
        self.sems = {}
        self.cnt = {}
        for k in ["pe", "act", "dve"]:
            self.sems[k] = stack.enter_context(nc.semaphore("s_" + k))
            self.cnt[k] = 0
        self.io_ch = []
        for i in range(n_io):
            k = "io%d" % i
            self.sems[k] = stack.enter_context(nc.semaphore("s_" + k))
            self.cnt[k] = 0
            self.io_ch.append(k)
        self.w_ch = []
        for i in range(n_w):
            k = "w%d" % i
            self.sems[k] = stack.enter_context(nc.semaphore("s_" + k))
            self.cnt[k] = 0
            self.w_ch.append(k)
        self.nio = 0
        self.nw = 0
        self.waited = {k: {} for k in self.engs}
        self.n_instr = 0
        self.prog = {k: [] for k in self.engs}

    def flush(self):
        prog = self.prog
        self.prog = {k: [] for k in self.engs}
        with self.nc.Block() as block:
            def mk(lst):
                def body(e):
                    for f in lst:
                        f(e)
                return body
            block.sync(mk(prog["sp"]))
            block.tensor(mk(prog["pe"]))
            block.scalar(mk(prog["act"]))
            block.vector(mk(prog["dve"]))
            block.gpsimd(mk(prog["pool"]))

    def sb(self, name, shape, dtype, nparts=1, stack=None):
        st = stack or self.stack
        self.uid = getattr(self, "uid", 0) + 1
        return T(st.enter_context(self.nc.sbuf_tensor("%s_%d" % (name, self.uid), list(shape), dtype)), nparts)

    def ps(self, name, shape, dtype, nparts=1):
        t = T(self.stack.enter_context(self.nc.psum_tensor(name, list(shape), dtype)), nparts)
        for d in t.deps:
            d.excl = True
        return t

    def _wait(self, issuer, key, val):
        if val <= 0:
            return
        w = self.waited[issuer]
        if w.get(key, 0) >= val:
            return
        w[key] = val
        sem = self.sems[key]
        self.prog[issuer].append(lambda e: e.wait_ge(sem, val))

    def _gather(self, reads, writes):
        need = {}
        for d in reads:
            if d.w is not None:
                k, c = d.w
                if need.get(k, 0) < c:
                    need[k] = c
            if d.excl:
                for k, c in d.r.items():
                    if need.get(k, 0) < c:
                        need[k] = c
        for d in writes:
            if d.w is not None:
                k, c = d.w
                if need.get(k, 0) < c:
                    need[k] = c
            for k, c in d.r.items():
                if need.get(k, 0) < c:
                    need[k] = c
        return need

    def _commit(self, key, val, reads, writes):
        for d in writes:
            d.w = (key, val)
            d.r = {}
        for d in reads:
            if d.r.get(key, 0) < val:
                d.r[key] = val

    def op(self, eng, fn, reads=(), writes=()):
        need = self._gather(reads, writes)
        for k, c in need.items():
            if k == eng and eng == "pe":
                continue
            self._wait(eng, k, c)
        sem = self.sems[eng]
        self.prog[eng].append(lambda e: fn(e).then_inc(sem, 1))
        self.cnt[eng] += 1
        self._commit(eng, self.cnt[eng], reads, writes)
        self.n_instr += 1

    def dma(self, out, in_, q="sp"):
        reads = _deps(in_)
        writes = _deps(out)
        if q == "pool":
            ch = self.w_ch[self.nw % len(self.w_ch)]
            self.nw += 1
        else:
            ch = self.io_ch[self.nio % len(self.io_ch)]
            self.nio += 1
        need = self._gather(reads, writes)
        need[ch] = max(need.get(ch, 0), self.cnt[ch])
        for k, c in need.items():
            self._wait(q, k, c)
        sem = self.sems[ch]
        o, i = _ap(out), _ap(in_)
        self.prog[q].append(lambda e: e.dma_start(out=o, in_=i).then_inc(sem, 16))
        self.cnt[ch] += 16
        self._commit(ch, self.cnt[ch], reads, writes)
        self.n_instr += 1

    def barrier(self, final=False):
        keys = ["pe", "act", "dve"] + self.io_ch + (self.w_ch if final else [])
        for issuer in ["sp", "pe", "act", "dve"] + (["pool"] if final else []):
            for k in keys:
                if k != issuer:
                    self._wait(issuer, k, self.cnt[k])

    def mm(self, out, lhsT, rhs, start=True, stop=True):
        o, l, r = _ap(out), _ap(lhsT), _ap(rhs)
        self.op("pe", lambda e: e.matmul(o, lhsT=l, rhs=r, start=start, stop=stop),
                _deps(lhsT, rhs), _deps(out))

    def tr(self, out, in_, ident):
        o, i, d = _ap(out), _ap(in_), _ap(ident)
        self.op("pe", lambda e: e.transpose(o, i, d), _deps(in_, ident), _deps(out))

    def act(self, out, in_, func, bias=None, scale=None, accum=None):
        o, i = _ap(out), _ap(in_)
        kw = {}
        if bias is not None:
            kw["bias"] = _ap(bias)
        if scale is not None:
            kw["scale"] = _ap(scale)
        if accum is not None:
            kw["accum_out"] = _ap(accum)
        self.op("act", lambda e: e.activation(out=o, in_=i, func=func, **kw),
                _deps(in_, bias, scale), _deps(out, accum))

    def tt(self, out, in0, in1, op, eng="dve"):
        o, a, b = _ap(out), _ap(in0), _ap(in1)
        self.op(eng, lambda e: e.tensor_tensor(out=o, in0=a, in1=b, op=op), _deps(in0, in1), _deps(out))

    def ts(self, out, in0, s1, op0, s2=None, op1=None, eng="dve"):
        o, a, x1, x2 = _ap(out), _ap(in0), _ap(s1), _ap(s2)
        if op1 is None:
            self.op(eng, lambda e: e.tensor_scalar(out=o, in0=a, scalar1=x1, scalar2=None, op0=op0),
                    _deps(in0, s1), _deps(out))
        else:
            self.op(eng, lambda e: e.tensor_scalar(out=o, in0=a, scalar1=x1, scalar2=x2, op0=op0, op1=op1),
                    _deps(in0, s1, s2), _deps(out))

    def stt(self, out, in0, scalar, in1, op0, op1):
        o, a, s, b = _ap(out), _ap(in0), _ap(scalar), _ap(in1)
        self.op("dve", lambda e: e.scalar_tensor_tensor(out=o, in0=a, scalar=s, in1=b, op0=op0, op1=op1),
                _deps(in0, scalar, in1), _deps(out))

    def cp(self, eng, out, in_):
        o, i = _ap(out), _ap(in_)
        if eng == "act":
            self.op("act", lambda e: e.activation(out=o, in_=i, func=AF.Copy), _deps(in_), _deps(out))
        else:
            self.op("dve", lambda e: e.tensor_copy(out=o, in_=i), _deps(in_), _deps(out))

    def recip(self, out, in_):
        o, i = _ap(out), _ap(in_)
        self.op("dve", lambda e: e.reciprocal(out=o, in_=i), _deps(in_), _deps(out))

    def memset(self, out, val):
        o = _ap(out)
        self.op("dve", lambda e: e.memset(o, val), (), _deps(out))


VEC_LAYOUT = [("gmix", 32), ("gmlp", 32), ("bmod", 192), ("fing", 8), ("qkg", 4), ("dconv", 72),
              ("dalog", 16), ("ddtb", 16), ("dng", 2), ("sconv", 256), ("salog", 128), ("sdtb", 128),
              ("sd", 64), ("sng", 32)]
VOFF = {}
_o = 0
for _n, _c in VEC_LAYOUT:
    VOFF[_n] = (_o, _c)
    _o += _c
NVEC = _o

CST_LAYOUT = [("ident", 128), ("ones", 128), ("blk", 128), ("triF", 128), ("triB", 128), ("ind0", 128),
              ("ind1", 128), ("nmF", 256), ("nmB", 256), ("nmFc", 128), ("nmBc", 128), ("ropeR", 64)]
COFF = {}
_o = 0
for _n, _c in CST_LAYOUT:
    COFF[_n] = (_o, _c)
    _o += _c
NCST = _o


def make_consts():
    c = np.zeros((128, NCST), np.float32)
    idx = np.arange(128)
    j = idx[:, None]
    i = idx[None, :]
    same = (j // 64) == (i // 64)

    def put(name, a):
        o, n = COFF[name]
        c[:a.shape[0], o:o + a.shape[1]] = a

    put("ident", np.eye(128, dtype=np.float32))
    put("ones", np.ones((128, 128), np.float32))
    put("blk", same.astype(np.float32))
    put("triF", (same & (j <= i)).astype(np.float32))
    put("triB", (same & (j >= i)).astype(np.float32))
    put("ind0", np.broadcast_to((idx < 64)[:, None], (128, 128)).astype(np.float32))
    put("ind1", np.broadcast_to((idx >= 64)[:, None], (128, 128)).astype(np.float32))
    fc = np.where(same & (i >= j), 0.0, NEG).astype(np.float32)
    fs = np.where(same & (i > j), 0.0, NEG).astype(np.float32)
    bc = np.where(same & (i <= j), 0.0, NEG).astype(np.float32)
    bs = np.where(same & (i < j), 0.0, NEG).astype(np.float32)
    put("nmF", np.concatenate([fc, fs], axis=1))
    put("nmB", np.concatenate([bc, bs], axis=1))
    put("nmFc", fc)
    put("nmBc", bc)
    R = np.zeros((64, 64), np.float32)
    for base in (0, 32):
        for t in range(16):
            R[base + 16 + t, base + t] = -1.0
            R[base + t, base + 16 + t] = 1.0
    put("ropeR", R)
    return c


def make_rope():
    t = np.arange(1024)
    row = (t // 64).astype(np.float32)
    col = (t % 64).astype(np.float32)
    inv = (10000.0 ** (-np.arange(0, 32, 2, dtype=np.float32) / 32)).astype(np.float32)
    ar = row[None, :] * inv[:, None]
    ac = col[None, :] * inv[:, None]
    ang = np.concatenate([ar, ar, ac, ac], axis=0)
    return np.stack([np.cos(ang), np.sin(ang)], axis=1).astype(np.float32)


def even_col_perm():
    cols = []
    for g2 in range(2):
        for hh in range(4):
            h = 4 * g2 + hh
            cols += list(range(h * 64, h * 64 + 64))
        cols += list(range(512 + g2 * 64, 512 + g2 * 64 + 64))
        cols += list(range(640 + g2 * 64, 640 + g2 * 64 + 64))
    for hd in range(4):
        cols += list(range(768 + hd * 128, 768 + hd * 128 + 128))
        cols += list(range(768 + 512 + hd * 128, 768 + 512 + hd * 128 + 128))
        cols += list(range(768 + 1024 + hd * 128, 768 + 1024 + hd * 128 + 128))
        cols += list(range(2304 + hd * 128, 2304 + hd * 128 + 128))
        cols += [2816 + hd, 2816 + 4 + hd, 2824 + hd, 2824 + 4 + hd]
        cols += [-1] * 124
    return np.array(cols)


def odd_col_perm():
    cols = []
    for sg in range(8):
        cols += list(range(2048 + sg * 256, 2048 + sg * 256 + 256))
        cols += list(range(4096 + sg * 128, 4096 + sg * 128 + 128))
        cols += list(range(5120 + sg * 128, 5120 + sg * 128 + 128))
        cols += list(range(sg * 256, sg * 256 + 256))
        cols += list(range(6144 + sg * 4, 6144 + sg * 4 + 4))
        cols += list(range(6144 + 32 + sg * 4, 6144 + 32 + sg * 4 + 4))
        cols += [-1] * 56
    return np.array(cols)


def permute_cols(w, perm):
    w = np.asarray(w, np.float32)
    out = np.zeros(w.shape[:-1] + (len(perm),), np.float32)
    m = perm >= 0
    out[..., m] = w[..., perm[m]]
    return out


def colmajor(v):
    v = np.asarray(v, np.float32)
    lead = v.shape[:-1]
    C = v.shape[-1] // 128
    a = v.reshape(lead + (C, 128))
    a = np.moveaxis(a, -1, 0)
    return np.ascontiguousarray(a).reshape(128, -1)


def bcast_rows(v):
    v = np.asarray(v, np.float32).reshape(1, -1)
    return np.ascontiguousarray(np.broadcast_to(v, (128, v.shape[1])))


def make_vecs(inp):
    vec = np.zeros((128, NVEC), np.float32)

    def put(name, a):
        o, n = VOFF[name]
        assert a.shape[1] == n, (name, a.shape, n)
        vec[:a.shape[0], o:o + n] = a

    put("gmix", colmajor(inp["norm_mix_g"]))
    put("gmlp", colmajor(inp["norm_mlp_g"]))
    put("bmod", colmajor(inp["b_mod"]))
    put("fing", colmajor(inp["final_norm_g"]))
    qk = np.stack([inp["attn_q_norm_g"], inp["attn_k_norm_g"]], axis=1)
    put("qkg", np.ascontiguousarray(np.moveaxis(qk, -1, 0)).reshape(64, 4))
    dc = np.asarray(inp["delta_conv_w"], np.float32).reshape(2, 3, 3, 4, 128)
    dc = np.transpose(dc, (4, 0, 3, 2, 1))
    put("dconv", np.ascontiguousarray(dc).reshape(128, 72))
    put("dalog", bcast_rows(inp["delta_a_log"]))
    put("ddtb", bcast_rows(inp["delta_dt_bias"]))
    put("dng", np.ascontiguousarray(np.asarray(inp["delta_norm_g"], np.float32).T))
    cw = np.asarray(inp["ssm_conv_w"], np.float32)
    cb = np.asarray(inp["ssm_conv_b"], np.float32)
    wb = np.concatenate([cw, cb[:, None, :]], axis=1)
    sc = np.zeros((128, 2, 8, 4, 4), np.float32)
    for sg in range(8):
        chans = [(sg * 256, 128), (sg * 256 + 128, 128), (2048 + sg * 128, 128), (3072 + sg * 128, 128)]
        for ci, (c0, n) in enumerate(chans):
            sc[:, :, sg, ci, :] = np.transpose(wb[:, :, c0:c0 + 128], (2, 0, 1))
    put("sconv", sc.reshape(128, 256))
    put("salog", bcast_rows(inp["ssm_a_log"]))
    put("sdtb", bcast_rows(inp["ssm_dt_bias"]))
    put("sd", bcast_rows(inp["ssm_d"]))
    put("sng", colmajor(inp["ssm_norm_g"]))
    return vec


import os as _os0
SILU = AF.Identity if "silu" in _os0.environ.get("DBGSKIP", "") else AF.Silu


def build(n_layers=DEPTH, stop=None):
    nc = bass.Bass("TRN2", target_bir_lowering=False)
    dr = {}

    def din(name, shape):
        dr[name] = nc.dram_tensor(name, list(shape), F32, kind="ExternalInput").ap()

    def dout(name, shape):
        dr[name] = nc.dram_tensor(name, list(shape), F32, kind="ExternalOutput").ap()

    din("xin", [2048, D])
    din("condT", [128, 16])
    din("consts", [128, NCST])
    din("rope", [64, 2048])
    din("vecs", [128, NVEC])
    din("ctxk", [2, 512, 128])
    din("ctxv", [2, 512, 128])
    din("sd0", [2, 2, 4, 128, 128])
    din("ss0", [2, 2, 16, 128, 128])
    din("w_mod", [DEPTH, D, 6 * D])
    din("w_mlp_in", [DEPTH, D, 4 * D])
    din("w_mlp_out", [DEPTH, 4 * D, D])
    din("wie", [2, D, 3328])
    din("w_out_even", [2, D, D])
    din("wio", [2, D, 6656])
    din("w_out_odd", [2, 2 * D, D])
    dout("y", [2048, D])
    dout("nk", [2, 1024, 128])
    dout("nv", [2, 1024, 128])
    dout("nsd", [4, 2, 2, 4, 128, 128])
    dout("nss", [4, 2, 2, 16, 128, 128])

    with ExitStack() as st:
        fw = FW(nc, st)
        dv = lambda name: V(dr[name], [])

        import os as _os
        _pad = int(_os.environ.get("DBGPAD", "0"))
        if _pad:
            fw.sb("dbgpad", [128, _pad * 256], F32)
        xT = fw.sb("xT", [128, 8, 2048], F32, 32)
        cst = fw.sb("cst", [128, NCST], F32)
        vec = fw.sb("vec", [128, NVEC], F32)
        identb = fw.sb("identb", [128, 128], BF16)
        onesb = fw.sb("onesb", [128, 128], BF16)
        scT = fw.sb("scT", [128, 8, 2], BF16)
        modv2 = [fw.sb("modv%d" % i, [128, 48, 2], F32) for i in range(2)]
        modA2 = [fw.sb("modA%d" % i, [128, 2, 8, 2], F32) for i in range(2)]
        modv, modA = modv2[0], modA2[0]
        pump_holder = [None]

        def pump(k=1):
            g_ = pump_holder[0]
            if g_ is None:
                return
            for _ in range(k):
                try:
                    next(g_)
                except StopIteration:
                    pump_holder[0] = None
                    return
        ring = Rot([fw.sb("wr%d" % i, [128, 4096], BF16, 2) for i in range(3)])
        mmp = Rot([fw.ps("pm%d" % i, [128, 512], F32) for i in range(4)])
        accp = Rot([fw.ps("pa%d" % i, [128, 512], F32) for i in range(2)])
        trp = Rot([fw.ps("pt%d" % i, [128, 1024], BF16) for i in range(2)])

        def C(name, rows=128, c0=0, c1=None):
            o, n = COFF[name]
            c1 = n if c1 is None else c1
            return cst[0:rows, o + c0:o + c1]

        def VC(name, col, rows=128, n=1):
            o, _ = VOFF[name]
            return vec[0:rows, o + col:o + col + n]

        def xv(kc, tt):
            return xT.p(kc * 4 + tt)[:, kc, tt * 512:(tt + 1) * 512]

        def wload(src, KC, cols):
            t = ring.next()
            h = KC // 2
            s3 = src.rearrange("(kc p) c -> p kc c", p=128)
            for a in range(2):
                dst = t.p(a)[:, a * h * cols:(a + 1) * h * cols].re("p (kc c) -> p kc c", kc=h)
                fw.dma(dst, V(s3[:, a * h:(a + 1) * h, :], []), q="pool")
            return t[:, 0:KC * cols].re("p (kc c) -> p kc c", kc=KC)

        evac_i = [0]

        def evac(out, in_):
            evac_i[0] ^= 1
            fw.cp("act" if evac_i[0] else "dve", out, in_)

        fw.dma(cst[:, :], dv("consts"))
        fw.dma(vec[:, :], dv("vecs"))
        fw.cp("act", identb[:, :], C("ident"))
        fw.cp("dve", onesb[:, :], C("ones"))
        with ExitStack() as ph:
            ctmp = fw.sb("ctmp", [128, 16], F32, stack=ph)
            fw.dma(ctmp[:, :], dv("condT"))
            fw.act(scT[:, :, :].re("p a b -> p (a b)"), ctmp[:, :], SILU)
            xst = Rot([fw.sb("xst%d" % i, [128, D], F32, stack=ph) for i in range(2)])
            for ti in range(16):
                s = xst.next()
                fw.dma(s[:, :], dv("xin")[ti * 128:(ti + 1) * 128, :])
                for half in range(2):
                    ps = mmp.next()
                    for q in range(4):
                        kc = half * 4 + q
                        fw.tr(ps[:, q * 128:(q + 1) * 128], s[:, kc * 128:(kc + 1) * 128], C("ident"))
                    tt = ti // 4
                    dst = xT.p(*[(half * 4 + q) * 4 + tt for q in range(4)])[
                        :, half * 4:half * 4 + 4, ti * 128:(ti + 1) * 128]
                    evac(dst, ps[:, :].re("p (a b) -> p a b", a=4))
            fw.barrier()
            fw.flush()

        def mod_vectors(layer):
            modv, modA = modv2[layer % 2], modA2[layer % 2]
            for ot in range(12):
                wt = wload(dr["w_mod"][layer, :, ot * 512:(ot + 1) * 512], 8, 512)
                ps = mmp.next()
                for o4 in range(4):
                    for kc in range(8):
                        fw.mm(ps[:, o4 * 2:o4 * 2 + 2], wt[:, kc, o4 * 128:(o4 + 1) * 128], scT[:, kc, :],
                              start=(kc == 0), stop=(kc == 7))
                fw.tt(modv[:, ot * 4:(ot + 1) * 4, :], ps[:, 0:8].re("p (a b) -> p a b", a=4),
                      VC("bmod", layer * 48 + ot * 4, n=4).un(2).bc([128, 4, 2]), ALU.add)
                yield
            for which, (gname, sc0) in enumerate((("gmix", 8), ("gmlp", 32))):
                fw.ts(modA[:, which, :, :], modv[:, sc0:sc0 + 8, :], 1.0, ALU.add)
                fw.tt(modA[:, which, :, :], modA[:, which, :, :],
                      VC(gname, layer * 8, n=8).un(2).bc([128, 8, 2]), ALU.mult)

        def rstd_bc(ph, name, nfeat):
            sqp = Rot([fw.sb("%s_sq%d" % (name, i), [128, 512], BF16, stack=ph) for i in range(3)])
            rsp = Rot([fw.sb("%s_rs%d" % (name, i), [128, 512], F32, stack=ph) for i in range(2)])

            def f(srcs):
                ps = mmp.next()
                n = len(srcs)
                for i, s in enumerate(srcs):
                    sq = sqp.next()
                    fw.act(sq[:, :], s, AF.Square)
                    fw.mm(ps[:, :], onesb[:, :], sq[:, :], start=(i == 0), stop=(i == n - 1))
                rs = rsp.next()
                fw.act(rs[:, :], ps[:, :], AF.Sqrt, bias=EPS, scale=1.0 / nfeat)
                fw.recip(rs[:, :], rs[:, :])
                return rs
            return f

        def modulate(ph, name, hT, which, tts, t0, ntt):
            rfn = rstd_bc(ph, name, D)
            tmpp = Rot([fw.sb("%s_tmp%d" % (name, i), [128, 512], F32, stack=ph) for i in range(3)])
            sh0 = 0 if which == 0 else 24
            for tt in tts:
                r = 0 if tt < 2 else 1
                rs = rfn([xv(kc, tt) for kc in range(8)])
                for kc in range(8):
                    tmp = tmpp.next()
                    fw.tt(tmp[:, :], xv(kc, tt), rs[:, :], ALU.mult)
                    lt = tt - t0
                    fw.act(hT.p(kc * ntt + lt)[:, kc, lt * 512:(lt + 1) * 512], tmp[:, :], AF.Identity,
                           bias=modv[:, sh0 + kc, r:r + 1], scale=modA[:, which, kc, r:r + 1])

        def mlp(layer):
            with ExitStack() as ph:
                hT = fw.sb("hTm", [128, 8, 2048], BF16, 32, stack=ph)
                h1 = fw.sb("h1", [128, 8, 2048], BF16, 32, stack=ph)
                rl = Rot([fw.sb("rl%d" % i, [128, 512], F32, stack=ph) for i in range(3)])
                modulate(ph, "nm", hT, 1, range(4), 0, 4)
                for blk in range(4):
                    for ht in range(2):
                        c0 = blk * 1024 + ht * 512
                        wt = wload(dr["w_mlp_in"][layer, :, c0:c0 + 512], 8, 512)
                        for h4 in range(4):
                            hc = ht * 4 + h4
                            for tt in range(4):
                                ps = mmp.next()
                                for kc in range(8):
                                    fw.mm(ps[:, :], wt[:, kc, h4 * 128:(h4 + 1) * 128],
                                          hT.p(kc * 4 + tt)[:, kc, tt * 512:(tt + 1) * 512],
                                          start=(kc == 0), stop=(kc == 7))
                                r = rl.next()
                                fw.act(r[:, :], ps[:, :], AF.Relu)
                                fw.tt(h1.p(hc * 4 + tt)[:, hc, tt * 512:(tt + 1) * 512], r[:, :], r[:, :], ALU.mult)
                    for ot in range(2):
                        wt = wload(dr["w_mlp_out"][layer, blk * 1024:(blk + 1) * 1024, ot * 512:(ot + 1) * 512], 8, 512)
                        for o4 in range(4):
                            oc = ot * 4 + o4
                            for tt in range(4):
                                r = 0 if tt < 2 else 1
                                ps = mmp.next()
                                for kc in range(8):
                                    fw.mm(ps[:, :], wt[:, kc, o4 * 128:(o4 + 1) * 128],
                                          h1.p(kc * 4 + tt)[:, kc, tt * 512:(tt + 1) * 512],
                                          start=(kc == 0), stop=(kc == 7))
                                fw.stt(xv(oc, tt), ps[:, :], modv[:, 40 + oc, r:r + 1], xv(oc, tt), ALU.mult, ALU.add)
                fw.barrier()
                fw.flush()

        def final_out(lvl=9):
            with ExitStack() as ph:
                rfn = rstd_bc(ph, "fn", D)
                yT = fw.sb("yT", [128, 8, 512], F32, stack=ph)
                ost = Rot([fw.sb("ost%d" % i, [128, D], F32, stack=ph) for i in range(2)])
                for tt in range(4):
                    rs = rfn([xv(kc, tt) for kc in range(8)])
                    if lvl < 2:
                        continue
                    for kc in range(8):
                        fw.stt(yT[:, kc, :], xv(kc, tt), VC("fing", kc), rs[:, :], ALU.mult, ALU.mult)
                    if lvl < 3:
                        continue
                    for q in range(4):
                        ti = tt * 4 + q
                        o = ost.next()
                        for half in range(2):
                            ps = mmp.next()
                            for a in range(4):
                                kc = half * 4 + a
                                fw.tr(ps[:, a * 128:(a + 1) * 128], yT[:, kc, q * 128:(q + 1) * 128], C("ident"))
                            evac(o[:, half * 512:(half + 1) * 512], ps[:, :])
                        if lvl >= 4:
                            fw.dma(dv("y")[ti * 128:(ti + 1) * 128, :], o[:, :])
                fw.barrier()
                fw.flush()

        def out_proj_even(layer, g, catT):
            j = layer // 2
            for ot in range(2):
                wt = wload(dr["w_out_even"][j, :, ot * 512:(ot + 1) * 512], 8, 512)
                for o4 in range(4):
                    oc = ot * 4 + o4
                    for tg in range(2):
                        tt = 2 * g + tg
                        ps = mmp.next()
                        for kc in range(8):
                            fw.mm(ps[:, :], wt[:, kc, o4 * 128:(o4 + 1) * 128],
                                  catT.p(kc * 2 + tg)[:, kc, tg * 512:(tg + 1) * 512], start=(kc == 0), stop=(kc == 7))
                        fw.stt(xv(oc, tt), ps[:, :], modv[:, 16 + oc, g:g + 1], xv(oc, tt), ALU.mult, ALU.add)

        def out_proj_odd(ph, layer, g, catT):
            j = layer // 2
            rfn = rstd_bc(ph, "on", 2 * D)
            rsk = fw.sb("rsk", [128, 1024], F32, stack=ph)
            tmpp = Rot([fw.sb("opt%d" % i, [128, 512], F32, stack=ph) for i in range(2)])
            for tg in range(2):
                rs = rfn([catT.p(c * 2 + tg)[:, c, tg * 512:(tg + 1) * 512] for c in range(16)])
                fw.cp("dve", rsk[:, tg * 512:(tg + 1) * 512], rs[:, :])
            for ot in range(4):
                wt = wload(dr["w_out_odd"][j, :, ot * 256:(ot + 1) * 256], 16, 256)
                fw.tt(wt, wt, VC("sng", j * 16, n=16).un(2).bc([128, 16, 256]), ALU.mult)
                for o2 in range(2):
                    oc = ot * 2 + o2
                    for tg in range(2):
                        tt = 2 * g + tg
                        ps = mmp.next()
                        for kc in range(16):
                            fw.mm(ps[:, :], wt[:, kc, o2 * 128:(o2 + 1) * 128],
                                  catT.p(kc * 2 + tg)[:, kc, tg * 512:(tg + 1) * 512], start=(kc == 0), stop=(kc == 15))
                        tmp = tmpp.next()
                        fw.tt(tmp[:, :], ps[:, :], rsk[:, tg * 512:(tg + 1) * 512], ALU.mult)
                        fw.stt(xv(oc, tt), tmp[:, :], modv[:, 16 + oc, g:g + 1], xv(oc, tt), ALU.mult, ALU.add)
        def rsum(out, in_):
            o, i = _ap(out), _ap(in_)
            fw.op("dve", lambda e: e.reduce_sum(out=o, in_=i, axis=mybir.AxisListType.X), _deps(in_), _deps(out))

        def even_mixer(ph, layer, g, hT, catT, dbg=None):
            j = layer // 2
            ctx = (g == 0)
            nseq, L = (4, 256) if ctx else (1, 1024)
            tps = L // 128

            def hv(kc, lo, n):
                return hT.p(kc * 2 + lo // 512)[:, kc, lo:lo + n]

            with ExitStack() as ua:
                qT = fw.sb("qT", [64, 4, 1024], BF16, 4, stack=ua)
                kT = fw.sb("kT", [64, 1536], BF16, stack=ua)
                V1 = fw.sb("V1", [128, 12, 65], BF16, stack=ua)
                otok = fw.sb("otok", [128, 8, 256], BF16, 8, stack=ua)
                raw = Rot([fw.sb("araw%d" % i, [64, 1024], F32, stack=ua) for i in range(2)])
                qn = Rot([fw.sb("aqn%d" % i, [64, 512], F32, stack=ua) for i in range(2)])
                sqb = Rot([fw.sb("asq%d" % i, [64, 512], BF16, stack=ua) for i in range(2)])
                rsb = Rot([fw.sb("ars%d" % i, [64, 512], F32, stack=ua) for i in range(2)])
                t1p = Rot([fw.sb("at1%d" % i, [64, 512], F32, stack=ua) for i in range(2)])
                t2p = Rot([fw.sb("at2%d" % i, [64, 512], F32, stack=ua) for i in range(2)])
                ptp = Rot([fw.sb("apt%d" % i, [128, 512], BF16, stack=ua) for i in range(3)])
                rdp = Rot([fw.sb("ard%d" % i, [128, 1], F32, stack=ua) for i in range(4)])
                vst = Rot([fw.sb("avs%d" % i, [128, 64], F32, stack=ua) for i in range(2)])
                kst = Rot([fw.sb("aks%d" % i, [128, 64], F32, stack=ua) for i in range(2)])
                cks = Rot([fw.sb("ack%d" % i, [128, 128], F32, stack=ua) for i in range(2)])
                if not ctx:
                    rope = fw.sb("ropeT", [64, 2048], F32, stack=ua)
                    fw.dma(rope[:, :], dv("rope"))
                fw.memset(V1[:, :, 64:65], 1.0)
                for g2 in range(2):
                    wt = wload(dr["wie"][j, :, g2 * 384:(g2 + 1) * 384], 8, 384)

                    def proj_norm(col0, gain_col, dst_fn, is_k):
                        r = raw.next()
                        for tg in range(2):
                            ps = mmp.next()
                            for kc in range(8):
                                fw.mm(ps[0:64, :], wt[:, kc, col0:col0 + 64], hv(kc, tg * 512, 512),
                                      start=(kc == 0), stop=(kc == 7))
                            evac(r[:, tg * 512:(tg + 1) * 512], ps[0:64, :])
                        for tg in range(2):
                            sl = slice(tg * 512, (tg + 1) * 512)
                            sq = sqb.next()
                            fw.act(sq[:, :], r[:, sl], AF.Square)
                            ps = mmp.next()
                            fw.mm(ps[0:64, :], onesb[0:64, 0:64], sq[:, :])
                            rs = rsb.next()
                            fw.act(rs[:, :], ps[0:64, :], AF.Sqrt, bias=EPS, scale=1.0 / 64)
                            fw.recip(rs[:, :], rs[:, :])
                            if ctx and not is_k:
                                fw.stt(dst_fn(sl), r[:, sl], gain_col, rs[:, :], ALU.mult, ALU.mult)
                                continue
                            q = qn.next()
                            fw.stt(q[:, :], r[:, sl], gain_col, rs[:, :], ALU.mult, ALU.mult)
                            if ctx:
                                fw.cp("act", dst_fn(sl), q[:, :])
                                for t4 in range(4):
                                    ti = tg * 4 + t4
                                    ps2 = mmp.next()
                                    fw.tr(ps2[:, 0:64], q[:, t4 * 128:(t4 + 1) * 128], C("ident", 64, 0, 64))
                                    ks = kst.next()
                                    evac(ks[:, :], ps2[:, 0:64])
                                    fw.dma(dv("nk")[j, ti * 128:(ti + 1) * 128, g2 * 64:(g2 + 1) * 64], ks[:, :])
                            else:
                                ps2 = mmp.next()
                                fw.mm(ps2[0:64, :], C("ropeR", 64), q[:, :])
                                t1 = t1p.next()
                                fw.tt(t1[:, :], q[:, :], rope[:, sl], ALU.mult)
                                t2 = t2p.next()
                                fw.tt(t2[:, :], ps2[0:64, :], rope[:, 1024 + tg * 512:1024 + (tg + 1) * 512], ALU.mult)
                                fw.tt(dst_fn(sl), t1[:, :], t2[:, :], ALU.add)

                    for hh in range(4):
                        proj_norm(hh * 64, VC("qkg", j * 2 + 0, rows=64),
                                  (lambda sl, hh=hh: qT.p(hh)[:, hh, sl]), False)
                    proj_norm(256, VC("qkg", j * 2 + 1, rows=64), (lambda sl: kT[:, sl]), True)
                    for ti in range(8):
                        ps = mmp.next()
                        for kc in range(8):
                            fw.mm(ps[:, 0:64], hv(kc, ti * 128, 128), wt[:, kc, 320:384], start=(kc == 0), stop=(kc == 7))
                        fw.cp("act", V1[:, ti, 0:64], ps[:, 0:64])
                        if ctx:
                            vs = vst.next()
                            fw.cp("dve", vs[:, :], ps[:, 0:64])
                            fw.dma(dv("nv")[j, ti * 128:(ti + 1) * 128, g2 * 64:(g2 + 1) * 64], vs[:, :])
                    if not ctx:
                        for t in range(4):
                            ck = cks.next()
                            fw.dma(ck[:, 0:64], dv("ctxk")[j, t * 128:(t + 1) * 128, g2 * 64:(g2 + 1) * 64])
                            fw.dma(ck[:, 64:128], dv("ctxv")[j, t * 128:(t + 1) * 128, g2 * 64:(g2 + 1) * 64])
                            ps2 = mmp.next()
                            fw.tr(ps2[0:64, 0:128], ck[:, 0:64], C("ident"))
                            fw.cp("act", kT[:, 1024 + t * 128:1024 + (t + 1) * 128], ps2[0:64, 0:128])
                            fw.cp("dve", V1[:, 8 + t, 0:64], ck[:, 64:128])
                    QB = 256 if ctx else 512
                    for s in range(nseq):
                        if ctx:
                            kts = [(slice(ti * 128, (ti + 1) * 128), ti) for ti in (2 * s, 2 * s + 1)]
                        else:
                            kts = [(slice(t * 128, (t + 1) * 128), t) for t in range(12)]
                        for hh in range(4):
                            for qb in range(L // QB):
                                q0 = s * L + qb * QB
                                oacc = accp.next()
                                for idx, (ksl, vt) in enumerate(kts):
                                    ps = mmp.next()
                                    fw.mm(ps[:, 0:QB], kT[:, ksl], qT.p(hh)[:, hh, q0:q0 + QB])
                                    pt = ptp.next()
                                    fw.act(pt[:, 0:QB], ps[:, 0:QB], AF.Exp, scale=0.125)
                                    nqs = QB // 128
                                    for qs in range(nqs):
                                        fw.mm(oacc[:, qs * 128:qs * 128 + 65], pt[:, qs * 128:(qs + 1) * 128],
                                              V1[:, vt, :], start=(idx == 0 and qs == 0),
                                              stop=(idx == len(kts) - 1 and qs == nqs - 1))
                                for qs in range(QB // 128):
                                    ti = q0 // 128 + qs
                                    rd = rdp.next()
                                    fw.recip(rd[:, :], oacc[:, qs * 128 + 64:qs * 128 + 65])
                                    fw.act(otok.p(ti)[:, ti, hh * 64:(hh + 1) * 64], oacc[:, qs * 128:qs * 128 + 64],
                                           AF.Identity, scale=rd[:, 0:1])
                    for ti in range(8):
                        for c2 in range(2):
                            pT = trp.next()
                            fw.tr(pT[:, 0:128], otok.p(ti)[:, ti, c2 * 128:(c2 + 1) * 128], identb[:, :])
                            c = 2 * g2 + c2
                            evac(catT.p(c * 2 + ti // 4)[:, c, ti * 128:(ti + 1) * 128], pT[:, 0:128])
                    pump()
                fw.barrier()
                fw.flush()

            if dbg == "dbg_att":
                return
            with ExitStack() as ug:
                raw = Rot([fw.sb("graw%d" % i, [128, 1024], F32, stack=ug) for i in range(2)])
                cac = Rot([fw.sb("gcac%d" % i, [128, 1024], F32, stack=ug) for i in range(2)])
                qkf = Rot([fw.sb("gqkf%d" % i, [128, 1024], F32, stack=ug) for i in range(1)])
                fT = fw.sb("gfT", [128, 3, 1024], BF16, 3, stack=ug)
                ktok = fw.sb("gktok", [128, 8, 128], BF16, 8, stack=ug)
                vtok = fw.sb("gvtok", [128, 8, 128], BF16, 8, stack=ug)
                sz = fw.sb("gsz", [128, 8, 128], F32, 8, stack=ug)
                zraw = fw.sb("gzraw", [128, 8, 132], F32, 8, stack=ug)
                scr = zraw[:, :, 128:132]
                gs = fw.sb("ggs", [128, 10, 8, 2], F32, stack=ug)
                tmpa = fw.sb("gtmpa", [128, 8], F32, stack=ug)
                tmpb = fw.sb("gtmpb", [128, 8], F32, stack=ug)
                negA = fw.sb("gnegA", [128, 2], F32, stack=ug)
                decb = fw.sb("gdecb", [128, 2, 16], F32, stack=ug)
                oac = fw.sb("goac", [128, 8, 128], F32, 8, stack=ug)
                ssc = fw.sb("gssc", [128, 8], F32, stack=ug)
                junk = Rot([fw.sb("gjunk%d" % i, [128, 128], F32, stack=ug) for i in range(2)])
                onb = Rot([fw.sb("gonb%d" % i, [128, 128], BF16, stack=ug) for i in range(2)])
                sqb = Rot([fw.sb("gsq%d" % i, [128, 512], BF16, stack=ug) for i in range(2)])
                rsb = Rot([fw.sb("grs%d" % i, [128, 512], F32, stack=ug) for i in range(2)])
                S = [fw.sb("gS%d" % d, [128, 128], F32, stack=ug) for d in range(2)]
                Sb = [fw.sb("gSb%d" % d, [128, 128], BF16, stack=ug) for d in range(2)]
                dg = [fw.sb("gdg%d" % d, [128, 256], F32, stack=ug) for d in range(2)]
                dgb = [fw.sb("gdgb%d" % d, [128, 128], BF16, stack=ug) for d in range(2)]
                nmb = [fw.sb("gnmb%d" % d, [128, 256], BF16, stack=ug) for d in range(2)]
                for d in range(2):
                    fw.cp("act", nmb[d][:, :], C("nmF" if d == 0 else "nmB"))
                E2 = [fw.sb("gE2%d" % d, [128, 256], F32, stack=ug) for d in range(2)]
                Mm = [fw.sb("gMm%d" % d, [128, 128], F32, stack=ug) for d in range(2)]
                qkT = [[fw.sb("gqkT%d_%d" % (d, i), [128, 128], BF16, stack=ug) for i in range(2)] for d in range(2)]
                Bm = [[fw.sb("gB%d_%d" % (d, i), [128, 128], F32, stack=ug) for i in range(2)] for d in range(2)]
                AS = [[fw.sb("gAS%d_%d" % (d, i), [128, 256], F32, stack=ug) for i in range(2)] for d in range(2)]
                TT = [fw.sb("gTT%d" % d, [128, 128], BF16, stack=ug) for d in range(2)]
                vb = [fw.sb("gvb%d" % d, [128, 128], BF16, stack=ug) for d in range(2)]
                kbg = [fw.sb("gkbg%d" % d, [128, 128], BF16, stack=ug) for d in range(2)]
                kdec = [[fw.sb("gkdec%d_%d" % (d, i), [128, 128], BF16, stack=ug) for i in range(2)] for d in range(2)]
                qdT = [[fw.sb("gqdT%d_%d" % (d, i), [128, 128], BF16, stack=ug) for i in range(2)] for d in range(2)]
                usb = [[fw.sb("gusb%d_%d" % (d, i), [128, 128], F32, stack=ug) for i in range(2)] for d in range(2)]
                wTs = [[fw.sb("gwT%d_%d" % (d, i), [128, 128], BF16, stack=ug) for i in range(2)] for d in range(2)]
                vnew = [fw.sb("gvn%d" % d, [128, 128], BF16, stack=ug) for d in range(2)]
                for d in range(2):
                    fw.memset(vnew[d][:, :], 0.0)
                for hd in range(4):
                    base = 768 + hd * 640
                    wtA = wload(dr["wie"][j, :, base:base + 384], 8, 384)
                    import os
                    SK = os.environ.get("DBGSKIP", "")
                    if "wtb" not in SK:
                        wtB = wload(dr["wie"][j, :, base + 384:base + 640], 8, 256)
                    for wi in range(3):
                        r = raw.next()
                        for tg in range(2):
                            ps = mmp.next()
                            for kc in range(8):
                                fw.mm(ps[:, :], wtA[:, kc, wi * 128:(wi + 1) * 128], hv(kc, tg * 512, 512),
                                      start=(kc == 0), stop=(kc == 7))
                            evac(r[:, tg * 512:(tg + 1) * 512], ps[:, :])
                        a = cac.next()
                        cwc = lambda tap: VC("dconv", ((j * 4 + hd) * 3 + wi) * 3 + tap)
                        r3 = r[:, :].re("p (s l) -> p s l", s=nseq)
                        a3 = a[:, :].re("p (s l) -> p s l", s=nseq)
                        fw.ts(a[:, :], r[:, :], cwc(1), ALU.mult)
                        if "conv" not in SK:
                            fw.stt(a3[:, :, 1:L], r3[:, :, 0:L - 1], cwc(0), a3[:, :, 1:L], ALU.mult, ALU.add)
                            fw.stt(a3[:, :, 0:L - 1], r3[:, :, 1:L], cwc(2), a3[:, :, 0:L - 1], ALU.mult, ALU.add)
                        if wi == 2:
                            fw.act(fT.p(2)[:, 2, :], a[:, :], SILU)
                        else:
                            f = qkf.next()
                            fw.act(f[:, :], a[:, :], SILU)
                            for tg in range(2):
                                sl = slice(tg * 512, (tg + 1) * 512)
                                sq = sqb.next()
                                fw.act(sq[:, :], f[:, sl], AF.Square)
                                ps = mmp.next()
                                fw.mm(ps[:, :], onesb[:, :], sq[:, :])
                                rs = rsb.next()
                                fw.act(rs[:, :], ps[:, :], AF.Sqrt, bias=EPS, scale=1.0)
                                fw.recip(rs[:, :], rs[:, :])
                                fw.stt(fT.p(wi)[:, wi, sl], f[:, sl], (128 ** -0.5) if wi == 0 else 1.0, rs[:, :],
                                       ALU.mult, ALU.mult)
                    for ti in range(8):
                        tsl = slice(ti * 128, (ti + 1) * 128)
                        pT = trp.next()
                        fw.tr(pT[:, 0:128], fT.p(1)[:, 1, tsl], identb[:, :])
                        evac(ktok.p(ti)[:, ti, :], pT[:, 0:128])
                        pT = trp.next()
                        fw.tr(pT[:, 0:128], fT.p(2)[:, 2, tsl], identb[:, :])
                        evac(vtok.p(ti)[:, ti, :], pT[:, 0:128])
                    if dbg == "dbg_gdn1":
                        break
                    for ti in range(8):
                        ps = mmp.next()
                        for kc in range(8):
                            fw.mm(ps[:, 0:256], hv(kc, ti * 128, 128), wtB[:, kc, :], start=(kc == 0), stop=(kc == 7))
                        fw.cp("act", zraw.p(ti)[:, ti, :], ps[:, 0:132])
                        fw.act(sz.p(ti)[:, ti, :], zraw.p(ti)[:, ti, 0:128], SILU)
                    if "scal" in SK:
                        break
                    for d in range(2):
                        col = j * 8 + d * 4 + hd
                        fw.act(negA[:, d:d + 1], VC("dalog", col), AF.Exp)
                        fw.ts(negA[:, d:d + 1], negA[:, d:d + 1], -1.0, ALU.mult)
                        fw.act(tmpa[:, :], scr[:, :, 2 + d:3 + d].re('p t o -> p (t o)'), AF.Exp, bias=VC("ddtb", col), scale=1.0)
                        fw.act(tmpa[:, :], tmpa[:, :], AF.Ln, bias=1.0, scale=1.0)
                        fw.ts(gs[:, 2, :, d], tmpa[:, :], negA[:, d:d + 1], ALU.mult)
                        fw.act(tmpb[:, :], scr[:, :, d:d + 1].re('p t o -> p (t o)'), AF.Exp, scale=-1.0)
                        fw.act(gs[:, 1, :, d], tmpb[:, :], AF.Ln, bias=1.0, scale=1.0)
                        fw.act(gs[:, 0, :, d], gs[:, 1, :, d], AF.Exp, scale=-1.0)
                    if "cums" in SK:
                        break
                    ps = mmp.next()
                    fw.mm(ps[:, 0:8], C("triF"), gs[:, 2, :, 0])
                    fw.mm(ps[:, 8:16], C("triB"), gs[:, 2, :, 1])
                    fw.mm(ps[:, 16:24], C("blk"), gs[:, 2, :, 0])
                    fw.mm(ps[:, 24:32], C("blk"), gs[:, 2, :, 1])
                    for d in range(2):
                        fw.cp("dve", gs[:, 3, :, d], ps[:, d * 8:(d + 1) * 8])
                        fw.cp("dve", gs[:, 4, :, d], ps[:, 16 + d * 8:16 + (d + 1) * 8])
                    fw.act(gs[:, 5, :, :], gs[:, 3, :, :], AF.Exp)
                    fw.tt(gs[:, 6, :, :], gs[:, 4, :, :], gs[:, 3, :, :], ALU.subtract)
                    fw.act(gs[:, 6, :, :], gs[:, 6, :, :], AF.Exp)
                    fw.tt(gs[:, 7, :, :], gs[:, 3, :, :], gs[:, 1, :, :], ALU.subtract)
                    fw.ts(gs[:, 8, :, :], gs[:, 3, :, :], -1.0, ALU.mult)
                    fw.tt(gs[:, 9, :, :], gs[:, 0, :, :], gs[:, 5, :, :], ALU.mult)
                    ps = mmp.next()
                    g2d = gs[:, 2, :, :].re("p t d -> p (t d)")
                    fw.mm(ps[:, 0:16], C("ind0"), g2d)
                    fw.mm(ps[:, 16:32], C("ind1"), g2d)
                    fw.act(decb[:, :, :].re("p c n -> p (c n)"), ps[:, 0:32], AF.Exp)
                    for ti in range(8):
                        fw.memset(oac.p(ti)[:, ti, :], 0.0)
                    if dbg == "dbg_gdn2":
                        break
                    def gcol(k, ti, d):
                        return gs[:, k, ti, d:d + 1]

                    def solve(n):
                        par = n % 2
                        slots = [(n, 0), (7 - n, 1)]
                        for ti, d in slots:
                            fw.ts(dg[d][:, 0:128], C("ident"), gcol(3, ti, d), ALU.mult)
                            fw.ts(dg[d][:, 128:256], C("ident"), gcol(7, ti, d), ALU.mult)
                            fw.ts(dgb[d][:, :], identb[:, :], gcol(5, ti, d), ALU.mult)
                        yield
                        for ti, d in slots:
                            p = mmp.next()
                            fw.mm(p[:, 0:256], C("ones"), dg[d][:, 0:256], start=True, stop=False)
                            fw.mm(p[:, 0:256], identb[:, :], nmb[d][:, :], start=False, stop=True)
                            fw.mm(p[:, 256:384], onesb[:, :], dgb[d][:, :], start=True, stop=True)
                            fw.act(E2[d][:, :], p[:, 0:256], AF.Exp, bias=gcol(8, ti, d), scale=1.0)
                            fw.tt(qdT[d][par][:, :], fT.p(0)[:, 0, ti * 128:(ti + 1) * 128], p[:, 256:384], ALU.mult)
                        yield
                        for ti, d in slots:
                            tsl = slice(ti * 128, (ti + 1) * 128)
                            p = mmp.next()
                            fw.mm(p[:, 0:128], fT.p(1)[:, 1, tsl], fT.p(0)[:, 0, tsl])
                            fw.mm(p[:, 128:256], fT.p(1)[:, 1, tsl], fT.p(1)[:, 1, tsl])
                            fw.tt(qkT[d][par][:, :], p[:, 0:128], E2[d][:, 0:128], ALU.mult)
                            fw.tt(Mm[d][:, :], p[:, 128:256], E2[d][:, 128:256], ALU.mult)
                        yield
                        for ti, d in slots:
                            p = mmp.next()
                            fw.tr(p[:, 0:128], Mm[d][:, :], C("ident"))
                            fw.cp("act", Bm[d][0][:, :], p[:, 0:128])
                        yield
                        for ti, d in slots:
                            p = mmp.next()
                            fw.mm(p[:, 0:128], Bm[d][0][:, :], Mm[d][:, :])
                            fw.mm(p[:, 128:256], Mm[d][:, :], Bm[d][0][:, :])
                            fw.cp("act", AS[d][0][:, 0:128], p[:, 0:128])
                            fw.cp("dve", Bm[d][1][:, :], p[:, 128:256])
                            fw.tt(AS[d][0][:, 128:256], C("ident"), Mm[d][:, :], ALU.subtract)
                        yield
                        for k in range(1, 6):
                            for ti, d in slots:
                                cur, nxt, Bk, Bn = AS[d][(k - 1) % 2], AS[d][k % 2], Bm[d][k % 2], Bm[d][(k + 1) % 2]
                                p = mmp.next()
                                if k < 5:
                                    if k < 4:
                                        fw.mm(p[:, 0:256], Bk[:, :], cur[:, 0:256])
                                    else:
                                        fw.mm(p[:, 128:256], Bk[:, :], cur[:, 128:256])
                                    fw.mm(p[:, 256:384], cur[:, 0:128], Bk[:, :])
                                    fw.cp("act", Bn[:, :], p[:, 256:384])
                                    if k < 4:
                                        fw.cp("act", nxt[:, 0:128], p[:, 0:128])
                                    fw.tt(nxt[:, 128:256], cur[:, 128:256], p[:, 128:256], ALU.add)
                                else:
                                    fw.mm(p[:, 0:128], Bk[:, :], cur[:, 128:256])
                                    fw.tt(TT[d][:, :], cur[:, 128:256], p[:, 0:128], ALU.add)
                            yield
                        for ti, d in slots:
                            fw.ts(vb[d][:, :], vtok.p(ti)[:, ti, :], gcol(0, ti, d), ALU.mult)
                            fw.ts(kbg[d][:, :], ktok.p(ti)[:, ti, :], gcol(9, ti, d), ALU.mult)
                            fw.act(kdec[d][par][:, :], ktok.p(ti)[:, ti, :], AF.Identity, scale=gcol(6, ti, d))
                        yield
                        for ti, d in slots:
                            p = mmp.next()
                            fw.mm(p[:, 0:128], TT[d][:, :], vb[d][:, :])
                            fw.mm(p[:, 128:256], kbg[d][:, :], TT[d][:, :])
                            fw.cp("act", usb[d][par][:, :], p[:, 0:128])
                            fw.cp("dve", wTs[d][par][:, :], p[:, 128:256])
                        yield

                    def recur(n):
                        par = n % 2
                        slots = [(n, 0), (7 - n, 1)]
                        for ci in range(2):
                            for ti, d in slots:
                                c = ci if d == 0 else 1 - ci
                                rows = slice(c * 64, (c + 1) * 64)
                                first = (ti % tps == 0 and c == 0) if d == 0 else (ti % tps == tps - 1 and c == 1)
                                last = (ti % tps == tps - 1 and c == 1) if d == 0 else (ti % tps == 0 and c == 0)
                                seq = ti // tps
                                if first:
                                    if ctx:
                                        fw.memset(S[d][:, :], 0.0)
                                    else:
                                        fw.dma(S[d][:, :], dv("sd0")[j, d, hd, :, :])
                                    fw.cp("act", Sb[d][:, :], S[d][:, :])
                                p = mmp.next()
                                fw.mm(p[:, 0:128], wTs[d][par][:, :], Sb[d][:, :])
                                fw.tt(vnew[d][rows, :], usb[d][par][rows, :], p[rows, 0:128], ALU.subtract)
                                yield
                                po = mmp.next()
                                fw.mm(po[:, 0:128], qdT[d][par][:, :], Sb[d][:, :], start=True, stop=False)
                                fw.mm(po[:, 0:128], qkT[d][par][rows, :], vnew[d][rows, :], start=False, stop=True)
                                pst = mmp.next()
                                fw.mm(pst[:, 0:128], kdec[d][par][rows, :], vnew[d][rows, :])
                                fw.stt(S[d][:, :], S[d][:, :], decb[:, c, ti * 2 + d:ti * 2 + d + 1], pst[:, 0:128],
                                       ALU.mult, ALU.add)
                                fw.cp("act", Sb[d][:, :], S[d][:, :])
                                fw.tt(oac.p(ti)[rows, ti, :], oac.p(ti)[rows, ti, :], po[rows, 0:128], ALU.add)
                                if last and ctx:
                                    fw.dma(dv("nsd")[seq, j, d, hd, :, :], S[d][:, :])
                                yield

                    def drive(gens):
                        gens = list(gens)
                        while gens:
                            for g_ in list(gens):
                                try:
                                    next(g_)
                                except StopIteration:
                                    gens.remove(g_)

                    drive([solve(0)])
                    for n in range(8):
                        drive([recur(n)] + ([solve(n + 1)] if n < 7 else []))

                    if dbg == "dbg_gdn5":
                        break
                    for ti in range(8):
                        jk = junk.next()
                        fw.tt(jk[:, :], oac.p(ti)[:, ti, :], oac.p(ti)[:, ti, :], ALU.mult)
                        rsum(ssc[:, ti:ti + 1], jk[:, :])
                    fw.act(ssc[:, :], ssc[:, :], AF.Sqrt, bias=EPS, scale=1.0 / 128)
                    fw.recip(ssc[:, :], ssc[:, :])
                    for ti in range(8):
                        ob = onb.next()
                        fw.stt(ob[:, :], oac.p(ti)[:, ti, :], ssc[:, ti:ti + 1], sz.p(ti)[:, ti, :], ALU.mult, ALU.mult)
                        pT = trp.next()
                        fw.tr(pT[:, 0:128], ob[:, :], identb[:, :])
                        c = 4 + hd
                        fw.act(catT.p(c * 2 + ti // 4)[:, c, ti * 128:(ti + 1) * 128], pT[:, 0:128], AF.Identity,
                               scale=VC("dng", j))
                    pump()
                fw.barrier()
                fw.flush()

        def odd_mixer(ph, layer, g, hT, catT):
            j = layer // 2
            ctx = (g == 0)
            nseq, L = (4, 256) if ctx else (1, 1024)
            tps = L // 128

            def hv(kc, lo, n):
                return hT.p(kc * 2 + lo // 512)[:, kc, lo:lo + n]

            with ExitStack() as us:
                raw = Rot([fw.sb("sraw%d" % i, [128, 1024], F32, stack=us) for i in range(1)])
                cac = Rot([fw.sb("scac%d" % i, [128, 1024], F32, stack=us) for i in range(1)])
                fT = fw.sb("sfT", [128, 4, 1024], BF16, 4, stack=us)
                xtok = fw.sb("sxtok", [128, 8, 256], BF16, 8, stack=us)
                Btok = fw.sb("sBtok", [128, 8, 128], BF16, 8, stack=us)
                zs = fw.sb("szs", [128, 8, 256], BF16, 8, stack=us)
                dtr = fw.sb("sdtr", [128, 8, 8], F32, stack=us)
                zr = Rot([fw.sb("szr%d" % i, [128, 264], F32, stack=us) for i in range(2)])
                ss = fw.sb("sss", [128, 7, 8, 8], F32, stack=us)
                negA = fw.sb("snegA", [128, 8], F32, stack=us)
                decb = fw.sb("sdecb", [128, 2, 64], F32, stack=us)
                yac = fw.sb("syac", [128, 8, 256], F32, 8, stack=us)
                xdt = [fw.sb("sxdt%d" % d, [128, 256], BF16, stack=us) for d in range(2)]
                nm4b = [fw.sb("snm4b%d" % d, [128, 4, 128], BF16, stack=us) for d in range(2)]
                for d in range(2):
                    for h in range(4):
                        fw.cp("act", nm4b[d][:, h, :], C("nmFc" if d == 0 else "nmBc"))
                xd = [[fw.sb("sxd%d_%d" % (d, i), [128, 256], BF16, stack=us) for i in range(2)] for d in range(2)]
                cbs = [fw.sb("scbs%d" % d, [128, 128], F32, stack=us) for d in range(2)]
                dgs = [fw.sb("sdgs%d" % d, [128, 4, 128], F32, stack=us) for d in range(2)]
                Ls = dgs
                cbL = [fw.sb("scbL%d" % d, [128, 4, 128], BF16, stack=us) for d in range(2)]
                sT = [fw.sb("ssT%d" % d, [128, 256], F32, stack=us) for d in range(2)]
                sTb = [fw.sb("ssTb%d" % d, [128, 256], BF16, stack=us) for d in range(2)]
                ytmp = Rot([fw.sb("sytmp%d" % i, [128, 256], F32, stack=us) for i in range(2)])
                ygb = Rot([fw.sb("sygb%d" % i, [128, 256], BF16, stack=us) for i in range(2)])
                sst = Rot([fw.sb("ssst%d" % i, [128, 128], F32, stack=us) for i in range(2)])
                for sg in range(8):
                    base = sg * 832
                    wtA = wload(dr["wio"][j, :, base:base + 512], 8, 512)
                    wtB = wload(dr["wio"][j, :, base + 512:base + 832], 8, 320)
                    for wi in range(4):
                        r = raw.next()
                        for tg in range(2):
                            ps = mmp.next()
                            for kc in range(8):
                                fw.mm(ps[:, :], wtA[:, kc, wi * 128:(wi + 1) * 128], hv(kc, tg * 512, 512),
                                      start=(kc == 0), stop=(kc == 7))
                            evac(r[:, tg * 512:(tg + 1) * 512], ps[:, :])
                        a = cac.next()
                        cwc = lambda tap: VC("sconv", ((j * 8 + sg) * 4 + wi) * 4 + tap)
                        r3 = r[:, :].re("p (s l) -> p s l", s=nseq)
                        a3 = a[:, :].re("p (s l) -> p s l", s=nseq)
                        fw.ts(a[:, :], r[:, :], cwc(1), ALU.mult)
                        fw.stt(a3[:, :, 1:L], r3[:, :, 0:L - 1], cwc(0), a3[:, :, 1:L], ALU.mult, ALU.add)
                        fw.stt(a3[:, :, 0:L - 1], r3[:, :, 1:L], cwc(2), a3[:, :, 0:L - 1], ALU.mult, ALU.add)
                        fw.act(fT.p(wi)[:, wi, :], a[:, :], SILU, bias=cwc(3), scale=1.0)
                    for ti in range(8):
                        tsl = slice(ti * 128, (ti + 1) * 128)
                        for c2 in range(2):
                            pT = trp.next()
                            fw.tr(pT[:, 0:128], fT.p(c2)[:, c2, tsl], identb[:, :])
                            evac(xtok.p(ti)[:, ti, c2 * 128:(c2 + 1) * 128], pT[:, 0:128])
                        pT = trp.next()
                        fw.tr(pT[:, 0:128], fT.p(2)[:, 2, tsl], identb[:, :])
                        evac(Btok.p(ti)[:, ti, :], pT[:, 0:128])
                    for ti in range(8):
                        ps = mmp.next()
                        for kc in range(8):
                            fw.mm(ps[:, 0:320], hv(kc, ti * 128, 128), wtB[:, kc, :], start=(kc == 0), stop=(kc == 7))
                        fw.cp("act", zr.next()[:, :], ps[:, 0:264])
                        zr_ = zr.items[(zr.i - 1) % len(zr.items)]
                        fw.act(zs.p(ti)[:, ti, :], zr_[:, 0:256], SILU)
                        fw.cp("act", dtr[:, ti, :], zr_[:, 256:264])
                    for d in range(2):
                        c0 = j * 64 + d * 32 + sg * 4
                        fw.act(negA[:, d * 4:(d + 1) * 4], VC("salog", c0, n=4), AF.Exp)
                        fw.tt(ss[:, 0, :, d * 4:(d + 1) * 4], dtr[:, :, d * 4:(d + 1) * 4],
                              VC("sdtb", c0, n=4).un(1).bc([128, 8, 4]), ALU.add)
                    fw.ts(negA[:, :], negA[:, :], -1.0, ALU.mult)
                    fw.act(ss[:, 0, :, :], ss[:, 0, :, :], AF.Exp)
                    fw.act(ss[:, 0, :, :], ss[:, 0, :, :], AF.Ln, bias=1.0, scale=1.0)
                    fw.tt(ss[:, 1, :, :], ss[:, 0, :, :], negA[:, :].un(1).bc([128, 8, 8]), ALU.mult)
                    ps = mmp.next()
                    fw.mm(ps[:, 0:32], C("triF"), ss[:, 1, :, 0:4])
                    fw.mm(ps[:, 32:64], C("triB"), ss[:, 1, :, 4:8])
                    fw.mm(ps[:, 64:128], C("blk"), ss[:, 1, :, :])
                    for d in range(2):
                        fw.cp("dve", ss[:, 2, :, d * 4:(d + 1) * 4], ps[:, d * 32:(d + 1) * 32].re("p (t h) -> p t h", h=4))
                    fw.cp("dve", ss[:, 3, :, :], ps[:, 64:128].re("p (t h) -> p t h", h=8))
                    fw.act(ss[:, 4, :, :], ss[:, 2, :, :], AF.Exp)
                    fw.tt(ss[:, 5, :, :], ss[:, 3, :, :], ss[:, 2, :, :], ALU.subtract)
                    fw.act(ss[:, 5, :, :], ss[:, 5, :, :], AF.Exp)
                    fw.ts(ss[:, 6, :, :], ss[:, 2, :, :], -1.0, ALU.mult)
                    ps = mmp.next()
                    a2d = ss[:, 1, :, :].re("p t h -> p (t h)")
                    fw.mm(ps[:, 0:64], C("ind0"), a2d)
                    fw.mm(ps[:, 64:128], C("ind1"), a2d)
                    fw.act(decb[:, :, :].re("p c n -> p (c n)"), ps[:, 0:128], AF.Exp)
                    for ti in range(8):
                        fw.tt(yac.p(ti)[:, ti, :].re("p (h e) -> p h e", h=4),
                              xtok.p(ti)[:, ti, :].re("p (h e) -> p h e", h=4),
                              VC("sd", j * 32 + sg * 4, n=4).un(2).bc([128, 4, 64]), ALU.mult)
                    def lpart(n):
                        par = n % 2
                        slots = [(n, 0), (7 - n, 1)]
                        for ti, d in slots:
                            tsl = slice(ti * 128, (ti + 1) * 128)
                            dh = slice(d * 4, (d + 1) * 4)
                            x3 = xtok.p(ti)[:, ti, :].re("p (h e) -> p h e", h=4)
                            fw.tt(xdt[d][:, :].re("p (h e) -> p h e", h=4), x3,
                                  ss[:, 0, ti, dh].un(2).bc([128, 4, 64]), ALU.mult)
                            fw.tt(xd[d][par][:, :].re("p (h e) -> p h e", h=4), xdt[d][:, :].re("p (h e) -> p h e", h=4),
                                  ss[:, 5, ti, dh].un(2).bc([128, 4, 64]), ALU.mult)
                            p = mmp.next()
                            fw.mm(p[:, 0:128], fT.p(2)[:, 2, tsl], fT.p(3)[:, 3, tsl])
                            fw.cp("act", cbs[d][:, :], p[:, 0:128])
                            fw.tt(dgs[d][:, :, :], C("ident").un(1).bc([128, 4, 128]),
                                  ss[:, 2, ti, dh].un(2).bc([128, 4, 128]), ALU.mult)
                        yield
                        for ti, d in slots:
                            pL = mmp.next()
                            fw.mm(pL[:, :], C("ones"), dgs[d][:, :, :].re("p h i -> p (h i)"), start=True, stop=False)
                            fw.mm(pL[:, :], identb[:, :], nm4b[d][:, :, :].re("p h i -> p (h i)"), start=False, stop=True)
                            for h in range(4):
                                fw.act(Ls[d][:, h, :], pL[:, h * 128:(h + 1) * 128], AF.Exp,
                                       bias=ss[:, 6, ti, d * 4 + h:d * 4 + h + 1], scale=1.0)
                            yield
                        for ti, d in slots:
                            fw.tt(cbL[d][:, :, :], Ls[d][:, :, :], cbs[d][:, :].un(1).bc([128, 4, 128]), ALU.mult)
                        yield
                        for ti, d in slots:
                            pY = mmp.next()
                            for h in range(4):
                                fw.mm(pY[:, h * 64:(h + 1) * 64], cbL[d][:, h, :], xdt[d][:, h * 64:(h + 1) * 64])
                            fw.tt(yac.p(ti)[:, ti, :], yac.p(ti)[:, ti, :], pY[:, 0:256], ALU.add)
                            yield

                    def srec(n):
                        par = n % 2
                        slots = [(n, 0), (7 - n, 1)]
                        for ci in range(2):
                            for ti, d in slots:
                                tsl = slice(ti * 128, (ti + 1) * 128)
                                c = ci if d == 0 else 1 - ci
                                rows = slice(c * 64, (c + 1) * 64)
                                first = (ti % tps == 0 and c == 0) if d == 0 else (ti % tps == tps - 1 and c == 1)
                                last = (ti % tps == tps - 1 and c == 1) if d == 0 else (ti % tps == 0 and c == 0)
                                seq = ti // tps
                                if first:
                                    if ctx:
                                        fw.memset(sT[d][:, :], 0.0)
                                    else:
                                        for half in range(2):
                                            st_ = sst.next()
                                            fw.dma(st_[:, :], dv("ss0")[j, d, 2 * sg + half, :, :])
                                            p = mmp.next()
                                            fw.tr(p[:, 0:128], st_[:, :], C("ident"))
                                            fw.cp("act", sT[d][:, half * 128:(half + 1) * 128], p[:, 0:128])
                                    fw.cp("act", sTb[d][:, :], sT[d][:, :])
                                po = mmp.next()
                                fw.mm(po[:, 0:256], fT.p(3)[:, 3, tsl], sTb[d][:, :])
                                pS = mmp.next()
                                fw.mm(pS[:, 0:256], Btok.p(ti)[rows, ti, :], xd[d][par][rows, :])
                                yt = ytmp.next()
                                fw.tt(yt[rows, :].re("p (h e) -> p h e", h=4), po[rows, 0:256].re("p (h e) -> p h e", h=4),
                                      ss[rows, 4, ti, d * 4:(d + 1) * 4].un(2).bc([64, 4, 64]), ALU.mult)
                                fw.tt(sT[d][:, :].re("p (h e) -> p h e", h=4), sT[d][:, :].re("p (h e) -> p h e", h=4),
                                      decb[:, c, ti * 8 + d * 4:ti * 8 + d * 4 + 4].un(2).bc([128, 4, 64]), ALU.mult)
                                fw.tt(sT[d][:, :], sT[d][:, :], pS[:, 0:256], ALU.add)
                                fw.cp("act", sTb[d][:, :], sT[d][:, :])
                                fw.tt(yac.p(ti)[rows, ti, :], yac.p(ti)[rows, ti, :], yt[rows, :], ALU.add)
                                if last and ctx:
                                    for half in range(2):
                                        p = mmp.next()
                                        fw.tr(p[:, 0:128], sT[d][:, half * 128:(half + 1) * 128], C("ident"))
                                        st_ = sst.next()
                                        evac(st_[:, :], p[:, 0:128])
                                        fw.dma(dv("nss")[seq, j, d, 2 * sg + half, :, :], st_[:, :])
                                yield

                    def sdrive(gens):
                        gens = list(gens)
                        while gens:
                            for g_ in list(gens):
                                try:
                                    next(g_)
                                except StopIteration:
                                    gens.remove(g_)

                    sdrive([lpart(0)])
                    for n in range(8):
                        sdrive([srec(n)] + ([lpart(n + 1)] if n < 7 else []))
                    for ti in range(8):
                        yg = ygb.next()
                        fw.tt(yg[:, :], yac.p(ti)[:, ti, :], zs.p(ti)[:, ti, :], ALU.mult)
                        for c2 in range(2):
                            pT = trp.next()
                            fw.tr(pT[:, 0:128], yg[:, c2 * 128:(c2 + 1) * 128], identb[:, :])
                            c = 2 * sg + c2
                            evac(catT.p(c * 2 + ti // 4)[:, c, ti * 128:(ti + 1) * 128], pT[:, 0:128])
                    pump()
                fw.barrier()
                fw.flush()

        dbg = stop if (stop or "").startswith("dbg") else None
        nl_run = n_layers if (stop is None or dbg) else 0
        if nl_run > 0:
            for _ in mod_vectors(0):
                pass
        for layer in range(nl_run):
            modv, modA = modv2[layer % 2], modA2[layer % 2]
            pump_holder[0] = mod_vectors(layer + 1) if layer + 1 < nl_run else None
            if dbg == "dbg_mod":
                break
            for g in range(2):
                with ExitStack() as ph:
                    hT = fw.sb("hTg", [128, 8, 1024], BF16, 16, stack=ph)
                    with ExitStack() as ph2:
                        modulate(ph2, "nx", hT, 0, [2 * g, 2 * g + 1], 2 * g, 2)
                        fw.barrier()
                        fw.flush()
                    if dbg == "dbg_norm":
                        continue
                    if layer % 2 == 0:
                        catT = fw.sb("catT", [128, 8, 1024], BF16, 16, stack=ph)
                        even_mixer(ph, layer, g, hT, catT, dbg)
                        if dbg is None or dbg == "dbg_mlp":
                            out_proj_even(layer, g, catT)
                    else:
                        catT = fw.sb("catT", [128, 16, 1024], BF16, 32, stack=ph)
                        odd_mixer(ph, layer, g, hT, catT)
                        out_proj_odd(ph, layer, g, catT)
                    fw.barrier()
                    fw.flush()
            pump(100)
            if dbg in (None, "dbg_mlp"):
                mlp(layer)
        if stop is None or dbg:
            final_out()
        elif stop.startswith("final"):
            final_out(int(stop[5:]))
        fw.barrier(final=True)
        fw.flush()
    return nc, fw.n_instr


_PROG = {}


def _run(inp, n_layers=DEPTH, ncores=8):
    if n_layers not in _PROG:
        _PROG[n_layers] = build(n_layers)
    nc, _ = _PROG[n_layers]
    f = lambda k: np.ascontiguousarray(np.asarray(inp[k], dtype=np.float32))
    consts = make_consts()
    rope = np.ascontiguousarray(make_rope().reshape(64, 2048))
    vecs = make_vecs(inp)
    wie = permute_cols(f("w_in_even"), even_col_perm())
    wio = permute_cols(f("w_in_odd"), odd_col_perm())
    shared = {
        "consts": consts, "rope": rope, "vecs": vecs,
        "w_mod": f("w_mod"), "w_mlp_in": f("w_mlp_in"), "w_mlp_out": f("w_mlp_out"),
        "wie": wie, "w_out_even": f("w_out_even"), "wio": wio, "w_out_odd": f("w_out_odd"),
    }
    xp, xs, c, cctx = f("x_prompt"), f("x_sample"), f("c"), f("c_ctx")
    ck, cv, sd, ssm = f("cache_attn_k"), f("cache_attn_v"), f("state_delta"), f("state_ssm")
    in_maps = []
    for i in range(ncores):
        b = i % 4
        cond = np.stack([cctx, c[b]], axis=0)
        condT = np.ascontiguousarray(cond.reshape(2, 8, 128).transpose(2, 1, 0)).reshape(128, 16)
        m = dict(shared)
        m["xin"] = np.ascontiguousarray(np.concatenate([xp[4 * i:4 * i + 4].reshape(1024, D), xs[b]], axis=0))
        m["condT"] = condT
        m["ctxk"] = np.ascontiguousarray(ck[b].reshape(2, 512, 128))
        m["ctxv"] = np.ascontiguousarray(cv[b].reshape(2, 512, 128))
        m["sd0"] = np.ascontiguousarray(sd[b])
        m["ss0"] = np.ascontiguousarray(ssm[b].reshape(2, 2, 16, 128, 128))
        in_maps.append(m)
    res = run_bass_kernel_spmd(nc, in_maps, core_ids=list(range(ncores)))
    R = res.results
    if ncores < 8:
        return R
    y_p = np.concatenate([R[i]["y"][:1024].reshape(4, 256, D) for i in range(8)], axis=0)
    y_s = np.stack([R[b]["y"][1024:] for b in range(4)], axis=0)
    nk = np.concatenate([R[i]["nk"].reshape(2, 4, 256, 2, 64).transpose(1, 0, 2, 3, 4) for i in range(8)], axis=0)
    nv = np.concatenate([R[i]["nv"].reshape(2, 4, 256, 2, 64).transpose(1, 0, 2, 3, 4) for i in range(8)], axis=0)
    nsd = np.concatenate([R[i]["nsd"] for i in range(8)], axis=0)
    nss = np.concatenate([R[i]["nss"].reshape(4, 2, 2, 32, 64, 128) for i in range(8)], axis=0)
    out = (y_p, y_s, nk, nv, nsd, nss)
    return tuple(np.ascontiguousarray(o, dtype=np.float32) for o in out)


def kernel(**inputs):
    return _run(inputs, DEPTH)
```

```python
import math
import numpy as np
from contextlib import ExitStack
import concourse.bass as bass
import concourse.mybir as mybir
from concourse.bass_utils import run_bass_kernel_spmd

F32 = mybir.dt.float32
BF16 = mybir.dt.bfloat16
AF = mybir.ActivationFunctionType
ALU = mybir.AluOpType

D = 1024
DEPTH = 4
EPS = 1e-6
NEG = -30000.0


class Dep:
    __slots__ = ("w", "r", "excl")

    def __init__(self):
        self.w = None
        self.r = {}
        self.excl = False


class V:
    __slots__ = ("ap", "deps")

    def __init__(self, ap, deps):
        self.ap = ap
        self.deps = deps

    def __getitem__(self, k):
        return V(self.ap[k], self.deps)

    def re(self, pat, **kw):
        return V(self.ap.rearrange(pat, **kw), self.deps)

    def bc(self, shape):
        return V(self.ap.to_broadcast(list(shape)), self.deps)

    def un(self, axis):
        return V(self.ap.unsqueeze(axis), self.deps)


class _P:
    __slots__ = ("t", "deps")

    def __init__(self, t, deps):
        self.t = t
        self.deps = deps

    def __getitem__(self, k):
        return V(self.t[k], self.deps)


class T:
    def __init__(self, t, nparts=1):
        self.t = t
        self.deps = [Dep() for _ in range(nparts)]

    def __getitem__(self, k):
        return V(self.t[k], self.deps)

    def p(self, *idx):
        return _P(self.t, [self.deps[i] for i in idx])


class Rot:
    def __init__(self, items):
        self.items = items
        self.i = 0

    def next(self):
        x = self.items[self.i]
        self.i = (self.i + 1) % len(self.items)
        return x


def _ap(x):
    return x.ap if isinstance(x, V) else x


def _deps(*xs):
    out = []
    for x in xs:
        if isinstance(x, V):
            out.extend(x.deps)
    return out


class FW:
    def __init__(self, nc, stack, n_io=16, n_w=4):
        self.nc = nc
        self.stack = stack
        self.engs = {"pe": nc.tensor, "act": nc.scalar, "dve": nc.vector, "pool": nc.gpsimd, "sp": nc.sync}
        self.sems = {}
        self.cnt = {}
        for k in ["pe", "act", "dve"]:
            self.sems[k] = stack.enter_context(nc.semaphore("s_" + k))
            self.cnt[k] = 0
        self.io_ch = []
        for i in range(n_io):
            k = "io%d" % i
            self.sems[k] = stack.enter_context(nc.semaphore("s_" + k))
            self.cnt[k] = 0
            self.io_ch.append(k)
        self.w_ch = []
        for i in range(n_w):
            k = "w%d" % i
            self.sems[k] = stack.enter_context(nc.semaphore("s_" + k))
            self.cnt[k] = 0
            self.w_ch.append(k)
        self.nio = 0
        self.nw = 0
        self.waited = {k: {} for k in self.engs}
        self.n_instr = 0
        self.prog = {k: [] for k in self.engs}

    def flush(self):
        prog = self.prog
        self.prog = {k: [] for k in self.engs}
        with self.nc.Block() as block:
            def mk(lst):
                def body(e):
                    for f in lst:
                        f(e)
                return body
            block.sync(mk(prog["sp"]))
            block.tensor(mk(prog["pe"]))
            block.scalar(mk(prog["act"]))
            block.vector(mk(prog["dve"]))
            block.gpsimd(mk(prog["pool"]))

    def sb(self, name, shape, dtype, nparts=1, stack=None):
        st = stack or self.stack
        self.uid = getattr(self, "uid", 0) + 1
        return T(st.enter_context(self.nc.sbuf_tensor("%s_%d" % (name, self.uid), list(shape), dtype)), nparts)

    def ps(self, name, shape, dtype, nparts=1):
        t = T(self.stack.enter_context(self.nc.psum_tensor(name, list(shape), dtype)), nparts)
        for d in t.deps:
            d.excl = True
        return t

    def _wait(self, issuer, key, val):
        if val <= 0:
            return
        w = self.waited[issuer]
        if w.get(key, 0) >= val:
            return
        w[key] = val
        sem = self.sems[key]
        self.prog[issuer].append(lambda e: e.wait_ge(sem, val))

    def _gather(self, reads, writes):
        need = {}
        for d in reads:
            if d.w is not None:
                k, c = d.w
                if need.get(k, 0) < c:
                    need[k] = c
            if d.excl:
                for k, c in d.r.items():
                    if need.get(k, 0) < c:
                        need[k] = c
        for d in writes:
            if d.w is not None:
                k, c = d.w
                if need.get(k, 0) < c:
                    need[k] = c
            for k, c in d.r.items():
                if need.get(k, 0) < c:
                    need[k] = c
        return need

    def _commit(self, key, val, reads, writes):
        for d in writes:
            d.w = (key, val)
            d.r = {}
        for d in reads:
            if d.r.get(key, 0) < val:
                d.r[key] = val

    def op(self, eng, fn, reads=(), writes=()):
        need = self._gather(reads, writes)
        for k, c in need.items():
            if k == eng and eng == "pe":
                continue
            self._wait(eng, k, c)
        sem = self.sems[eng]
        self.prog[eng].append(lambda e: fn(e).then_inc(sem, 1))
        self.cnt[eng] += 1
        self._commit(eng, self.cnt[eng], reads, writes)
        self.n_instr += 1

    def dma(self, out, in_, q="sp"):
        reads = _deps(in_)
        writes = _deps(out)
        if q == "pool":
            ch = self.w_ch[self.nw % len(self.w_ch)]
            self.nw += 1
        else:
            ch = self.io_ch[self.nio % len(self.io_ch)]
            self.nio += 1
        need = self._gather(reads, writes)
        need[ch] = max(need.get(ch, 0), self.cnt[ch])
        for k, c in need.items():
            self._wait(q, k, c)
        sem = self.sems[ch]
        o, i = _ap(out), _ap(in_)
        self.prog[q].append(lambda e: e.dma_start(out=o, in_=i).then_inc(sem, 16))
        self.cnt[ch] += 16
        self._commit(ch, self.cnt[ch], reads, writes)
        self.n_instr += 1

    def barrier(self, final=False):
        keys = ["pe", "act", "dve"] + self.io_ch + (self.w_ch if final else [])
        for issuer in ["sp", "pe", "act", "dve"] + (["pool"] if final else []):
            for k in keys:
                if k != issuer:
                    self._wait(issuer, k, self.cnt[k])

    def mm(self, out, lhsT, rhs, start=True, stop=True):
        o, l, r = _ap(out), _ap(lhsT), _ap(rhs)
        self.op("pe", lambda e: e.matmul(o, lhsT=l, rhs=r, start=start, stop=stop),
                _deps(lhsT, rhs), _deps(out))

    def tr(self, out, in_, ident):
        o, i, d = _ap(out), _ap(in_), _ap(ident)
        self.op("pe", lambda e: e.transpose(o, i, d), _deps(in_, ident), _deps(out))

    def act(self, out, in_, func, bias=None, scale=None, accum=None):
        o, i = _ap(out), _ap(in_)
        kw = {}
        if bias is not None:
            kw["bias"] = _ap(bias)
        if scale is not None:
            kw["scale"] = _ap(scale)
        if accum is not None:
            kw["accum_out"] = _ap(accum)
        self.op("act", lambda e: e.activation(out=o, in_=i, func=func, **kw),
                _deps(in_, bias, scale), _deps(out, accum))

    def tt(self, out, in0, in1, op, eng="dve"):
        o, a, b = _ap(out), _ap(in0), _ap(in1)
        self.op(eng, lambda e: e.tensor_tensor(out=o, in0=a, in1=b, op=op), _deps(in0, in1), _deps(out))

    def ts(self, out, in0, s1, op0, s2=None, op1=None, eng="dve"):
        o, a, x1, x2 = _ap(out), _ap(in0), _ap(s1), _ap(s2)
        if op1 is None:
            self.op(eng, lambda e: e.tensor_scalar(out=o, in0=a, scalar1=x1, scalar2=None, op0=op0),
                    _deps(in0, s1), _deps(out))
        else:
            self.op(eng, lambda e: e.tensor_scalar(out=o, in0=a, scalar1=x1, scalar2=x2, op0=op0, op1=op1),
                    _deps(in0, s1, s2), _deps(out))

    def stt(self, out, in0, scalar, in1, op0, op1):
        o, a, s, b = _ap(out), _ap(in0), _ap(scalar), _ap(in1)
        self.op("dve", lambda e: e.scalar_tensor_tensor(out=o, in0=a, scalar=s, in1=b, op0=op0, op1=op1),
                _deps(in0, scalar, in1), _deps(out))

    def cp(self, eng, out, in_):
        o, i = _ap(out), _ap(in_)
        if eng == "act":
            self.op("act", lambda e: e.activation(out=o, in_=i, func=AF.Copy), _deps(in_), _deps(out))
        else:
            self.op("dve", lambda e: e.tensor_copy(out=o, in_=i), _deps(in_), _deps(out))

    def recip(self, out, in_):
        o, i = _ap(out), _ap(in_)
        self.op("dve", lambda e: e.reciprocal(out=o, in_=i), _deps(in_), _deps(out))

    def memset(self, out, val):
        o = _ap(out)
        self.op("dve", lambda e: e.memset(o, val), (), _deps(out))


VEC_LAYOUT = [("gmix", 32), ("gmlp", 32), ("bmod", 192), ("fing", 8), ("qkg", 4), ("dconv", 72),
              ("dalog", 16), ("ddtb", 16), ("dng", 2), ("sconv", 256), ("salog", 128), ("sdtb", 128),
              ("sd", 64), ("sng", 32)]
VOFF = {}
_o = 0
for _n, _c in VEC_LAYOUT:
    VOFF[_n] = (_o, _c)
    _o += _c
NVEC = _o

CST_LAYOUT = [("ident", 128), ("ones", 128), ("blk", 128), ("triF", 128), ("triB", 128), ("ind0", 128),
              ("ind1", 128), ("nmF", 256), ("nmB", 256), ("nmFc", 128), ("nmBc", 128), ("ropeR", 64)]
COFF = {}
_o = 0
for _n, _c in CST_LAYOUT:
    COFF[_n] = (_o, _c)
    _o += _c
NCST = _o


def make_consts():
    c = np.zeros((128, NCST), np.float32)
    idx = np.arange(128)
    j = idx[:, None]
    i = idx[None, :]
    same = (j // 64) == (i // 64)

    def put(name, a):
        o, n = COFF[name]
        c[:a.shape[0], o:o + a.shape[1]] = a

    put("ident", np.eye(128, dtype=np.float32))
    put("ones", np.ones((128, 128), np.float32))
    put("blk", same.astype(np.float32))
    put("triF", (same & (j <= i)).astype(np.float32))
    put("triB", (same & (j >= i)).astype(np.float32))
    put("ind0", np.broadcast_to((idx < 64)[:, None], (128, 128)).astype(np.float32))
    put("ind1", np.broadcast_to((idx >= 64)[:, None], (128, 128)).astype(np.float32))
    fc = np.where(same & (i >= j), 0.0, NEG).astype(np.float32)
    fs = np.where(same & (i > j), 0.0, NEG).astype(np.float32)
    bc = np.where(same & (i <= j), 0.0, NEG).astype(np.float32)
    bs = np.where(same & (i < j), 0.0, NEG).astype(np.float32)
    put("nmF", np.concatenate([fc, fs], axis=1))
    put("nmB", np.concatenate([bc, bs], axis=1))
    put("nmFc", fc)
    put("nmBc", bc)
    R = np.zeros((64, 64), np.float32)
    for base in (0, 32):
        for t in range(16):
            R[base + 16 + t, base + t] = -1.0
            R[base + t, base + 16 + t] = 1.0
    put("ropeR", R)
    return c


def make_rope():
    t = np.arange(1024)
    row = (t // 64).astype(np.float32)
    col = (t % 64).astype(np.float32)
    inv = (10000.0 ** (-np.arange(0, 32, 2, dtype=np.float32) / 32)).astype(np.float32)
    ar = row[None, :] * inv[:, None]
    ac = col[None, :] * inv[:, None]
    ang = np.concatenate([ar, ar, ac, ac], axis=0)
    return np.stack([np.cos(ang), np.sin(ang)], axis=1).astype(np.float32)


def even_col_perm():
    cols = []
    for g2 in range(2):
        for hh in range(4):
            h = 4 * g2 + hh
            cols += list(range(h * 64, h * 64 + 64))
        cols += list(range(512 + g2 * 64, 512 + g2 * 64 + 64))
        cols += list(range(640 + g2 * 64, 640 + g2 * 64 + 64))
    for hd in range(4):
        cols += list(range(768 + hd * 128, 768 + hd * 128 + 128))
        cols += list(range(768 + 512 + hd * 128, 768 + 512 + hd * 128 + 128))
        cols += list(range(768 + 1024 + hd * 128, 768 + 1024 + hd * 128 + 128))
        cols += list(range(2304 + hd * 128, 2304 + hd * 128 + 128))
        cols += [2816 + hd, 2816 + 4 + hd, 2824 + hd, 2824 + 4 + hd]
        cols += [-1] * 124
    return np.array(cols)


def odd_col_perm():
    cols = []
    for sg in range(8):
        cols += list(range(2048 + sg * 256, 2048 + sg * 256 + 256))
        cols += list(range(4096 + sg * 128, 4096 + sg * 128 + 128))
        cols += list(range(5120 + sg * 128, 5120 + sg * 128 + 128))
        cols += list(range(sg * 256, sg * 256 + 256))
        cols += list(range(6144 + sg * 4, 6144 + sg * 4 + 4))
        cols += list(range(6144 + 32 + sg * 4, 6144 + 32 + sg * 4 + 4))
        cols += [-1] * 56
    return np.array(cols)


def permute_cols(w, perm):
    w = np.asarray(w, np.float32)
    out = np.zeros(w.shape[:-1] + (len(perm),), np.float32)
    m = perm >= 0
    out[..., m] = w[..., perm[m]]
    return out


def colmajor(v):
    v = np.asarray(v, np.float32)
    lead = v.shape[:-1]
    C = v.shape[-1] // 128
    a = v.reshape(lead + (C, 128))
    a = np.moveaxis(a, -1, 0)
    return np.ascontiguousarray(a).reshape(128, -1)


def bcast_rows(v):
    v = np.asarray(v, np.float32).reshape(1, -1)
    return np.ascontiguousarray(np.broadcast_to(v, (128, v.shape[1])))


def make_vecs(inp):
    vec = np.zeros((128, NVEC), np.float32)

    def put(name, a):
        o, n = VOFF[name]
        assert a.shape[1] == n, (name, a.shape, n)
        vec[:a.shape[0], o:o + n] = a

    put("gmix", colmajor(inp["norm_mix_g"]))
    put("gmlp", colmajor(inp["norm_mlp_g"]))
    put("bmod", colmajor(inp["b_mod"]))
    put("fing", colmajor(inp["final_norm_g"]))
    qk = np.stack([inp["attn_q_norm_g"], inp["attn_k_norm_g"]], axis=1)
    put("qkg", np.ascontiguousarray(np.moveaxis(qk, -1, 0)).reshape(64, 4))
    dc = np.asarray(inp["delta_conv_w"], np.float32).reshape(2, 3, 3, 4, 128)
    dc = np.transpose(dc, (4, 0, 3, 2, 1))
    put("dconv", np.ascontiguousarray(dc).reshape(128, 72))
    put("dalog", bcast_rows(inp["delta_a_log"]))
    put("ddtb", bcast_rows(inp["delta_dt_bias"]))
    put("dng", np.ascontiguousarray(np.asarray(inp["delta_norm_g"], np.float32).T))
    cw = np.asarray(inp["ssm_conv_w"], np.float32)
    cb = np.asarray(inp["ssm_conv_b"], np.float32)
    wb = np.concatenate([cw, cb[:, None, :]], axis=1)
    sc = np.zeros((128, 2, 8, 4, 4), np.float32)
    for sg in range(8):
        chans = [(sg * 256, 128), (sg * 256 + 128, 128), (2048 + sg * 128, 128), (3072 + sg * 128, 128)]
        for ci, (c0, n) in enumerate(chans):
            sc[:, :, sg, ci, :] = np.transpose(wb[:, :, c0:c0 + 128], (2, 0, 1))
    put("sconv", sc.reshape(128, 256))
    put("salog", bcast_rows(inp["ssm_a_log"]))
    put("sdtb", bcast_rows(inp["ssm_dt_bias"]))
    put("sd", bcast_rows(inp["ssm_d"]))
    put("sng", colmajor(inp["ssm_norm_g"]))
    return vec


import os as _os0
SILU = AF.Identity if "silu" in _os0.environ.get("DBGSKIP", "") else AF.Silu


def build(n_layers=DEPTH, stop=None):
    nc = bass.Bass("TRN2", target_bir_lowering=False)
    dr = {}

    def din(name, shape):
        dr[name] = nc.dram_tensor(name, list(shape), F32, kind="ExternalInput").ap()

    def dout(name, shape):
        dr[name] = nc.dram_tensor(name, list(shape), F32, kind="ExternalOutput").ap()

    din("xin", [2048, D])
    din("condT", [128, 16])
    din("consts", [128, NCST])
    din("rope", [64, 2048])
    din("vecs", [128, NVEC])
    din("ctxk", [2, 512, 128])
    din("ctxv", [2, 512, 128])
    din("sd0", [2, 2, 4, 128, 128])
    din("ss0", [2, 2, 16, 128, 128])
    din("w_mod", [DEPTH, D, 6 * D])
    din("w_mlp_in", [DEPTH, D, 4 * D])
    din("w_mlp_out", [DEPTH, 4 * D, D])
    din("wie", [2, D, 3328])
    din("w_out_even", [2, D, D])
    din("wio", [2, D, 6656])
    din("w_out_odd", [2, 2 * D, D])
    dout("y", [2048, D])
    dout("nk", [2, 1024, 128])
    dout("nv", [2, 1024, 128])
    dout("nsd", [4, 2, 2, 4, 128, 128])
    dout("nss", [4, 2, 2, 16, 128, 128])

    with ExitStack() as st:
        fw = FW(nc, st)
        dv = lambda name: V(dr[name], [])

        import os as _os
        _pad = int(_os.environ.get("DBGPAD", "0"))
        if _pad:
            fw.sb("dbgpad", [128, _pad * 256], F32)
        xT = fw.sb("xT", [128, 8, 2048], F32, 32)
        cst = fw.sb("cst", [128, NCST], F32)
        vec = fw.sb("vec", [128, NVEC], F32)
        identb = fw.sb("identb", [128, 128], BF16)
        onesb = fw.sb("onesb", [128, 128], BF16)
        scT = fw.sb("scT", [128, 8, 2], BF16)
        modv2 = [fw.sb("modv%d" % i, [128, 48, 2], F32) for i in range(2)]
        modA2 = [fw.sb("modA%d" % i, [128, 2, 8, 2], F32) for i in range(2)]
        modv, modA = modv2[0], modA2[0]
        pump_holder = [None]

        def pump(k=1):
            g_ = pump_holder[0]
            if g_ is None:
                return
            for _ in range(k):
                try:
                    next(g_)
                except StopIteration:
                    pump_holder[0] = None
                    return
        ring = Rot([fw.sb("wr%d" % i, [128, 4096], BF16, 2) for i in range(3)])
        mmp = Rot([fw.ps("pm%d" % i, [128, 512], F32) for i in range(4)])
        accp = Rot([fw.ps("pa%d" % i, [128, 512], F32) for i in range(2)])
        trp = Rot([fw.ps("pt%d" % i, [128, 1024], BF16) for i in range(2)])

        def C(name, rows=128, c0=0, c1=None):
            o, n = COFF[name]
            c1 = n if c1 is None else c1
            return cst[0:rows, o + c0:o + c1]

        def VC(name, col, rows=128, n=1):
            o, _ = VOFF[name]
            return vec[0:rows, o + col:o + col + n]

        def xv(kc, tt):
            return xT.p(kc * 4 + tt)[:, kc, tt * 512:(tt + 1) * 512]

        def wload(src, KC, cols):
            t = ring.next()
            h = KC // 2
            s3 = src.rearrange("(kc p) c -> p kc c", p=128)
            for a in range(2):
                dst = t.p(a)[:, a * h * cols:(a + 1) * h * cols].re("p (kc c) -> p kc c", kc=h)
                fw.dma(dst, V(s3[:, a * h:(a + 1) * h, :], []), q="pool")
            return t[:, 0:KC * cols].re("p (kc c) -> p kc c", kc=KC)

        import os as _osw
        WARM = int(_osw.environ.get("KWARM", "2"))

        def warm(k=None):
            k = WARM if k is None else k
            for _ in range(k):
                fw.mm(accp.items[0][:, 0:256], identb[:, :], cst_b[:, 0:256])

        evac_i = [0]

        def evac(out, in_):
            evac_i[0] ^= 1
            fw.cp("act" if evac_i[0] else "dve", out, in_)

        fw.dma(cst[:, :], dv("consts"))
        fw.dma(vec[:, :], dv("vecs"))
        fw.cp("act", identb[:, :], C("ident"))
        fw.cp("dve", onesb[:, :], C("ones"))
        cst_b = fw.sb("cst_b", [128, 256], BF16)
        fw.cp("act", cst_b[:, :], C("nmF"))
        with ExitStack() as ph:
            ctmp = fw.sb("ctmp", [128, 16], F32, stack=ph)
            fw.dma(ctmp[:, :], dv("condT"))
            fw.act(scT[:, :, :].re("p a b -> p (a b)"), ctmp[:, :], SILU)
            xst = Rot([fw.sb("xst%d" % i, [128, D], F32, stack=ph) for i in range(2)])
            for ti in range(16):
                s = xst.next()
                fw.dma(s[:, :], dv("xin")[ti * 128:(ti + 1) * 128, :])
                for half in range(2):
                    ps = mmp.next()
                    for q in range(4):
                        kc = half * 4 + q
                        fw.tr(ps[:, q * 128:(q + 1) * 128], s[:, kc * 128:(kc + 1) * 128], C("ident"))
                    tt = ti // 4
                    dst = xT.p(*[(half * 4 + q) * 4 + tt for q in range(4)])[
                        :, half * 4:half * 4 + 4, ti * 128:(ti + 1) * 128]
                    evac(dst, ps[:, :].re("p (a b) -> p a b", a=4))
            fw.barrier()
            fw.flush()

        def mod_vectors(layer):
            modv, modA = modv2[layer % 2], modA2[layer % 2]
            for ot in range(12):
                wt = wload(dr["w_mod"][layer, :, ot * 512:(ot + 1) * 512], 8, 512)
                ps = mmp.next()
                for o4 in range(4):
                    for kc in range(8):
                        fw.mm(ps[:, o4 * 2:o4 * 2 + 2], wt[:, kc, o4 * 128:(o4 + 1) * 128], scT[:, kc, :],
                              start=(kc == 0), stop=(kc == 7))
                fw.tt(modv[:, ot * 4:(ot + 1) * 4, :], ps[:, 0:8].re("p (a b) -> p a b", a=4),
                      VC("bmod", layer * 48 + ot * 4, n=4).un(2).bc([128, 4, 2]), ALU.add)
                yield
            for which, (gname, sc0) in enumerate((("gmix", 8), ("gmlp", 32))):
                fw.ts(modA[:, which, :, :], modv[:, sc0:sc0 + 8, :], 1.0, ALU.add)
                fw.tt(modA[:, which, :, :], modA[:, which, :, :],
                      VC(gname, layer * 8, n=8).un(2).bc([128, 8, 2]), ALU.mult)

        def rstd_bc(ph, name, nfeat):
            sqp = Rot([fw.sb("%s_sq%d" % (name, i), [128, 512], BF16, stack=ph) for i in range(3)])
            rsp = Rot([fw.sb("%s_rs%d" % (name, i), [128, 512], F32, stack=ph) for i in range(2)])

            def f(srcs):
                ps = mmp.next()
                n = len(srcs)
                for i, s in enumerate(srcs):
                    sq = sqp.next()
                    fw.act(sq[:, :], s, AF.Square)
                    fw.mm(ps[:, :], onesb[:, :], sq[:, :], start=(i == 0), stop=(i == n - 1))
                rs = rsp.next()
                fw.act(rs[:, :], ps[:, :], AF.Sqrt, bias=EPS, scale=1.0 / nfeat)
                fw.recip(rs[:, :], rs[:, :])
                return rs
            return f

        def modulate(ph, name, hT, which, tts, t0, ntt):
            rfn = rstd_bc(ph, name, D)
            tmpp = Rot([fw.sb("%s_tmp%d" % (name, i), [128, 512], F32, stack=ph) for i in range(3)])
            sh0 = 0 if which == 0 else 24
            for tt in tts:
                r = 0 if tt < 2 else 1
                rs = rfn([xv(kc, tt) for kc in range(8)])
                for kc in range(8):
                    tmp = tmpp.next()
                    fw.tt(tmp[:, :], xv(kc, tt), rs[:, :], ALU.mult)
                    lt = tt - t0
                    fw.act(hT.p(kc * ntt + lt)[:, kc, lt * 512:(lt + 1) * 512], tmp[:, :], AF.Identity,
                           bias=modv[:, sh0 + kc, r:r + 1], scale=modA[:, which, kc, r:r + 1])

        def mlp(layer):
            with ExitStack() as ph:
                hT = fw.sb("hTm", [128, 8, 2048], BF16, 32, stack=ph)
                h1 = fw.sb("h1", [128, 8, 2048], BF16, 32, stack=ph)
                rl = Rot([fw.sb("rl%d" % i, [128, 512], F32, stack=ph) for i in range(3)])
                modulate(ph, "nm", hT, 1, range(4), 0, 4)
                for blk in range(4):
                    for ht in range(2):
                        c0 = blk * 1024 + ht * 512
                        wt = wload(dr["w_mlp_in"][layer, :, c0:c0 + 512], 8, 512)
                        for h4 in range(4):
                            hc = ht * 4 + h4
                            for tt in range(4):
                                ps = mmp.next()
                                for kc in range(8):
                                    fw.mm(ps[:, :], wt[:, kc, h4 * 128:(h4 + 1) * 128],
                                          hT.p(kc * 4 + tt)[:, kc, tt * 512:(tt + 1) * 512],
                                          start=(kc == 0), stop=(kc == 7))
                                r = rl.next()
                                fw.act(r[:, :], ps[:, :], AF.Relu)
                                fw.tt(h1.p(hc * 4 + tt)[:, hc, tt * 512:(tt + 1) * 512], r[:, :], r[:, :], ALU.mult)
                    for ot in range(2):
                        wt = wload(dr["w_mlp_out"][layer, blk * 1024:(blk + 1) * 1024, ot * 512:(ot + 1) * 512], 8, 512)
                        for o4 in range(4):
                            oc = ot * 4 + o4
                            for tt in range(4):
                                r = 0 if tt < 2 else 1
                                ps = mmp.next()
                                for kc in range(8):
                                    fw.mm(ps[:, :], wt[:, kc, o4 * 128:(o4 + 1) * 128],
                                          h1.p(kc * 4 + tt)[:, kc, tt * 512:(tt + 1) * 512],
                                          start=(kc == 0), stop=(kc == 7))
                                fw.stt(xv(oc, tt), ps[:, :], modv[:, 40 + oc, r:r + 1], xv(oc, tt), ALU.mult, ALU.add)
                fw.barrier()
                fw.flush()

        def final_out(lvl=9):
            with ExitStack() as ph:
                rfn = rstd_bc(ph, "fn", D)
                yT = fw.sb("yT", [128, 8, 512], F32, stack=ph)
                ost = Rot([fw.sb("ost%d" % i, [128, D], F32, stack=ph) for i in range(2)])
                for tt in range(4):
                    rs = rfn([xv(kc, tt) for kc in range(8)])
                    if lvl < 2:
                        continue
                    for kc in range(8):
                        fw.stt(yT[:, kc, :], xv(kc, tt), VC("fing", kc), rs[:, :], ALU.mult, ALU.mult)
                    if lvl < 3:
                        continue
                    for q in range(4):
                        ti = tt * 4 + q
                        o = ost.next()
                        for half in range(2):
                            ps = mmp.next()
                            for a in range(4):
                                kc = half * 4 + a
                                fw.tr(ps[:, a * 128:(a + 1) * 128], yT[:, kc, q * 128:(q + 1) * 128], C("ident"))
                            evac(o[:, half * 512:(half + 1) * 512], ps[:, :])
                        if lvl >= 4:
                            fw.dma(dv("y")[ti * 128:(ti + 1) * 128, :], o[:, :])
                fw.barrier()
                fw.flush()

        def out_proj_even(layer, g, catT):
            j = layer // 2
            for ot in range(2):
                wt = wload(dr["w_out_even"][j, :, ot * 512:(ot + 1) * 512], 8, 512)
                for o4 in range(4):
                    oc = ot * 4 + o4
                    for tg in range(2):
                        tt = 2 * g + tg
                        ps = mmp.next()
                        for kc in range(8):
                            fw.mm(ps[:, :], wt[:, kc, o4 * 128:(o4 + 1) * 128],
                                  catT.p(kc * 2 + tg)[:, kc, tg * 512:(tg + 1) * 512], start=(kc == 0), stop=(kc == 7))
                        fw.stt(xv(oc, tt), ps[:, :], modv[:, 16 + oc, g:g + 1], xv(oc, tt), ALU.mult, ALU.add)

        def out_proj_odd(ph, layer, g, catT):
            j = layer // 2
            rfn = rstd_bc(ph, "on", 2 * D)
            rsk = fw.sb("rsk", [128, 1024], F32, stack=ph)
            tmpp = Rot([fw.sb("opt%d" % i, [128, 512], F32, stack=ph) for i in range(2)])
            for tg in range(2):
                rs = rfn([catT.p(c * 2 + tg)[:, c, tg * 512:(tg + 1) * 512] for c in range(16)])
                fw.cp("dve", rsk[:, tg * 512:(tg + 1) * 512], rs[:, :])
            for ot in range(4):
                wt = wload(dr["w_out_odd"][j, :, ot * 256:(ot + 1) * 256], 16, 256)
                fw.tt(wt, wt, VC("sng", j * 16, n=16).un(2).bc([128, 16, 256]), ALU.mult)
                for o2 in range(2):
                    oc = ot * 2 + o2
                    for tg in range(2):
                        tt = 2 * g + tg
                        ps = mmp.next()
                        for kc in range(16):
                            fw.mm(ps[:, :], wt[:, kc, o2 * 128:(o2 + 1) * 128],
                                  catT.p(kc * 2 + tg)[:, kc, tg * 512:(tg + 1) * 512], start=(kc == 0), stop=(kc == 15))
                        tmp = tmpp.next()
                        fw.tt(tmp[:, :], ps[:, :], rsk[:, tg * 512:(tg + 1) * 512], ALU.mult)
                        fw.stt(xv(oc, tt), tmp[:, :], modv[:, 16 + oc, g:g + 1], xv(oc, tt), ALU.mult, ALU.add)
        def rsum(out, in_):
            o, i = _ap(out), _ap(in_)
            fw.op("dve", lambda e: e.reduce_sum(out=o, in_=i, axis=mybir.AxisListType.X), _deps(in_), _deps(out))

        def even_mixer(ph, layer, g, hT, catT, dbg=None):
            j = layer // 2
            ctx = (g == 0)
            nseq, L = (4, 256) if ctx else (1, 1024)
            tps = L // 128

            def hv(kc, lo, n):
                return hT.p(kc * 2 + lo // 512)[:, kc, lo:lo + n]

            with ExitStack() as ua:
                qT = fw.sb("qT", [64, 4, 1024], BF16, 4, stack=ua)
                kT = fw.sb("kT", [64, 1536], BF16, stack=ua)
                V1 = fw.sb("V1", [128, 12, 65], BF16, stack=ua)
                otok = fw.sb("otok", [128, 8, 256], BF16, 8, stack=ua)
                raw = Rot([fw.sb("araw%d" % i, [64, 1024], F32, stack=ua) for i in range(2)])
                qn = Rot([fw.sb("aqn%d" % i, [64, 512], F32, stack=ua) for i in range(2)])
                sqb = Rot([fw.sb("asq%d" % i, [64, 512], BF16, stack=ua) for i in range(2)])
                rsb = Rot([fw.sb("ars%d" % i, [64, 512], F32, stack=ua) for i in range(2)])
                t1p = Rot([fw.sb("at1%d" % i, [64, 512], F32, stack=ua) for i in range(2)])
                t2p = Rot([fw.sb("at2%d" % i, [64, 512], F32, stack=ua) for i in range(2)])
                ptp = Rot([fw.sb("apt%d" % i, [128, 512], BF16, stack=ua) for i in range(3)])
                rdp = Rot([fw.sb("ard%d" % i, [128, 1], F32, stack=ua) for i in range(4)])
                vst = Rot([fw.sb("avs%d" % i, [128, 64], F32, stack=ua) for i in range(2)])
                kst = Rot([fw.sb("aks%d" % i, [128, 64], F32, stack=ua) for i in range(2)])
                cks = Rot([fw.sb("ack%d" % i, [128, 128], F32, stack=ua) for i in range(2)])
                if not ctx:
                    rope = fw.sb("ropeT", [64, 2048], F32, stack=ua)
                    fw.dma(rope[:, :], dv("rope"))
                fw.memset(V1[:, :, 64:65], 1.0)
                for g2 in range(2):
                    wt = wload(dr["wie"][j, :, g2 * 384:(g2 + 1) * 384], 8, 384)

                    def proj_norm(col0, gain_col, dst_fn, is_k):
                        r = raw.next()
                        for tg in range(2):
                            ps = mmp.next()
                            for kc in range(8):
                                fw.mm(ps[0:64, :], wt[:, kc, col0:col0 + 64], hv(kc, tg * 512, 512),
                                      start=(kc == 0), stop=(kc == 7))
                            evac(r[:, tg * 512:(tg + 1) * 512], ps[0:64, :])
                        for tg in range(2):
                            sl = slice(tg * 512, (tg + 1) * 512)
                            sq = sqb.next()
                            fw.act(sq[:, :], r[:, sl], AF.Square)
                            ps = mmp.next()
                            fw.mm(ps[0:64, :], onesb[0:64, 0:64], sq[:, :])
                            rs = rsb.next()
                            fw.act(rs[:, :], ps[0:64, :], AF.Sqrt, bias=EPS, scale=1.0 / 64)
                            fw.recip(rs[:, :], rs[:, :])
                            if ctx and not is_k:
                                fw.stt(dst_fn(sl), r[:, sl], gain_col, rs[:, :], ALU.mult, ALU.mult)
                                continue
                            q = qn.next()
                            fw.stt(q[:, :], r[:, sl], gain_col, rs[:, :], ALU.mult, ALU.mult)
                            if ctx:
                                fw.cp("act", dst_fn(sl), q[:, :])
                                for t4 in range(4):
                                    ti = tg * 4 + t4
                                    ps2 = mmp.next()
                                    fw.tr(ps2[:, 0:64], q[:, t4 * 128:(t4 + 1) * 128], C("ident", 64, 0, 64))
                                    ks = kst.next()
                                    evac(ks[:, :], ps2[:, 0:64])
                                    fw.dma(dv("nk")[j, ti * 128:(ti + 1) * 128, g2 * 64:(g2 + 1) * 64], ks[:, :])
                            else:
                                ps2 = mmp.next()
                                fw.mm(ps2[0:64, :], C("ropeR", 64), q[:, :])
                                t1 = t1p.next()
                                fw.tt(t1[:, :], q[:, :], rope[:, sl], ALU.mult)
                                t2 = t2p.next()
                                fw.tt(t2[:, :], ps2[0:64, :], rope[:, 1024 + tg * 512:1024 + (tg + 1) * 512], ALU.mult)
                                fw.tt(dst_fn(sl), t1[:, :], t2[:, :], ALU.add)

                    for hh in range(4):
                        proj_norm(hh * 64, VC("qkg", j * 2 + 0, rows=64),
                                  (lambda sl, hh=hh: qT.p(hh)[:, hh, sl]), False)
                    proj_norm(256, VC("qkg", j * 2 + 1, rows=64), (lambda sl: kT[:, sl]), True)
                    for ti in range(8):
                        ps = mmp.next()
                        for kc in range(8):
                            fw.mm(ps[:, 0:64], hv(kc, ti * 128, 128), wt[:, kc, 320:384], start=(kc == 0), stop=(kc == 7))
                        fw.cp("act", V1[:, ti, 0:64], ps[:, 0:64])
                        if ctx:
                            vs = vst.next()
                            fw.cp("dve", vs[:, :], ps[:, 0:64])
                            fw.dma(dv("nv")[j, ti * 128:(ti + 1) * 128, g2 * 64:(g2 + 1) * 64], vs[:, :])
                    if not ctx:
                        for t in range(4):
                            ck = cks.next()
                            fw.dma(ck[:, 0:64], dv("ctxk")[j, t * 128:(t + 1) * 128, g2 * 64:(g2 + 1) * 64])
                            fw.dma(ck[:, 64:128], dv("ctxv")[j, t * 128:(t + 1) * 128, g2 * 64:(g2 + 1) * 64])
                            ps2 = mmp.next()
                            fw.tr(ps2[0:64, 0:128], ck[:, 0:64], C("ident"))
                            fw.cp("act", kT[:, 1024 + t * 128:1024 + (t + 1) * 128], ps2[0:64, 0:128])
                            fw.cp("dve", V1[:, 8 + t, 0:64], ck[:, 64:128])
                    QB = 256 if ctx else 512
                    for s in range(nseq):
                        if ctx:
                            kts = [(slice(ti * 128, (ti + 1) * 128), ti) for ti in (2 * s, 2 * s + 1)]
                        else:
                            kts = [(slice(t * 128, (t + 1) * 128), t) for t in range(12)]
                        for hh in range(4):
                            for qb in range(L // QB):
                                q0 = s * L + qb * QB
                                oacc = accp.next()
                                for idx, (ksl, vt) in enumerate(kts):
                                    ps = mmp.next()
                                    fw.mm(ps[:, 0:QB], kT[:, ksl], qT.p(hh)[:, hh, q0:q0 + QB])
                                    pt = ptp.next()
                                    fw.act(pt[:, 0:QB], ps[:, 0:QB], AF.Exp, scale=0.125)
                                    nqs = QB // 128
                                    for qs in range(nqs):
                                        fw.mm(oacc[:, qs * 128:qs * 128 + 65], pt[:, qs * 128:(qs + 1) * 128],
                                              V1[:, vt, :], start=(idx == 0 and qs == 0),
                                              stop=(idx == len(kts) - 1 and qs == nqs - 1))
                                for qs in range(QB // 128):
                                    ti = q0 // 128 + qs
                                    rd = rdp.next()
                                    fw.recip(rd[:, :], oacc[:, qs * 128 + 64:qs * 128 + 65])
                                    fw.act(otok.p(ti)[:, ti, hh * 64:(hh + 1) * 64], oacc[:, qs * 128:qs * 128 + 64],
                                           AF.Identity, scale=rd[:, 0:1])
                    for ti in range(8):
                        for c2 in range(2):
                            pT = trp.next()
                            fw.tr(pT[:, 0:128], otok.p(ti)[:, ti, c2 * 128:(c2 + 1) * 128], identb[:, :])
                            c = 2 * g2 + c2
                            evac(catT.p(c * 2 + ti // 4)[:, c, ti * 128:(ti + 1) * 128], pT[:, 0:128])
                    pump()
                fw.barrier()
                fw.flush()

            if dbg == "dbg_att":
                return
            with ExitStack() as ug:
                raw = Rot([fw.sb("graw%d" % i, [128, 1024], F32, stack=ug) for i in range(2)])
                cac = Rot([fw.sb("gcac%d" % i, [128, 1024], F32, stack=ug) for i in range(2)])
                qkf = Rot([fw.sb("gqkf%d" % i, [128, 1024], F32, stack=ug) for i in range(1)])
                fT = fw.sb("gfT", [128, 3, 1024], BF16, 3, stack=ug)
                ktok = fw.sb("gktok", [128, 8, 128], BF16, 8, stack=ug)
                vtok = fw.sb("gvtok", [128, 8, 128], BF16, 8, stack=ug)
                sz = fw.sb("gsz", [128, 8, 128], F32, 8, stack=ug)
                zraw = fw.sb("gzraw", [128, 8, 132], F32, 8, stack=ug)
                scr = zraw[:, :, 128:132]
                gs = fw.sb("ggs", [128, 10, 8, 2], F32, stack=ug)
                tmpa = fw.sb("gtmpa", [128, 8], F32, stack=ug)
                tmpb = fw.sb("gtmpb", [128, 8], F32, stack=ug)
                negA = fw.sb("gnegA", [128, 2], F32, stack=ug)
                decb = fw.sb("gdecb", [128, 2, 16], F32, stack=ug)
                oac = fw.sb("goac", [128, 8, 128], F32, 8, stack=ug)
                ssc = fw.sb("gssc", [128, 8], F32, stack=ug)
                junk = Rot([fw.sb("gjunk%d" % i, [128, 128], F32, stack=ug) for i in range(2)])
                onb = Rot([fw.sb("gonb%d" % i, [128, 128], BF16, stack=ug) for i in range(2)])
                sqb = Rot([fw.sb("gsq%d" % i, [128, 512], BF16, stack=ug) for i in range(2)])
                rsb = Rot([fw.sb("grs%d" % i, [128, 512], F32, stack=ug) for i in range(2)])
                S = [fw.sb("gS%d" % d, [128, 128], F32, stack=ug) for d in range(2)]
                Sb = [fw.sb("gSb%d" % d, [128, 128], BF16, stack=ug) for d in range(2)]
                dg = [fw.sb("gdg%d" % d, [128, 256], F32, stack=ug) for d in range(2)]
                dgb = [fw.sb("gdgb%d" % d, [128, 128], BF16, stack=ug) for d in range(2)]
                nmb = [fw.sb("gnmb%d" % d, [128, 256], BF16, stack=ug) for d in range(2)]
                for d in range(2):
                    fw.cp("act", nmb[d][:, :], C("nmF" if d == 0 else "nmB"))
                E2 = [fw.sb("gE2%d" % d, [128, 256], F32, stack=ug) for d in range(2)]
                Mm = [fw.sb("gMm%d" % d, [128, 128], F32, stack=ug) for d in range(2)]
                qkT = [[fw.sb("gqkT%d_%d" % (d, i), [128, 128], BF16, stack=ug) for i in range(2)] for d in range(2)]
                Bm = [[fw.sb("gB%d_%d" % (d, i), [128, 128], F32, stack=ug) for i in range(2)] for d in range(2)]
                AS = [[fw.sb("gAS%d_%d" % (d, i), [128, 256], F32, stack=ug) for i in range(2)] for d in range(2)]
                TT = [fw.sb("gTT%d" % d, [128, 128], BF16, stack=ug) for d in range(2)]
                vb = [fw.sb("gvb%d" % d, [128, 128], BF16, stack=ug) for d in range(2)]
                kbg = [fw.sb("gkbg%d" % d, [128, 128], BF16, stack=ug) for d in range(2)]
                kdec = [[fw.sb("gkdec%d_%d" % (d, i), [128, 128], BF16, stack=ug) for i in range(2)] for d in range(2)]
                qdT = [[fw.sb("gqdT%d_%d" % (d, i), [128, 128], BF16, stack=ug) for i in range(2)] for d in range(2)]
                usb = [[fw.sb("gusb%d_%d" % (d, i), [128, 128], F32, stack=ug) for i in range(2)] for d in range(2)]
                wTs = [[fw.sb("gwT%d_%d" % (d, i), [128, 128], BF16, stack=ug) for i in range(2)] for d in range(2)]
                vnew = [fw.sb("gvn%d" % d, [128, 128], BF16, stack=ug) for d in range(2)]
                for d in range(2):
                    fw.memset(vnew[d][:, :], 0.0)
                for hd in range(4):
                    base = 768 + hd * 640
                    wtA = wload(dr["wie"][j, :, base:base + 384], 8, 384)
                    import os
                    SK = os.environ.get("DBGSKIP", "")
                    if "wtb" not in SK:
                        wtB = wload(dr["wie"][j, :, base + 384:base + 640], 8, 256)
                    for wi in range(3):
                        r = raw.next()
                        for tg in range(2):
                            ps = mmp.next()
                            for kc in range(8):
                                fw.mm(ps[:, :], wtA[:, kc, wi * 128:(wi + 1) * 128], hv(kc, tg * 512, 512),
                                      start=(kc == 0), stop=(kc == 7))
                            evac(r[:, tg * 512:(tg + 1) * 512], ps[:, :])
                        a = cac.next()
                        cwc = lambda tap: VC("dconv", ((j * 4 + hd) * 3 + wi) * 3 + tap)
                        r3 = r[:, :].re("p (s l) -> p s l", s=nseq)
                        a3 = a[:, :].re("p (s l) -> p s l", s=nseq)
                        fw.ts(a[:, :], r[:, :], cwc(1), ALU.mult)
                        if "conv" not in SK:
                            fw.stt(a3[:, :, 1:L], r3[:, :, 0:L - 1], cwc(0), a3[:, :, 1:L], ALU.mult, ALU.add)
                            fw.stt(a3[:, :, 0:L - 1], r3[:, :, 1:L], cwc(2), a3[:, :, 0:L - 1], ALU.mult, ALU.add)
                        if wi == 2:
                            fw.act(fT.p(2)[:, 2, :], a[:, :], SILU)
                        else:
                            f = qkf.next()
                            fw.act(f[:, :], a[:, :], SILU)
                            for tg in range(2):
                                sl = slice(tg * 512, (tg + 1) * 512)
                                sq = sqb.next()
                                fw.act(sq[:, :], f[:, sl], AF.Square)
                                ps = mmp.next()
                                fw.mm(ps[:, :], onesb[:, :], sq[:, :])
                                rs = rsb.next()
                                fw.act(rs[:, :], ps[:, :], AF.Sqrt, bias=EPS, scale=1.0)
                                fw.recip(rs[:, :], rs[:, :])
                                fw.stt(fT.p(wi)[:, wi, sl], f[:, sl], (128 ** -0.5) if wi == 0 else 1.0, rs[:, :],
                                       ALU.mult, ALU.mult)
                    for ti in range(8):
                        tsl = slice(ti * 128, (ti + 1) * 128)
                        pT = trp.next()
                        fw.tr(pT[:, 0:128], fT.p(1)[:, 1, tsl], identb[:, :])
                        evac(ktok.p(ti)[:, ti, :], pT[:, 0:128])
                        pT = trp.next()
                        fw.tr(pT[:, 0:128], fT.p(2)[:, 2, tsl], identb[:, :])
                        evac(vtok.p(ti)[:, ti, :], pT[:, 0:128])
                    if dbg == "dbg_gdn1":
                        break
                    for ti in range(8):
                        ps = mmp.next()
                        for kc in range(8):
                            fw.mm(ps[:, 0:256], hv(kc, ti * 128, 128), wtB[:, kc, :], start=(kc == 0), stop=(kc == 7))
                        fw.cp("act", zraw.p(ti)[:, ti, :], ps[:, 0:132])
                        fw.act(sz.p(ti)[:, ti, :], zraw.p(ti)[:, ti, 0:128], SILU)
                    if "scal" in SK:
                        break
                    for d in range(2):
                        col = j * 8 + d * 4 + hd
                        fw.act(negA[:, d:d + 1], VC("dalog", col), AF.Exp)
                        fw.ts(negA[:, d:d + 1], negA[:, d:d + 1], -1.0, ALU.mult)
                        fw.act(tmpa[:, :], scr[:, :, 2 + d:3 + d].re('p t o -> p (t o)'), AF.Exp, bias=VC("ddtb", col), scale=1.0)
                        fw.act(tmpa[:, :], tmpa[:, :], AF.Ln, bias=1.0, scale=1.0)
                        fw.ts(gs[:, 2, :, d], tmpa[:, :], negA[:, d:d + 1], ALU.mult)
                        fw.act(tmpb[:, :], scr[:, :, d:d + 1].re('p t o -> p (t o)'), AF.Exp, scale=-1.0)
                        fw.act(gs[:, 1, :, d], tmpb[:, :], AF.Ln, bias=1.0, scale=1.0)
                        fw.act(gs[:, 0, :, d], gs[:, 1, :, d], AF.Exp, scale=-1.0)
                    if "cums" in SK:
                        break
                    ps = mmp.next()
                    fw.mm(ps[:, 0:8], C("triF"), gs[:, 2, :, 0])
                    fw.mm(ps[:, 8:16], C("triB"), gs[:, 2, :, 1])
                    fw.mm(ps[:, 16:24], C("blk"), gs[:, 2, :, 0])
                    fw.mm(ps[:, 24:32], C("blk"), gs[:, 2, :, 1])
                    for d in range(2):
                        fw.cp("dve", gs[:, 3, :, d], ps[:, d * 8:(d + 1) * 8])
                        fw.cp("dve", gs[:, 4, :, d], ps[:, 16 + d * 8:16 + (d + 1) * 8])
                    fw.act(gs[:, 5, :, :], gs[:, 3, :, :], AF.Exp)
                    fw.tt(gs[:, 6, :, :], gs[:, 4, :, :], gs[:, 3, :, :], ALU.subtract)
                    fw.act(gs[:, 6, :, :], gs[:, 6, :, :], AF.Exp)
                    fw.tt(gs[:, 7, :, :], gs[:, 3, :, :], gs[:, 1, :, :], ALU.subtract)
                    fw.ts(gs[:, 8, :, :], gs[:, 3, :, :], -1.0, ALU.mult)
                    fw.tt(gs[:, 9, :, :], gs[:, 0, :, :], gs[:, 5, :, :], ALU.mult)
                    ps = mmp.next()
                    g2d = gs[:, 2, :, :].re("p t d -> p (t d)")
                    fw.mm(ps[:, 0:16], C("ind0"), g2d)
                    fw.mm(ps[:, 16:32], C("ind1"), g2d)
                    fw.act(decb[:, :, :].re("p c n -> p (c n)"), ps[:, 0:32], AF.Exp)
                    for ti in range(8):
                        fw.memset(oac.p(ti)[:, ti, :], 0.0)
                    if dbg == "dbg_gdn2":
                        break
                    def gcol(k, ti, d):
                        return gs[:, k, ti, d:d + 1]

                    def solve(n):
                        par = n % 2
                        slots = [(n, 0), (7 - n, 1)]
                        for ti, d in slots:
                            fw.ts(dg[d][:, 0:128], C("ident"), gcol(3, ti, d), ALU.mult)
                            fw.ts(dg[d][:, 128:256], C("ident"), gcol(7, ti, d), ALU.mult)
                            fw.ts(dgb[d][:, :], identb[:, :], gcol(5, ti, d), ALU.mult)
                        warm()
                        yield
                        for ti, d in slots:
                            p = mmp.next()
                            fw.mm(p[:, 0:256], C("ones"), dg[d][:, 0:256], start=True, stop=False)
                            fw.mm(p[:, 0:256], identb[:, :], nmb[d][:, :], start=False, stop=True)
                            fw.mm(p[:, 256:384], onesb[:, :], dgb[d][:, :], start=True, stop=True)
                            fw.act(E2[d][:, :], p[:, 0:256], AF.Exp, bias=gcol(8, ti, d), scale=1.0)
                            fw.tt(qdT[d][par][:, :], fT.p(0)[:, 0, ti * 128:(ti + 1) * 128], p[:, 256:384], ALU.mult)
                        warm()
                        yield
                        for ti, d in slots:
                            tsl = slice(ti * 128, (ti + 1) * 128)
                            p = mmp.next()
                            fw.mm(p[:, 0:128], fT.p(1)[:, 1, tsl], fT.p(0)[:, 0, tsl])
                            fw.mm(p[:, 128:256], fT.p(1)[:, 1, tsl], fT.p(1)[:, 1, tsl])
                            fw.tt(qkT[d][par][:, :], p[:, 0:128], E2[d][:, 0:128], ALU.mult)
                            fw.tt(Mm[d][:, :], p[:, 128:256], E2[d][:, 128:256], ALU.mult)
                        warm()
                        yield
                        for ti, d in slots:
                            p = mmp.next()
                            fw.tr(p[:, 0:128], Mm[d][:, :], C("ident"))
                            fw.cp("act", Bm[d][0][:, :], p[:, 0:128])
                        warm()
                        yield
                        for ti, d in slots:
                            p = mmp.next()
                            fw.mm(p[:, 0:128], Bm[d][0][:, :], Mm[d][:, :])
                            fw.mm(p[:, 128:256], Mm[d][:, :], Bm[d][0][:, :])
                            fw.cp("act", AS[d][0][:, 0:128], p[:, 0:128])
                            fw.cp("dve", Bm[d][1][:, :], p[:, 128:256])
                            fw.tt(AS[d][0][:, 128:256], C("ident"), Mm[d][:, :], ALU.subtract)
                        warm()
                        yield
                        for k in range(1, 6):
                            for ti, d in slots:
                                cur, nxt, Bk, Bn = AS[d][(k - 1) % 2], AS[d][k % 2], Bm[d][k % 2], Bm[d][(k + 1) % 2]
                                p = mmp.next()
                                if k < 5:
                                    if k < 4:
                                        fw.mm(p[:, 0:256], Bk[:, :], cur[:, 0:256])
                                    else:
                                        fw.mm(p[:, 128:256], Bk[:, :], cur[:, 128:256])
                                    fw.mm(p[:, 256:384], cur[:, 0:128], Bk[:, :])
                                    fw.cp("act", Bn[:, :], p[:, 256:384])
                                    if k < 4:
                                        fw.cp("act", nxt[:, 0:128], p[:, 0:128])
                                    fw.tt(nxt[:, 128:256], cur[:, 128:256], p[:, 128:256], ALU.add)
                                else:
                                    fw.mm(p[:, 0:128], Bk[:, :], cur[:, 128:256])
                                    fw.tt(TT[d][:, :], cur[:, 128:256], p[:, 0:128], ALU.add)
                            warm()
                            yield
                        for ti, d in slots:
                            fw.ts(vb[d][:, :], vtok.p(ti)[:, ti, :], gcol(0, ti, d), ALU.mult)
                            fw.ts(kbg[d][:, :], ktok.p(ti)[:, ti, :], gcol(9, ti, d), ALU.mult)
                            fw.act(kdec[d][par][:, :], ktok.p(ti)[:, ti, :], AF.Identity, scale=gcol(6, ti, d))
                        warm()
                        yield
                        for ti, d in slots:
                            p = mmp.next()
                            fw.mm(p[:, 0:128], TT[d][:, :], vb[d][:, :])
                            fw.mm(p[:, 128:256], kbg[d][:, :], TT[d][:, :])
                            fw.cp("act", usb[d][par][:, :], p[:, 0:128])
                            fw.cp("dve", wTs[d][par][:, :], p[:, 128:256])
                        warm()
                        yield

                    def recur(n):
                        par = n % 2
                        slots = [(n, 0), (7 - n, 1)]
                        for ci in range(2):
                            for ti, d in slots:
                                c = ci if d == 0 else 1 - ci
                                rows = slice(c * 64, (c + 1) * 64)
                                first = (ti % tps == 0 and c == 0) if d == 0 else (ti % tps == tps - 1 and c == 1)
                                last = (ti % tps == tps - 1 and c == 1) if d == 0 else (ti % tps == 0 and c == 0)
                                seq = ti // tps
                                if first:
                                    if ctx:
                                        fw.memset(S[d][:, :], 0.0)
                                    else:
                                        fw.dma(S[d][:, :], dv("sd0")[j, d, hd, :, :])
                                    fw.cp("act", Sb[d][:, :], S[d][:, :])
                                p = mmp.next()
                                fw.mm(p[:, 0:128], wTs[d][par][:, :], Sb[d][:, :])
                                fw.tt(vnew[d][rows, :], usb[d][par][rows, :], p[rows, 0:128], ALU.subtract)
                                warm()
                                yield
                                po = mmp.next()
                                fw.mm(po[:, 0:128], qdT[d][par][:, :], Sb[d][:, :], start=True, stop=False)
                                fw.mm(po[:, 0:128], qkT[d][par][rows, :], vnew[d][rows, :], start=False, stop=True)
                                pst = mmp.next()
                                fw.mm(pst[:, 0:128], kdec[d][par][rows, :], vnew[d][rows, :])
                                fw.stt(S[d][:, :], S[d][:, :], decb[:, c, ti * 2 + d:ti * 2 + d + 1], pst[:, 0:128],
                                       ALU.mult, ALU.add)
                                fw.cp("act", Sb[d][:, :], S[d][:, :])
                                fw.tt(oac.p(ti)[rows, ti, :], oac.p(ti)[rows, ti, :], po[rows, 0:128], ALU.add)
                                if last and ctx:
                                    fw.dma(dv("nsd")[seq, j, d, hd, :, :], S[d][:, :])
                                warm()
                                yield

                    def drive(gens):
                        gens = list(gens)
                        while gens:
                            for g_ in list(gens):
                                try:
                                    next(g_)
                                except StopIteration:
                                    gens.remove(g_)

                    drive([solve(0)])
                    for n in range(8):
                        drive([recur(n)] + ([solve(n + 1)] if n < 7 else []))

                    if dbg == "dbg_gdn5":
                        break
                    for ti in range(8):
                        jk = junk.next()
                        fw.tt(jk[:, :], oac.p(ti)[:, ti, :], oac.p(ti)[:, ti, :], ALU.mult)
                        rsum(ssc[:, ti:ti + 1], jk[:, :])
                    fw.act(ssc[:, :], ssc[:, :], AF.Sqrt, bias=EPS, scale=1.0 / 128)
                    fw.recip(ssc[:, :], ssc[:, :])
                    for ti in range(8):
                        ob = onb.next()
                        fw.stt(ob[:, :], oac.p(ti)[:, ti, :], ssc[:, ti:ti + 1], sz.p(ti)[:, ti, :], ALU.mult, ALU.mult)
                        pT = trp.next()
                        fw.tr(pT[:, 0:128], ob[:, :], identb[:, :])
                        c = 4 + hd
                        fw.act(catT.p(c * 2 + ti // 4)[:, c, ti * 128:(ti + 1) * 128], pT[:, 0:128], AF.Identity,
                               scale=VC("dng", j))
                    pump()
                fw.barrier()
                fw.flush()

        def odd_mixer(ph, layer, g, hT, catT):
            j = layer // 2
            ctx = (g == 0)
            nseq, L = (4, 256) if ctx else (1, 1024)
            tps = L // 128

            def hv(kc, lo, n):
                return hT.p(kc * 2 + lo // 512)[:, kc, lo:lo + n]

            with ExitStack() as us:
                raw = Rot([fw.sb("sraw%d" % i, [128, 1024], F32, stack=us) for i in range(1)])
                cac = Rot([fw.sb("scac%d" % i, [128, 1024], F32, stack=us) for i in range(1)])
                fT = fw.sb("sfT", [128, 4, 1024], BF16, 4, stack=us)
                xtok = fw.sb("sxtok", [128, 8, 256], BF16, 8, stack=us)
                Btok = fw.sb("sBtok", [128, 8, 128], BF16, 8, stack=us)
                zs = fw.sb("szs", [128, 8, 256], BF16, 8, stack=us)
                dtr = fw.sb("sdtr", [128, 8, 8], F32, stack=us)
                zr = Rot([fw.sb("szr%d" % i, [128, 264], F32, stack=us) for i in range(2)])
                ss = fw.sb("sss", [128, 7, 8, 8], F32, stack=us)
                negA = fw.sb("snegA", [128, 8], F32, stack=us)
                decb = fw.sb("sdecb", [128, 2, 64], F32, stack=us)
                yac = fw.sb("syac", [128, 8, 256], F32, 8, stack=us)
                xdt = [fw.sb("sxdt%d" % d, [128, 256], BF16, stack=us) for d in range(2)]
                nm4b = [fw.sb("snm4b%d" % d, [128, 4, 128], BF16, stack=us) for d in range(2)]
                for d in range(2):
                    for h in range(4):
                        fw.cp("act", nm4b[d][:, h, :], C("nmFc" if d == 0 else "nmBc"))
                xd = [[fw.sb("sxd%d_%d" % (d, i), [128, 256], BF16, stack=us) for i in range(2)] for d in range(2)]
                cbs = [fw.sb("scbs%d" % d, [128, 128], F32, stack=us) for d in range(2)]
                dgs = [fw.sb("sdgs%d" % d, [128, 4, 128], F32, stack=us) for d in range(2)]
                Ls = dgs
                cbL = [fw.sb("scbL%d" % d, [128, 4, 128], BF16, stack=us) for d in range(2)]
                sT = [fw.sb("ssT%d" % d, [128, 256], F32, stack=us) for d in range(2)]
                sTb = [fw.sb("ssTb%d" % d, [128, 256], BF16, stack=us) for d in range(2)]
                ytmp = Rot([fw.sb("sytmp%d" % i, [128, 256], F32, stack=us) for i in range(2)])
                ygb = Rot([fw.sb("sygb%d" % i, [128, 256], BF16, stack=us) for i in range(2)])
                sst = Rot([fw.sb("ssst%d" % i, [128, 128], F32, stack=us) for i in range(2)])
                for sg in range(8):
                    base = sg * 832
                    wtA = wload(dr["wio"][j, :, base:base + 512], 8, 512)
                    wtB = wload(dr["wio"][j, :, base + 512:base + 832], 8, 320)
                    for wi in range(4):
                        r = raw.next()
                        for tg in range(2):
                            ps = mmp.next()
                            for kc in range(8):
                                fw.mm(ps[:, :], wtA[:, kc, wi * 128:(wi + 1) * 128], hv(kc, tg * 512, 512),
                                      start=(kc == 0), stop=(kc == 7))
                            evac(r[:, tg * 512:(tg + 1) * 512], ps[:, :])
                        a = cac.next()
                        cwc = lambda tap: VC("sconv", ((j * 8 + sg) * 4 + wi) * 4 + tap)
                        r3 = r[:, :].re("p (s l) -> p s l", s=nseq)
                        a3 = a[:, :].re("p (s l) -> p s l", s=nseq)
                        fw.ts(a[:, :], r[:, :], cwc(1), ALU.mult)
                        fw.stt(a3[:, :, 1:L], r3[:, :, 0:L - 1], cwc(0), a3[:, :, 1:L], ALU.mult, ALU.add)
                        fw.stt(a3[:, :, 0:L - 1], r3[:, :, 1:L], cwc(2), a3[:, :, 0:L - 1], ALU.mult, ALU.add)
                        fw.act(fT.p(wi)[:, wi, :], a[:, :], SILU, bias=cwc(3), scale=1.0)
                    for ti in range(8):
                        tsl = slice(ti * 128, (ti + 1) * 128)
                        for c2 in range(2):
                            pT = trp.next()
                            fw.tr(pT[:, 0:128], fT.p(c2)[:, c2, tsl], identb[:, :])
                            evac(xtok.p(ti)[:, ti, c2 * 128:(c2 + 1) * 128], pT[:, 0:128])
                        pT = trp.next()
                        fw.tr(pT[:, 0:128], fT.p(2)[:, 2, tsl], identb[:, :])
                        evac(Btok.p(ti)[:, ti, :], pT[:, 0:128])
                    for ti in range(8):
                        ps = mmp.next()
                        for kc in range(8):
                            fw.mm(ps[:, 0:320], hv(kc, ti * 128, 128), wtB[:, kc, :], start=(kc == 0), stop=(kc == 7))
                        fw.cp("act", zr.next()[:, :], ps[:, 0:264])
                        zr_ = zr.items[(zr.i - 1) % len(zr.items)]
                        fw.act(zs.p(ti)[:, ti, :], zr_[:, 0:256], SILU)
                        fw.cp("act", dtr[:, ti, :], zr_[:, 256:264])
                    for d in range(2):
                        c0 = j * 64 + d * 32 + sg * 4
                        fw.act(negA[:, d * 4:(d + 1) * 4], VC("salog", c0, n=4), AF.Exp)
                        fw.tt(ss[:, 0, :, d * 4:(d + 1) * 4], dtr[:, :, d * 4:(d + 1) * 4],
                              VC("sdtb", c0, n=4).un(1).bc([128, 8, 4]), ALU.add)
                    fw.ts(negA[:, :], negA[:, :], -1.0, ALU.mult)
                    fw.act(ss[:, 0, :, :], ss[:, 0, :, :], AF.Exp)
                    fw.act(ss[:, 0, :, :], ss[:, 0, :, :], AF.Ln, bias=1.0, scale=1.0)
                    fw.tt(ss[:, 1, :, :], ss[:, 0, :, :], negA[:, :].un(1).bc([128, 8, 8]), ALU.mult)
                    ps = mmp.next()
                    fw.mm(ps[:, 0:32], C("triF"), ss[:, 1, :, 0:4])
                    fw.mm(ps[:, 32:64], C("triB"), ss[:, 1, :, 4:8])
                    fw.mm(ps[:, 64:128], C("blk"), ss[:, 1, :, :])
                    for d in range(2):
                        fw.cp("dve", ss[:, 2, :, d * 4:(d + 1) * 4], ps[:, d * 32:(d + 1) * 32].re("p (t h) -> p t h", h=4))
                    fw.cp("dve", ss[:, 3, :, :], ps[:, 64:128].re("p (t h) -> p t h", h=8))
                    fw.act(ss[:, 4, :, :], ss[:, 2, :, :], AF.Exp)
                    fw.tt(ss[:, 5, :, :], ss[:, 3, :, :], ss[:, 2, :, :], ALU.subtract)
                    fw.act(ss[:, 5, :, :], ss[:, 5, :, :], AF.Exp)
                    fw.ts(ss[:, 6, :, :], ss[:, 2, :, :], -1.0, ALU.mult)
                    ps = mmp.next()
                    a2d = ss[:, 1, :, :].re("p t h -> p (t h)")
                    fw.mm(ps[:, 0:64], C("ind0"), a2d)
                    fw.mm(ps[:, 64:128], C("ind1"), a2d)
                    fw.act(decb[:, :, :].re("p c n -> p (c n)"), ps[:, 0:128], AF.Exp)
                    for ti in range(8):
                        fw.tt(yac.p(ti)[:, ti, :].re("p (h e) -> p h e", h=4),
                              xtok.p(ti)[:, ti, :].re("p (h e) -> p h e", h=4),
                              VC("sd", j * 32 + sg * 4, n=4).un(2).bc([128, 4, 64]), ALU.mult)
                    def lpart(n):
                        par = n % 2
                        slots = [(n, 0), (7 - n, 1)]
                        for ti, d in slots:
                            tsl = slice(ti * 128, (ti + 1) * 128)
                            dh = slice(d * 4, (d + 1) * 4)
                            x3 = xtok.p(ti)[:, ti, :].re("p (h e) -> p h e", h=4)
                            fw.tt(xdt[d][:, :].re("p (h e) -> p h e", h=4), x3,
                                  ss[:, 0, ti, dh].un(2).bc([128, 4, 64]), ALU.mult)
                            fw.tt(xd[d][par][:, :].re("p (h e) -> p h e", h=4), xdt[d][:, :].re("p (h e) -> p h e", h=4),
                                  ss[:, 5, ti, dh].un(2).bc([128, 4, 64]), ALU.mult)
                            p = mmp.next()
                            fw.mm(p[:, 0:128], fT.p(2)[:, 2, tsl], fT.p(3)[:, 3, tsl])
                            fw.cp("act", cbs[d][:, :], p[:, 0:128])
                            fw.tt(dgs[d][:, :, :], C("ident").un(1).bc([128, 4, 128]),
                                  ss[:, 2, ti, dh].un(2).bc([128, 4, 128]), ALU.mult)
                        warm()
                        yield
                        for ti, d in slots:
                            pL = mmp.next()
                            fw.mm(pL[:, :], C("ones"), dgs[d][:, :, :].re("p h i -> p (h i)"), start=True, stop=False)
                            fw.mm(pL[:, :], identb[:, :], nm4b[d][:, :, :].re("p h i -> p (h i)"), start=False, stop=True)
                            for h in range(4):
                                fw.act(Ls[d][:, h, :], pL[:, h * 128:(h + 1) * 128], AF.Exp,
                                       bias=ss[:, 6, ti, d * 4 + h:d * 4 + h + 1], scale=1.0)
                            warm()
                            yield
                        for ti, d in slots:
                            fw.tt(cbL[d][:, :, :], Ls[d][:, :, :], cbs[d][:, :].un(1).bc([128, 4, 128]), ALU.mult)
                        warm()
                        yield
                        for ti, d in slots:
                            pY = mmp.next()
                            for h in range(4):
                                fw.mm(pY[:, h * 64:(h + 1) * 64], cbL[d][:, h, :], xdt[d][:, h * 64:(h + 1) * 64])
                            fw.tt(yac.p(ti)[:, ti, :], yac.p(ti)[:, ti, :], pY[:, 0:256], ALU.add)
                            warm()
                            yield

                    def srec(n):
                        par = n % 2
                        slots = [(n, 0), (7 - n, 1)]
                        for ci in range(2):
                            for ti, d in slots:
                                tsl = slice(ti * 128, (ti + 1) * 128)
                                c = ci if d == 0 else 1 - ci
                                rows = slice(c * 64, (c + 1) * 64)
                                first = (ti % tps == 0 and c == 0) if d == 0 else (ti % tps == tps - 1 and c == 1)
                                last = (ti % tps == tps - 1 and c == 1) if d == 0 else (ti % tps == 0 and c == 0)
                                seq = ti // tps
                                if first:
                                    if ctx:
                                        fw.memset(sT[d][:, :], 0.0)
                                    else:
                                        for half in range(2):
                                            st_ = sst.next()
                                            fw.dma(st_[:, :], dv("ss0")[j, d, 2 * sg + half, :, :])
                                            p = mmp.next()
                                            fw.tr(p[:, 0:128], st_[:, :], C("ident"))
                                            fw.cp("act", sT[d][:, half * 128:(half + 1) * 128], p[:, 0:128])
                                    fw.cp("act", sTb[d][:, :], sT[d][:, :])
                                po = mmp.next()
                                fw.mm(po[:, 0:256], fT.p(3)[:, 3, tsl], sTb[d][:, :])
                                pS = mmp.next()
                                fw.mm(pS[:, 0:256], Btok.p(ti)[rows, ti, :], xd[d][par][rows, :])
                                yt = ytmp.next()
                                fw.tt(yt[rows, :].re("p (h e) -> p h e", h=4), po[rows, 0:256].re("p (h e) -> p h e", h=4),
                                      ss[rows, 4, ti, d * 4:(d + 1) * 4].un(2).bc([64, 4, 64]), ALU.mult)
                                fw.tt(sT[d][:, :].re("p (h e) -> p h e", h=4), sT[d][:, :].re("p (h e) -> p h e", h=4),
                                      decb[:, c, ti * 8 + d * 4:ti * 8 + d * 4 + 4].un(2).bc([128, 4, 64]), ALU.mult)
                                fw.tt(sT[d][:, :], sT[d][:, :], pS[:, 0:256], ALU.add)
                                fw.cp("act", sTb[d][:, :], sT[d][:, :])
                                fw.tt(yac.p(ti)[rows, ti, :], yac.p(ti)[rows, ti, :], yt[rows, :], ALU.add)
                                if last and ctx:
                                    for half in range(2):
                                        p = mmp.next()
                                        fw.tr(p[:, 0:128], sT[d][:, half * 128:(half + 1) * 128], C("ident"))
                                        st_ = sst.next()
                                        evac(st_[:, :], p[:, 0:128])
                                        fw.dma(dv("nss")[seq, j, d, 2 * sg + half, :, :], st_[:, :])
                                warm()
                                yield

                    def sdrive(gens):
                        gens = list(gens)
                        while gens:
                            for g_ in list(gens):
                                try:
                                    next(g_)
                                except StopIteration:
                                    gens.remove(g_)

                    sdrive([lpart(0)])
                    for n in range(8):
                        sdrive([srec(n)] + ([lpart(n + 1)] if n < 7 else []))
                    for ti in range(8):
                        yg = ygb.next()
                        fw.tt(yg[:, :], yac.p(ti)[:, ti, :], zs.p(ti)[:, ti, :], ALU.mult)
                        for c2 in range(2):
                            pT = trp.next()
                            fw.tr(pT[:, 0:128], yg[:, c2 * 128:(c2 + 1) * 128], identb[:, :])
                            c = 2 * sg + c2
                            evac(catT.p(c * 2 + ti // 4)[:, c, ti * 128:(ti + 1) * 128], pT[:, 0:128])
                    pump()
                fw.barrier()
                fw.flush()

        dbg = stop if (stop or "").startswith("dbg") else None
        nl_run = n_layers if (stop is None or dbg) else 0
        if nl_run > 0:
            for _ in mod_vectors(0):
                pass
        for layer in range(nl_run):
            modv, modA = modv2[layer % 2], modA2[layer % 2]
            pump_holder[0] = mod_vectors(layer + 1) if layer + 1 < nl_run else None
            if dbg == "dbg_mod":
                break
            for g in range(2):
                with ExitStack() as ph:
                    hT = fw.sb("hTg", [128, 8, 1024], BF16, 16, stack=ph)
                    with ExitStack() as ph2:
                        modulate(ph2, "nx", hT, 0, [2 * g, 2 * g + 1], 2 * g, 2)
                        fw.barrier()
                        fw.flush()
                    if dbg == "dbg_norm":
                        continue
                    if layer % 2 == 0:
                        catT = fw.sb("catT", [128, 8, 1024], BF16, 16, stack=ph)
                        even_mixer(ph, layer, g, hT, catT, dbg)
                        if dbg is None or dbg == "dbg_mlp":
                            out_proj_even(layer, g, catT)
                    else:
                        catT = fw.sb("catT", [128, 16, 1024], BF16, 32, stack=ph)
                        odd_mixer(ph, layer, g, hT, catT)
                        out_proj_odd(ph, layer, g, catT)
                    fw.barrier()
                    fw.flush()
            pump(100)
            if dbg in (None, "dbg_mlp"):
                mlp(layer)
        if stop is None or dbg:
            final_out()
        elif stop.startswith("final"):
            final_out(int(stop[5:]))
        fw.barrier(final=True)
        fw.flush()
    return nc, fw.n_instr


_PROG = {}


def _run(inp, n_layers=DEPTH, ncores=8):
    if n_layers not in _PROG:
        _PROG[n_layers] = build(n_layers)
    nc, _ = _PROG[n_layers]
    f = lambda k: np.ascontiguousarray(np.asarray(inp[k], dtype=np.float32))
    consts = make_consts()
    rope = np.ascontiguousarray(make_rope().reshape(64, 2048))
    vecs = make_vecs(inp)
    wie = permute_cols(f("w_in_even"), even_col_perm())
    wio = permute_cols(f("w_in_odd"), odd_col_perm())
    shared = {
        "consts": consts, "rope": rope, "vecs": vecs,
        "w_mod": f("w_mod"), "w_mlp_in": f("w_mlp_in"), "w_mlp_out": f("w_mlp_out"),
        "wie": wie, "w_out_even": f("w_out_even"), "wio": wio, "w_out_odd": f("w_out_odd"),
    }
    xp, xs, c, cctx = f("x_prompt"), f("x_sample"), f("c"), f("c_ctx")
    ck, cv, sd, ssm = f("cache_attn_k"), f("cache_attn_v"), f("state_delta"), f("state_ssm")
    in_maps = []
    for i in range(ncores):
        b = i % 4
        cond = np.stack([cctx, c[b]], axis=0)
        condT = np.ascontiguousarray(cond.reshape(2, 8, 128).transpose(2, 1, 0)).reshape(128, 16)
        m = dict(shared)
        m["xin"] = np.ascontiguousarray(np.concatenate([xp[4 * i:4 * i + 4].reshape(1024, D), xs[b]], axis=0))
        m["condT"] = condT
        m["ctxk"] = np.ascontiguousarray(ck[b].reshape(2, 512, 128))
        m["ctxv"] = np.ascontiguousarray(cv[b].reshape(2, 512, 128))
        m["sd0"] = np.ascontiguousarray(sd[b])
        m["ss0"] = np.ascontiguousarray(ssm[b].reshape(2, 2, 16, 128, 128))
        in_maps.append(m)
    res = run_bass_kernel_spmd(nc, in_maps, core_ids=list(range(ncores)))
    R = res.results
    if ncores < 8:
        return R
    y_p = np.concatenate([R[i]["y"][:1024].reshape(4, 256, D) for i in range(8)], axis=0)
    y_s = np.stack([R[b]["y"][1024:] for b in range(4)], axis=0)
    nk = np.concatenate([R[i]["nk"].reshape(2, 4, 256, 2, 64).transpose(1, 0, 2, 3, 4) for i in range(8)], axis=0)
    nv = np.concatenate([R[i]["nv"].reshape(2, 4, 256, 2, 64).transpose(1, 0, 2, 3, 4) for i in range(8)], axis=0)
    nsd = np.concatenate([R[i]["nsd"] for i in range(8)], axis=0)
    nss = np.concatenate([R[i]["nss"].reshape(4, 2, 2, 32, 64, 128) for i in range(8)], axis=0)
    out = (y_p, y_s, nk, nv, nsd, nss)
    return tuple(np.ascontiguousarray(o, dtype=np.float32) for o in out)


def kernel(**inputs):
    return _run(inputs, DEPTH)
```
